# Optimizing a Trainium2 kernel written in Bass

```python
import jax
import jax.numpy as jnp
from jax import lax
import numpy as np


D_MODEL = 1024
BATCH = 4
SEQ = 8192
DEPTH = 2
DEC_BATCH = 8
DEC_SEQ = 2048
PAST_LEN = 128

GRID_W = 64
NA_HEADS = 8
NA_HEAD_DIM = 64
NA_WIDTH = NA_HEADS * NA_HEAD_DIM
NA_WIN_ROWS = 8
NA_WIN_COLS = 16
RW_HEADS = 8
RW_HEAD_DIM = 64
RW_WIDTH = RW_HEADS * RW_HEAD_DIM
DECAY_LORA = 64
AAA_LORA = 64
GATE_LORA = 160
MEM_TOKENS = 256
MEM_HEADS = 4
MEM_HEAD_DIM = 128
MEM_WIDTH = MEM_HEADS * MEM_HEAD_DIM
BRANCH_WIDTH = 512
N_BRANCH = 3
D_FF = 2816
RMS_EPS = 1e-6
GN_EPS = 64e-5
RW_COLS = 3 * RW_WIDTH + 2 * DECAY_LORA + 2 * AAA_LORA + GATE_LORA
IN_COLS = 3 * NA_WIDTH + RW_COLS + MEM_WIDTH + N_BRANCH * D_MODEL

kernel_name = 'hybrid_natten_rwkv7_memory_encoder'


def rmsnorm(x, gain):
    xf = x.astype(jnp.float32)
    y = xf * lax.rsqrt(jnp.mean(xf * xf, axis=-1, keepdims=True) + RMS_EPS)
    return (y * gain.astype(jnp.float32)).astype(x.dtype)


def dwconv3(u, w):
    up = jnp.pad(u, ((0, 0), (1, 1), (0, 0)))
    return up[:, :-2] * w[0] + up[:, 1:-1] * w[1] + up[:, 2:] * w[2]


def neighbourhood_attention(q, k, v, rpb):
    bsz, l = q.shape[0], q.shape[1]
    rows = l // GRID_W
    wr = min(NA_WIN_ROWS, rows)
    wc = NA_WIN_COLS
    qg = q.reshape(bsz, rows, GRID_W, NA_HEADS, NA_HEAD_DIM)
    kg = k.reshape(bsz, rows, GRID_W, NA_HEADS, NA_HEAD_DIM)
    vg = v.reshape(bsz, rows, GRID_W, NA_HEADS, NA_HEAD_DIM)
    cols = np.arange(GRID_W)
    c0 = np.clip(cols - wc // 2, 0, GRID_W - wc)
    col_idx = c0[:, None] + np.arange(wc)[None, :]
    col_off = col_idx - cols[:, None] + (NA_WIN_COLS - 1)
    rpb_c = rpb[:, :, col_off]
    scale = NA_HEAD_DIM ** -0.5

    def row_block(i):
        r0 = jnp.clip(i - wr // 2, 0, rows - wr)
        q_row = lax.dynamic_index_in_dim(qg, i, axis=1, keepdims=False)
        k_band = lax.dynamic_slice_in_dim(kg, r0, wr, axis=1)
        v_band = lax.dynamic_slice_in_dim(vg, r0, wr, axis=1)
        k_win = k_band[:, :, col_idx]
        v_win = v_band[:, :, col_idx]
        row_off = r0 + jnp.arange(wr) - i + (NA_WIN_ROWS - 1)
        bias = jnp.take(rpb_c, row_off, axis=1).transpose(0, 2, 1, 3)
        s = jnp.einsum('bqhd,brqchd->bhqrc', q_row, k_win).astype(jnp.float32) * scale
        s = s + bias[None].astype(jnp.float32)
        p = jax.nn.softmax(s.reshape(bsz, NA_HEADS, GRID_W, wr * wc), axis=-1)
        p = p.reshape(s.shape).astype(v.dtype)
        return jnp.einsum('bhqrc,brqchd->bqhd', p, v_win)

    out = lax.map(row_block, jnp.arange(rows))
    return out.transpose(1, 0, 2, 3, 4).reshape(bsz, l, NA_WIDTH)


def wkv7_scan(r, w, k, v, a, b, reverse):
    bsz = r.shape[0]
    s0 = jnp.zeros((bsz, RW_HEADS, RW_HEAD_DIM, RW_HEAD_DIM), jnp.float32)

    def step(s, inp):
        r_t, w_t, k_t, v_t, a_t, b_t = inp
        sa = jnp.einsum('bhij,bhj->bhi', s, a_t)
        s = s * w_t[:, :, None, :] + sa[..., None] * b_t[:, :, None, :] + v_t[..., None] * k_t[:, :, None, :]
        return s, jnp.einsum('bhij,bhj->bhi', s, r_t)

    xs = tuple(t.transpose(1, 0, 2, 3) for t in (r, w, k, v, a, b))
    _, y = lax.scan(step, s0, xs, reverse=reverse)
    return y.transpose(1, 0, 2, 3)


def rwkv7_bidirectional(z, conv_w, decay0, decay2, a0, a2, g2, k_k, k_a, r_k, lnx_w, lnx_b):
    bsz, l = z.shape[0], z.shape[1]
    zf = dwconv3(z, conv_w).astype(jnp.float32)
    o1 = RW_WIDTH
    o2 = 2 * RW_WIDTH
    o3 = 3 * RW_WIDTH
    o4 = o3 + 2 * DECAY_LORA
    o5 = o4 + 2 * AAA_LORA
    r, k, v, xw, xa, xg = jnp.split(zf, [o1, o2, o3, o4, o5], axis=-1)

    def heads(t):
        return t.reshape(bsz, l, RW_HEADS, RW_HEAD_DIM)

    kk = heads(k * k_k)
    kk = kk * lax.rsqrt(jnp.maximum(jnp.sum(kk * kk, axis=-1, keepdims=True), 1e-24))
    rh = heads(r)
    vh = heads(v)
    g = jax.nn.sigmoid(xg) @ g2
    ys = []
    bonus = []
    for d in range(2):
        xw_d = xw[..., d * DECAY_LORA:(d + 1) * DECAY_LORA]
        xa_d = xa[..., d * AAA_LORA:(d + 1) * AAA_LORA]
        w_log = -jax.nn.softplus(-(decay0[d] + jnp.tanh(xw_d) @ decay2[d])) - 0.5
        decay = heads(jnp.exp(-jnp.exp(w_log)))
        a = jax.nn.sigmoid(a0[d] + xa_d @ a2[d])
        k_d = heads(k * (1.0 + (a - 1.0) * k_a))
        a_h = heads(a)
        ys.append(wkv7_scan(rh, decay, k_d, vh, -kk, kk * a_h, reverse=(d == 1)))
        bonus.append(jnp.sum(rh * k_d * r_k, axis=-1, keepdims=True) * vh)
    y = ys[0] + ys[1]
    mu = jnp.mean(y, axis=-1, keepdims=True)
    var = jnp.mean(jnp.square(y - mu), axis=-1, keepdims=True)
    y = ((y - mu) * lax.rsqrt(var + GN_EPS)).reshape(bsz, l, RW_WIDTH)
    y = y * lnx_w + lnx_b + (bonus[0] + bonus[1]).reshape(bsz, l, RW_WIDTH)
    return (y * g).astype(z.dtype)


def memory_cross_attention(q, mem_n, w_mem_kv):
    bsz, l = q.shape[0], q.shape[1]
    km, vm = jnp.split(mem_n @ w_mem_kv, 2, axis=-1)
    qh = q.reshape(bsz, l, MEM_HEADS, MEM_HEAD_DIM)
    kh = km.reshape(bsz, MEM_TOKENS, MEM_HEADS, MEM_HEAD_DIM)
    vh = vm.reshape(bsz, MEM_TOKENS, MEM_HEADS, MEM_HEAD_DIM)
    s = jnp.einsum('blhd,bmhd->bhlm', qh, kh).astype(jnp.float32) * (MEM_HEAD_DIM ** -0.5)
    p = jax.nn.softmax(s, axis=-1).astype(q.dtype)
    return jnp.einsum('bhlm,bmhd->blhd', p, vh).reshape(bsz, l, MEM_WIDTH)


def encoder_layer(x, mem, attn_norm, w_in, na_rpb, rw_conv, rw_decay0, rw_decay2, rw_a0, rw_a2,
                  rw_g2, rw_k_k, rw_k_a, rw_r_k, rw_lnx_w, rw_lnx_b, mem_norm, w_mem_kv,
                  w_branch, w_out, ffn_norm, w_up, ffn_conv, ffn_conv_b, w_down):
    bsz, l = x.shape[0], x.shape[1]
    h = rmsnorm(x, attn_norm)
    z = h @ w_in
    c1 = 3 * NA_WIDTH
    c2 = c1 + RW_COLS
    c3 = c2 + MEM_WIDTH
    z_na, z_rw, z_mem, z_gate = jnp.split(z, [c1, c2, c3], axis=-1)
    q, k, v = (t.reshape(bsz, l, NA_HEADS, NA_HEAD_DIM) for t in jnp.split(z_na, 3, axis=-1))
    o_na = neighbourhood_attention(q, k, v, na_rpb)
    o_rw = rwkv7_bidirectional(z_rw, rw_conv, rw_decay0, rw_decay2, rw_a0, rw_a2, rw_g2,
                               rw_k_k, rw_k_a, rw_r_k, rw_lnx_w, rw_lnx_b)
    o_mem = memory_cross_attention(z_mem, rmsnorm(mem, mem_norm), w_mem_kv)
    gates = jax.nn.sigmoid(z_gate.astype(jnp.float32)).astype(x.dtype).reshape(bsz, l, N_BRANCH, D_MODEL)
    merged = (gates[:, :, 0] * (o_na @ w_branch[0])
              + gates[:, :, 1] * (o_rw @ w_branch[1])
              + gates[:, :, 2] * (o_mem @ w_branch[2]))
    x = x + merged @ w_out
    h = rmsnorm(x, ffn_norm)
    u = dwconv3(h @ w_up, ffn_conv) + ffn_conv_b
    u_val, u_gate = jnp.split(u, 2, axis=-1)
    return x + (jax.nn.silu(u_gate) * u_val) @ w_down


def encoder_trunk(x, mem, layer_params, final_norm):
    for i in range(DEPTH):
        x = encoder_layer(x, mem, *[p[i] for p in layer_params])
    return rmsnorm(x, final_norm)


def setup_inputs(seed: int = 0) -> dict:
    key = jax.random.key(seed)
    ks = jax.random.split(key, 32)
    f32 = jnp.float32

    def nrm(k, shape, scale):
        return jax.random.normal(k, shape, f32) * scale

    conv_base = jnp.array([0.2, 0.6, 0.2], f32)[None, :, None]
    return {
        'x_prompt': nrm(ks[0], (BATCH, SEQ, D_MODEL), 1.0),
        'x_sample': nrm(ks[1], (DEC_BATCH, DEC_SEQ, D_MODEL), 1.0),
        'mem_prompt': nrm(ks[2], (BATCH, MEM_TOKENS, D_MODEL), 1.0),
        'mem_sample': nrm(ks[3], (DEC_BATCH, MEM_TOKENS, D_MODEL), 1.0),
        'attn_norm': 1.0 + nrm(ks[4], (DEPTH, D_MODEL), 0.05),
        'w_in': nrm(ks[5], (DEPTH, D_MODEL, IN_COLS), D_MODEL ** -0.5),
        'na_rpb': nrm(ks[6], (DEPTH, NA_HEADS, 2 * NA_WIN_ROWS - 1, 2 * NA_WIN_COLS - 1), 0.3),
        'rw_conv': conv_base + nrm(ks[7], (DEPTH, 3, RW_COLS), 0.1),
        'rw_decay0': jax.random.uniform(ks[8], (DEPTH, 2, RW_WIDTH), f32, -6.0, 1.0),
        'rw_decay2': nrm(ks[9], (DEPTH, 2, DECAY_LORA, RW_WIDTH), 0.1),
        'rw_a0': nrm(ks[10], (DEPTH, 2, RW_WIDTH), 0.5),
        'rw_a2': nrm(ks[11], (DEPTH, 2, AAA_LORA, RW_WIDTH), AAA_LORA ** -0.5),
        'rw_g2': nrm(ks[12], (DEPTH, GATE_LORA, RW_WIDTH), GATE_LORA ** -0.5),
        'rw_k_k': 0.85 + nrm(ks[13], (DEPTH, RW_WIDTH), 0.05),
        'rw_k_a': 1.0 + nrm(ks[14], (DEPTH, RW_WIDTH), 0.05),
        'rw_r_k': nrm(ks[15], (DEPTH, RW_HEADS, RW_HEAD_DIM), 0.1),
        'rw_lnx_w': 1.0 + nrm(ks[16], (DEPTH, RW_WIDTH), 0.05),
        'rw_lnx_b': nrm(ks[17], (DEPTH, RW_WIDTH), 0.01),
        'mem_norm': 1.0 + nrm(ks[18], (DEPTH, D_MODEL), 0.05),
        'w_mem_kv': nrm(ks[19], (DEPTH, D_MODEL, 2 * MEM_WIDTH), D_MODEL ** -0.5),
        'w_branch': nrm(ks[20], (DEPTH, N_BRANCH, BRANCH_WIDTH, D_MODEL), BRANCH_WIDTH ** -0.5),
        'w_out': nrm(ks[21], (DEPTH, D_MODEL, D_MODEL), D_MODEL ** -0.5),
        'ffn_norm': 1.0 + nrm(ks[22], (DEPTH, D_MODEL), 0.05),
        'w_up': nrm(ks[23], (DEPTH, D_MODEL, 2 * D_FF), D_MODEL ** -0.5),
        'ffn_conv': conv_base + nrm(ks[24], (DEPTH, 3, 2 * D_FF), 0.1),
        'ffn_conv_b': nrm(ks[25], (DEPTH, 2 * D_FF), 0.01),
        'w_down': nrm(ks[26], (DEPTH, D_FF, D_MODEL), D_FF ** -0.5),
        'final_norm': 1.0 + nrm(ks[27], (D_MODEL,), 0.05),
    }


def reference(x_prompt, x_sample, mem_prompt, mem_sample, attn_norm, w_in, na_rpb, rw_conv,
              rw_decay0, rw_decay2, rw_a0, rw_a2, rw_g2, rw_k_k, rw_k_a, rw_r_k, rw_lnx_w,
              rw_lnx_b, mem_norm, w_mem_kv, w_branch, w_out, ffn_norm, w_up, ffn_conv,
              ffn_conv_b, w_down, final_norm):
    layer_params = (attn_norm, w_in, na_rpb, rw_conv, rw_decay0, rw_decay2, rw_a0, rw_a2,
                    rw_g2, rw_k_k, rw_k_a, rw_r_k, rw_lnx_w, rw_lnx_b, mem_norm, w_mem_kv,
                    w_branch, w_out, ffn_norm, w_up, ffn_conv, ffn_conv_b, w_down)
    y_prompt = encoder_trunk(x_prompt, mem_prompt, layer_params, final_norm)
    y_sample = encoder_trunk(x_sample, mem_sample, layer_params, final_norm)
    return (y_prompt, y_sample)
```

```python
from contextlib import ExitStack
import os
LVL = int(os.environ.get('B2DBG', '9'))
B2X = int(os.environ.get('B2X', '0'))
import numpy as np
import concourse.bass as bass
import concourse.mybir as mybir
from concourse.bass_utils import run_bass_kernel_spmd

F32 = mybir.dt.float32
BF16 = mybir.dt.bfloat16
AF = mybir.ActivationFunctionType
ALU = mybir.AluOpType
AX = mybir.AxisListType

D = 1024
NEG = -30000.0
ENGS = ("pe", "act", "dve", "pool", "sp")


class Res:
    __slots__ = ("name", "w", "r", "dsem", "multi")

    def __init__(self, name="", multi=False):
        self.name = name
        self.w = {}
        self.r = {}
        self.dsem = None
        self.multi = multi


class Sched:
    def __init__(self, nc, stack, n_dma_sems=64):
        self.nc = nc
        self.q = {e: [] for e in ENGS}
        self.cnt = {e: 0 for e in ENGS}
        self.sem = {e: stack.enter_context(nc.semaphore("s_" + e)) for e in ENGS}
        self.dma_sems = [stack.enter_context(nc.semaphore("d%d" % i)) for i in range(n_dma_sems)]
        self.dma_cnt = [0] * n_dma_sems
        self.dma_next = 0
        self.seen = {e: {} for e in ENGS}
        self.ninst = 0

    def _semobj(self, key):
        return self.sem[key] if isinstance(key, str) else self.dma_sems[key]

    def _deps(self, eng, reads, writes):
        toks = {}
        for r in reads:
            for k, v in r.w.items():
                if toks.get(k, 0) < v:
                    toks[k] = v
        for w in writes:
            for k, v in w.w.items():
                if toks.get(k, 0) < v:
                    toks[k] = v
            for k, v in w.r.items():
                if toks.get(k, 0) < v:
                    toks[k] = v
        waits = []
        seen = self.seen[eng]
        for k, v in toks.items():
            if k == eng and eng == "pe":
                continue
            if seen.get(k, 0) >= v:
                continue
            seen[k] = v
            waits.append((k, v))
        return waits

    def _mark(self, tok, reads, writes):
        k, v = tok
        for r in reads:
            if r.r.get(k, 0) < v:
                r.r[k] = v
        for w in writes:
            w.w[k] = v
            if not w.multi:
                w.r = {}

    def op(self, eng, fn, reads=(), writes=()):
        waits = self._deps(eng, reads, writes)
        self.cnt[eng] += 1
        self._mark((eng, self.cnt[eng]), reads, writes)
        self.q[eng].append((waits, fn, (eng, 1)))
        self.ninst += 1

    def dma(self, eng, fn, reads=(), writes=(), sem_res=None):
        waits = self._deps(eng, reads, writes)
        if sem_res is None:
            sem_res = writes[0]
        if sem_res.dsem is None:
            sem_res.dsem = self.dma_next % len(self.dma_sems)
            self.dma_next += 1
        k = sem_res.dsem
        self.dma_cnt[k] += 16
        self._mark((k, self.dma_cnt[k]), reads, writes)
        self.q[eng].append((waits, fn, (k, 16)))
        self.ninst += 1

    def barrier(self):
        tot = {e: self.cnt[e] for e in ENGS if self.cnt[e]}
        for k, c in enumerate(self.dma_cnt):
            if c:
                tot[k] = c
        for e in ENGS:
            waits = []
            for k, v in tot.items():
                if k == e:
                    continue
                if self.seen[e].get(k, 0) < v:
                    self.seen[e][k] = v
                    waits.append((k, v))
            if waits:
                self.q[e].append((waits, None, None))

    def emit(self):
        nc = self.nc
        fin = {e: self.cnt[e] for e in ENGS if self.cnt[e]}
        for k, c in enumerate(self.dma_cnt):
            if c:
                fin[k] = c
        handles = {"pe": "tensor", "act": "scalar", "dve": "vector", "pool": "gpsimd", "sp": "sync"}
        with nc.Block() as block:
            for e in ENGS:
                def body(engine, ops=self.q[e], is_last=(e == "sp")):
                    for waits, fn, inc in ops:
                        for wi, (wk, wv) in enumerate(waits):
                            engine.wait_ge(self._semobj(wk), wv)
                            if wi < len(waits) - 1 or fn is None:
                                engine.nop(nofuse=True)
                        if fn is not None:
                            fn(engine).then_inc(self._semobj(inc[0]), inc[1])
                    if is_last:
                        for k, v in fin.items():
                            engine.wait_ge(self._semobj(k), v)
                getattr(block, handles[e])(body)


def na_bands(ROWS, RS):
    bands = []
    for i in range(ROWS):
        p0 = min(max(i - 4, 0), ROWS - 8)
        b, il = divmod(i, RS)
        s0 = b * RS + min(max(il - 4, 0), RS - 8)
        lo, hi = min(p0, s0), max(p0, s0) + 8
        n = (hi - lo + 1) // 2
        if lo + 2 * n > ROWS:
            lo = ROWS - 2 * n
        bands.append([lo + 2 * c for c in range(n)])
    return bands


def na_rowbias(ROWS, RS, typ):
    bands = na_bands(ROWS, RS)
    cols = []
    for i, band in enumerate(bands):
        if typ == "P":
            w0 = min(max(i - 4, 0), ROWS - 8)
        else:
            b, il = divmod(i, RS)
            w0 = b * RS + min(max(il - 4, 0), RS - 8)
        for r in band:
            col = np.full(128, NEG, np.float32)
            for j in range(2):
                if w0 <= r + j < w0 + 8:
                    col[j * 64:(j + 1) * 64] = 0.0
            cols.append(col)
    return np.stack(cols, axis=1)


def na_btab(rpb):
    H = rpb.shape[0]
    out = np.zeros((64, H, 17, 64), np.float32)
    qc = np.arange(64)
    c0 = np.clip(qc - 8, 0, 48)
    for off in range(-7, 8):
        blk = np.full((64, H, 64), NEG, np.float32)
        for q in range(64):
            ks = np.arange(c0[q], c0[q] + 16)
            blk[q][:, ks] = rpb[:, off + 7, ks - q + 15]
        out[:, :, off + 8, :] = blk
    return out


def make_consts():
    c = {}
    c["ident"] = np.eye(128, dtype=np.float32)
    bd = np.zeros((128, 128), np.float32)
    bd[:64, :64] = 1.0
    bd[64:, 64:] = 1.0
    c["bd64"] = bd
    sm = np.ones((128, 512), np.float32)
    sm[:, ::64] = 0.0
    c["scanmask"] = sm
    s = np.arange(128)[:, None]
    t = np.arange(128)[None, :]
    same = (s // 64) == (t // 64)
    LS = (same & (s < t)).astype(np.float32)
    LI = (same & (s <= t)).astype(np.float32)
    US = (same & (s > t)).astype(np.float32)
    UI = (same & (s >= t)).astype(np.float32)
    c["ls4"] = np.tile(LS, (1, 4))
    c["li4"] = np.tile(LI, (1, 4))
    c["us4"] = np.tile(US, (1, 4))
    c["ui4"] = np.tile(UI, (1, 4))
    return c


CONST_ORDER = ("ident", "bd64", "scanmask", "ls4", "li4", "us4", "ui4")


def build(T, RS, depth=2, debug_outs=(), phases=None):
    NT = T // 512
    NTL = T // 128
    ROWS = T // 64
    NCH = T // 64
    SLOT_ST = max(NT // 4, 1)
    NSLOT = NT // SLOT_ST
    bands = na_bands(ROWS, RS)
    nslots_na = sum(len(b) for b in bands)
    NH2 = 2 * (T // 256 - 1)
    NH5 = 2 * (NT - 1)

    nc = bass.Bass("TRN2", target_bir_lowering=False)

    def din(name, shape, dt=F32):
        return nc.dram_tensor(name, list(shape), dt, kind="ExternalInput").ap()

    def dscr(name, shape, dt):
        kind = "ExternalOutput" if name in debug_outs else "Internal"
        return nc.dram_tensor(name, list(shape), dt, kind=kind).ap()

    xin = din("xin", [T, D])
    mem = din("mem", [NSLOT, 256, D])
    bm_in = din("bm", [128, max(NT - 1, 1)])
    bm2_in = din("bm2", [128, max(T // 256 - 1, 1)])
    rowbias_in = din("rowbias", [128, nslots_na])
    btab_in = din("btab", [depth, 64, 8 * 17 * 64])
    cst_in = {k: din("c_" + k, v.shape) for k, v in make_consts().items()}
    W = {}
    for name, shape in (("attn_norm", [depth, D]), ("w_in", [depth, D, 7072]), ("rw_conv", [depth, 3, 1952]),
                        ("rw_decay0", [depth, 2, 512]), ("rw_decay2", [depth, 2, 64, 512]), ("rw_a0", [depth, 2, 512]),
                        ("rw_a2", [depth, 2, 64, 512]), ("rw_g2", [depth, 160, 512]), ("rw_k_k", [depth, 512]),
                        ("rw_k_a", [depth, 512]), ("rw_r_k", [depth, 512]), ("rw_lnx_w", [depth, 512]),
                        ("rw_lnx_b", [depth, 512]), ("mem_norm", [depth, D]), ("w_mem_kv", [depth, D, 1024]),
                        ("w_branch", [depth, 3, 512, D]), ("w_out", [depth, D, D]), ("ffn_norm", [depth, D]),
                        ("w_up", [depth, D, 5632]), ("ffn_conv", [depth, 3, 5632]), ("ffn_conv_b", [depth, 5632]),
                        ("w_down", [depth, 2816, D]), ("final_norm", [1, D])):
        W[name] = din(name, shape)
    yout = nc.dram_tensor("yout", [T, D], F32, kind="ExternalOutput").ap()

    xT = [dscr("xT0", [D, T], F32), dscr("xT1", [D, T], F32)]
    qT = dscr("qT", [512, T], BF16)
    kT = dscr("kT", [512, T], BF16)
    vtm = dscr("vtm", [T, 512], BF16)
    omT = dscr("omT", [512, T], BF16)
    onaT = dscr("onaT", [512, T], BF16)
    orwT = dscr("orwT", [512, T], BF16)
    rws = {}
    for d in range(2):
        for nm in ("at", "bt", "bb", "rt", "kt", "kb"):
            rws[nm, d] = dscr("rw_%s%d" % (nm, d), [512, T], BF16)
        rws["wtot", d] = dscr("rw_wtot%d" % d, [512, NCH], F32)
    rws["v"] = dscr("rw_v", [512, T], BF16)
    rws["bonus"] = dscr("rw_bonus", [512, T], F32)
    rws["g"] = dscr("rw_g", [512, T], BF16)
    yfw = dscr("yfw", [T, 512], F32)

    R_xT = [Res("xT0", True), Res("xT1", True)]
    R_scr = {}

    def rscr(key):
        if key not in R_scr:
            R_scr[key] = Res(str(key), True)
        return R_scr[key]

    with ExitStack() as top:
        S = Sched(nc, top)
        psum = [top.enter_context(nc.psum_tensor("ps%d" % i, [128, 512], F32)) for i in range(7)]
        psT = top.enter_context(nc.psum_tensor("psT", [128, 1024], BF16))
        R_ps = [Res("ps%d" % i) for i in range(7)]
        R_psT = Res("psT")

        def mm(out, lhsT, rhs, start, stop, reads, writes):
            S.op("pe", lambda e: e.matmul(out, lhsT=lhsT, rhs=rhs, start=start, stop=stop), reads, writes)

        def tr(out, in_, ident, reads, writes):
            S.op("pe", lambda e: e.transpose(out, in_, ident), reads, writes)

        def act(out, in_, func, reads, writes, bias=None, scale=None, accum_out=None):
            kw = {}
            if bias is not None:
                kw["bias"] = bias
            if scale is not None:
                kw["scale"] = scale
            if accum_out is not None:
                kw["accum_out"] = accum_out
            S.op("act", lambda e: e.activation(out=out, in_=in_, func=func, **kw), reads, writes)

        def is_ps(ap):
            return hasattr(ap, "space") and "PSUM" in str(ap.space)

        def tt(eng, out, in0, in1, op, reads, writes):
            if eng == "pool" and (is_ps(out) or is_ps(in0) or is_ps(in1)):
                eng = "dve"
            S.op(eng, lambda e: e.tensor_tensor(out=out, in0=in0, in1=in1, op=op), reads, writes)

        def ts(eng, out, in0, s1, s2, op0, op1, reads, writes):
            if eng == "pool" and (is_ps(out) or is_ps(in0)):
                eng = "dve"
            if op1 is None and op0 == ALU.pow:
                assert s1 == -0.5
                S.op("act", lambda e: e.activation(out=out, in_=in0, func=AF.Ln), reads, writes)
                S.op("act", lambda e: e.activation(out=out, in_=out, func=AF.Exp, scale=-0.5), list(reads) + list(writes), writes)
            elif op1 is None:
                S.op(eng, lambda e: e.tensor_scalar(out=out, in0=in0, scalar1=s1, scalar2=None, op0=op0), reads, writes)
            else:
                S.op(eng, lambda e: e.tensor_scalar(out=out, in0=in0, scalar1=s1, scalar2=s2, op0=op0, op1=op1), reads, writes)

        def stt(eng, out, in0, scalar, in1, op0, op1, reads, writes):
            eng = "dve"
            S.op(eng, lambda e: e.scalar_tensor_tensor(out=out, in0=in0, scalar=scalar, in1=in1, op0=op0, op1=op1), reads, writes)

        def cp(eng, out, in_, reads, writes):
            if eng == "act":
                S.op("act", lambda e: e.copy(out=out, in_=in_), reads, writes)
            else:
                S.op(eng, lambda e: e.tensor_copy(out=out, in_=in_), reads, writes)

        def memset(eng, ap, val, writes):
            S.op(eng, lambda e: e.memset(ap, val), (), writes)

        def ld(out, in_, reads, writes, q="sp", nonc=False):
            if nonc:
                def f(e):
                    with nc.allow_non_contiguous_dma(reason="small strided load"):
                        return e.dma_start(out=out, in_=in_)
                S.dma(q, f, reads, writes)
            else:
                S.dma(q, lambda e: e.dma_start(out=out, in_=in_), reads, writes)

        def stor(out, in_, reads, writes, sem_res, q="sp"):
            S.dma(q, lambda e: e.dma_start(out=out, in_=in_), reads, writes, sem_res=sem_res)

        rr = [0]

        def next_ps():
            rr[0] = (rr[0] + 1) % 4
            return rr[0]

        ve = [0]

        def veng():
            ve[0] ^= 1
            return "dve" if ve[0] else "pool"

        uniq = [0]

        def sbt(stack, name, shape, dt):
            uniq[0] += 1
            return stack.enter_context(nc.sbuf_tensor("sb%d_%s" % (uniq[0], name), list(shape), dt))

        ident_f = sbt(top, "ident_f", [128, 128], F32)
        ident_b = sbt(top, "ident_b", [128, 128], BF16)
        ones_b = sbt(top, "ones_b", [128, 128], BF16)
        bd64_b = sbt(top, "bd64_b", [128, 128], BF16)
        bd64_f = sbt(top, "bd64_f", [128, 128], F32)
        R_c = Res("consts")
        ld(ident_f[:], cst_in["ident"][:, :], (), [R_c])
        ld(ident_b[:], cst_in["ident"][:, :], (), [R_c], q="pool")
        ld(bd64_b[:], cst_in["bd64"][:, :], (), [R_c], q="pool")
        ld(bd64_f[:], cst_in["bd64"][:, :], (), [R_c])
        memset("dve", ones_b[:], 1.0, [R_c])
        bm = sbt(top, "bm", [128, max(NT - 1, 1)], F32)
        bm2 = sbt(top, "bm2", [128, max(T // 256 - 1, 1)], F32)
        ld(bm[:], bm_in[:, :], (), [R_c])
        ld(bm2[:], bm2_in[:, :], (), [R_c])

        def rmsnorm_fm(X, rX, XN, rXN, gain, rg, n, SQ, rSQ, RSt, rRS, bank=4):
            act(SQ[:, :, 0:n], X[:, :, 0:n], AF.Square, [rX], [rSQ])
            for c in range(8):
                mm(psum[bank][:, 0:n], ones_b[:], SQ[:, c, 0:n], c == 0, c == 7, [rSQ, R_c], [R_ps[bank]])
            ts("dve", RSt[:, 0:n], psum[bank][:, 0:n], 1.0 / D, 1e-6, ALU.mult, ALU.add, [R_ps[bank]], [rRS])
            ts("dve", RSt[:, 0:n], RSt[:, 0:n], -0.5, None, ALU.pow, None, [rRS], [rRS])
            for c in range(8):
                stt(veng(), XN[:, c, 0:n], X[:, c, 0:n], gain[:, c:c + 1], RSt[:, 0:n], ALU.mult, ALU.mult,
                    [rX, rRS, rg], [rXN])

        def load_vec_fm(stack, name, src, nchunk, q="sp"):
            t = sbt(stack, name, [128, nchunk], F32)
            r = Res(name)
            ld(t[:], src.rearrange("(c p) -> p c", p=128), (), [r], nonc=True)
            return t, r

        def load_w(stack, name, src, kc, ncols, col0=0, eng_q="pool"):
            t = sbt(stack, name, [128, kc, ncols], BF16)
            r = Res(name)
            for c in range(kc):
                ld(t[:, c, :], src[c * 128:(c + 1) * 128, col0:col0 + ncols], (), [r], q=eng_q)
            return t, r

        def phase_p0():
            with ExitStack() as st:
                XI = [sbt(st, "p0_xi%d" % i, [128, D], F32) for i in range(2)]
                rXI = [Res("xi0"), Res("xi1")]
                XO = [sbt(st, "p0_xo%d" % i, [128, 8, 128], F32) for i in range(2)]
                rXO = [Res("xo0"), Res("xo1")]
                for i in range(NTL):
                    b = i % 2
                    ld(XI[b][:], xin[i * 128:(i + 1) * 128, :], (), [rXI[b]])
                    for half in range(2):
                        bank = next_ps()
                        for c4 in range(4):
                            c = half * 4 + c4
                            tr(psum[bank][:, c4 * 128:(c4 + 1) * 128], XI[b][:, c * 128:(c + 1) * 128], ident_f[:],
                               [rXI[b], R_c], [R_ps[bank]])
                        cp("act" if half else "dve", XO[b][:, half * 4:(half + 1) * 4, :],
                           psum[bank][:, :].rearrange("p (c t) -> p c t", c=4), [R_ps[bank]], [rXO[b]])
                    stor(xT[0].rearrange("(c p) t -> p c t", p=128)[:, :, i * 128:(i + 1) * 128], XO[b][:],
                         [rXO[b]], [R_xT[0]], rXO[b])
            S.barrier()

        def phase_e(src):
            with ExitStack() as st:
                gain, rg = load_vec_fm(st, "e_gain", W["final_norm"][0], 8)
                X = [sbt(st, "e_x%d" % i, [128, 8, 512], F32) for i in range(2)]
                rX = [Res(), Res()]
                XN = sbt(st, "e_xn", [128, 8, 512], F32)
                rXN = Res()
                SQ = sbt(st, "e_sq", [128, 8, 512], BF16)
                rSQ = Res()
                RSt = sbt(st, "e_rs", [128, 512], F32)
                rRS = Res()
                YO = [sbt(st, "e_yo%d" % i, [128, D], F32) for i in range(2)]
                rYO = [Res(), Res()]
                xv = xT[src].rearrange("(c p) t -> p c t", p=128)
                ld(X[0][:], xv[:, :, 0:512], [R_xT[src]], [rX[0]])
                k = 0
                for s in range(NT):
                    b = s % 2
                    if s + 1 < NT:
                        ld(X[1 - b][:], xv[:, :, (s + 1) * 512:(s + 2) * 512], [R_xT[src]], [rX[1 - b]])
                    rmsnorm_fm(X[b], rX[b], XN, rXN, gain, rg, 512, SQ, rSQ, RSt, rRS)
                    for tl in range(4):
                        yb = k % 2
                        k += 1
                        for half in range(2):
                            bank = next_ps()
                            for c4 in range(4):
                                c = half * 4 + c4
                                tr(psum[bank][:, c4 * 128:(c4 + 1) * 128], XN[:, c, tl * 128:(tl + 1) * 128], ident_f[:],
                                   [rXN, R_c], [R_ps[bank]])
                            cp("act" if half else "dve", YO[yb][:, half * 512:(half + 1) * 512], psum[bank][:, :],
                               [R_ps[bank]], [rYO[yb]])
                        t0 = s * 512 + tl * 128
                        stor(yout[t0:t0 + 128, :], YO[yb][:], [rYO[yb]], [Res()], rYO[yb])
            S.barrier()

        def phase_c2(l, src, dst):
            TW = 256
            NTW = T // TW
            NW = TW + 2
            banks = (0, 1, 2, 3, 5, 6)
            bk = [0]

            def nb():
                bk[0] = (bk[0] + 1) % len(banks)
                return banks[bk[0]]

            with ExitStack() as st:
                Wu, rWu = load_w(st, "c2_wu", W["w_up"][l], 8, 5632)
                Wd, rWd = load_w(st, "c2_wd", W["w_down"][l], 22, D)
                gain, rg = load_vec_fm(st, "c2_gain", W["ffn_norm"][l], 8)
                cw = sbt(st, "c2_cw", [128, 3, 44], F32)
                rcw = Res()
                for k in range(3):
                    ld(cw[:, k, :], W["ffn_conv"][l, k].rearrange("(c p) -> p c", p=128), (), [rcw], nonc=True)
                cb, rcb = load_vec_fm(st, "c2_cb", W["ffn_conv_b"][l], 44)
                X = [sbt(st, "c2_x%d" % i, [128, 8, NW], F32) for i in range(2)]
                rX = [Res(), Res()]
                XN = sbt(st, "c2_xn", [128, 8, NW], BF16)
                rXN = Res()
                SQ = sbt(st, "c2_sq", [128, 8, NW], BF16)
                rSQ = Res()
                RSt = sbt(st, "c2_rs", [128, NW], F32)
                rRS = Res()
                G = sbt(st, "c2_g", [128, 22, TW], BF16)
                rG = Res()
                NB = 3
                CV = [sbt(st, "c2_cv%d" % i, [128, TW], F32) for i in range(NB)]
                rCV = [Res() for _ in range(NB)]
                CG = [sbt(st, "c2_cg%d" % i, [128, TW], F32) for i in range(NB)]
                rCG = [Res() for _ in range(NB)]
                SGt = [sbt(st, "c2_sg%d" % i, [128, TW], F32) for i in range(NB)]
                rSGt = [Res() for _ in range(NB)]
                xv = xT[src].rearrange("(c p) t -> p c t", p=128)
                xo = xT[dst].rearrange("(c p) t -> p c t", p=128)

                def load_x(s):
                    b = s % 2
                    t0 = s * TW
                    lo = max(t0 - 1, 0)
                    hi = min(t0 + TW + 1, T)
                    ld(X[b][:, :, lo - (t0 - 1):hi - (t0 - 1)], xv[:, :, lo:hi], [R_xT[src]], [rX[b]])
                    if s == 0:
                        memset("pool", X[b][:, :, 0:1], 0.0, [rX[b]])
                    if s == NTW - 1:
                        memset("pool", X[b][:, :, NW - 1:NW], 0.0, [rX[b]])

                load_x(0)
                for s in range(NTW):
                    b = s % 2
                    t0 = s * TW
                    if s + 1 < NTW:
                        load_x(s + 1)
                    rmsnorm_fm(X[b], rX[b], XN, rXN, gain, rg, NW, SQ, rSQ, RSt, rRS)
                    if s > 0:
                        ts("pool", XN[:, :, 0:1], XN[:, :, 0:1], bm2[:, s - 1:s], None, ALU.mult, None, [rXN, R_c], [rXN])
                    if s < NTW - 1:
                        ts("pool", XN[:, :, NW - 1:NW], XN[:, :, NW - 1:NW], bm2[:, s:s + 1], None, ALU.mult, None, [rXN, R_c], [rXN])
                    for j in range(22):
                        cbuf = j % NB
                        for (uc, Ct, rC) in ((j, CV[cbuf], rCV[cbuf]), (22 + j, CG[cbuf], rCG[cbuf])):
                            bank = nb()
                            P = psum[bank]
                            for kc in range(8):
                                mm(P[:, 0:NW], Wu[:, kc, uc * 128:(uc + 1) * 128], XN[:, kc, :], kc == 0, kc == 7,
                                   [rWu, rXN], [R_ps[bank]])
                            act(Ct[:, :], P[:, 1:TW + 1], AF.Identity, [R_ps[bank], rcw, rcb], [rC], bias=cb[:, uc:uc + 1],
                                scale=cw[:, 1, uc:uc + 1])
                            stt("dve", Ct[:, :], P[:, 0:TW], cw[:, 0, uc:uc + 1], Ct[:, :], ALU.mult, ALU.add,
                                [R_ps[bank], rcw, rC], [rC])
                            stt("dve", Ct[:, :], P[:, 2:TW + 2], cw[:, 2, uc:uc + 1], Ct[:, :], ALU.mult, ALU.add,
                                [R_ps[bank], rcw, rC], [rC])
                        act(SGt[cbuf][:], CG[cbuf][:], AF.Silu, [rCG[cbuf]], [rSGt[cbuf]])
                        tt("pool", G[:, j, :], SGt[cbuf][:], CV[cbuf][:], ALU.mult, [rSGt[cbuf], rCV[cbuf]], [rG])
                    for oc in range(8):
                        bank = nb()
                        for kc in range(22):
                            mm(psum[bank][:, 0:TW], Wd[:, kc, oc * 128:(oc + 1) * 128], G[:, kc, :], kc == 0, kc == 21,
                               [rWd, rG], [R_ps[bank]])
                        tt("dve", X[b][:, oc, 1:TW + 1], X[b][:, oc, 1:TW + 1], psum[bank][:, 0:TW], ALU.add, [rX[b], R_ps[bank]], [rX[b]])
                    stor(xo[:, :, t0:t0 + TW], X[b][:, :, 1:TW + 1], [rX[b]], [R_xT[dst]], rX[b])
            S.barrier()

        def phase_c1(l, src, dst):
            with ExitStack() as st:
                Wg, rWg = load_w(st, "c1_wg", W["w_in"][l], 8, 3072, col0=4000)
                Wb = sbt(st, "c1_wb", [128, 12, D], BF16)
                rWb = Res()
                for b in range(3):
                    for kc in range(4):
                        ld(Wb[:, b * 4 + kc, :], W["w_branch"][l, b, kc * 128:(kc + 1) * 128, :], (), [rWb], q="pool")
                Wo, rWo = load_w(st, "c1_wo", W["w_out"][l], 8, D)
                gain, rg = load_vec_fm(st, "c1_gain", W["attn_norm"][l], 8)
                X = sbt(st, "c1_x", [128, 8, 512], F32)
                rX = Res()
                XN = sbt(st, "c1_xn", [128, 8, 512], BF16)
                rXN = Res()
                SQ = sbt(st, "c1_sq", [128, 8, 512], BF16)
                rSQ = Res()
                RSt = sbt(st, "c1_rs", [128, 512], F32)
                rRS = Res()
                OB = sbt(st, "c1_ob", [128, 12, 512], BF16)
                rOB = Res()
                MG = sbt(st, "c1_mg", [128, 8, 512], BF16)
                rMG = Res()
                SGt = [sbt(st, "c1_sg%d" % i, [128, 512], F32) for i in range(2)]
                rSGt = [Res(), Res()]
                ACC = sbt(st, "c1_acc", [128, 512], F32)
                rACC = Res()
                xv = xT[src].rearrange("(c p) t -> p c t", p=128)
                xo = xT[dst].rearrange("(c p) t -> p c t", p=128)
                srcs = (onaT, orwT, omT)
                rsrcs = (rscr("onaT"), rscr("orwT"), rscr("omT"))
                k = 0
                for s in range(NT):
                    t0 = s * 512
                    ld(X[:], xv[:, :, t0:t0 + 512], [R_xT[src]], [rX])
                    for b in range(3):
                        ld(OB[:, b * 4:(b + 1) * 4, :], srcs[b].rearrange("(c p) t -> p c t", p=128)[:, :, t0:t0 + 512],
                           [rsrcs[b]], [rOB])
                    rmsnorm_fm(X, rX, XN, rXN, gain, rg, 512, SQ, rSQ, RSt, rRS)
                    for mc in range(8):
                        for b in range(3):
                            bg = next_ps()
                            for kc in range(8):
                                mm(psum[bg][:, :], Wg[:, kc, b * 1024 + mc * 128:b * 1024 + (mc + 1) * 128], XN[:, kc, :],
                                   kc == 0, kc == 7, [rWg, rXN], [R_ps[bg]])
                            sb_ = k % 2
                            k += 1
                            act(SGt[sb_][:], psum[bg][:, :], AF.Sigmoid, [R_ps[bg]], [rSGt[sb_]])
                            bp = next_ps()
                            for kc in range(4):
                                mm(psum[bp][:, :], Wb[:, b * 4 + kc, mc * 128:(mc + 1) * 128], OB[:, b * 4 + kc, :],
                                   kc == 0, kc == 3, [rWb, rOB], [R_ps[bp]])
                            if b == 0:
                                tt("dve", ACC[:], SGt[sb_][:], psum[bp][:, :], ALU.mult, [rSGt[sb_], R_ps[bp]], [rACC])
                            else:
                                tt("dve", SGt[sb_][:], SGt[sb_][:], psum[bp][:, :], ALU.mult, [rSGt[sb_], R_ps[bp]],
                                   [rSGt[sb_]])
                                if b == 1:
                                    tt("pool", ACC[:], ACC[:], SGt[sb_][:], ALU.add, [rACC, rSGt[sb_]], [rACC])
                                else:
                                    tt("pool", MG[:, mc, :], ACC[:], SGt[sb_][:], ALU.add, [rACC, rSGt[sb_]], [rMG])
                    for oc in range(8):
                        bank = next_ps()
                        for kc in range(8):
                            mm(psum[bank][:, :], Wo[:, kc, oc * 128:(oc + 1) * 128], MG[:, kc, :], kc == 0, kc == 7,
                               [rWo, rMG], [R_ps[bank]])
                        tt(veng(), X[:, oc, :], X[:, oc, :], psum[bank][:, :], ALU.add, [rX, R_ps[bank]], [rX])
                    stor(xo[:, :, t0:t0 + 512], X[:], [rX], [R_xT[dst]], rX)
            S.barrier()

        def phase_a(l, src):
            with ExitStack() as st:
                Wi, rWi = load_w(st, "a_wi", W["w_in"][l], 8, 4000)
                gain, rg = load_vec_fm(st, "a_gain", W["attn_norm"][l], 8)
                KmT = sbt(st, "a_kmT", [128, NSLOT, 4, 256], BF16)
                Vm = sbt(st, "a_vm", [128, NSLOT, 2, 512], BF16)
                rKV = Res()
                with ExitStack() as st2:
                    Wkv, rWkv = load_w(st2, "a_wkv", W["w_mem_kv"][l], 8, 1024)
                    gm, rgm = load_vec_fm(st2, "a_gm", W["mem_norm"][l], 8)
                    MT = sbt(st2, "a_mt", [128, D], F32)
                    rMT = Res()
                    MS = sbt(st2, "a_ms", [128, D], F32)
                    rMS = Res()
                    MB = sbt(st2, "a_mb", [128, D], BF16)
                    rMB = Res()
                    ssq = sbt(st2, "a_ssq", [128, 1], F32)
                    rssq = Res()
                    memT = sbt(st2, "a_memT", [128, 8, 256], BF16)
                    rmemT = Res()
                    for sl in range(NSLOT):
                        for mc in range(2):
                            ld(MT[:], mem[sl, mc * 128:(mc + 1) * 128, :], (), [rMT])
                            act(MS[:], MT[:], AF.Square, [rMT], [rMS, rssq], accum_out=ssq[:])
                            ts("dve", ssq[:], ssq[:], 1.0 / D, 1e-6, ALU.mult, ALU.add, [rssq], [rssq])
                            ts("dve", ssq[:], ssq[:], -0.5, None, ALU.pow, None, [rssq], [rssq])
                            ts("dve", MB[:], MT[:], ssq[:, 0:1], None, ALU.mult, None, [rMT, rssq], [rMB])
                            for c in range(8):
                                tr(psT[:, c * 128:(c + 1) * 128], MB[:, c * 128:(c + 1) * 128], ident_b[:], [rMB, R_c],
                                   [R_psT])
                            for c in range(8):
                                ts(veng(), memT[:, c, mc * 128:(mc + 1) * 128], psT[:, c * 128:(c + 1) * 128],
                                   gm[:, c:c + 1], None, ALU.mult, None, [R_psT, rgm], [rmemT])
                        for h in range(4):
                            bank = next_ps()
                            for kc in range(8):
                                mm(psum[bank][:, 0:256], Wkv[:, kc, h * 128:(h + 1) * 128], memT[:, kc, :], kc == 0, kc == 7,
                                   [rWkv, rmemT], [R_ps[bank]])
                            cp("act", KmT[:, sl, h, :], psum[bank][:, 0:256], [R_ps[bank]], [rKV])
                        for mc in range(2):
                            bank = next_ps()
                            for kc in range(8):
                                mm(psum[bank][:, :], memT[:, kc, mc * 128:(mc + 1) * 128], Wkv[:, kc, 512:1024], kc == 0,
                                   kc == 7, [rWkv, rmemT], [R_ps[bank]])
                            cp("dve", Vm[:, sl, mc, :], psum[bank][:, :], [R_ps[bank]], [rKV])
                    S.barrier()
                cw = sbt(st, "a_cw", [128, 3, 16], F32)
                rcw = Res()
                memset("dve", cw[:], 0.0, [rcw])
                for k in range(3):
                    ld(cw[:, k, 0:15], W["rw_conv"][l, k, 0:1920].rearrange("(c p) -> p c", p=128), (), [rcw], nonc=True)
                    ld(cw[0:32, k, 15:16], W["rw_conv"][l, k, 1920:1952].rearrange("(c p) -> p c", p=32), (), [rcw],
                       nonc=True)
                dec0 = sbt(st, "a_dec0", [128, 2, 4], F32)
                a0 = sbt(st, "a_a0", [128, 2, 4], F32)
                rsm = Res()
                for d in range(2):
                    ld(dec0[:, d, :], W["rw_decay0"][l, d].rearrange("(c p) -> p c", p=128), (), [rsm], nonc=True)
                    ld(a0[:, d, :], W["rw_a0"][l, d].rearrange("(c p) -> p c", p=128), (), [rsm], nonc=True)
                kkv, r1 = load_vec_fm(st, "a_kk", W["rw_k_k"][l], 4)
                kav, r2 = load_vec_fm(st, "a_ka", W["rw_k_a"][l], 4)
                rkv, r3 = load_vec_fm(st, "a_rk", W["rw_r_k"][l], 4)
                D2 = sbt(st, "a_d2", [128, 512], BF16)
                A2 = sbt(st, "a_a2", [128, 512], BF16)
                G2 = sbt(st, "a_g2", [128, 2, 512], BF16)
                ld(D2[:], W["rw_decay2"][l].rearrange("d l f -> (d l) f"), (), [rsm], q="pool")
                ld(A2[:], W["rw_a2"][l].rearrange("d l f -> (d l) f"), (), [rsm], q="pool")
                ld(G2[:, 0, :], W["rw_g2"][l, 0:128, :], (), [rsm], q="pool")
                ld(G2[0:32, 1, :], W["rw_g2"][l, 128:160, :], (), [rsm], q="pool")
                scanm = sbt(st, "a_scanm", [128, 512], F32)
                ld(scanm[:], cst_in["scanmask"][:, :], (), [rsm])
                rsmall = [rsm, r1, r2, r3, rcw]

                XN = sbt(st, "a_xn", [128, 8, 512], BF16)
                rXN = Res()
                RSt = sbt(st, "a_rs", [128, 512], F32)
                rRS = Res()
                QK = sbt(st, "a_qk", [128, 8, 512], BF16)
                rQK = Res()
                SQ, rSQ = QK, rQK
                VT = sbt(st, "a_vt", [128, 4, 512], BF16)
                rVT = Res()
                QM = sbt(st, "a_qm", [128, 4, 512], BF16)
                rQM = Res()
                PT = [sbt(st, "a_pt%d" % i, [128, 512], BF16) for i in range(2)]
                rPT = [Res(), Res()]
                RD = sbt(st, "a_rd", [128, 512], F32)
                rRD = Res()
                OM = sbt(st, "a_om", [128, 4, 512], BF16)
                rOM = Res()
                ZC = sbt(st, "a_zc", [128, 16, 512], F32)
                rZC = Res()
                X, rX = ZC[:, 0:8, :], rZC
                HZ = sbt(st, "a_hz", [128, 16, max(NH5, 2)], F32)
                rHZ = Res()
                xv = xT[src].rearrange("(c p) t -> p c t", p=128)
                rwcols = [1536 + 128 * i for i in range(15)] + [1536 + 1920]
                rwm = [128] * 15 + [32]
                if NT > 1:
                    XH = sbt(st, "a_xh", [128, 8, NH5], F32)
                    rXH = Res()
                    XHN = sbt(st, "a_xhn", [128, 8, NH5], BF16)
                    rXHN = Res()
                    for c in range(8):
                        srcv = xT[src][c * 128:(c + 1) * 128, 511:T - 1].rearrange("p (j w) -> p j w", w=512)[:, :, 0:2]
                        ld(XH[:, c, :].rearrange("p (j w) -> p j w", w=2), srcv, [R_xT[src]], [rXH], nonc=True)
                    rmsnorm_fm(XH, rXH, XHN, rXHN, gain, rg, NH5, SQ, rSQ, RSt, rRS)
                    memset("dve", HZ[:], 0.0, [rHZ])
                    for zc in range(16):
                        bank = next_ps()
                        m = rwm[zc]
                        for kc in range(8):
                            mm(psum[bank][0:m, 0:NH5], Wi[:, kc, rwcols[zc]:rwcols[zc] + m], XHN[:, kc, :], kc == 0, kc == 7,
                               [rWi, rXHN], [R_ps[bank]])
                        tt("dve", HZ[0:m, zc, :].rearrange("p (j w) -> p j w", w=2),
                           psum[bank][0:m, 0:NH5].rearrange("p (j w) -> p j w", w=2),
                           bm[0:m, 0:NT - 1].unsqueeze(2).to_broadcast([m, NT - 1, 2]), ALU.mult, [R_ps[bank], R_c], [rHZ])

                def tmp(name, dt=F32):
                    return sbt(st, "a_t_" + name, [128, 512], dt), Res(name)

                TH, rTH = tmp("th", BF16)
                XAb, rXAb = tmp("xab", BF16)
                XGb = sbt(st, "a_t_xgb", [128, 2, 512], BF16)
                rXGb = Res()
                SGm, rSGm = tmp("sg")
                Pc, rPc = tmp("pc")
                Pe, rPe = tmp("pe")
                Pt, rPt = tmp("pt")
                E1, rE1 = tmp("e1")
                E2, rE2 = tmp("e2")
                Ee, rEe = tmp("ee")
                Eb, rEb = tmp("eb")
                AS, rAS = tmp("as")
                KD, rKD = tmp("kd")
                KK, rKK = tmp("kk")
                KQ, rKQ = tmp("kq", BF16)
                Bv, rBv = tmp("b")
                BON, rBON = tmp("bon")
                RKb, rRKb = tmp("rkb", BF16)
                TMP, rTMP = tmp("tmp")
                RI, rRI = TMP, rTMP
                OUTS = {}
                for nm in ("at", "bt", "bb", "rt", "kt", "kb"):
                    OUTS[nm] = (sbt(st, "a_o_" + nm, [128, 512], BF16), Res(nm))
                VRb, rVRb = tmp("vrb", BF16)
                Gb, rGb = tmp("gb", BF16)
                WTo = sbt(st, "a_wto", [128, 8], F32)
                rWTo = Res()

                for s in range(NT):
                    t0 = s * 512
                    slot = s // SLOT_ST
                    ld(X[:], xv[:, :, t0:t0 + 512], [R_xT[src]], [rX])
                    rmsnorm_fm(X, rX, XN, rXN, gain, rg, 512, SQ, rSQ, RSt, rRS)
                    def d_qk(c):
                        bank = next_ps()
                        for kc in range(8):
                            mm(psum[bank][:, :], Wi[:, kc, c * 128:(c + 1) * 128], XN[:, kc, :], kc == 0, kc == 7,
                               [rWi, rXN], [R_ps[bank]])
                        if c < 4:
                            act(QK[:, c, :], psum[bank][:, :], AF.Copy, [R_ps[bank]], [rQK], scale=0.125)
                        else:
                            cp("dve", QK[:, c, :], psum[bank][:, :], [R_ps[bank]], [rQK])
                        if c == 3:
                            stor(qT.rearrange("(c p) t -> p c t", p=128)[:, :, t0:t0 + 512], QK[:, 0:4, :], [rQK], [rscr("qT")], rQK)
                        if c == 7:
                            stor(kT.rearrange("(c p) t -> p c t", p=128)[:, :, t0:t0 + 512], QK[:, 4:8, :], [rQK], [rscr("kT")], rQK)

                    def d_v(tl):
                        bank = next_ps()
                        for kc in range(8):
                            mm(psum[bank][:, :], XN[:, kc, tl * 128:(tl + 1) * 128], Wi[:, kc, 1024:1536], kc == 0, kc == 7,
                               [rWi, rXN], [R_ps[bank]])
                        cp("act" if tl % 2 else "dve", VT[:, tl, :], psum[bank][:, :], [R_ps[bank]], [rVT])
                        if tl == 3:
                            stor(vtm[t0:t0 + 512, :].rearrange("(c p) f -> p c f", p=128), VT[:], [rVT], [rscr("vtm")], rVT)

                    def d_mq(h):
                        bank = next_ps()
                        for kc in range(8):
                            mm(psum[bank][:, :], Wi[:, kc, 3488 + h * 128:3488 + (h + 1) * 128], XN[:, kc, :], kc == 0, kc == 7,
                               [rWi, rXN], [R_ps[bank]])
                        act(QM[:, h, :], psum[bank][:, :], AF.Copy, [R_ps[bank]], [rQM], scale=128 ** -0.5)

                    def d_ma(h):
                        for mc in range(2):
                            bank = next_ps()
                            mm(psum[bank][:, :], KmT[:, slot, h, mc * 128:(mc + 1) * 128], QM[:, h, :], True, True,
                               [rKV, rQM], [R_ps[bank]])
                            act(PT[mc][:], psum[bank][:, :], AF.Exp, [R_ps[bank]], [rPT[mc]])
                        bo, bd_ = next_ps(), next_ps()
                        for mc in range(2):
                            mm(psum[bo][:, :], Vm[:, slot, mc, h * 128:(h + 1) * 128], PT[mc][:], mc == 0, mc == 1,
                               [rKV, rPT[mc]], [R_ps[bo]])
                        for mc in range(2):
                            mm(psum[bd_][:, :], ones_b[:], PT[mc][:], mc == 0, mc == 1, [R_c, rPT[mc]], [R_ps[bd_]])
                        act(RD[:], psum[bd_][:, :], AF.Ln, [R_ps[bd_]], [rRD])
                        act(RD[:], RD[:], AF.Exp, [rRD], [rRD], scale=-1.0)
                        tt("dve", OM[:, h, :], psum[bo][:, :], RD[:], ALU.mult, [R_ps[bo], rRD], [rOM])
                        if h == 3:
                            stor(omT.rearrange("(c p) t -> p c t", p=128)[:, :, t0:t0 + 512], OM[:], [rOM], [rscr("omT")], rOM)

                    def dense_slice(it):
                        d_qk(it)
                        if it % 2 == 0:
                            d_v(it // 2)
                            d_mq(it // 2)
                        else:
                            d_ma(it // 2)

                    for zc in range(16):
                        bank = next_ps()
                        m = rwm[zc]
                        P = psum[bank]
                        for kc in range(8):
                            mm(P[0:m, :], Wi[:, kc, rwcols[zc]:rwcols[zc] + m], XN[:, kc, :], kc == 0, kc == 7,
                               [rWi, rXN], [R_ps[bank]])
                        act(ZC[0:m, zc, :], P[0:m, :], AF.Copy, [R_ps[bank], rcw], [rZC], scale=cw[0:m, 1, zc:zc + 1])
                        stt("dve", ZC[0:m, zc, 1:512], P[0:m, 0:511], cw[0:m, 0, zc:zc + 1], ZC[0:m, zc, 1:512], ALU.mult, ALU.add,
                            [R_ps[bank], rcw, rZC], [rZC])
                        stt("dve", ZC[0:m, zc, 0:511], P[0:m, 1:512], cw[0:m, 2, zc:zc + 1], ZC[0:m, zc, 0:511], ALU.mult, ALU.add,
                            [R_ps[bank], rcw, rZC], [rZC])
                        if s > 0:
                            stt("dve", ZC[0:m, zc, 0:1], HZ[0:m, zc, 2 * (s - 1):2 * (s - 1) + 1], cw[0:m, 0, zc:zc + 1],
                                ZC[0:m, zc, 0:1], ALU.mult, ALU.add, [rHZ, rcw, rZC], [rZC])
                        if s < NT - 1:
                            stt("dve", ZC[0:m, zc, 511:512], HZ[0:m, zc, 2 * s + 1:2 * s + 2], cw[0:m, 2, zc:zc + 1],
                                ZC[0:m, zc, 511:512], ALU.mult, ALU.add, [rHZ, rcw, rZC], [rZC])
                    act(TH[:], ZC[:, 12, :], AF.Tanh, [rZC], [rTH])
                    cp("pool", XAb[:], ZC[:, 13, :], [rZC], [rXAb])
                    act(XGb[:, 0, :], ZC[:, 14, :], AF.Sigmoid, [rZC], [rXGb])
                    act(XGb[0:32, 1, :], ZC[0:32, 15, :], AF.Sigmoid, [rZC], [rXGb])
                    for hp in range(4):
                        fs = slice(hp * 128, (hp + 1) * 128)
                        Rr = ZC[:, hp, :]
                        Kr = ZC[:, 4 + hp, :]
                        Vr = ZC[:, 8 + hp, :]
                        mm(psum[4][:, :], G2[:, 0, fs], XGb[:, 0, :], True, False, rsmall + [rXGb], [R_ps[4]])
                        mm(psum[4][:, :], G2[0:32, 1, fs], XGb[0:32, 1, :], False, True, rsmall + [rXGb], [R_ps[4]])
                        cp("act", Gb[:], psum[4][:, :], [R_ps[4]], [rGb])
                        stor(rws["g"][fs, t0:t0 + 512], Gb[:], [rGb], [rscr("g")], rGb)
                        cp("pool", VRb[:], Vr, [rZC], [rVRb])
                        stor(rws["v"][fs, t0:t0 + 512], VRb[:], [rVRb], [rscr("v")], rVRb)
                        ts("dve", KK[:], Kr, kkv[:, hp:hp + 1], None, ALU.mult, None, [rZC] + rsmall, [rKK])
                        tt("pool", KQ[:], KK[:], KK[:], ALU.mult, [rKK], [rKQ])
                        mm(psum[4][:, :], bd64_b[:], KQ[:], True, True, [R_c, rKQ], [R_ps[4]])
                        ts("dve", RI[:], psum[4][:, :], 1e-24, None, ALU.max, None, [R_ps[4]], [rRI])
                        ts("dve", RI[:], RI[:], -0.5, None, ALU.pow, None, [rRI], [rRI])
                        tt("dve", KK[:], KK[:], RI[:], ALU.mult, [rKK, rRI], [rKK])
                        for d in range(2):
                            ds = slice(d * 64, (d + 1) * 64)
                            cexp = 0.6065306597126334
                            mm(psum[5][:, :], D2[ds, fs], TH[ds, :], True, True, rsmall + [rTH], [R_ps[5]])
                            mm(psum[6][:, :], A2[ds, fs], XAb[ds, :], True, True, rsmall + [rXAb], [R_ps[6]])
                            act(SGm[:], psum[5][:, :], AF.Sigmoid, [R_ps[5]] + rsmall, [rSGm], bias=dec0[:, d, hp:hp + 1])
                            act(AS[:], psum[6][:, :], AF.Sigmoid, [R_ps[6]] + rsmall, [rAS], bias=a0[:, d, hp:hp + 1])
                            S.op("dve", lambda e: e.tensor_tensor_scan(out=Pc[:], data0=scanm[:], data1=SGm[:], initial=0.0,
                                                                        op0=ALU.mult, op1=ALU.add), [rSGm] + rsmall, [rPc])
                            Pc3 = Pc[:].rearrange("p (c t) -> p c t", t=64)
                            if d == 1:
                                tt("pool", Pe[:].rearrange("p (c t) -> p c t", t=64), Pc3[:, :, 63:64].to_broadcast([128, 8, 64]),
                                   Pc3, ALU.subtract, [rPc], [rPe])
                                tt("pool", Pc[:], Pe[:], SGm[:], ALU.add, [rPe, rSGm], [rPc])
                                totcol = 0
                            else:
                                totcol = 63
                            tt("pool", Pe[:], Pc[:], SGm[:], ALU.subtract, [rPc, rSGm], [rPe])
                            tt("pool", Pt[:].rearrange("p (c t) -> p c t", t=64),
                               Pc3[:, :, totcol:totcol + 1].to_broadcast([128, 8, 64]), Pc3, ALU.subtract, [rPc], [rPt])
                            ts("dve", TMP[:], AS[:], -1.0, kav[:, hp:hp + 1], ALU.add, ALU.mult, [rAS] + rsmall, [rTMP])
                            stt("dve", KD[:], TMP[:], 1.0, Kr, ALU.add, ALU.mult, [rTMP, rZC], [rKD])
                            tt("dve", Bv[:], KK[:], AS[:], ALU.mult, [rKK, rAS], [rBv])
                            stt("dve", RKb[:], Rr, rkv[:, hp:hp + 1], KD[:], ALU.mult, ALU.mult, [rZC, rKD] + rsmall, [rRKb])
                            mm(psum[4][:, :], bd64_b[:], RKb[:], True, True, [R_c, rRKb], [R_ps[4]])
                            act(E1[:], Pc[:], AF.Exp, [rPc], [rE1], scale=-cexp)
                            act(E2[:], Pc[:], AF.Exp, [rPc], [rE2], scale=cexp)
                            act(Ee[:], Pe[:], AF.Exp, [rPe], [rEe], scale=-cexp)
                            act(Eb[:], Pt[:], AF.Exp, [rPt], [rEb], scale=-cexp)
                            if d == 0:
                                tt("dve", BON[:], psum[4][:, :], Vr, ALU.mult, [R_ps[4], rZC], [rBON])
                            else:
                                tt("dve", TMP[:], psum[4][:, :], Vr, ALU.mult, [R_ps[4], rZC], [rTMP])
                                tt("pool", BON[:], BON[:], TMP[:], ALU.add, [rBON, rTMP], [rBON])
                            cp("pool", WTo[:, :], E1[:].rearrange("p (c t) -> p c t", t=64)[:, :, totcol], [rE1], [rWTo])
                            stor(rws["wtot", d][fs, s * 8:(s + 1) * 8], WTo[:], [rWTo], [rscr(("wtot", d))], rWTo)
                            o, ro = OUTS["rt"]
                            tt("pool", o[:], Rr, E1[:], ALU.mult, [rZC, rE1], [ro])
                            o, ro = OUTS["bt"]
                            tt("dve", o[:], Bv[:], E2[:], ALU.mult, [rBv, rE2], [ro])
                            o, ro = OUTS["kt"]
                            tt("pool", o[:], KD[:], E2[:], ALU.mult, [rKD, rE2], [ro])
                            o, ro = OUTS["at"]
                            stt("dve", o[:], KK[:], -1.0, Ee[:], ALU.mult, ALU.mult, [rKK, rEe], [ro])
                            o, ro = OUTS["bb"]
                            tt("dve", o[:], Bv[:], Eb[:], ALU.mult, [rBv, rEb], [ro])
                            o, ro = OUTS["kb"]
                            tt("pool", o[:], KD[:], Eb[:], ALU.mult, [rKD, rEb], [ro])
                            for nm in ("rt", "bt", "kt", "at", "bb", "kb"):
                                o, ro = OUTS[nm]
                                stor(rws[nm, d][fs, t0:t0 + 512], o[:], [ro], [rscr((nm, d))], ro)
                            dense_slice(hp * 2 + d)
                        stor(rws["bonus"][fs, t0:t0 + 512], BON[:], [rBON], [rscr("bonus")], rBON)
            S.barrier()

        def phase_b1(l):
            with ExitStack() as st:
                Bt = sbt(st, "b1_bt", [64, 8 * 17 * 64], BF16)
                rBt = Res()
                ld(Bt[:], btab_in[l], (), [rBt], q="pool")
                Bt4 = Bt[:].rearrange("p (h o k) -> p h o k", h=8, o=17)
                RB = sbt(st, "b1_rb", [128, nslots_na], F32)
                rRB = Res()
                ld(RB[:], rowbias_in[:, :], (), [rRB])
                ExpB = sbt(st, "b1_expb", [128, 16, 512], BF16)
                rExpB = Res()
                for o in range(16):
                    bank = next_ps()
                    for h in range(8):
                        mm(psum[bank][:, h * 64:(h + 1) * 64], Bt4[:, h, o:o + 2, :].rearrange("p o k -> p (o k)"),
                           ident_b[0:64, 0:64], True, True, [rBt, R_c], [R_ps[bank]])
                    act(ExpB[:, o, :], psum[bank][:, :], AF.Exp, [R_ps[bank]], [rExpB])
                NR = min(24, ROWS)
                KW = [sbt(st, "b1_kw%d" % i, [64, 8, NR * 64], BF16) for i in range(2)]
                rKW = [Res(), Res()]
                QW = [sbt(st, "b1_qw%d" % i, [64, 8, 512], BF16) for i in range(2)]
                rQW = [Res(), Res()]
                VE = [sbt(st, "b1_ve%d" % i, [128, NR // 2, 512], BF16) for i in range(2)]
                rVE = [Res(), Res()]
                VO = [sbt(st, "b1_vo%d" % i, [128, NR // 2 - 1, 512], BF16) for i in range(2)]
                rVO = [Res(), Res()]
                PTt = [sbt(st, "b1_pt%d" % i, [128, 512], BF16) for i in range(4)]
                rPTt = [Res() for _ in range(4)]
                RDt = sbt(st, "b1_rd", [64, 512], F32)
                rRDt = Res()
                ON = [sbt(st, "b1_on%d" % i, [64, 8, 512], BF16) for i in range(2)]
                rON = [Res(), Res()]
                qv = qT.rearrange("(h d) t -> d h t", d=64)
                kv = kT.rearrange("(h d) t -> d h t", d=64)

                def loads(s):
                    b = s % 2
                    i0 = s * 8
                    rb = min(max(i0 - 8, 0), ROWS - NR)
                    ld(QW[b][:], qv[:, :, s * 512:(s + 1) * 512], [rscr("qT")], [rQW[b]])
                    ld(KW[b][:], kv[:, :, rb * 64:(rb + NR) * 64], [rscr("kT")], [rKW[b]])
                    ld(VE[b][:], vtm[rb * 64:(rb + NR) * 64, :].rearrange("(c p) f -> p c f", p=128), [rscr("vtm")], [rVE[b]])
                    ld(VO[b][:], vtm[rb * 64 + 64:(rb + NR) * 64 - 64, :].rearrange("(c p) f -> p c f", p=128), [rscr("vtm")],
                       [rVO[b]])
                    return rb

                slot = 0
                pk = 0
                sk = [0]
                rbs = {0: loads(0)}
                for s in range(NT):
                    b = s % 2
                    if s + 1 < NT:
                        rbs[s + 1] = loads(s + 1)
                    rb = rbs[s]
                    for iq in range(8):
                        i = s * 8 + iq
                        band = bands[i]
                        for ci, r in enumerate(band):
                            o = r - i + 8
                            kc0 = (r - rb) * 64
                            sk[0] = (sk[0] + 1) % 3
                            bank = sk[0]
                            for h in range(8):
                                mm(psum[bank][:, h * 64:(h + 1) * 64], KW[b][:, h, kc0:kc0 + 128], QW[b][:, h, iq * 64:(iq + 1) * 64],
                                   True, True, [rKW[b], rQW[b]], [R_ps[bank]])
                            pb = pk % 4
                            pk += 1
                            act(PTt[pb][:], psum[bank][:, :], AF.Exp, [R_ps[bank], rRB], [rPTt[pb]], bias=RB[:, slot:slot + 1])
                            tt("pool", PTt[pb][:], PTt[pb][:], ExpB[:, o, :], ALU.mult, [rPTt[pb], rExpB], [rPTt[pb]])
                            slot += 1
                            if (r - rb) % 2 == 0:
                                Vc, rVc = VE[b][:, (r - rb) // 2, :], rVE[b]
                            else:
                                Vc, rVc = VO[b][:, (r - rb - 1) // 2, :], rVO[b]
                            first, last = ci == 0, ci == len(band) - 1
                            bo, bd_ = (5, 6) if i % 2 == 0 else (3, 4)
                            for h in range(8):
                                mm(psum[bo][0:64, h * 64:(h + 1) * 64], Vc[:, h * 64:(h + 1) * 64], PTt[pb][:, h * 64:(h + 1) * 64],
                                   first and h == 0, last, [rVc, rPTt[pb]], [R_ps[bo]])
                            mm(psum[bd_][0:64, :], ones_b[:, 0:64], PTt[pb][:], first, last, [R_c, rPTt[pb]], [R_ps[bd_]])
                        act(RDt[:], psum[bd_][0:64, :], AF.Ln, [R_ps[bd_]], [rRDt])
                        act(RDt[:], RDt[:], AF.Exp, [rRDt], [rRDt], scale=-1.0)
                        tt("dve", ON[b][:, :, iq * 64:(iq + 1) * 64], psum[bo][0:64, :].rearrange("p (h q) -> p h q", h=8),
                           RDt[:].rearrange("p (h q) -> p h q", h=8), ALU.mult, [R_ps[bo], rRDt], [rON[b]])
                    stor(onaT.rearrange("(h d) t -> d h t", d=64)[:, :, s * 512:(s + 1) * 512], ON[b][:], [rON[b]],
                         [rscr("onaT")], rON[b])
            S.barrier()

        def phase_b2(l):
            with ExitStack() as st:
                mk = {}
                rM = Res()
                for nm in ("ls4", "li4", "us4", "ui4"):
                    mk[nm] = sbt(st, "b2_" + nm, [128, 512], F32)
                    ld(mk[nm][:], cst_in[nm][:, :], (), [rM])
                MS = [mk["ls4"], mk["us4"]]
                MI = [mk["li4"], mk["ui4"]]
                MP = [mk["us4"], mk["ls4"]]
                lnw = sbt(st, "b2_lnw", [128, 4], F32)
                lnb = sbt(st, "b2_lnb", [128, 4], F32)
                ld(lnw[:], W["rw_lnx_w"][l].rearrange("(c p) -> p c", p=128), (), [rM], nonc=True)
                ld(lnb[:], W["rw_lnx_b"][l].rearrange("(c p) -> p c", p=128), (), [rM], nonc=True)
                names = ("at", "bt", "bb", "rt", "kt", "kb", "v")
                IN = {nm: [sbt(st, "b2_i_%s%d" % (nm, i), [64, 8, 512], BF16) for i in range(2)] for nm in names}
                rIN = [Res(), Res()]
                WT = [sbt(st, "b2_wt%d" % i, [64, 8, 8], F32) for i in range(2)]
                AM = sbt(st, "b2_am", [128, 4, 8, 128], BF16)
                rAM = [Res(), Res()]
                PPp = [sbt(st, "b2_ppp%d" % i, [128, 8, 128], BF16) for i in range(2)]
                PPt = [sbt(st, "b2_ppt%d" % i, [128, 8, 128], BF16) for i in range(2)]
                rPP = [[Res(), Res()], [Res(), Res()]]
                Xc = [sbt(st, "b2_x%d" % i, [128, 8, 128], BF16) for i in range(2)]
                rXc = [[Res(), Res()], [Res(), Res()]]
                TMx = sbt(st, "b2_tm", [128, 9 * 192], BF16)
                rTM = [Res(), Res()]
                memset("dve", TMx[:], 0.0, rTM)
                TMf = TMx[:, :]
                TM = TMx[:, 0:8 * 192].rearrange("p (h q) -> p h q", h=8)
                RH = sbt(st, "b2_rh", [64, 8, 2, 128], BF16)
                rRH = Res()
                MTt = sbt(st, "b2_mt", [64, 8, 2, 128], BF16)
                rMTt = Res()
                MTf = sbt(st, "b2_mtf", [64, 8, 2, 64], F32)
                rMTf = Res()
                DW = sbt(st, "b2_dw", [64, 8, 2, 64], F32)
                rDW = Res()
                N0 = sbt(st, "b2_n0", [64, 8, 2, 64], F32)
                rN0 = Res()
                Sb = sbt(st, "b2_sb", [64, 8, 64], BF16)
                rSb = Res()
                YF = [sbt(st, "b2_yf%d" % i, [128, 512], F32) for i in range(2)]
                rYF = [Res(), Res()]
                YL = [sbt(st, "b2_yl%d" % i, [128, 512], F32) for i in range(2)]
                rYL = [Res(), Res()]
                YC = sbt(st, "b2_yc", [128, 512], F32)
                rYC = Res()
                YS = sbt(st, "b2_ys", [128, 512], F32)
                rYS = Res()
                YN = sbt(st, "b2_yn", [128, 512], BF16)
                rYN = Res()
                st8 = sbt(st, "b2_st8", [128, 8], F32)
                rst8 = Res()
                st8b = sbt(st, "b2_st8b", [128, 8], F32)
                rst8b = Res()
                BONt = [sbt(st, "b2_bon%d" % i, [128, 4, 128], F32) for i in range(2)]
                Gt = [sbt(st, "b2_g%d" % i, [128, 4, 128], BF16) for i in range(2)]
                rBG = [Res(), Res()]
                OT = sbt(st, "b2_ot", [128, 4, 128], F32)
                rOT = Res()
                OR = [sbt(st, "b2_or%d" % i, [128, 4, 128], BF16) for i in range(2)]
                rOR = [Res(), Res()]
                memset("dve", RH[:], 0.0, [rRH])
                memset("dve", MTt[:], 0.0, [rMTt])

                def hview(ap):
                    return ap.rearrange("(h j) t -> j h t", j=64)

                def v4(ap):
                    return ap.rearrange("p (h t) -> p h t", h=4)

                for d in range(2):
                    memset("dve", Sb[:], 0.0, [rSb])
                    order = list(range(NT)) if d == 0 else list(range(NT - 1, -1, -1))

                    def loads(idx, d=d, order=order):
                        s = order[idx]
                        b = idx % 2
                        for nm in names:
                            srcap = rws["v"] if nm == "v" else rws[nm, d]
                            rk = rscr("v") if nm == "v" else rscr((nm, d))
                            ld(IN[nm][b][:], hview(srcap)[:, :, s * 512:(s + 1) * 512], [rk], [rIN[b]])
                        ld(WT[b][:], hview(rws["wtot", d])[:, :, s * 8:(s + 1) * 8], [rscr(("wtot", d))], [rIN[b]], nonc=True)

                    loads(0)
                    tcount = 0
                    for idx, s in enumerate(order):
                        b = idx % 2
                        if idx + 1 < NT:
                            loads(idx + 1)
                        rI = rIN[b]
                        if d == 0 and s > 0:
                            ts("dve", Sb[:], Sb[:], bm[0:64, s - 1:s], None, ALU.mult, None, [rSb, R_c], [rSb])
                        if d == 1 and s < NT - 1:
                            ts("dve", Sb[:], Sb[:], bm[0:64, s:s + 1], None, ALU.mult, None, [rSb, R_c], [rSb])
                        tiles = list(range(4)) if d == 0 else [3, 2, 1, 0]
                        for tl in tiles:
                            tc_ = slice(tl * 128, (tl + 1) * 128)
                            gt0 = s * 512 + tl * 128
                            yb = tcount % 2
                            tcount += 1
                            if d == 1:
                                ld(YL[yb][:], yfw[gt0:gt0 + 128, :], [rscr("yfw")], [rYL[yb]])
                                ld(BONt[yb][:], rws["bonus"].rearrange("(c p) t -> p c t", p=128)[:, :, gt0:gt0 + 128],
                                   [rscr("bonus")], [rBG[yb]])
                                ld(Gt[yb][:], rws["g"].rearrange("(c p) t -> p c t", p=128)[:, :, gt0:gt0 + 128], [rscr("g")],
                                   [rBG[yb]])

                            def I(nm, h):
                                return IN[nm][b][:, h, tc_]

                            typs = (("bt", "at", MS), ("kt", "at", MS), ("bt", "rt", MI), ("kt", "rt", MI))
                            for ty, (ln, rn, msk) in enumerate(typs):
                                for hg in range(2):
                                    bank = next_ps()
                                    for j in range(4):
                                        h = hg * 4 + j
                                        mm(psum[bank][:, j * 128:(j + 1) * 128], I(ln, h), I(rn, h), True, True, [rI], [R_ps[bank]])
                                    tt("dve", AM[:, ty, hg * 4:(hg + 1) * 4, :], v4(psum[bank][:, :]), v4(msk[d][:]), ALU.mult,
                                       [R_ps[bank], rM], [rAM[hg]])
                            for hg in range(2):
                                bank = next_ps()
                                for j in range(4):
                                    h = hg * 4 + j
                                    mm(psum[bank][:, j * 128:(j + 1) * 128], I("at", h), I("bt", h), True, True, [rI], [R_ps[bank]])
                                tt("dve", PPp[0][:, hg * 4:(hg + 1) * 4, :], v4(psum[bank][:, :]), v4(MP[d][:]), ALU.mult,
                                   [R_ps[bank], rM], [rPP[0][hg]])
                            for hg in range(2):
                                for j in range(4):
                                    h = hg * 4 + j
                                    for q, nm in enumerate(("v", "bb", "kb", "at")):
                                        tr(psT[:, j * 256 + q * 64:j * 256 + (q + 1) * 64], I(nm, h), ident_b[0:64, 0:64],
                                           [rI, R_c], [R_psT])
                                pv4 = psT[:, :].rearrange("p (h q) -> p h q", h=4)
                                cp("act", TM[:, hg * 4:(hg + 1) * 4, :], pv4[:, :, 0:192], [R_psT], [rTM[hg]])
                                cp("act", Xc[0][:, hg * 4:(hg + 1) * 4, 0:64], pv4[:, :, 192:256], [R_psT], [rXc[0][hg]])
                            for hg in range(2):
                                bank = next_ps()
                                for j in range(4):
                                    h = hg * 4 + j
                                    mm(psum[bank][:, j * 64:(j + 1) * 64], AM[:, 1, h, :], TM[:, h, 0:64], True, True,
                                       [rAM[hg], rTM[hg]], [R_ps[bank]])
                                cp("dve", Xc[0][:, hg * 4:(hg + 1) * 4, 64:128],
                                   psum[bank][:, 0:256].rearrange("p (h i) -> p h i", h=4), [R_ps[bank]], [rXc[0][hg]])
                            if LVL < 2:
                                continue
                            for hg in range(2):
                                for j in range(4):
                                    h = hg * 4 + j
                                    mm(psum[hg][:, j * 128:(j + 1) * 128], ident_b[:], Xc[0][:, h, :], j == 0, False,
                                       [R_c, rXc[0][hg]], [R_ps[hg]])
                            for k in range(6):
                                cur, nxt = k % 2, (k + 1) % 2
                                for hg in range(2):
                                    rP = rAM[hg] if k == 0 else rPP[cur][hg]
                                    bx = hg
                                    for j in range(4):
                                        h = hg * 4 + j
                                        Pt_h = AM[:, 0, h, :] if k == 0 else PPt[cur][:, h, :]
                                        mm(psum[bx][:, j * 128:(j + 1) * 128], Pt_h, Xc[cur][:, h, :], False, k == 5,
                                           [rP, rXc[cur][hg]], [R_ps[bx]])
                                    cp("act", Xc[nxt][:, hg * 4:(hg + 1) * 4, :], v4(psum[bx][:, :]), [R_ps[bx]], [rXc[nxt][hg]])
                                    if k < 5:
                                        bp_, bt_ = 2 + hg, 5 + hg
                                        rPp = rPP[cur][hg]
                                        for j in range(4):
                                            h = hg * 4 + j
                                            Pt_h = AM[:, 0, h, :] if k == 0 else PPt[cur][:, h, :]
                                            Pp_h = PPp[cur][:, h, :]
                                            mm(psum[bp_][:, j * 128:(j + 1) * 128], Pt_h, Pp_h, True, True, [rP, rPp], [R_ps[bp_]])
                                            mm(psum[bt_][:, j * 128:(j + 1) * 128], Pp_h, Pt_h, True, True, [rP, rPp], [R_ps[bt_]])
                                        cp("dve", PPp[nxt][:, hg * 4:(hg + 1) * 4, :], v4(psum[bp_][:, :]), [R_ps[bp_]], [rPP[nxt][hg]])
                                        cp("act", PPt[nxt][:, hg * 4:(hg + 1) * 4, :], v4(psum[bt_][:, :]), [R_ps[bt_]], [rPP[nxt][hg]])
                            XF = Xc[0]
                            rXF = rXc[0]
                            if LVL < 3:
                                continue
                            for h in range(8):
                                hg = h // 4
                                ys = slice(h * 64, (h + 1) * 64)
                                mm(psum[4][:, ys], AM[:, 2, h, :], XF[:, h, 64:128], h == 0, False, [rAM[hg], rXF[hg]], [R_ps[4]])
                                mm(psum[4][:, ys], AM[:, 3, h, :], TM[:, h, 0:64], False, False, [rAM[hg], rTM[hg]], [R_ps[4]])
                            for hg in range(2):
                                bank = hg
                                for j in range(4):
                                    h = hg * 4 + j
                                    mm(psum[bank][0:64, j * 128:(j + 1) * 128], XF[:, h, 0:64], AM[:, 2, h, :], True, True,
                                       [rXF[hg], rAM[hg]], [R_ps[bank]])
                                for c in range(2):
                                    cs = slice(c * 64, (c + 1) * 64)
                                    tt("dve", RH[:, hg * 4:(hg + 1) * 4, c, cs], v4(psum[bank][0:64, :])[:, :, cs],
                                       IN["rt"][b][:, hg * 4:(hg + 1) * 4, tl * 128 + c * 64:tl * 128 + (c + 1) * 64], ALU.add,
                                       [R_ps[bank], rI], [rRH])
                            if LVL < 4:
                                continue
                            tt("pool", DW[:], ident_f[0:64, 0:64].unsqueeze(1).unsqueeze(1).to_broadcast([64, 8, 2, 64]),
                               WT[b][:, :, tl * 2:tl * 2 + 2].unsqueeze(3).to_broadcast([64, 8, 2, 64]), ALU.mult, [rI, R_c], [rDW])
                            for hg in range(2):
                                bm_, bn_ = 2 + hg, 5 + hg
                                for j in range(4):
                                    h = hg * 4 + j
                                    for c in range(2):
                                        ps_ = slice(0, 64) if c == 0 else slice(0, 128)
                                        col = (j * 2 + c) * 64
                                        mm(psum[bm_][:, col:col + 64], XF[ps_, h, 0:128], TM[ps_, h, 64:128], True, True,
                                           [rXF[hg], rTM[hg]], [R_ps[bm_]])
                                        mm(psum[bn_][:, col:col + 64], TM[ps_, h, 64:192], XF[ps_, h, 64:128], True, False,
                                           [rXF[hg], rTM[hg]], [R_ps[bn_]])
                                        mm(psum[bn_][:, col:col + 64], TMf[ps_, h * 192 + 128:h * 192 + 256], TM[ps_, h, 0:64], False, True,
                                           [rTM[hg]], [R_ps[bn_]])
                                hs = slice(hg * 4, (hg + 1) * 4)
                                cp("dve", MTf[:, hs, :, :], psum[bm_][0:64, :].rearrange("p (h c i) -> p h c i", h=4, c=2),
                                   [R_ps[bm_]], [rMTf])
                                tt("pool", MTf[:, hs, 1, :], MTf[:, hs, 1, :], MTf[:, hs, 0, :], ALU.subtract, [rMTf], [rMTf])
                                tt("pool", MTt[:, hs, :, 0:64], MTf[:, hs, :, :], DW[:, hs, :, :], ALU.add, [rMTf, rDW], [rMTt])
                                cp("act", N0[:, hs, :, :], psum[bn_][0:64, :].rearrange("p (h c i) -> p h c i", h=4, c=2),
                                   [R_ps[bn_]], [rN0])
                                tt("pool", N0[:, hs, 1, :], N0[:, hs, 1, :], N0[:, hs, 0, :], ALU.subtract, [rN0], [rN0])
                            if LVL < 5:
                                continue
                            for ci, c in enumerate((0, 1) if d == 0 else (1, 0)):
                                for h in range(8):
                                    ys = slice(h * 64, (h + 1) * 64)
                                    mm(psum[4][:, ys], RH[:, h, c, :], Sb[:, h, :], False, ci == 1, [rRH, rSb], [R_ps[4]])
                                bank = ci
                                for h in range(8):
                                    mm(psum[bank][:, h * 64:(h + 1) * 64], MTt[:, h, c, :], Sb[:, h, :], True, True,
                                       [rMTt, rSb], [R_ps[bank]])
                                tt("dve", Sb[:], psum[bank][0:64, :].rearrange("p (h i) -> p h i", h=8), N0[:, :, c, :], ALU.add,
                                   [R_ps[bank], rN0], [rSb])
                            if LVL < 6:
                                continue
                            if d == 0:
                                cp("act", YF[yb][:], psum[4][:, :], [R_ps[4]], [rYF[yb]])
                                stor(yfw[gt0:gt0 + 128, :], YF[yb][:], [rYF[yb]], [rscr("yfw")], rYF[yb])
                            else:
                                tt("dve", YF[yb][:], psum[4][:, :], YL[yb][:], ALU.add, [R_ps[4], rYL[yb]], [rYF[yb]])
                                Y3 = YF[yb][:].rearrange("p (h i) -> p h i", h=8)
                                S.op("dve", lambda e, Y3=Y3: e.tensor_reduce(out=st8[:], in_=Y3, axis=AX.X, op=ALU.add),
                                     [rYF[yb]], [rst8])
                                ts("dve", st8[:], st8[:], -1.0 / 64, None, ALU.mult, None, [rst8], [rst8])
                                tt("pool", YC[:].rearrange("p (h i) -> p h i", h=8), Y3,
                                   st8[:].unsqueeze(2).to_broadcast([128, 8, 64]), ALU.add, [rYF[yb], rst8], [rYC])
                                act(YS[:], YC[:], AF.Square, [rYC], [rYS])
                                S.op("dve", lambda e: e.tensor_reduce(out=st8b[:], in_=YS[:].rearrange("p (h i) -> p h i", h=8),
                                                                      axis=AX.X, op=ALU.add), [rYS], [rst8b])
                                ts("dve", st8b[:], st8b[:], 1.0 / 64, 64e-5, ALU.mult, ALU.add, [rst8b], [rst8b])
                                ts("dve", st8b[:], st8b[:], -0.5, None, ALU.pow, None, [rst8b], [rst8b])
                                tt("pool", YN[:].rearrange("p (h i) -> p h i", h=8), YC[:].rearrange("p (h i) -> p h i", h=8),
                                   st8b[:].unsqueeze(2).to_broadcast([128, 8, 64]), ALU.mult, [rYC, rst8b], [rYN])
                                for fc in range(4):
                                    tr(psT[:, fc * 128:(fc + 1) * 128], YN[:, fc * 128:(fc + 1) * 128], ident_b[:],
                                       [rYN, R_c], [R_psT])
                                pv = psT[:, 0:512].rearrange("p (c t) -> p c t", c=4)
                                tt("dve", OT[:], pv, lnw[:].unsqueeze(2).to_broadcast([128, 4, 128]), ALU.mult, [R_psT, rM], [rOT])
                                tt("pool", OT[:], OT[:], lnb[:].unsqueeze(2).to_broadcast([128, 4, 128]), ALU.add, [rOT, rM], [rOT])
                                tt("dve", OT[:], OT[:], BONt[yb][:], ALU.add, [rOT, rBG[yb]], [rOT])
                                tt("pool", OR[yb][:], OT[:], Gt[yb][:], ALU.mult, [rOT, rBG[yb]], [rOR[yb]])
                                stor(orwT.rearrange("(c p) t -> p c t", p=128)[:, :, gt0:gt0 + 128], OR[yb][:], [rOR[yb]],
                                     [rscr("orwT")], rOR[yb])
            S.barrier()

        if phases is None:
            phases = ["p0"] + sum([["a%d" % l, "b1%d" % l, "b2%d" % l, "c1%d" % l, "c2%d" % l] for l in range(depth)], []) + ["e"]
        for ph in phases:
            if ph == "p0":
                phase_p0()
            elif ph == "e":
                phase_e(0)
            elif ph[0] == "a":
                phase_a(int(ph[1:]), 0)
            elif ph[:2] == "b1":
                phase_b1(int(ph[2:]))
            elif ph[:2] == "b2":
                phase_b2(int(ph[2:]))
            elif ph[:2] == "c1":
                phase_c1(int(ph[2:]), 0, 1)
            elif ph[:2] == "c2":
                phase_c2(int(ph[2:]), 1, 0)
        S.emit()
        nc._n_inst = S.ninst
    return nc


def core_inputs(x_seqs, mem_seqs, T, RS, typ, weights, depth):
    NT = T // 512
    SLOT_ST = max(NT // 4, 1)
    NSLOT = NT // SLOT_ST
    ROWS = T // 64
    xin = np.zeros((T, D), np.float32)
    memv = np.zeros((NSLOT, 256, D), np.float32)
    if typ == "P":
        xin[:] = x_seqs[0]
        for sl in range(NSLOT):
            memv[sl] = mem_seqs[0]
        starts = {0}
    else:
        L = RS * 64
        for i, xs in enumerate(x_seqs):
            xin[i * L:(i + 1) * L] = xs
            sl0 = (i * L) // (SLOT_ST * 512)
            memv[sl0] = mem_seqs[i]
        starts = set(range(0, T, L))
    bmv = np.ones((128, max(NT - 1, 1)), np.float32)
    for j in range(NT - 1):
        if (j + 1) * 512 in starts:
            bmv[:, j] = 0.0
    bm2 = np.ones((128, max(T // 256 - 1, 1)), np.float32)
    for j in range(T // 256 - 1):
        if (j + 1) * 256 in starts:
            bm2[:, j] = 0.0
    m = {"xin": xin, "mem": memv, "bm": bmv, "bm2": bm2, "rowbias": na_rowbias(ROWS, RS, typ)}
    m["btab"] = np.stack([na_btab(weights["na_rpb"][l]).reshape(64, -1) for l in range(depth)])
    for k, v in make_consts().items():
        m["c_" + k] = v
    for k, v in weights.items():
        if k == "na_rpb":
            continue
        if k == "rw_r_k":
            v = v.reshape(v.shape[0], 512)
        if k == "final_norm":
            v = v.reshape(1, D)
        m[k] = np.ascontiguousarray(v, dtype=np.float32)
    return m


_CACHE = {}


def kernel(**inputs):
    T, RS, depth = 8192, 32, 2
    xp = np.asarray(inputs["x_prompt"], np.float32)
    xs = np.asarray(inputs["x_sample"], np.float32)
    mp = np.asarray(inputs["mem_prompt"], np.float32)
    ms = np.asarray(inputs["mem_sample"], np.float32)
    weights = {k: np.asarray(v, np.float32) for k, v in inputs.items()
               if k not in ("x_prompt", "x_sample", "mem_prompt", "mem_sample")}
    in_maps = []
    for c in range(4):
        in_maps.append(core_inputs([xp[c]], [mp[c]], T, RS, "P", weights, depth))
    for c in range(4):
        sq = [2 * c, 2 * c + 1, 2 * c, 2 * c + 1]
        in_maps.append(core_inputs([xs[i] for i in sq], [ms[i] for i in sq], T, RS, "S", weights, depth))
    if "nc" not in _CACHE:
        _CACHE["nc"] = build(T, RS, depth)
    res = run_bass_kernel_spmd(_CACHE["nc"], in_maps, core_ids=list(range(8)))
    yp = np.stack([res.results[c]["yout"] for c in range(4)]).astype(np.float32)
    ys = np.zeros_like(xs)
    for c in range(4):
        y = res.results[4 + c]["yout"]
        ys[2 * c] = y[0:2048]
        ys[2 * c + 1] = y[2048:4096]
    return (yp, ys)
```

```python
from contextlib import ExitStack
import os
LVL = int(os.environ.get('B2DBG', '9'))
B2X = int(os.environ.get('B2X', '0'))
import numpy as np
import concourse.bass as bass
import concourse.mybir as mybir
from concourse.bass_utils import run_bass_kernel_spmd

F32 = mybir.dt.float32
BF16 = mybir.dt.bfloat16
AF = mybir.ActivationFunctionType
ALU = mybir.AluOpType
AX = mybir.AxisListType

D = 1024
NEG = -30000.0
ENGS = ("pe", "act", "dve", "pool", "sp")


class Res:
    __slots__ = ("name", "w", "r", "dsem", "multi")

    def __init__(self, name="", multi=False):
        self.name = name
        self.w = {}
        self.r = {}
        self.dsem = None
        self.multi = multi


class Sched:
    def __init__(self, nc, stack, n_dma_sems=64):
        self.nc = nc
        self.q = {e: [] for e in ENGS}
        self.cnt = {e: 0 for e in ENGS}
        self.sem = {e: stack.enter_context(nc.semaphore("s_" + e)) for e in ENGS}
        self.dma_sems = [stack.enter_context(nc.semaphore("d%d" % i)) for i in range(n_dma_sems)]
        self.dma_cnt = [0] * n_dma_sems
        self.dma_next = 0
        self.seen = {e: {} for e in ENGS}
        self.ninst = 0

    def _semobj(self, key):
        return self.sem[key] if isinstance(key, str) else self.dma_sems[key]

    def _deps(self, eng, reads, writes):
        toks = {}
        for r in reads:
            for k, v in r.w.items():
                if toks.get(k, 0) < v:
                    toks[k] = v
        for w in writes:
            for k, v in w.w.items():
                if toks.get(k, 0) < v:
                    toks[k] = v
            for k, v in w.r.items():
                if toks.get(k, 0) < v:
                    toks[k] = v
        waits = []
        seen = self.seen[eng]
        for k, v in toks.items():
            if k == eng and eng == "pe":
                continue
            if seen.get(k, 0) >= v:
                continue
            seen[k] = v
            waits.append((k, v))
        return waits

    def _mark(self, tok, reads, writes):
        k, v = tok
        for r in reads:
            if r.r.get(k, 0) < v:
                r.r[k] = v
        for w in writes:
            w.w[k] = v
            if not w.multi:
                w.r = {}

    def op(self, eng, fn, reads=(), writes=()):
        waits = self._deps(eng, reads, writes)
        self.cnt[eng] += 1
        self._mark((eng, self.cnt[eng]), reads, writes)
        self.q[eng].append((waits, fn, (eng, 1)))
        self.ninst += 1

    def dma(self, eng, fn, reads=(), writes=(), sem_res=None):
        waits = self._deps(eng, reads, writes)
        if sem_res is None:
            sem_res = writes[0]
        if sem_res.dsem is None:
            sem_res.dsem = self.dma_next % len(self.dma_sems)
            self.dma_next += 1
        k = sem_res.dsem
        self.dma_cnt[k] += 16
        self._mark((k, self.dma_cnt[k]), reads, writes)
        self.q[eng].append((waits, fn, (k, 16)))
        self.ninst += 1

    def barrier(self):
        tot = {e: self.cnt[e] for e in ENGS if self.cnt[e]}
        for k, c in enumerate(self.dma_cnt):
            if c:
                tot[k] = c
        for e in ENGS:
            waits = []
            for k, v in tot.items():
                if k == e:
                    continue
                if self.seen[e].get(k, 0) < v:
                    self.seen[e][k] = v
                    waits.append((k, v))
            if waits:
                self.q[e].append((waits, None, None))

    def emit(self):
        nc = self.nc
        fin = {e: self.cnt[e] for e in ENGS if self.cnt[e]}
        for k, c in enumerate(self.dma_cnt):
            if c:
                fin[k] = c
        handles = {"pe": "tensor", "act": "scalar", "dve": "vector", "pool": "gpsimd", "sp": "sync"}
        with nc.Block() as block:
            for e in ENGS:
                def body(engine, ops=self.q[e], is_last=(e == "sp")):
                    for waits, fn, inc in ops:
                        for wi, (wk, wv) in enumerate(waits):
                            engine.wait_ge(self._semobj(wk), wv)
                            if wi < len(waits) - 1 or fn is None:
                                engine.nop(nofuse=True)
                        if fn is not None:
                            fn(engine).then_inc(self._semobj(inc[0]), inc[1])
                    if is_last:
                        for k, v in fin.items():
                            engine.wait_ge(self._semobj(k), v)
                getattr(block, handles[e])(body)


def na_bands(ROWS, RS):
    bands = []
    for i in range(ROWS):
        p0 = min(max(i - 4, 0), ROWS - 8)
        b, il = divmod(i, RS)
        s0 = b * RS + min(max(il - 4, 0), RS - 8)
        lo, hi = min(p0, s0), max(p0, s0) + 8
        n = (hi - lo + 1) // 2
        if lo + 2 * n > ROWS:
            lo = ROWS - 2 * n
        bands.append([lo + 2 * c for c in range(n)])
    return bands


def na_rowbias(ROWS, RS, typ):
    bands = na_bands(ROWS, RS)
    cols = []
    for i, band in enumerate(bands):
        if typ == "P":
            w0 = min(max(i - 4, 0), ROWS - 8)
        else:
            b, il = divmod(i, RS)
            w0 = b * RS + min(max(il - 4, 0), RS - 8)
        for r in band:
            col = np.full(128, NEG, np.float32)
            for j in range(2):
                if w0 <= r + j < w0 + 8:
                    col[j * 64:(j + 1) * 64] = 0.0
            cols.append(col)
    return np.stack(cols, axis=1)


def na_btab(rpb):
    H = rpb.shape[0]
    out = np.zeros((64, H, 17, 64), np.float32)
    qc = np.arange(64)
    c0 = np.clip(qc - 8, 0, 48)
    for off in range(-7, 8):
        blk = np.full((64, H, 64), NEG, np.float32)
        for q in range(64):
            ks = np.arange(c0[q], c0[q] + 16)
            blk[q][:, ks] = rpb[:, off + 7, ks - q + 15]
        out[:, :, off + 8, :] = blk
    return out


def make_consts():
    c = {}
    c["ident"] = np.eye(128, dtype=np.float32)
    bd = np.zeros((128, 128), np.float32)
    bd[:64, :64] = 1.0
    bd[64:, 64:] = 1.0
    c["bd64"] = bd
    sm = np.ones((128, 512), np.float32)
    sm[:, ::64] = 0.0
    c["scanmask"] = sm
    s = np.arange(128)[:, None]
    t = np.arange(128)[None, :]
    same = (s // 64) == (t // 64)
    LS = (same & (s < t)).astype(np.float32)
    LI = (same & (s <= t)).astype(np.float32)
    US = (same & (s > t)).astype(np.float32)
    UI = (same & (s >= t)).astype(np.float32)
    c["ls4"] = np.tile(LS, (1, 4))
    c["li4"] = np.tile(LI, (1, 4))
    c["us4"] = np.tile(US, (1, 4))
    c["ui4"] = np.tile(UI, (1, 4))
    return c


CONST_ORDER = ("ident", "bd64", "scanmask", "ls4", "li4", "us4", "ui4")


def build(T, RS, depth=2, debug_outs=(), phases=None):
    NT = T // 512
    NTL = T // 128
    ROWS = T // 64
    NCH = T // 64
    SLOT_ST = max(NT // 4, 1)
    NSLOT = NT // SLOT_ST
    bands = na_bands(ROWS, RS)
    nslots_na = sum(len(b) for b in bands)
    NH2 = 2 * (T // 256 - 1)
    NH5 = 2 * (NT - 1)

    nc = bass.Bass("TRN2", target_bir_lowering=False)

    def din(name, shape, dt=F32):
        return nc.dram_tensor(name, list(shape), dt, kind="ExternalInput").ap()

    def dscr(name, shape, dt):
        kind = "ExternalOutput" if name in debug_outs else "Internal"
        return nc.dram_tensor(name, list(shape), dt, kind=kind).ap()

    xin = din("xin", [T, D])
    mem = din("mem", [NSLOT, 256, D])
    bm_in = din("bm", [128, max(NT - 1, 1)])
    bm2_in = din("bm2", [128, max(T // 256 - 1, 1)])
    rowbias_in = din("rowbias", [128, nslots_na])
    btab_in = din("btab", [depth, 64, 8 * 17 * 64])
    cst_in = {k: din("c_" + k, v.shape) for k, v in make_consts().items()}
    W = {}
    for name, shape in (("attn_norm", [depth, D]), ("w_in", [depth, D, 7072]), ("rw_conv", [depth, 3, 1952]),
                        ("rw_decay0", [depth, 2, 512]), ("rw_decay2", [depth, 2, 64, 512]), ("rw_a0", [depth, 2, 512]),
                        ("rw_a2", [depth, 2, 64, 512]), ("rw_g2", [depth, 160, 512]), ("rw_k_k", [depth, 512]),
                        ("rw_k_a", [depth, 512]), ("rw_r_k", [depth, 512]), ("rw_lnx_w", [depth, 512]),
                        ("rw_lnx_b", [depth, 512]), ("mem_norm", [depth, D]), ("w_mem_kv", [depth, D, 1024]),
                        ("w_branch", [depth, 3, 512, D]), ("w_out", [depth, D, D]), ("ffn_norm", [depth, D]),
                        ("w_up", [depth, D, 5632]), ("ffn_conv", [depth, 3, 5632]), ("ffn_conv_b", [depth, 5632]),
                        ("w_down", [depth, 2816, D]), ("final_norm", [1, D])):
        W[name] = din(name, shape)
    yout = nc.dram_tensor("yout", [T, D], F32, kind="ExternalOutput").ap()

    xT = [dscr("xT0", [D, T], F32), dscr("xT1", [D, T], F32)]
    qT = dscr("qT", [512, T], BF16)
    kT = dscr("kT", [512, T], BF16)
    vtm = dscr("vtm", [T, 512], BF16)
    omT = dscr("omT", [512, T], BF16)
    onaT = dscr("onaT", [512, T], BF16)
    orwT = dscr("orwT", [512, T], BF16)
    rws = {}
    for d in range(2):
        for nm in ("at", "bt", "bb", "rt", "kt", "kb"):
            rws[nm, d] = dscr("rw_%s%d" % (nm, d), [512, T], BF16)
        rws["wtot", d] = dscr("rw_wtot%d" % d, [512, NCH], F32)
    rws["v"] = dscr("rw_v", [512, T], BF16)
    rws["bonus"] = dscr("rw_bonus", [512, T], F32)
    rws["g"] = dscr("rw_g", [512, T], BF16)
    yfw = dscr("yfw", [T, 512], F32)

    R_xT = [Res("xT0", True), Res("xT1", True)]
    R_scr = {}

    def rscr(key):
        if key not in R_scr:
            R_scr[key] = Res(str(key), True)
        return R_scr[key]

    with ExitStack() as top:
        S = Sched(nc, top)
        psum = [top.enter_context(nc.psum_tensor("ps%d" % i, [128, 512], F32)) for i in range(7)]
        psT = top.enter_context(nc.psum_tensor("psT", [128, 1024], BF16))
        R_ps = [Res("ps%d" % i) for i in range(7)]
        R_psT = Res("psT")

        def mm(out, lhsT, rhs, start, stop, reads, writes):
            S.op("pe", lambda e: e.matmul(out, lhsT=lhsT, rhs=rhs, start=start, stop=stop), reads, writes)

        def tr(out, in_, ident, reads, writes):
            S.op("pe", lambda e: e.transpose(out, in_, ident), reads, writes)

        def act(out, in_, func, reads, writes, bias=None, scale=None, accum_out=None):
            kw = {}
            if bias is not None:
                kw["bias"] = bias
            if scale is not None:
                kw["scale"] = scale
            if accum_out is not None:
                kw["accum_out"] = accum_out
            S.op("act", lambda e: e.activation(out=out, in_=in_, func=func, **kw), reads, writes)

        def is_ps(ap):
            return hasattr(ap, "space") and "PSUM" in str(ap.space)

        def tt(eng, out, in0, in1, op, reads, writes):
            if eng == "pool" and (is_ps(out) or is_ps(in0) or is_ps(in1)):
                eng = "dve"
            S.op(eng, lambda e: e.tensor_tensor(out=out, in0=in0, in1=in1, op=op), reads, writes)

        def ts(eng, out, in0, s1, s2, op0, op1, reads, writes):
            if eng == "pool" and (is_ps(out) or is_ps(in0)):
                eng = "dve"
            if op1 is None and op0 == ALU.pow:
                assert s1 == -0.5
                S.op("act", lambda e: e.activation(out=out, in_=in0, func=AF.Ln), reads, writes)
                S.op("act", lambda e: e.activation(out=out, in_=out, func=AF.Exp, scale=-0.5), list(reads) + list(writes), writes)
            elif op1 is None:
                S.op(eng, lambda e: e.tensor_scalar(out=out, in0=in0, scalar1=s1, scalar2=None, op0=op0), reads, writes)
            else:
                S.op(eng, lambda e: e.tensor_scalar(out=out, in0=in0, scalar1=s1, scalar2=s2, op0=op0, op1=op1), reads, writes)

        def stt(eng, out, in0, scalar, in1, op0, op1, reads, writes):
            eng = "dve"
            S.op(eng, lambda e: e.scalar_tensor_tensor(out=out, in0=in0, scalar=scalar, in1=in1, op0=op0, op1=op1), reads, writes)

        def cp(eng, out, in_, reads, writes):
            if eng == "act":
                S.op("act", lambda e: e.copy(out=out, in_=in_), reads, writes)
            else:
                S.op(eng, lambda e: e.tensor_copy(out=out, in_=in_), reads, writes)

        def memset(eng, ap, val, writes):
            S.op(eng, lambda e: e.memset(ap, val), (), writes)

        def ld(out, in_, reads, writes, q="sp", nonc=False):
            if nonc:
                def f(e):
                    with nc.allow_non_contiguous_dma(reason="small strided load"):
                        return e.dma_start(out=out, in_=in_)
                S.dma(q, f, reads, writes)
            else:
                S.dma(q, lambda e: e.dma_start(out=out, in_=in_), reads, writes)

        def stor(out, in_, reads, writes, sem_res, q="sp"):
            S.dma(q, lambda e: e.dma_start(out=out, in_=in_), reads, writes, sem_res=sem_res)

        rr = [0]

        def next_ps():
            rr[0] = (rr[0] + 1) % 4
            return rr[0]

        ve = [0]

        def veng():
            ve[0] ^= 1
            return "dve" if ve[0] else "pool"

        uniq = [0]

        def sbt(stack, name, shape, dt):
            uniq[0] += 1
            return stack.enter_context(nc.sbuf_tensor("sb%d_%s" % (uniq[0], name), list(shape), dt))

        ident_f = sbt(top, "ident_f", [128, 128], F32)
        ident_b = sbt(top, "ident_b", [128, 128], BF16)
        ones_b = sbt(top, "ones_b", [128, 128], BF16)
        bd64_b = sbt(top, "bd64_b", [128, 128], BF16)
        bd64_f = sbt(top, "bd64_f", [128, 128], F32)
        R_c = Res("consts")
        ld(ident_f[:], cst_in["ident"][:, :], (), [R_c])
        ld(ident_b[:], cst_in["ident"][:, :], (), [R_c], q="pool")
        ld(bd64_b[:], cst_in["bd64"][:, :], (), [R_c], q="pool")
        ld(bd64_f[:], cst_in["bd64"][:, :], (), [R_c])
        memset("dve", ones_b[:], 1.0, [R_c])
        bm = sbt(top, "bm", [128, max(NT - 1, 1)], F32)
        bm2 = sbt(top, "bm2", [128, max(T // 256 - 1, 1)], F32)
        ld(bm[:], bm_in[:, :], (), [R_c])
        ld(bm2[:], bm2_in[:, :], (), [R_c])

        def rmsnorm_fm(X, rX, XN, rXN, gain, rg, n, SQ, rSQ, RSt, rRS, bank=4):
            act(SQ[:, :, 0:n], X[:, :, 0:n], AF.Square, [rX], [rSQ])
            for c in range(8):
                mm(psum[bank][:, 0:n], ones_b[:], SQ[:, c, 0:n], c == 0, c == 7, [rSQ, R_c], [R_ps[bank]])
            ts("dve", RSt[:, 0:n], psum[bank][:, 0:n], 1.0 / D, 1e-6, ALU.mult, ALU.add, [R_ps[bank]], [rRS])
            ts("dve", RSt[:, 0:n], RSt[:, 0:n], -0.5, None, ALU.pow, None, [rRS], [rRS])
            for c in range(8):
                stt(veng(), XN[:, c, 0:n], X[:, c, 0:n], gain[:, c:c + 1], RSt[:, 0:n], ALU.mult, ALU.mult,
                    [rX, rRS, rg], [rXN])

        def load_vec_fm(stack, name, src, nchunk, q="sp"):
            t = sbt(stack, name, [128, nchunk], F32)
            r = Res(name)
            ld(t[:], src.rearrange("(c p) -> p c", p=128), (), [r], nonc=True)
            return t, r

        def load_w(stack, name, src, kc, ncols, col0=0, eng_q="pool"):
            t = sbt(stack, name, [128, kc, ncols], BF16)
            r = Res(name)
            for c in range(kc):
                ld(t[:, c, :], src[c * 128:(c + 1) * 128, col0:col0 + ncols], (), [r], q=eng_q)
            return t, r

        def phase_p0():
            with ExitStack() as st:
                XI = [sbt(st, "p0_xi%d" % i, [128, D], F32) for i in range(2)]
                rXI = [Res("xi0"), Res("xi1")]
                XO = [sbt(st, "p0_xo%d" % i, [128, 8, 128], F32) for i in range(2)]
                rXO = [Res("xo0"), Res("xo1")]
                for i in range(NTL):
                    b = i % 2
                    ld(XI[b][:], xin[i * 128:(i + 1) * 128, :], (), [rXI[b]])
                    for half in range(2):
                        bank = next_ps()
                        for c4 in range(4):
                            c = half * 4 + c4
                            tr(psum[bank][:, c4 * 128:(c4 + 1) * 128], XI[b][:, c * 128:(c + 1) * 128], ident_f[:],
                               [rXI[b], R_c], [R_ps[bank]])
                        cp("act" if half else "dve", XO[b][:, half * 4:(half + 1) * 4, :],
                           psum[bank][:, :].rearrange("p (c t) -> p c t", c=4), [R_ps[bank]], [rXO[b]])
                    stor(xT[0].rearrange("(c p) t -> p c t", p=128)[:, :, i * 128:(i + 1) * 128], XO[b][:],
                         [rXO[b]], [R_xT[0]], rXO[b])
            S.barrier()

        def phase_e(src):
            with ExitStack() as st:
                gain, rg = load_vec_fm(st, "e_gain", W["final_norm"][0], 8)
                X = [sbt(st, "e_x%d" % i, [128, 8, 512], F32) for i in range(2)]
                rX = [Res(), Res()]
                XN = sbt(st, "e_xn", [128, 8, 512], F32)
                rXN = Res()
                SQ = sbt(st, "e_sq", [128, 8, 512], BF16)
                rSQ = Res()
                RSt = sbt(st, "e_rs", [128, 512], F32)
                rRS = Res()
                YO = [sbt(st, "e_yo%d" % i, [128, D], F32) for i in range(2)]
                rYO = [Res(), Res()]
                xv = xT[src].rearrange("(c p) t -> p c t", p=128)
                ld(X[0][:], xv[:, :, 0:512], [R_xT[src]], [rX[0]])
                k = 0
                for s in range(NT):
                    b = s % 2
                    if s + 1 < NT:
                        ld(X[1 - b][:], xv[:, :, (s + 1) * 512:(s + 2) * 512], [R_xT[src]], [rX[1 - b]])
                    rmsnorm_fm(X[b], rX[b], XN, rXN, gain, rg, 512, SQ, rSQ, RSt, rRS)
                    for tl in range(4):
                        yb = k % 2
                        k += 1
                        for half in range(2):
                            bank = next_ps()
                            for c4 in range(4):
                                c = half * 4 + c4
                                tr(psum[bank][:, c4 * 128:(c4 + 1) * 128], XN[:, c, tl * 128:(tl + 1) * 128], ident_f[:],
                                   [rXN, R_c], [R_ps[bank]])
                            cp("act" if half else "dve", YO[yb][:, half * 512:(half + 1) * 512], psum[bank][:, :],
                               [R_ps[bank]], [rYO[yb]])
                        t0 = s * 512 + tl * 128
                        stor(yout[t0:t0 + 128, :], YO[yb][:], [rYO[yb]], [Res()], rYO[yb])
            S.barrier()

        def phase_c2(l, src, dst):
            TW = 256
            NTW = T // TW
            NW = TW + 2
            banks = (0, 1, 2, 3, 5, 6)
            bk = [0]

            def nb():
                bk[0] = (bk[0] + 1) % len(banks)
                return banks[bk[0]]

            with ExitStack() as st:
                Wu, rWu = load_w(st, "c2_wu", W["w_up"][l], 8, 5632)
                Wd, rWd = load_w(st, "c2_wd", W["w_down"][l], 22, D)
                gain, rg = load_vec_fm(st, "c2_gain", W["ffn_norm"][l], 8)
                cw = sbt(st, "c2_cw", [128, 3, 44], F32)
                rcw = Res()
                for k in range(3):
                    ld(cw[:, k, :], W["ffn_conv"][l, k].rearrange("(c p) -> p c", p=128), (), [rcw], nonc=True)
                cb, rcb = load_vec_fm(st, "c2_cb", W["ffn_conv_b"][l], 44)
                X = [sbt(st, "c2_x%d" % i, [128, 8, NW], F32) for i in range(2)]
                rX = [Res(), Res()]
                XN = sbt(st, "c2_xn", [128, 8, NW], BF16)
                rXN = Res()
                SQ = sbt(st, "c2_sq", [128, 8, NW], BF16)
                rSQ = Res()
                RSt = sbt(st, "c2_rs", [128, NW], F32)
                rRS = Res()
                G = sbt(st, "c2_g", [128, 22, TW], BF16)
                rG = Res()
                NB = 3
                CV = [sbt(st, "c2_cv%d" % i, [128, TW], F32) for i in range(NB)]
                rCV = [Res() for _ in range(NB)]
                CG = [sbt(st, "c2_cg%d" % i, [128, TW], F32) for i in range(NB)]
                rCG = [Res() for _ in range(NB)]
                SGt = [sbt(st, "c2_sg%d" % i, [128, TW], F32) for i in range(NB)]
                rSGt = [Res() for _ in range(NB)]
                xv = xT[src].rearrange("(c p) t -> p c t", p=128)
                xo = xT[dst].rearrange("(c p) t -> p c t", p=128)

                def load_x(s):
                    b = s % 2
                    t0 = s * TW
                    lo = max(t0 - 1, 0)
                    hi = min(t0 + TW + 1, T)
                    ld(X[b][:, :, lo - (t0 - 1):hi - (t0 - 1)], xv[:, :, lo:hi], [R_xT[src]], [rX[b]])
                    if s == 0:
                        memset("pool", X[b][:, :, 0:1], 0.0, [rX[b]])
                    if s == NTW - 1:
                        memset("pool", X[b][:, :, NW - 1:NW], 0.0, [rX[b]])

                load_x(0)
                for s in range(NTW):
                    b = s % 2
                    t0 = s * TW
                    if s + 1 < NTW:
                        load_x(s + 1)
                    rmsnorm_fm(X[b], rX[b], XN, rXN, gain, rg, NW, SQ, rSQ, RSt, rRS)
                    if s > 0:
                        ts("pool", XN[:, :, 0:1], XN[:, :, 0:1], bm2[:, s - 1:s], None, ALU.mult, None, [rXN, R_c], [rXN])
                    if s < NTW - 1:
                        ts("pool", XN[:, :, NW - 1:NW], XN[:, :, NW - 1:NW], bm2[:, s:s + 1], None, ALU.mult, None, [rXN, R_c], [rXN])
                    for j in range(22):
                        cbuf = j % NB
                        pair = ((j, CV[cbuf], rCV[cbuf], nb()), (22 + j, CG[cbuf], rCG[cbuf], nb()))
                        for (uc, Ct, rC, bank) in pair:
                            for kc in range(8):
                                mm(psum[bank][:, 0:NW], Wu[:, kc, uc * 128:(uc + 1) * 128], XN[:, kc, :], kc == 0, kc == 7,
                                   [rWu, rXN], [R_ps[bank]])
                        for (uc, Ct, rC, bank) in pair:
                            act(Ct[:, :], psum[bank][:, 1:TW + 1], AF.Identity, [R_ps[bank], rcw, rcb], [rC], bias=cb[:, uc:uc + 1],
                                scale=cw[:, 1, uc:uc + 1])
                        for (uc, Ct, rC, bank) in pair:
                            stt("dve", Ct[:, :], psum[bank][:, 0:TW], cw[:, 0, uc:uc + 1], Ct[:, :], ALU.mult, ALU.add,
                                [R_ps[bank], rcw, rC], [rC])
                        for (uc, Ct, rC, bank) in pair:
                            stt("dve", Ct[:, :], psum[bank][:, 2:TW + 2], cw[:, 2, uc:uc + 1], Ct[:, :], ALU.mult, ALU.add,
                                [R_ps[bank], rcw, rC], [rC])
                        act(SGt[cbuf][:], CG[cbuf][:], AF.Silu, [rCG[cbuf]], [rSGt[cbuf]])
                        tt("pool", G[:, j, :], SGt[cbuf][:], CV[cbuf][:], ALU.mult, [rSGt[cbuf], rCV[cbuf]], [rG])
                    for oc in range(8):
                        bank = nb()
                        for kc in range(22):
                            mm(psum[bank][:, 0:TW], Wd[:, kc, oc * 128:(oc + 1) * 128], G[:, kc, :], kc == 0, kc == 21,
                               [rWd, rG], [R_ps[bank]])
                        tt("dve", X[b][:, oc, 1:TW + 1], X[b][:, oc, 1:TW + 1], psum[bank][:, 0:TW], ALU.add, [rX[b], R_ps[bank]], [rX[b]])
                    stor(xo[:, :, t0:t0 + TW], X[b][:, :, 1:TW + 1], [rX[b]], [R_xT[dst]], rX[b])
            S.barrier()

        def phase_c1(l, src, dst):
            with ExitStack() as st:
                Wg, rWg = load_w(st, "c1_wg", W["w_in"][l], 8, 3072, col0=4000)
                Wb = sbt(st, "c1_wb", [128, 12, D], BF16)
                rWb = Res()
                for b in range(3):
                    for kc in range(4):
                        ld(Wb[:, b * 4 + kc, :], W["w_branch"][l, b, kc * 128:(kc + 1) * 128, :], (), [rWb], q="pool")
                Wo, rWo = load_w(st, "c1_wo", W["w_out"][l], 8, D)
                gain, rg = load_vec_fm(st, "c1_gain", W["attn_norm"][l], 8)
                Xs = [sbt(st, "c1_x%d" % i, [128, 8, 512], F32) for i in range(2)]
                rXs = [Res(), Res()]
                XN = sbt(st, "c1_xn", [128, 8, 512], BF16)
                rXN = Res()
                SQ = sbt(st, "c1_sq", [128, 8, 512], BF16)
                rSQ = Res()
                RSt = sbt(st, "c1_rs", [128, 512], F32)
                rRS = Res()
                OBs = [sbt(st, "c1_ob%d" % i, [128, 12, 512], BF16) for i in range(2)]
                rOBs = [Res(), Res()]
                MG = sbt(st, "c1_mg", [128, 8, 512], BF16)
                rMG = Res()
                SGt = [sbt(st, "c1_sg%d" % i, [128, 512], F32) for i in range(2)]
                rSGt = [Res(), Res()]
                ACC = sbt(st, "c1_acc", [128, 512], F32)
                rACC = Res()
                xv = xT[src].rearrange("(c p) t -> p c t", p=128)
                xo = xT[dst].rearrange("(c p) t -> p c t", p=128)
                srcs = (onaT, orwT, omT)
                rsrcs = (rscr("onaT"), rscr("orwT"), rscr("omT"))
                k = 0

                def loads(s):
                    t0 = s * 512
                    ld(Xs[s % 2][:], xv[:, :, t0:t0 + 512], [R_xT[src]], [rXs[s % 2]])
                    for b in range(3):
                        ld(OBs[s % 2][:, b * 4:(b + 1) * 4, :], srcs[b].rearrange("(c p) t -> p c t", p=128)[:, :, t0:t0 + 512],
                           [rsrcs[b]], [rOBs[s % 2]])

                loads(0)
                for s in range(NT):
                    t0 = s * 512
                    if s + 1 < NT:
                        loads(s + 1)
                    X, rX, OB, rOB = Xs[s % 2], rXs[s % 2], OBs[s % 2], rOBs[s % 2]
                    rmsnorm_fm(X, rX, XN, rXN, gain, rg, 512, SQ, rSQ, RSt, rRS)
                    for mc in range(8):
                        for b in range(3):
                            bg = next_ps()
                            for kc in range(8):
                                mm(psum[bg][:, :], Wg[:, kc, b * 1024 + mc * 128:b * 1024 + (mc + 1) * 128], XN[:, kc, :],
                                   kc == 0, kc == 7, [rWg, rXN], [R_ps[bg]])
                            sb_ = k % 2
                            k += 1
                            act(SGt[sb_][:], psum[bg][:, :], AF.Sigmoid, [R_ps[bg]], [rSGt[sb_]])
                            bp = next_ps()
                            for kc in range(4):
                                mm(psum[bp][:, :], Wb[:, b * 4 + kc, mc * 128:(mc + 1) * 128], OB[:, b * 4 + kc, :],
                                   kc == 0, kc == 3, [rWb, rOB], [R_ps[bp]])
                            if b == 0:
                                tt("dve", ACC[:], SGt[sb_][:], psum[bp][:, :], ALU.mult, [rSGt[sb_], R_ps[bp]], [rACC])
                            else:
                                tt("dve", SGt[sb_][:], SGt[sb_][:], psum[bp][:, :], ALU.mult, [rSGt[sb_], R_ps[bp]],
                                   [rSGt[sb_]])
                                if b == 1:
                                    tt("pool", ACC[:], ACC[:], SGt[sb_][:], ALU.add, [rACC, rSGt[sb_]], [rACC])
                                else:
                                    tt("pool", MG[:, mc, :], ACC[:], SGt[sb_][:], ALU.add, [rACC, rSGt[sb_]], [rMG])
                    for oc in range(8):
                        bank = next_ps()
                        for kc in range(8):
                            mm(psum[bank][:, :], Wo[:, kc, oc * 128:(oc + 1) * 128], MG[:, kc, :], kc == 0, kc == 7,
                               [rWo, rMG], [R_ps[bank]])
                        tt(veng(), X[:, oc, :], X[:, oc, :], psum[bank][:, :], ALU.add, [rX, R_ps[bank]], [rX])
                    stor(xo[:, :, t0:t0 + 512], X[:], [rX], [R_xT[dst]], rX)
            S.barrier()

        def phase_a(l, src):
            with ExitStack() as st:
                Wi, rWi = load_w(st, "a_wi", W["w_in"][l], 8, 4000)
                gain, rg = load_vec_fm(st, "a_gain", W["attn_norm"][l], 8)
                KmT = sbt(st, "a_kmT", [128, NSLOT, 4, 256], BF16)
                Vm = sbt(st, "a_vm", [128, NSLOT, 2, 512], BF16)
                rKV = Res()
                with ExitStack() as st2:
                    Wkv, rWkv = load_w(st2, "a_wkv", W["w_mem_kv"][l], 8, 1024)
                    gm, rgm = load_vec_fm(st2, "a_gm", W["mem_norm"][l], 8)
                    MT = sbt(st2, "a_mt", [128, D], F32)
                    rMT = Res()
                    MS = sbt(st2, "a_ms", [128, D], F32)
                    rMS = Res()
                    MB = sbt(st2, "a_mb", [128, D], BF16)
                    rMB = Res()
                    ssq = sbt(st2, "a_ssq", [128, 1], F32)
                    rssq = Res()
                    memT = sbt(st2, "a_memT", [128, 8, 256], BF16)
                    rmemT = Res()
                    for sl in range(NSLOT):
                        for mc in range(2):
                            ld(MT[:], mem[sl, mc * 128:(mc + 1) * 128, :], (), [rMT])
                            act(MS[:], MT[:], AF.Square, [rMT], [rMS, rssq], accum_out=ssq[:])
                            ts("dve", ssq[:], ssq[:], 1.0 / D, 1e-6, ALU.mult, ALU.add, [rssq], [rssq])
                            ts("dve", ssq[:], ssq[:], -0.5, None, ALU.pow, None, [rssq], [rssq])
                            ts("dve", MB[:], MT[:], ssq[:, 0:1], None, ALU.mult, None, [rMT, rssq], [rMB])
                            for c in range(8):
                                tr(psT[:, c * 128:(c + 1) * 128], MB[:, c * 128:(c + 1) * 128], ident_b[:], [rMB, R_c],
                                   [R_psT])
                            for c in range(8):
                                ts(veng(), memT[:, c, mc * 128:(mc + 1) * 128], psT[:, c * 128:(c + 1) * 128],
                                   gm[:, c:c + 1], None, ALU.mult, None, [R_psT, rgm], [rmemT])
                        for h in range(4):
                            bank = next_ps()
                            for kc in range(8):
                                mm(psum[bank][:, 0:256], Wkv[:, kc, h * 128:(h + 1) * 128], memT[:, kc, :], kc == 0, kc == 7,
                                   [rWkv, rmemT], [R_ps[bank]])
                            cp("act", KmT[:, sl, h, :], psum[bank][:, 0:256], [R_ps[bank]], [rKV])
                        for mc in range(2):
                            bank = next_ps()
                            for kc in range(8):
                                mm(psum[bank][:, :], memT[:, kc, mc * 128:(mc + 1) * 128], Wkv[:, kc, 512:1024], kc == 0,
                                   kc == 7, [rWkv, rmemT], [R_ps[bank]])
                            cp("dve", Vm[:, sl, mc, :], psum[bank][:, :], [R_ps[bank]], [rKV])
                    S.barrier()
                cw = sbt(st, "a_cw", [128, 3, 16], F32)
                rcw = Res()
                memset("dve", cw[:], 0.0, [rcw])
                for k in range(3):
                    ld(cw[:, k, 0:15], W["rw_conv"][l, k, 0:1920].rearrange("(c p) -> p c", p=128), (), [rcw], nonc=True)
                    ld(cw[0:32, k, 15:16], W["rw_conv"][l, k, 1920:1952].rearrange("(c p) -> p c", p=32), (), [rcw],
                       nonc=True)
                dec0 = sbt(st, "a_dec0", [128, 2, 4], F32)
                a0 = sbt(st, "a_a0", [128, 2, 4], F32)
                rsm = Res()
                for d in range(2):
                    ld(dec0[:, d, :], W["rw_decay0"][l, d].rearrange("(c p) -> p c", p=128), (), [rsm], nonc=True)
                    ld(a0[:, d, :], W["rw_a0"][l, d].rearrange("(c p) -> p c", p=128), (), [rsm], nonc=True)
                kkv, r1 = load_vec_fm(st, "a_kk", W["rw_k_k"][l], 4)
                kav, r2 = load_vec_fm(st, "a_ka", W["rw_k_a"][l], 4)
                rkv, r3 = load_vec_fm(st, "a_rk", W["rw_r_k"][l], 4)
                D2 = sbt(st, "a_d2", [128, 512], BF16)
                A2 = sbt(st, "a_a2", [128, 512], BF16)
                G2 = sbt(st, "a_g2", [128, 2, 512], BF16)
                ld(D2[:], W["rw_decay2"][l].rearrange("d l f -> (d l) f"), (), [rsm], q="pool")
                ld(A2[:], W["rw_a2"][l].rearrange("d l f -> (d l) f"), (), [rsm], q="pool")
                ld(G2[:, 0, :], W["rw_g2"][l, 0:128, :], (), [rsm], q="pool")
                ld(G2[0:32, 1, :], W["rw_g2"][l, 128:160, :], (), [rsm], q="pool")
                scanm = sbt(st, "a_scanm", [128, 512], F32)
                ld(scanm[:], cst_in["scanmask"][:, :], (), [rsm])
                rsmall = [rsm, r1, r2, r3, rcw]

                XN = sbt(st, "a_xn", [128, 8, 512], BF16)
                rXN = Res()
                RSt = sbt(st, "a_rs", [128, 512], F32)
                rRS = Res()
                QK = sbt(st, "a_qk", [128, 8, 512], BF16)
                rQK = Res()
                SQ, rSQ = QK, rQK
                VT = sbt(st, "a_vt", [128, 4, 512], BF16)
                rVT = Res()
                QM = sbt(st, "a_qm", [128, 4, 512], BF16)
                rQM = Res()
                PT = [sbt(st, "a_pt%d" % i, [128, 512], BF16) for i in range(2)]
                rPT = [Res(), Res()]
                RD = sbt(st, "a_rd", [128, 512], F32)
                rRD = Res()
                OM = sbt(st, "a_om", [128, 4, 512], BF16)
                rOM = Res()
                ZC = sbt(st, "a_zc", [128, 16, 512], F32)
                rZC = Res()
                X, rX = ZC[:, 0:8, :], rZC
                HZ = sbt(st, "a_hz", [128, 16, max(NH5, 2)], F32)
                rHZ = Res()
                xv = xT[src].rearrange("(c p) t -> p c t", p=128)
                rwcols = [1536 + 128 * i for i in range(15)] + [1536 + 1920]
                rwm = [128] * 15 + [32]
                if NT > 1:
                    XH = sbt(st, "a_xh", [128, 8, NH5], F32)
                    rXH = Res()
                    XHN = sbt(st, "a_xhn", [128, 8, NH5], BF16)
                    rXHN = Res()
                    for c in range(8):
                        srcv = xT[src][c * 128:(c + 1) * 128, 511:T - 1].rearrange("p (j w) -> p j w", w=512)[:, :, 0:2]
                        ld(XH[:, c, :].rearrange("p (j w) -> p j w", w=2), srcv, [R_xT[src]], [rXH], nonc=True)
                    rmsnorm_fm(XH, rXH, XHN, rXHN, gain, rg, NH5, SQ, rSQ, RSt, rRS)
                    memset("dve", HZ[:], 0.0, [rHZ])
                    for zc in range(16):
                        bank = next_ps()
                        m = rwm[zc]
                        for kc in range(8):
                            mm(psum[bank][0:m, 0:NH5], Wi[:, kc, rwcols[zc]:rwcols[zc] + m], XHN[:, kc, :], kc == 0, kc == 7,
                               [rWi, rXHN], [R_ps[bank]])
                        tt("dve", HZ[0:m, zc, :].rearrange("p (j w) -> p j w", w=2),
                           psum[bank][0:m, 0:NH5].rearrange("p (j w) -> p j w", w=2),
                           bm[0:m, 0:NT - 1].unsqueeze(2).to_broadcast([m, NT - 1, 2]), ALU.mult, [R_ps[bank], R_c], [rHZ])

                def tmp(name, dt=F32):
                    return sbt(st, "a_t_" + name, [128, 512], dt), Res(name)

                TH, rTH = tmp("th", BF16)
                XAb, rXAb = tmp("xab", BF16)
                XGb = sbt(st, "a_t_xgb", [128, 2, 512], BF16)
                rXGb = Res()
                SGm, rSGm = tmp("sg")
                Pc, rPc = tmp("pc")
                Pe, rPe = tmp("pe")
                Pt, rPt = tmp("pt")
                E1, rE1 = tmp("e1")
                E2, rE2 = tmp("e2")
                Ee, rEe = tmp("ee")
                Eb, rEb = tmp("eb")
                AS, rAS = tmp("as")
                KD, rKD = tmp("kd")
                KK, rKK = tmp("kk")
                KQ, rKQ = tmp("kq", BF16)
                Bv, rBv = tmp("b")
                BON, rBON = tmp("bon")
                RKb, rRKb = tmp("rkb", BF16)
                TMP, rTMP = tmp("tmp")
                RI, rRI = TMP, rTMP
                OUTS = {}
                for nm in ("at", "bt", "bb", "rt", "kt", "kb"):
                    OUTS[nm] = (sbt(st, "a_o_" + nm, [128, 512], BF16), Res(nm))
                VRb, rVRb = tmp("vrb", BF16)
                Gb, rGb = tmp("gb", BF16)
                WTo = sbt(st, "a_wto", [128, 8], F32)
                rWTo = Res()

                for s in range(NT):
                    t0 = s * 512
                    slot = s // SLOT_ST
                    ld(X[:], xv[:, :, t0:t0 + 512], [R_xT[src]], [rX])
                    rmsnorm_fm(X, rX, XN, rXN, gain, rg, 512, SQ, rSQ, RSt, rRS)
                    def d_qk(c):
                        bank = next_ps()
                        for kc in range(8):
                            mm(psum[bank][:, :], Wi[:, kc, c * 128:(c + 1) * 128], XN[:, kc, :], kc == 0, kc == 7,
                               [rWi, rXN], [R_ps[bank]])
                        if c < 4:
                            act(QK[:, c, :], psum[bank][:, :], AF.Copy, [R_ps[bank]], [rQK], scale=0.125)
                        else:
                            cp("dve", QK[:, c, :], psum[bank][:, :], [R_ps[bank]], [rQK])
                        if c == 3:
                            stor(qT.rearrange("(c p) t -> p c t", p=128)[:, :, t0:t0 + 512], QK[:, 0:4, :], [rQK], [rscr("qT")], rQK)
                        if c == 7:
                            stor(kT.rearrange("(c p) t -> p c t", p=128)[:, :, t0:t0 + 512], QK[:, 4:8, :], [rQK], [rscr("kT")], rQK)

                    def d_v(tl):
                        bank = next_ps()
                        for kc in range(8):
                            mm(psum[bank][:, :], XN[:, kc, tl * 128:(tl + 1) * 128], Wi[:, kc, 1024:1536], kc == 0, kc == 7,
                               [rWi, rXN], [R_ps[bank]])
                        cp("act" if tl % 2 else "dve", VT[:, tl, :], psum[bank][:, :], [R_ps[bank]], [rVT])
                        if tl == 3:
                            stor(vtm[t0:t0 + 512, :].rearrange("(c p) f -> p c f", p=128), VT[:], [rVT], [rscr("vtm")], rVT)

                    def d_mq(h):
                        bank = next_ps()
                        for kc in range(8):
                            mm(psum[bank][:, :], Wi[:, kc, 3488 + h * 128:3488 + (h + 1) * 128], XN[:, kc, :], kc == 0, kc == 7,
                               [rWi, rXN], [R_ps[bank]])
                        act(QM[:, h, :], psum[bank][:, :], AF.Copy, [R_ps[bank]], [rQM], scale=128 ** -0.5)

                    def d_ma(h):
                        for mc in range(2):
                            bank = next_ps()
                            mm(psum[bank][:, :], KmT[:, slot, h, mc * 128:(mc + 1) * 128], QM[:, h, :], True, True,
                               [rKV, rQM], [R_ps[bank]])
                            act(PT[mc][:], psum[bank][:, :], AF.Exp, [R_ps[bank]], [rPT[mc]])
                        bo, bd_ = next_ps(), next_ps()
                        for mc in range(2):
                            mm(psum[bo][:, :], Vm[:, slot, mc, h * 128:(h + 1) * 128], PT[mc][:], mc == 0, mc == 1,
                               [rKV, rPT[mc]], [R_ps[bo]])
                        for mc in range(2):
                            mm(psum[bd_][:, :], ones_b[:], PT[mc][:], mc == 0, mc == 1, [R_c, rPT[mc]], [R_ps[bd_]])
                        act(RD[:], psum[bd_][:, :], AF.Ln, [R_ps[bd_]], [rRD])
                        act(RD[:], RD[:], AF.Exp, [rRD], [rRD], scale=-1.0)
                        tt("dve", OM[:, h, :], psum[bo][:, :], RD[:], ALU.mult, [R_ps[bo], rRD], [rOM])
                        if h == 3:
                            stor(omT.rearrange("(c p) t -> p c t", p=128)[:, :, t0:t0 + 512], OM[:], [rOM], [rscr("omT")], rOM)

                    def dense_slice(it):
                        d_qk(it)
                        if it % 2 == 0:
                            d_v(it // 2)
                            d_mq(it // 2)
                        else:
                            d_ma(it // 2)

                    for zc in range(16):
                        bank = next_ps()
                        m = rwm[zc]
                        P = psum[bank]
                        for kc in range(8):
                            mm(P[0:m, :], Wi[:, kc, rwcols[zc]:rwcols[zc] + m], XN[:, kc, :], kc == 0, kc == 7,
                               [rWi, rXN], [R_ps[bank]])
                        act(ZC[0:m, zc, :], P[0:m, :], AF.Copy, [R_ps[bank], rcw], [rZC], scale=cw[0:m, 1, zc:zc + 1])
                        stt("dve", ZC[0:m, zc, 1:512], P[0:m, 0:511], cw[0:m, 0, zc:zc + 1], ZC[0:m, zc, 1:512], ALU.mult, ALU.add,
                            [R_ps[bank], rcw, rZC], [rZC])
                        stt("dve", ZC[0:m, zc, 0:511], P[0:m, 1:512], cw[0:m, 2, zc:zc + 1], ZC[0:m, zc, 0:511], ALU.mult, ALU.add,
                            [R_ps[bank], rcw, rZC], [rZC])
                        if s > 0:
                            stt("dve", ZC[0:m, zc, 0:1], HZ[0:m, zc, 2 * (s - 1):2 * (s - 1) + 1], cw[0:m, 0, zc:zc + 1],
                                ZC[0:m, zc, 0:1], ALU.mult, ALU.add, [rHZ, rcw, rZC], [rZC])
                        if s < NT - 1:
                            stt("dve", ZC[0:m, zc, 511:512], HZ[0:m, zc, 2 * s + 1:2 * s + 2], cw[0:m, 2, zc:zc + 1],
                                ZC[0:m, zc, 511:512], ALU.mult, ALU.add, [rHZ, rcw, rZC], [rZC])
                    act(TH[:], ZC[:, 12, :], AF.Tanh, [rZC], [rTH])
                    cp("pool", XAb[:], ZC[:, 13, :], [rZC], [rXAb])
                    act(XGb[:, 0, :], ZC[:, 14, :], AF.Sigmoid, [rZC], [rXGb])
                    act(XGb[0:32, 1, :], ZC[0:32, 15, :], AF.Sigmoid, [rZC], [rXGb])
                    for hp in range(4):
                        fs = slice(hp * 128, (hp + 1) * 128)
                        Rr = ZC[:, hp, :]
                        Kr = ZC[:, 4 + hp, :]
                        Vr = ZC[:, 8 + hp, :]
                        mm(psum[4][:, :], G2[:, 0, fs], XGb[:, 0, :], True, False, rsmall + [rXGb], [R_ps[4]])
                        mm(psum[4][:, :], G2[0:32, 1, fs], XGb[0:32, 1, :], False, True, rsmall + [rXGb], [R_ps[4]])
                        cp("act", Gb[:], psum[4][:, :], [R_ps[4]], [rGb])
                        stor(rws["g"][fs, t0:t0 + 512], Gb[:], [rGb], [rscr("g")], rGb)
                        cp("pool", VRb[:], Vr, [rZC], [rVRb])
                        stor(rws["v"][fs, t0:t0 + 512], VRb[:], [rVRb], [rscr("v")], rVRb)
                        ts("dve", KK[:], Kr, kkv[:, hp:hp + 1], None, ALU.mult, None, [rZC] + rsmall, [rKK])
                        tt("pool", KQ[:], KK[:], KK[:], ALU.mult, [rKK], [rKQ])
                        mm(psum[4][:, :], bd64_b[:], KQ[:], True, True, [R_c, rKQ], [R_ps[4]])
                        ts("dve", RI[:], psum[4][:, :], 1e-24, None, ALU.max, None, [R_ps[4]], [rRI])
                        ts("dve", RI[:], RI[:], -0.5, None, ALU.pow, None, [rRI], [rRI])
                        tt("dve", KK[:], KK[:], RI[:], ALU.mult, [rKK, rRI], [rKK])
                        for d in range(2):
                            ds = slice(d * 64, (d + 1) * 64)
                            cexp = 0.6065306597126334
                            mm(psum[5][:, :], D2[ds, fs], TH[ds, :], True, True, rsmall + [rTH], [R_ps[5]])
                            mm(psum[6][:, :], A2[ds, fs], XAb[ds, :], True, True, rsmall + [rXAb], [R_ps[6]])
                            act(SGm[:], psum[5][:, :], AF.Sigmoid, [R_ps[5]] + rsmall, [rSGm], bias=dec0[:, d, hp:hp + 1])
                            act(AS[:], psum[6][:, :], AF.Sigmoid, [R_ps[6]] + rsmall, [rAS], bias=a0[:, d, hp:hp + 1])
                            S.op("dve", lambda e: e.tensor_tensor_scan(out=Pc[:], data0=scanm[:], data1=SGm[:], initial=0.0,
                                                                        op0=ALU.mult, op1=ALU.add), [rSGm] + rsmall, [rPc])
                            Pc3 = Pc[:].rearrange("p (c t) -> p c t", t=64)
                            if d == 1:
                                tt("pool", Pe[:].rearrange("p (c t) -> p c t", t=64), Pc3[:, :, 63:64].to_broadcast([128, 8, 64]),
                                   Pc3, ALU.subtract, [rPc], [rPe])
                                tt("pool", Pc[:], Pe[:], SGm[:], ALU.add, [rPe, rSGm], [rPc])
                                totcol = 0
                            else:
                                totcol = 63
                            tt("pool", Pe[:], Pc[:], SGm[:], ALU.subtract, [rPc, rSGm], [rPe])
                            tt("pool", Pt[:].rearrange("p (c t) -> p c t", t=64),
                               Pc3[:, :, totcol:totcol + 1].to_broadcast([128, 8, 64]), Pc3, ALU.subtract, [rPc], [rPt])
                            ts("dve", TMP[:], AS[:], -1.0, kav[:, hp:hp + 1], ALU.add, ALU.mult, [rAS] + rsmall, [rTMP])
                            stt("dve", KD[:], TMP[:], 1.0, Kr, ALU.add, ALU.mult, [rTMP, rZC], [rKD])
                            tt("dve", Bv[:], KK[:], AS[:], ALU.mult, [rKK, rAS], [rBv])
                            stt("dve", RKb[:], Rr, rkv[:, hp:hp + 1], KD[:], ALU.mult, ALU.mult, [rZC, rKD] + rsmall, [rRKb])
                            mm(psum[4][:, :], bd64_b[:], RKb[:], True, True, [R_c, rRKb], [R_ps[4]])
                            act(E1[:], Pc[:], AF.Exp, [rPc], [rE1], scale=-cexp)
                            act(E2[:], Pc[:], AF.Exp, [rPc], [rE2], scale=cexp)
                            act(Ee[:], Pe[:], AF.Exp, [rPe], [rEe], scale=-cexp)
                            act(Eb[:], Pt[:], AF.Exp, [rPt], [rEb], scale=-cexp)
                            if d == 0:
                                tt("dve", BON[:], psum[4][:, :], Vr, ALU.mult, [R_ps[4], rZC], [rBON])
                            else:
                                tt("dve", TMP[:], psum[4][:, :], Vr, ALU.mult, [R_ps[4], rZC], [rTMP])
                                tt("pool", BON[:], BON[:], TMP[:], ALU.add, [rBON, rTMP], [rBON])
                            cp("pool", WTo[:, :], E1[:].rearrange("p (c t) -> p c t", t=64)[:, :, totcol], [rE1], [rWTo])
                            stor(rws["wtot", d][fs, s * 8:(s + 1) * 8], WTo[:], [rWTo], [rscr(("wtot", d))], rWTo)
                            o, ro = OUTS["rt"]
                            tt("pool", o[:], Rr, E1[:], ALU.mult, [rZC, rE1], [ro])
                            o, ro = OUTS["bt"]
                            tt("dve", o[:], Bv[:], E2[:], ALU.mult, [rBv, rE2], [ro])
                            o, ro = OUTS["kt"]
                            tt("pool", o[:], KD[:], E2[:], ALU.mult, [rKD, rE2], [ro])
                            o, ro = OUTS["at"]
                            stt("dve", o[:], KK[:], -1.0, Ee[:], ALU.mult, ALU.mult, [rKK, rEe], [ro])
                            o, ro = OUTS["bb"]
                            tt("dve", o[:], Bv[:], Eb[:], ALU.mult, [rBv, rEb], [ro])
                            o, ro = OUTS["kb"]
                            tt("pool", o[:], KD[:], Eb[:], ALU.mult, [rKD, rEb], [ro])
                            for nm in ("rt", "bt", "kt", "at", "bb", "kb"):
                                o, ro = OUTS[nm]
                                stor(rws[nm, d][fs, t0:t0 + 512], o[:], [ro], [rscr((nm, d))], ro)
                            dense_slice(hp * 2 + d)
                        stor(rws["bonus"][fs, t0:t0 + 512], BON[:], [rBON], [rscr("bonus")], rBON)
            S.barrier()

        def phase_b1(l):
            with ExitStack() as st:
                Bt = sbt(st, "b1_bt", [64, 8 * 17 * 64], BF16)
                rBt = Res()
                ld(Bt[:], btab_in[l], (), [rBt], q="pool")
                Bt4 = Bt[:].rearrange("p (h o k) -> p h o k", h=8, o=17)
                RB = sbt(st, "b1_rb", [128, nslots_na], F32)
                rRB = Res()
                ld(RB[:], rowbias_in[:, :], (), [rRB])
                NR = min(24, ROWS)
                KW = [sbt(st, "b1_kw%d" % i, [64, 8, NR * 64], BF16) for i in range(2)]
                rKW = [Res(), Res()]
                QW = [sbt(st, "b1_qw%d" % i, [64, 8, 512], BF16) for i in range(2)]
                rQW = [Res(), Res()]
                VE = [sbt(st, "b1_ve%d" % i, [128, NR // 2, 512], BF16) for i in range(2)]
                rVE = [Res(), Res()]
                VO = [sbt(st, "b1_vo%d" % i, [128, NR // 2 - 1, 512], BF16) for i in range(2)]
                rVO = [Res(), Res()]
                PTt = [sbt(st, "b1_pt%d" % i, [128, 512], BF16) for i in range(2)]
                rPTt = [Res(), Res()]
                RDt = sbt(st, "b1_rd", [64, 512], F32)
                rRDt = Res()
                ON = [sbt(st, "b1_on%d" % i, [64, 8, 512], BF16) for i in range(2)]
                rON = [Res(), Res()]
                qv = qT.rearrange("(h d) t -> d h t", d=64)
                kv = kT.rearrange("(h d) t -> d h t", d=64)

                def loads(s):
                    b = s % 2
                    i0 = s * 8
                    rb = min(max(i0 - 8, 0), ROWS - NR)
                    ld(QW[b][:], qv[:, :, s * 512:(s + 1) * 512], [rscr("qT")], [rQW[b]])
                    ld(KW[b][:], kv[:, :, rb * 64:(rb + NR) * 64], [rscr("kT")], [rKW[b]])
                    ld(VE[b][:], vtm[rb * 64:(rb + NR) * 64, :].rearrange("(c p) f -> p c f", p=128), [rscr("vtm")], [rVE[b]])
                    ld(VO[b][:], vtm[rb * 64 + 64:(rb + NR) * 64 - 64, :].rearrange("(c p) f -> p c f", p=128), [rscr("vtm")],
                       [rVO[b]])
                    return rb

                slot = 0
                pk = 0
                sk = [0]
                rbs = {0: loads(0)}
                for s in range(NT):
                    b = s % 2
                    if s + 1 < NT:
                        rbs[s + 1] = loads(s + 1)
                    rb = rbs[s]
                    for iq in range(8):
                        i = s * 8 + iq
                        band = bands[i]
                        for ci, r in enumerate(band):
                            o = r - i + 8
                            kc0 = (r - rb) * 64
                            sk[0] = (sk[0] + 1) % 3
                            bank = sk[0]
                            for h in range(8):
                                mm(psum[bank][:, h * 64:(h + 1) * 64], KW[b][:, h, kc0:kc0 + 128], QW[b][:, h, iq * 64:(iq + 1) * 64],
                                   True, False, [rKW[b], rQW[b]], [R_ps[bank]])
                                mm(psum[bank][:, h * 64:(h + 1) * 64], Bt4[:, h, o:o + 2, :].rearrange("p o k -> p (o k)"),
                                   ident_b[0:64, 0:64], False, True, [rBt, R_c], [R_ps[bank]])
                            pb = pk % 2
                            pk += 1
                            act(PTt[pb][:], psum[bank][:, :], AF.Exp, [R_ps[bank], rRB], [rPTt[pb]], bias=RB[:, slot:slot + 1])
                            slot += 1
                            if (r - rb) % 2 == 0:
                                Vc, rVc = VE[b][:, (r - rb) // 2, :], rVE[b]
                            else:
                                Vc, rVc = VO[b][:, (r - rb - 1) // 2, :], rVO[b]
                            first, last = ci == 0, ci == len(band) - 1
                            bo, bd_ = (5, 6) if i % 2 == 0 else (3, 4)
                            for h in range(8):
                                mm(psum[bo][0:64, h * 64:(h + 1) * 64], Vc[:, h * 64:(h + 1) * 64], PTt[pb][:, h * 64:(h + 1) * 64],
                                   first and h == 0, last, [rVc, rPTt[pb]], [R_ps[bo]])
                            mm(psum[bd_][0:64, :], ones_b[:, 0:64], PTt[pb][:], first, last, [R_c, rPTt[pb]], [R_ps[bd_]])
                        act(RDt[:], psum[bd_][0:64, :], AF.Ln, [R_ps[bd_]], [rRDt])
                        act(RDt[:], RDt[:], AF.Exp, [rRDt], [rRDt], scale=-1.0)
                        tt("dve", ON[b][:, :, iq * 64:(iq + 1) * 64], psum[bo][0:64, :].rearrange("p (h q) -> p h q", h=8),
                           RDt[:].rearrange("p (h q) -> p h q", h=8), ALU.mult, [R_ps[bo], rRDt], [rON[b]])
                    stor(onaT.rearrange("(h d) t -> d h t", d=64)[:, :, s * 512:(s + 1) * 512], ON[b][:], [rON[b]],
                         [rscr("onaT")], rON[b])
            S.barrier()

        def phase_b2(l):
            with ExitStack() as st:
                mk = {}
                rM = Res()
                for nm in ("ls4", "li4", "us4", "ui4"):
                    mk[nm] = sbt(st, "b2_" + nm, [128, 512], F32)
                    ld(mk[nm][:], cst_in[nm][:, :], (), [rM])
                MS = [mk["ls4"], mk["us4"]]
                MI = [mk["li4"], mk["ui4"]]
                MP = [mk["us4"], mk["ls4"]]
                lnw = sbt(st, "b2_lnw", [128, 4], F32)
                lnb = sbt(st, "b2_lnb", [128, 4], F32)
                ld(lnw[:], W["rw_lnx_w"][l].rearrange("(c p) -> p c", p=128), (), [rM], nonc=True)
                ld(lnb[:], W["rw_lnx_b"][l].rearrange("(c p) -> p c", p=128), (), [rM], nonc=True)
                names = ("at", "bt", "bb", "rt", "kt", "kb", "v")
                IN = {nm: [sbt(st, "b2_i_%s%d" % (nm, i), [64, 8, 512], BF16) for i in range(2)] for nm in names}
                rIN = [Res(), Res()]
                WT = [sbt(st, "b2_wt%d" % i, [64, 8, 8], F32) for i in range(2)]
                AM = sbt(st, "b2_am", [128, 4, 8, 128], BF16)
                rAM = [Res(), Res()]
                PPp = [sbt(st, "b2_ppp%d" % i, [128, 8, 128], BF16) for i in range(2)]
                PPt = [sbt(st, "b2_ppt%d" % i, [128, 8, 128], BF16) for i in range(2)]
                rPP = [[Res(), Res()], [Res(), Res()]]
                Xc = [sbt(st, "b2_x%d" % i, [128, 8, 128], BF16) for i in range(2)]
                rXc = [[Res(), Res()], [Res(), Res()]]
                TMx = sbt(st, "b2_tm", [128, 9 * 192], BF16)
                rTM = [Res(), Res()]
                memset("dve", TMx[:], 0.0, rTM)
                TMf = TMx[:, :]
                TM = TMx[:, 0:8 * 192].rearrange("p (h q) -> p h q", h=8)
                RH = sbt(st, "b2_rh", [64, 8, 2, 128], BF16)
                rRH = Res()
                MTt = sbt(st, "b2_mt", [64, 8, 2, 128], BF16)
                rMTt = Res()
                MTf = sbt(st, "b2_mtf", [64, 8, 2, 64], F32)
                rMTf = Res()
                DW = sbt(st, "b2_dw", [64, 8, 2, 64], F32)
                rDW = Res()
                N0 = sbt(st, "b2_n0", [64, 8, 2, 64], F32)
                rN0 = Res()
                Sb = sbt(st, "b2_sb", [64, 8, 64], BF16)
                rSb = Res()
                YF = [sbt(st, "b2_yf%d" % i, [128, 512], F32) for i in range(2)]
                rYF = [Res(), Res()]
                YL = [sbt(st, "b2_yl%d" % i, [128, 512], F32) for i in range(2)]
                rYL = [Res(), Res()]
                YC = sbt(st, "b2_yc", [128, 512], F32)
                rYC = Res()
                YS = sbt(st, "b2_ys", [128, 512], F32)
                rYS = Res()
                YN = sbt(st, "b2_yn", [128, 512], BF16)
                rYN = Res()
                st8 = sbt(st, "b2_st8", [128, 8], F32)
                rst8 = Res()
                st8b = sbt(st, "b2_st8b", [128, 8], F32)
                rst8b = Res()
                BONt = [sbt(st, "b2_bon%d" % i, [128, 4, 128], F32) for i in range(2)]
                Gt = [sbt(st, "b2_g%d" % i, [128, 4, 128], BF16) for i in range(2)]
                rBG = [Res(), Res()]
                OT = sbt(st, "b2_ot", [128, 4, 128], F32)
                rOT = Res()
                OR = [sbt(st, "b2_or%d" % i, [128, 4, 128], BF16) for i in range(2)]
                rOR = [Res(), Res()]
                memset("dve", RH[:], 0.0, [rRH])
                memset("dve", MTt[:], 0.0, [rMTt])

                def hview(ap):
                    return ap.rearrange("(h j) t -> j h t", j=64)

                def v4(ap):
                    return ap.rearrange("p (h t) -> p h t", h=4)

                for d in range(2):
                    memset("dve", Sb[:], 0.0, [rSb])
                    order = list(range(NT)) if d == 0 else list(range(NT - 1, -1, -1))

                    def loads(idx, d=d, order=order):
                        s = order[idx]
                        b = idx % 2
                        for nm in names:
                            srcap = rws["v"] if nm == "v" else rws[nm, d]
                            rk = rscr("v") if nm == "v" else rscr((nm, d))
                            ld(IN[nm][b][:], hview(srcap)[:, :, s * 512:(s + 1) * 512], [rk], [rIN[b]])
                        ld(WT[b][:], hview(rws["wtot", d])[:, :, s * 8:(s + 1) * 8], [rscr(("wtot", d))], [rIN[b]], nonc=True)

                    loads(0)
                    tcount = 0
                    for idx, s in enumerate(order):
                        b = idx % 2
                        if idx + 1 < NT:
                            loads(idx + 1)
                        rI = rIN[b]
                        if d == 0 and s > 0:
                            ts("dve", Sb[:], Sb[:], bm[0:64, s - 1:s], None, ALU.mult, None, [rSb, R_c], [rSb])
                        if d == 1 and s < NT - 1:
                            ts("dve", Sb[:], Sb[:], bm[0:64, s:s + 1], None, ALU.mult, None, [rSb, R_c], [rSb])
                        tiles = list(range(4)) if d == 0 else [3, 2, 1, 0]
                        for tl in tiles:
                            tc_ = slice(tl * 128, (tl + 1) * 128)
                            gt0 = s * 512 + tl * 128
                            yb = tcount % 2
                            tcount += 1
                            if d == 1:
                                ld(YL[yb][:], yfw[gt0:gt0 + 128, :], [rscr("yfw")], [rYL[yb]])
                                ld(BONt[yb][:], rws["bonus"].rearrange("(c p) t -> p c t", p=128)[:, :, gt0:gt0 + 128],
                                   [rscr("bonus")], [rBG[yb]])
                                ld(Gt[yb][:], rws["g"].rearrange("(c p) t -> p c t", p=128)[:, :, gt0:gt0 + 128], [rscr("g")],
                                   [rBG[yb]])

                            def I(nm, h):
                                return IN[nm][b][:, h, tc_]

                            typs = (("bt", "at", MS), ("kt", "at", MS), ("bt", "rt", MI), ("kt", "rt", MI))
                            for ty, (ln, rn, msk) in enumerate(typs):
                                for hg in range(2):
                                    bank = next_ps()
                                    for j in range(4):
                                        h = hg * 4 + j
                                        mm(psum[bank][:, j * 128:(j + 1) * 128], I(ln, h), I(rn, h), True, True, [rI], [R_ps[bank]])
                                    tt("dve", AM[:, ty, hg * 4:(hg + 1) * 4, :], v4(psum[bank][:, :]), v4(msk[d][:]), ALU.mult,
                                       [R_ps[bank], rM], [rAM[hg]])
                            for hg in range(2):
                                bank = next_ps()
                                for j in range(4):
                                    h = hg * 4 + j
                                    mm(psum[bank][:, j * 128:(j + 1) * 128], I("at", h), I("bt", h), True, True, [rI], [R_ps[bank]])
                                tt("dve", PPp[0][:, hg * 4:(hg + 1) * 4, :], v4(psum[bank][:, :]), v4(MP[d][:]), ALU.mult,
                                   [R_ps[bank], rM], [rPP[0][hg]])
                            for hg in range(2):
                                for j in range(4):
                                    h = hg * 4 + j
                                    for q, nm in enumerate(("v", "bb", "kb", "at")):
                                        tr(psT[:, j * 256 + q * 64:j * 256 + (q + 1) * 64], I(nm, h), ident_b[0:64, 0:64],
                                           [rI, R_c], [R_psT])
                                pv4 = psT[:, :].rearrange("p (h q) -> p h q", h=4)
                                cp("act", TM[:, hg * 4:(hg + 1) * 4, :], pv4[:, :, 0:192], [R_psT], [rTM[hg]])
                                cp("act", Xc[0][:, hg * 4:(hg + 1) * 4, 0:64], pv4[:, :, 192:256], [R_psT], [rXc[0][hg]])
                            for hg in range(2):
                                bank = next_ps()
                                for j in range(4):
                                    h = hg * 4 + j
                                    mm(psum[bank][:, j * 64:(j + 1) * 64], AM[:, 1, h, :], TM[:, h, 0:64], True, True,
                                       [rAM[hg], rTM[hg]], [R_ps[bank]])
                                cp("dve", Xc[0][:, hg * 4:(hg + 1) * 4, 64:128],
                                   psum[bank][:, 0:256].rearrange("p (h i) -> p h i", h=4), [R_ps[bank]], [rXc[0][hg]])
                            if LVL < 2:
                                continue
                            for hg in range(2):
                                for j in range(4):
                                    h = hg * 4 + j
                                    mm(psum[hg][:, j * 128:(j + 1) * 128], ident_b[:], Xc[0][:, h, :], j == 0, False,
                                       [R_c, rXc[0][hg]], [R_ps[hg]])
                            for k in range(6):
                                cur, nxt = k % 2, (k + 1) % 2
                                for hg in range(2):
                                    rP = rAM[hg] if k == 0 else rPP[cur][hg]
                                    bx = hg
                                    for j in range(4):
                                        h = hg * 4 + j
                                        Pt_h = AM[:, 0, h, :] if k == 0 else PPt[cur][:, h, :]
                                        mm(psum[bx][:, j * 128:(j + 1) * 128], Pt_h, Xc[cur][:, h, :], False, k == 5,
                                           [rP, rXc[cur][hg]], [R_ps[bx]])
                                    cp("act", Xc[nxt][:, hg * 4:(hg + 1) * 4, :], v4(psum[bx][:, :]), [R_ps[bx]], [rXc[nxt][hg]])
                                    if k < 5:
                                        bp_, bt_ = 2 + hg, 5 + hg
                                        rPp = rPP[cur][hg]
                                        for j in range(4):
                                            h = hg * 4 + j
                                            Pt_h = AM[:, 0, h, :] if k == 0 else PPt[cur][:, h, :]
                                            Pp_h = PPp[cur][:, h, :]
                                            mm(psum[bp_][:, j * 128:(j + 1) * 128], Pt_h, Pp_h, True, True, [rP, rPp], [R_ps[bp_]])
                                            mm(psum[bt_][:, j * 128:(j + 1) * 128], Pp_h, Pt_h, True, True, [rP, rPp], [R_ps[bt_]])
                                        cp("dve", PPp[nxt][:, hg * 4:(hg + 1) * 4, :], v4(psum[bp_][:, :]), [R_ps[bp_]], [rPP[nxt][hg]])
                                        cp("act", PPt[nxt][:, hg * 4:(hg + 1) * 4, :], v4(psum[bt_][:, :]), [R_ps[bt_]], [rPP[nxt][hg]])
                            XF = Xc[0]
                            rXF = rXc[0]
                            if LVL < 3:
                                continue
                            for h in range(8):
                                hg = h // 4
                                ys = slice(h * 64, (h + 1) * 64)
                                mm(psum[4][:, ys], AM[:, 2, h, :], XF[:, h, 64:128], h == 0, False, [rAM[hg], rXF[hg]], [R_ps[4]])
                                mm(psum[4][:, ys], AM[:, 3, h, :], TM[:, h, 0:64], False, False, [rAM[hg], rTM[hg]], [R_ps[4]])
                            for hg in range(2):
                                bank = hg
                                for j in range(4):
                                    h = hg * 4 + j
                                    mm(psum[bank][0:64, j * 128:(j + 1) * 128], XF[:, h, 0:64], AM[:, 2, h, :], True, True,
                                       [rXF[hg], rAM[hg]], [R_ps[bank]])
                                for c in range(2):
                                    cs = slice(c * 64, (c + 1) * 64)
                                    tt("dve", RH[:, hg * 4:(hg + 1) * 4, c, cs], v4(psum[bank][0:64, :])[:, :, cs],
                                       IN["rt"][b][:, hg * 4:(hg + 1) * 4, tl * 128 + c * 64:tl * 128 + (c + 1) * 64], ALU.add,
                                       [R_ps[bank], rI], [rRH])
                            if LVL < 4:
                                continue
                            tt("pool", DW[:], ident_f[0:64, 0:64].unsqueeze(1).unsqueeze(1).to_broadcast([64, 8, 2, 64]),
                               WT[b][:, :, tl * 2:tl * 2 + 2].unsqueeze(3).to_broadcast([64, 8, 2, 64]), ALU.mult, [rI, R_c], [rDW])
                            for hg in range(2):
                                bm_, bn_ = 2 + hg, 5 + hg
                                for j in range(4):
                                    h = hg * 4 + j
                                    for c in range(2):
                                        ps_ = slice(0, 64) if c == 0 else slice(0, 128)
                                        col = (j * 2 + c) * 64
                                        mm(psum[bm_][:, col:col + 64], XF[ps_, h, 0:128], TM[ps_, h, 64:128], True, True,
                                           [rXF[hg], rTM[hg]], [R_ps[bm_]])
                                        mm(psum[bn_][:, col:col + 64], TM[ps_, h, 64:192], XF[ps_, h, 64:128], True, False,
                                           [rXF[hg], rTM[hg]], [R_ps[bn_]])
                                        mm(psum[bn_][:, col:col + 64], TMf[ps_, h * 192 + 128:h * 192 + 256], TM[ps_, h, 0:64], False, True,
                                           [rTM[hg]], [R_ps[bn_]])
                                hs = slice(hg * 4, (hg + 1) * 4)
                                cp("dve", MTf[:, hs, :, :], psum[bm_][0:64, :].rearrange("p (h c i) -> p h c i", h=4, c=2),
                                   [R_ps[bm_]], [rMTf])
                                tt("pool", MTf[:, hs, 1, :], MTf[:, hs, 1, :], MTf[:, hs, 0, :], ALU.subtract, [rMTf], [rMTf])
                                tt("pool", MTt[:, hs, :, 0:64], MTf[:, hs, :, :], DW[:, hs, :, :], ALU.add, [rMTf, rDW], [rMTt])
                                cp("act", N0[:, hs, :, :], psum[bn_][0:64, :].rearrange("p (h c i) -> p h c i", h=4, c=2),
                                   [R_ps[bn_]], [rN0])
                                tt("pool", N0[:, hs, 1, :], N0[:, hs, 1, :], N0[:, hs, 0, :], ALU.subtract, [rN0], [rN0])
                            if LVL < 5:
                                continue
                            for ci, c in enumerate((0, 1) if d == 0 else (1, 0)):
                                for h in range(8):
                                    ys = slice(h * 64, (h + 1) * 64)
                                    mm(psum[4][:, ys], RH[:, h, c, :], Sb[:, h, :], False, ci == 1, [rRH, rSb], [R_ps[4]])
                                bank = ci
                                for h in range(8):
                                    mm(psum[bank][:, h * 64:(h + 1) * 64], MTt[:, h, c, :], Sb[:, h, :], True, True,
                                       [rMTt, rSb], [R_ps[bank]])
                                tt("dve", Sb[:], psum[bank][0:64, :].rearrange("p (h i) -> p h i", h=8), N0[:, :, c, :], ALU.add,
                                   [R_ps[bank], rN0], [rSb])
                            if LVL < 6:
                                continue
                            if d == 0:
                                cp("act", YF[yb][:], psum[4][:, :], [R_ps[4]], [rYF[yb]])
                                stor(yfw[gt0:gt0 + 128, :], YF[yb][:], [rYF[yb]], [rscr("yfw")], rYF[yb])
                            else:
                                tt("dve", YF[yb][:], psum[4][:, :], YL[yb][:], ALU.add, [R_ps[4], rYL[yb]], [rYF[yb]])
                                Y3 = YF[yb][:].rearrange("p (h i) -> p h i", h=8)
                                S.op("dve", lambda e, Y3=Y3: e.tensor_reduce(out=st8[:], in_=Y3, axis=AX.X, op=ALU.add),
                                     [rYF[yb]], [rst8])
                                ts("dve", st8[:], st8[:], -1.0 / 64, None, ALU.mult, None, [rst8], [rst8])
                                tt("pool", YC[:].rearrange("p (h i) -> p h i", h=8), Y3,
                                   st8[:].unsqueeze(2).to_broadcast([128, 8, 64]), ALU.add, [rYF[yb], rst8], [rYC])
                                act(YS[:], YC[:], AF.Square, [rYC], [rYS])
                                S.op("dve", lambda e: e.tensor_reduce(out=st8b[:], in_=YS[:].rearrange("p (h i) -> p h i", h=8),
                                                                      axis=AX.X, op=ALU.add), [rYS], [rst8b])
                                ts("dve", st8b[:], st8b[:], 1.0 / 64, 64e-5, ALU.mult, ALU.add, [rst8b], [rst8b])
                                ts("dve", st8b[:], st8b[:], -0.5, None, ALU.pow, None, [rst8b], [rst8b])
                                tt("pool", YN[:].rearrange("p (h i) -> p h i", h=8), YC[:].rearrange("p (h i) -> p h i", h=8),
                                   st8b[:].unsqueeze(2).to_broadcast([128, 8, 64]), ALU.mult, [rYC, rst8b], [rYN])
                                for fc in range(4):
                                    tr(psT[:, fc * 128:(fc + 1) * 128], YN[:, fc * 128:(fc + 1) * 128], ident_b[:],
                                       [rYN, R_c], [R_psT])
                                pv = psT[:, 0:512].rearrange("p (c t) -> p c t", c=4)
                                tt("dve", OT[:], pv, lnw[:].unsqueeze(2).to_broadcast([128, 4, 128]), ALU.mult, [R_psT, rM], [rOT])
                                tt("pool", OT[:], OT[:], lnb[:].unsqueeze(2).to_broadcast([128, 4, 128]), ALU.add, [rOT, rM], [rOT])
                                tt("dve", OT[:], OT[:], BONt[yb][:], ALU.add, [rOT, rBG[yb]], [rOT])
                                tt("pool", OR[yb][:], OT[:], Gt[yb][:], ALU.mult, [rOT, rBG[yb]], [rOR[yb]])
                                stor(orwT.rearrange("(c p) t -> p c t", p=128)[:, :, gt0:gt0 + 128], OR[yb][:], [rOR[yb]],
                                     [rscr("orwT")], rOR[yb])
            S.barrier()

        if phases is None:
            phases = ["p0"] + sum([["a%d" % l, "b1%d" % l, "b2%d" % l, "c1%d" % l, "c2%d" % l] for l in range(depth)], []) + ["e"]
        for ph in phases:
            if ph == "p0":
                phase_p0()
            elif ph == "e":
                phase_e(0)
            elif ph[0] == "a":
                phase_a(int(ph[1:]), 0)
            elif ph[:2] == "b1":
                phase_b1(int(ph[2:]))
            elif ph[:2] == "b2":
                phase_b2(int(ph[2:]))
            elif ph[:2] == "c1":
                phase_c1(int(ph[2:]), 0, 1)
            elif ph[:2] == "c2":
                phase_c2(int(ph[2:]), 1, 0)
        S.emit()
        nc._n_inst = S.ninst
    return nc


def core_inputs(x_seqs, mem_seqs, T, RS, typ, weights, depth):
    NT = T // 512
    SLOT_ST = max(NT // 4, 1)
    NSLOT = NT // SLOT_ST
    ROWS = T // 64
    xin = np.zeros((T, D), np.float32)
    memv = np.zeros((NSLOT, 256, D), np.float32)
    if typ == "P":
        xin[:] = x_seqs[0]
        for sl in range(NSLOT):
            memv[sl] = mem_seqs[0]
        starts = {0}
    else:
        L = RS * 64
        for i, xs in enumerate(x_seqs):
            xin[i * L:(i + 1) * L] = xs
            sl0 = (i * L) // (SLOT_ST * 512)
            memv[sl0] = mem_seqs[i]
        starts = set(range(0, T, L))
    bmv = np.ones((128, max(NT - 1, 1)), np.float32)
    for j in range(NT - 1):
        if (j + 1) * 512 in starts:
            bmv[:, j] = 0.0
    bm2 = np.ones((128, max(T // 256 - 1, 1)), np.float32)
    for j in range(T // 256 - 1):
        if (j + 1) * 256 in starts:
            bm2[:, j] = 0.0
    m = {"xin": xin, "mem": memv, "bm": bmv, "bm2": bm2, "rowbias": na_rowbias(ROWS, RS, typ)}
    m["btab"] = np.stack([na_btab(weights["na_rpb"][l]).reshape(64, -1) for l in range(depth)])
    for k, v in make_consts().items():
        m["c_" + k] = v
    for k, v in weights.items():
        if k == "na_rpb":
            continue
        if k == "rw_r_k":
            v = v.reshape(v.shape[0], 512)
        if k == "final_norm":
            v = v.reshape(1, D)
        m[k] = np.ascontiguousarray(v, dtype=np.float32)
    return m


_CACHE = {}


def kernel(**inputs):
    T, RS, depth = 8192, 32, 2
    xp = np.asarray(inputs["x_prompt"], np.float32)
    xs = np.asarray(inputs["x_sample"], np.float32)
    mp = np.asarray(inputs["mem_prompt"], np.float32)
    ms = np.asarray(inputs["mem_sample"], np.float32)
    weights = {k: np.asarray(v, np.float32) for k, v in inputs.items()
               if k not in ("x_prompt", "x_sample", "mem_prompt", "mem_sample")}
    in_maps = []
    for c in range(4):
        in_maps.append(core_inputs([xp[c]], [mp[c]], T, RS, "P", weights, depth))
    for c in range(4):
        sq = [2 * c, 2 * c + 1, 2 * c, 2 * c + 1]
        in_maps.append(core_inputs([xs[i] for i in sq], [ms[i] for i in sq], T, RS, "S", weights, depth))
    if "nc" not in _CACHE:
        _CACHE["nc"] = build(T, RS, depth)
    res = run_bass_kernel_spmd(_CACHE["nc"], in_maps, core_ids=list(range(8)))
    yp = np.stack([res.results[c]["yout"] for c in range(4)]).astype(np.float32)
    ys = np.zeros_like(xs)
    for c in range(4):
        y = res.results[4 + c]["yout"]
        ys[2 * c] = y[0:2048]
        ys[2 * c + 1] = y[2048:4096]
    return (yp, ys)
```

```python
from contextlib import ExitStack
import os
LVL = int(os.environ.get('B2DBG', '9'))
B2X = int(os.environ.get('B2X', '0'))
import numpy as np
import concourse.bass as bass
import concourse.mybir as mybir
from concourse.bass_utils import run_bass_kernel_spmd

F32 = mybir.dt.float32
BF16 = mybir.dt.bfloat16
AF = mybir.ActivationFunctionType
ALU = mybir.AluOpType
AX = mybir.AxisListType

D = 1024
NEG = -30000.0
ENGS = ("pe", "act", "dve", "pool", "sp")


class Res:
    __slots__ = ("name", "w", "r", "dsem", "multi")

    def __init__(self, name="", multi=False):
        self.name = name
        self.w = {}
        self.r = {}
        self.dsem = None
        self.multi = multi


class Sched:
    def __init__(self, nc, stack, n_dma_sems=64):
        self.nc = nc
        self.q = {e: [] for e in ENGS}
        self.cnt = {e: 0 for e in ENGS}
        self.sem = {e: stack.enter_context(nc.semaphore("s_" + e)) for e in ENGS}
        self.dma_sems = [stack.enter_context(nc.semaphore("d%d" % i)) for i in range(n_dma_sems)]
        self.dma_cnt = [0] * n_dma_sems
        self.dma_next = 0
        self.seen = {e: {} for e in ENGS}
        self.ninst = 0

    def _semobj(self, key):
        return self.sem[key] if isinstance(key, str) else self.dma_sems[key]

    def _deps(self, eng, reads, writes):
        toks = {}
        for r in reads:
            for k, v in r.w.items():
                if toks.get(k, 0) < v:
                    toks[k] = v
        for w in writes:
            for k, v in w.w.items():
                if toks.get(k, 0) < v:
                    toks[k] = v
            for k, v in w.r.items():
                if toks.get(k, 0) < v:
                    toks[k] = v
        waits = []
        seen = self.seen[eng]
        for k, v in toks.items():
            if k == eng and eng == "pe":
                continue
            if seen.get(k, 0) >= v:
                continue
            seen[k] = v
            waits.append((k, v))
        return waits

    def _mark(self, tok, reads, writes):
        k, v = tok
        for r in reads:
            if r.r.get(k, 0) < v:
                r.r[k] = v
        for w in writes:
            w.w[k] = v
            if not w.multi:
                w.r = {}

    def op(self, eng, fn, reads=(), writes=()):
        waits = self._deps(eng, reads, writes)
        self.cnt[eng] += 1
        self._mark((eng, self.cnt[eng]), reads, writes)
        self.q[eng].append((waits, fn, (eng, 1)))
        self.ninst += 1

    def dma(self, eng, fn, reads=(), writes=(), sem_res=None):
        waits = self._deps(eng, reads, writes)
        if sem_res is None:
            sem_res = writes[0]
        if sem_res.dsem is None:
            sem_res.dsem = self.dma_next % len(self.dma_sems)
            self.dma_next += 1
        k = sem_res.dsem
        self.dma_cnt[k] += 16
        self._mark((k, self.dma_cnt[k]), reads, writes)
        self.q[eng].append((waits, fn, (k, 16)))
        self.ninst += 1

    def barrier(self):
        tot = {e: self.cnt[e] for e in ENGS if self.cnt[e]}
        for k, c in enumerate(self.dma_cnt):
            if c:
                tot[k] = c
        for e in ENGS:
            waits = []
            for k, v in tot.items():
                if k == e:
                    continue
                if self.seen[e].get(k, 0) < v:
                    self.seen[e][k] = v
                    waits.append((k, v))
            if waits:
                self.q[e].append((waits, None, None))

    def emit(self):
        nc = self.nc
        fin = {e: self.cnt[e] for e in ENGS if self.cnt[e]}
        for k, c in enumerate(self.dma_cnt):
            if c:
                fin[k] = c
        handles = {"pe": "tensor", "act": "scalar", "dve": "vector", "pool": "gpsimd", "sp": "sync"}
        with nc.Block() as block:
            for e in ENGS:
                def body(engine, ops=self.q[e], is_last=(e == "sp")):
                    for waits, fn, inc in ops:
                        for wi, (wk, wv) in enumerate(waits):
                            engine.wait_ge(self._semobj(wk), wv)
                            if wi < len(waits) - 1 or fn is None:
                                engine.nop(nofuse=True)
                        if fn is not None:
                            fn(engine).then_inc(self._semobj(inc[0]), inc[1])
                    if is_last:
                        for k, v in fin.items():
                            engine.wait_ge(self._semobj(k), v)
                getattr(block, handles[e])(body)


def na_bands(ROWS, RS):
    bands = []
    for i in range(ROWS):
        p0 = min(max(i - 4, 0), ROWS - 8)
        b, il = divmod(i, RS)
        s0 = b * RS + min(max(il - 4, 0), RS - 8)
        lo, hi = min(p0, s0), max(p0, s0) + 8
        n = (hi - lo + 1) // 2
        if lo + 2 * n > ROWS:
            lo = ROWS - 2 * n
        bands.append([lo + 2 * c for c in range(n)])
    return bands


def na_rowbias(ROWS, RS, typ):
    bands = na_bands(ROWS, RS)
    cols = []
    for i, band in enumerate(bands):
        if typ == "P":
            w0 = min(max(i - 4, 0), ROWS - 8)
        else:
            b, il = divmod(i, RS)
            w0 = b * RS + min(max(il - 4, 0), RS - 8)
        for r in band:
            col = np.full(128, NEG, np.float32)
            for j in range(2):
                if w0 <= r + j < w0 + 8:
                    col[j * 64:(j + 1) * 64] = 0.0
            cols.append(col)
    return np.stack(cols, axis=1)


def na_btab(rpb):
    H = rpb.shape[0]
    out = np.zeros((64, H, 17, 64), np.float32)
    qc = np.arange(64)
    c0 = np.clip(qc - 8, 0, 48)
    for off in range(-7, 8):
        blk = np.full((64, H, 64), NEG, np.float32)
        for q in range(64):
            ks = np.arange(c0[q], c0[q] + 16)
            blk[q][:, ks] = rpb[:, off + 7, ks - q + 15]
        out[:, :, off + 8, :] = blk
    return out


def make_consts():
    c = {}
    c["ident"] = np.eye(128, dtype=np.float32)
    bd = np.zeros((128, 128), np.float32)
    bd[:64, :64] = 1.0
    bd[64:, 64:] = 1.0
    c["bd64"] = bd
    sm = np.ones((128, 512), np.float32)
    sm[:, ::64] = 0.0
    c["scanmask"] = sm
    s = np.arange(128)[:, None]
    t = np.arange(128)[None, :]
    same = (s // 64) == (t // 64)
    LS = (same & (s < t)).astype(np.float32)
    LI = (same & (s <= t)).astype(np.float32)
    US = (same & (s > t)).astype(np.float32)
    UI = (same & (s >= t)).astype(np.float32)
    c["ls4"] = np.tile(LS, (1, 4))
    c["li4"] = np.tile(LI, (1, 4))
    c["us4"] = np.tile(US, (1, 4))
    c["ui4"] = np.tile(UI, (1, 4))
    return c


CONST_ORDER = ("ident", "bd64", "scanmask", "ls4", "li4", "us4", "ui4")


def build(T, RS, depth=2, debug_outs=(), phases=None):
    NT = T // 512
    NTL = T // 128
    ROWS = T // 64
    NCH = T // 64
    SLOT_ST = max(NT // 4, 1)
    NSLOT = NT // SLOT_ST
    bands = na_bands(ROWS, RS)
    nslots_na = sum(len(b) for b in bands)
    NH2 = 2 * (T // 256 - 1)
    NH5 = 2 * (NT - 1)

    nc = bass.Bass("TRN2", target_bir_lowering=False)

    def din(name, shape, dt=F32):
        return nc.dram_tensor(name, list(shape), dt, kind="ExternalInput").ap()

    def dscr(name, shape, dt):
        kind = "ExternalOutput" if name in debug_outs else "Internal"
        return nc.dram_tensor(name, list(shape), dt, kind=kind).ap()

    xin = din("xin", [T, D])
    mem = din("mem", [NSLOT, 256, D])
    bm_in = din("bm", [128, max(NT - 1, 1)])
    bm2_in = din("bm2", [128, max(T // 256 - 1, 1)])
    rowbias_in = din("rowbias", [128, nslots_na])
    btab_in = din("btab", [depth, 64, 8 * 17 * 64])
    cst_in = {k: din("c_" + k, v.shape) for k, v in make_consts().items()}
    W = {}
    for name, shape in (("attn_norm", [depth, D]), ("w_in", [depth, D, 7072]), ("rw_conv", [depth, 3, 1952]),
                        ("rw_decay0", [depth, 2, 512]), ("rw_decay2", [depth, 2, 64, 512]), ("rw_a0", [depth, 2, 512]),
                        ("rw_a2", [depth, 2, 64, 512]), ("rw_g2", [depth, 160, 512]), ("rw_k_k", [depth, 512]),
                        ("rw_k_a", [depth, 512]), ("rw_r_k", [depth, 512]), ("rw_lnx_w", [depth, 512]),
                        ("rw_lnx_b", [depth, 512]), ("mem_norm", [depth, D]), ("w_mem_kv", [depth, D, 1024]),
                        ("w_branch", [depth, 3, 512, D]), ("w_out", [depth, D, D]), ("ffn_norm", [depth, D]),
                        ("w_up", [depth, D, 5632]), ("ffn_conv", [depth, 3, 5632]), ("ffn_conv_b", [depth, 5632]),
                        ("w_down", [depth, 2816, D]), ("final_norm", [1, D])):
        W[name] = din(name, shape)
    yout = nc.dram_tensor("yout", [T, D], F32, kind="ExternalOutput").ap()

    xT = [dscr("xT0", [D, T], F32), dscr("xT1", [D, T], F32)]
    qT = dscr("qT", [512, T], BF16)
    kT = dscr("kT", [512, T], BF16)
    vtm = dscr("vtm", [T, 512], BF16)
    omT = dscr("omT", [512, T], BF16)
    onaT = dscr("onaT", [512, T], BF16)
    orwT = dscr("orwT", [512, T], BF16)
    rws = {}
    for d in range(2):
        for nm in ("at", "bt", "bb", "rt", "kt", "kb"):
            rws[nm, d] = dscr("rw_%s%d" % (nm, d), [512, T], BF16)
        rws["wtot", d] = dscr("rw_wtot%d" % d, [512, NCH], F32)
    rws["v"] = dscr("rw_v", [512, T], BF16)
    rws["bonus"] = dscr("rw_bonus", [512, T], F32)
    rws["g"] = dscr("rw_g", [512, T], BF16)
    yfw = dscr("yfw", [T, 512], F32)

    R_xT = [Res("xT0", True), Res("xT1", True)]
    R_scr = {}

    def rscr(key):
        if key not in R_scr:
            R_scr[key] = Res(str(key), True)
        return R_scr[key]

    with ExitStack() as top:
        S = Sched(nc, top)
        psum = [top.enter_context(nc.psum_tensor("ps%d" % i, [128, 512], F32)) for i in range(7)]
        psT = top.enter_context(nc.psum_tensor("psT", [128, 1024], BF16))
        R_ps = [Res("ps%d" % i) for i in range(7)]
        R_psT = Res("psT")

        def mm(out, lhsT, rhs, start, stop, reads, writes):
            S.op("pe", lambda e: e.matmul(out, lhsT=lhsT, rhs=rhs, start=start, stop=stop), reads, writes)

        def tr(out, in_, ident, reads, writes):
            S.op("pe", lambda e: e.transpose(out, in_, ident), reads, writes)

        def act(out, in_, func, reads, writes, bias=None, scale=None, accum_out=None):
            kw = {}
            if bias is not None:
                kw["bias"] = bias
            if scale is not None:
                kw["scale"] = scale
            if accum_out is not None:
                kw["accum_out"] = accum_out
            S.op("act", lambda e: e.activation(out=out, in_=in_, func=func, **kw), reads, writes)

        def is_ps(ap):
            return hasattr(ap, "space") and "PSUM" in str(ap.space)

        def tt(eng, out, in0, in1, op, reads, writes):
            if eng == "pool" and (is_ps(out) or is_ps(in0) or is_ps(in1)):
                eng = "dve"
            S.op(eng, lambda e: e.tensor_tensor(out=out, in0=in0, in1=in1, op=op), reads, writes)

        def ts(eng, out, in0, s1, s2, op0, op1, reads, writes):
            if eng == "pool" and (is_ps(out) or is_ps(in0)):
                eng = "dve"
            if op1 is None and op0 == ALU.pow:
                assert s1 == -0.5
                S.op("act", lambda e: e.activation(out=out, in_=in0, func=AF.Ln), reads, writes)
                S.op("act", lambda e: e.activation(out=out, in_=out, func=AF.Exp, scale=-0.5), list(reads) + list(writes), writes)
            elif op1 is None:
                S.op(eng, lambda e: e.tensor_scalar(out=out, in0=in0, scalar1=s1, scalar2=None, op0=op0), reads, writes)
            else:
                S.op(eng, lambda e: e.tensor_scalar(out=out, in0=in0, scalar1=s1, scalar2=s2, op0=op0, op1=op1), reads, writes)

        def stt(eng, out, in0, scalar, in1, op0, op1, reads, writes):
            eng = "dve"
            S.op(eng, lambda e: e.scalar_tensor_tensor(out=out, in0=in0, scalar=scalar, in1=in1, op0=op0, op1=op1), reads, writes)

        def cp(eng, out, in_, reads, writes):
            if eng == "act":
                S.op("act", lambda e: e.copy(out=out, in_=in_), reads, writes)
            else:
                S.op(eng, lambda e: e.tensor_copy(out=out, in_=in_), reads, writes)

        def memset(eng, ap, val, writes):
            S.op(eng, lambda e: e.memset(ap, val), (), writes)

        def ld(out, in_, reads, writes, q="sp", nonc=False):
            if nonc:
                def f(e):
                    with nc.allow_non_contiguous_dma(reason="small strided load"):
                        return e.dma_start(out=out, in_=in_)
                S.dma(q, f, reads, writes)
            else:
                S.dma(q, lambda e: e.dma_start(out=out, in_=in_), reads, writes)

        def stor(out, in_, reads, writes, sem_res, q="sp"):
            S.dma(q, lambda e: e.dma_start(out=out, in_=in_), reads, writes, sem_res=sem_res)

        rr = [0]

        def next_ps():
            rr[0] = (rr[0] + 1) % 4
            return rr[0]

        ve = [0]

        def veng():
            ve[0] ^= 1
            return "dve" if ve[0] else "pool"

        uniq = [0]

        def sbt(stack, name, shape, dt):
            uniq[0] += 1
            return stack.enter_context(nc.sbuf_tensor("sb%d_%s" % (uniq[0], name), list(shape), dt))

        ident_f = sbt(top, "ident_f", [128, 128], F32)
        ident_b = sbt(top, "ident_b", [128, 128], BF16)
        ones_b = sbt(top, "ones_b", [128, 128], BF16)
        bd64_b = sbt(top, "bd64_b", [128, 128], BF16)
        bd64_f = sbt(top, "bd64_f", [128, 128], F32)
        R_c = Res("consts")
        ld(ident_f[:], cst_in["ident"][:, :], (), [R_c])
        ld(ident_b[:], cst_in["ident"][:, :], (), [R_c], q="pool")
        ld(bd64_b[:], cst_in["bd64"][:, :], (), [R_c], q="pool")
        ld(bd64_f[:], cst_in["bd64"][:, :], (), [R_c])
        memset("dve", ones_b[:], 1.0, [R_c])
        bm = sbt(top, "bm", [128, max(NT - 1, 1)], F32)
        bm2 = sbt(top, "bm2", [128, max(T // 256 - 1, 1)], F32)
        ld(bm[:], bm_in[:, :], (), [R_c])
        ld(bm2[:], bm2_in[:, :], (), [R_c])

        def rmsnorm_fm(X, rX, XN, rXN, gain, rg, n, SQ, rSQ, RSt, rRS, bank=4):
            act(SQ[:, :, 0:n], X[:, :, 0:n], AF.Square, [rX], [rSQ])
            for c in range(8):
                mm(psum[bank][:, 0:n], ones_b[:], SQ[:, c, 0:n], c == 0, c == 7, [rSQ, R_c], [R_ps[bank]])
            ts("dve", RSt[:, 0:n], psum[bank][:, 0:n], 1.0 / D, 1e-6, ALU.mult, ALU.add, [R_ps[bank]], [rRS])
            ts("dve", RSt[:, 0:n], RSt[:, 0:n], -0.5, None, ALU.pow, None, [rRS], [rRS])
            for c in range(8):
                stt(veng(), XN[:, c, 0:n], X[:, c, 0:n], gain[:, c:c + 1], RSt[:, 0:n], ALU.mult, ALU.mult,
                    [rX, rRS, rg], [rXN])

        def load_vec_fm(stack, name, src, nchunk, q="sp"):
            t = sbt(stack, name, [128, nchunk], F32)
            r = Res(name)
            ld(t[:], src.rearrange("(c p) -> p c", p=128), (), [r], nonc=True)
            return t, r

        def load_w(stack, name, src, kc, ncols, col0=0, eng_q="pool"):
            t = sbt(stack, name, [128, kc, ncols], BF16)
            r = Res(name)
            for c in range(kc):
                ld(t[:, c, :], src[c * 128:(c + 1) * 128, col0:col0 + ncols], (), [r], q=eng_q)
            return t, r

        def phase_p0():
            with ExitStack() as st:
                XI = [sbt(st, "p0_xi%d" % i, [128, D], F32) for i in range(2)]
                rXI = [Res("xi0"), Res("xi1")]
                XO = [sbt(st, "p0_xo%d" % i, [128, 8, 128], F32) for i in range(2)]
                rXO = [Res("xo0"), Res("xo1")]
                for i in range(NTL):
                    b = i % 2
                    ld(XI[b][:], xin[i * 128:(i + 1) * 128, :], (), [rXI[b]])
                    for half in range(2):
                        bank = next_ps()
                        for c4 in range(4):
                            c = half * 4 + c4
                            tr(psum[bank][:, c4 * 128:(c4 + 1) * 128], XI[b][:, c * 128:(c + 1) * 128], ident_f[:],
                               [rXI[b], R_c], [R_ps[bank]])
                        cp("act" if half else "dve", XO[b][:, half * 4:(half + 1) * 4, :],
                           psum[bank][:, :].rearrange("p (c t) -> p c t", c=4), [R_ps[bank]], [rXO[b]])
                    stor(xT[0].rearrange("(c p) t -> p c t", p=128)[:, :, i * 128:(i + 1) * 128], XO[b][:],
                         [rXO[b]], [R_xT[0]], rXO[b])
            S.barrier()

        def phase_e(src):
            with ExitStack() as st:
                gain, rg = load_vec_fm(st, "e_gain", W["final_norm"][0], 8)
                X = [sbt(st, "e_x%d" % i, [128, 8, 512], F32) for i in range(2)]
                rX = [Res(), Res()]
                XN = sbt(st, "e_xn", [128, 8, 512], F32)
                rXN = Res()
                SQ = sbt(st, "e_sq", [128, 8, 512], BF16)
                rSQ = Res()
                RSt = sbt(st, "e_rs", [128, 512], F32)
                rRS = Res()
                YO = [sbt(st, "e_yo%d" % i, [128, D], F32) for i in range(2)]
                rYO = [Res(), Res()]
                xv = xT[src].rearrange("(c p) t -> p c t", p=128)
                ld(X[0][:], xv[:, :, 0:512], [R_xT[src]], [rX[0]])
                k = 0
                for s in range(NT):
                    b = s % 2
                    if s + 1 < NT:
                        ld(X[1 - b][:], xv[:, :, (s + 1) * 512:(s + 2) * 512], [R_xT[src]], [rX[1 - b]])
                    rmsnorm_fm(X[b], rX[b], XN, rXN, gain, rg, 512, SQ, rSQ, RSt, rRS)
                    for tl in range(4):
                        yb = k % 2
                        k += 1
                        for half in range(2):
                            bank = next_ps()
                            for c4 in range(4):
                                c = half * 4 + c4
                                tr(psum[bank][:, c4 * 128:(c4 + 1) * 128], XN[:, c, tl * 128:(tl + 1) * 128], ident_f[:],
                                   [rXN, R_c], [R_ps[bank]])
                            cp("act" if half else "dve", YO[yb][:, half * 512:(half + 1) * 512], psum[bank][:, :],
                               [R_ps[bank]], [rYO[yb]])
                        t0 = s * 512 + tl * 128
                        stor(yout[t0:t0 + 128, :], YO[yb][:], [rYO[yb]], [Res()], rYO[yb])
            S.barrier()

        def phase_c2(l, src, dst):
            TW = 256
            NTW = T // TW
            NW = TW + 2
            banks = (0, 1, 2, 3, 5, 6)
            bk = [0]

            def nb():
                bk[0] = (bk[0] + 1) % len(banks)
                return banks[bk[0]]

            with ExitStack() as st:
                Wu, rWu = load_w(st, "c2_wu", W["w_up"][l], 8, 5632)
                Wd, rWd = load_w(st, "c2_wd", W["w_down"][l], 22, D)
                gain, rg = load_vec_fm(st, "c2_gain", W["ffn_norm"][l], 8)
                cw = sbt(st, "c2_cw", [128, 3, 44], F32)
                rcw = Res()
                for k in range(3):
                    ld(cw[:, k, :], W["ffn_conv"][l, k].rearrange("(c p) -> p c", p=128), (), [rcw], nonc=True)
                cb, rcb = load_vec_fm(st, "c2_cb", W["ffn_conv_b"][l], 44)
                X = [sbt(st, "c2_x%d" % i, [128, 8, NW], F32) for i in range(2)]
                rX = [Res(), Res()]
                XN = sbt(st, "c2_xn", [128, 8, NW], BF16)
                rXN = Res()
                SQ = sbt(st, "c2_sq", [128, 8, NW], BF16)
                rSQ = Res()
                RSt = sbt(st, "c2_rs", [128, NW], F32)
                rRS = Res()
                G = sbt(st, "c2_g", [128, 22, TW], BF16)
                rG = Res()
                NB = 3
                CV = [sbt(st, "c2_cv%d" % i, [128, TW], F32) for i in range(NB)]
                rCV = [Res() for _ in range(NB)]
                CG = [sbt(st, "c2_cg%d" % i, [128, TW], F32) for i in range(NB)]
                rCG = [Res() for _ in range(NB)]
                SGt = [sbt(st, "c2_sg%d" % i, [128, TW], F32) for i in range(NB)]
                rSGt = [Res() for _ in range(NB)]
                xv = xT[src].rearrange("(c p) t -> p c t", p=128)
                xo = xT[dst].rearrange("(c p) t -> p c t", p=128)

                def load_x(s):
                    b = s % 2
                    t0 = s * TW
                    lo = max(t0 - 1, 0)
                    hi = min(t0 + TW + 1, T)
                    ld(X[b][:, :, lo - (t0 - 1):hi - (t0 - 1)], xv[:, :, lo:hi], [R_xT[src]], [rX[b]])
                    if s == 0:
                        memset("pool", X[b][:, :, 0:1], 0.0, [rX[b]])
                    if s == NTW - 1:
                        memset("pool", X[b][:, :, NW - 1:NW], 0.0, [rX[b]])

                load_x(0)
                for s in range(NTW):
                    b = s % 2
                    t0 = s * TW
                    if s + 1 < NTW:
                        load_x(s + 1)
                    rmsnorm_fm(X[b], rX[b], XN, rXN, gain, rg, NW, SQ, rSQ, RSt, rRS)
                    if s > 0:
                        ts("pool", XN[:, :, 0:1], XN[:, :, 0:1], bm2[:, s - 1:s], None, ALU.mult, None, [rXN, R_c], [rXN])
                    if s < NTW - 1:
                        ts("pool", XN[:, :, NW - 1:NW], XN[:, :, NW - 1:NW], bm2[:, s:s + 1], None, ALU.mult, None, [rXN, R_c], [rXN])
                    for j in range(22):
                        cbuf = j % NB
                        pair = ((j, CV[cbuf], rCV[cbuf], nb()), (22 + j, CG[cbuf], rCG[cbuf], nb()))
                        for (uc, Ct, rC, bank) in pair:
                            for kc in range(8):
                                mm(psum[bank][:, 0:NW], Wu[:, kc, uc * 128:(uc + 1) * 128], XN[:, kc, :], kc == 0, kc == 7,
                                   [rWu, rXN], [R_ps[bank]])
                        for (uc, Ct, rC, bank) in pair:
                            act(Ct[:, :], psum[bank][:, 1:TW + 1], AF.Identity, [R_ps[bank], rcw, rcb], [rC], bias=cb[:, uc:uc + 1],
                                scale=cw[:, 1, uc:uc + 1])
                        for (uc, Ct, rC, bank) in pair:
                            stt("dve", Ct[:, :], psum[bank][:, 0:TW], cw[:, 0, uc:uc + 1], Ct[:, :], ALU.mult, ALU.add,
                                [R_ps[bank], rcw, rC], [rC])
                        for (uc, Ct, rC, bank) in pair:
                            stt("dve", Ct[:, :], psum[bank][:, 2:TW + 2], cw[:, 2, uc:uc + 1], Ct[:, :], ALU.mult, ALU.add,
                                [R_ps[bank], rcw, rC], [rC])
                        act(SGt[cbuf][:], CG[cbuf][:], AF.Silu, [rCG[cbuf]], [rSGt[cbuf]])
                        tt("pool", G[:, j, :], SGt[cbuf][:], CV[cbuf][:], ALU.mult, [rSGt[cbuf], rCV[cbuf]], [rG])
                    for oc in range(8):
                        bank = nb()
                        for kc in range(22):
                            mm(psum[bank][:, 0:TW], Wd[:, kc, oc * 128:(oc + 1) * 128], G[:, kc, :], kc == 0, kc == 21,
                               [rWd, rG], [R_ps[bank]])
                        tt("dve", X[b][:, oc, 1:TW + 1], X[b][:, oc, 1:TW + 1], psum[bank][:, 0:TW], ALU.add, [rX[b], R_ps[bank]], [rX[b]])
                    stor(xo[:, :, t0:t0 + TW], X[b][:, :, 1:TW + 1], [rX[b]], [R_xT[dst]], rX[b])
            S.barrier()

        def phase_c1(l, src, dst):
            with ExitStack() as st:
                Wg, rWg = load_w(st, "c1_wg", W["w_in"][l], 8, 3072, col0=4000)
                Wb = sbt(st, "c1_wb", [128, 12, D], BF16)
                rWb = Res()
                for b in range(3):
                    for kc in range(4):
                        ld(Wb[:, b * 4 + kc, :], W["w_branch"][l, b, kc * 128:(kc + 1) * 128, :], (), [rWb], q="pool")
                Wo, rWo = load_w(st, "c1_wo", W["w_out"][l], 8, D)
                gain, rg = load_vec_fm(st, "c1_gain", W["attn_norm"][l], 8)
                Xs = [sbt(st, "c1_x%d" % i, [128, 8, 512], F32) for i in range(2)]
                rXs = [Res(), Res()]
                XN = sbt(st, "c1_xn", [128, 8, 512], BF16)
                rXN = Res()
                SQ = sbt(st, "c1_sq", [128, 8, 512], BF16)
                rSQ = Res()
                RSt = sbt(st, "c1_rs", [128, 512], F32)
                rRS = Res()
                OBs = [sbt(st, "c1_ob%d" % i, [128, 12, 512], BF16) for i in range(2)]
                rOBs = [Res(), Res()]
                MG = sbt(st, "c1_mg", [128, 8, 512], BF16)
                rMG = Res()
                SGt = [sbt(st, "c1_sg%d" % i, [128, 512], F32) for i in range(2)]
                rSGt = [Res(), Res()]
                ACCs = [sbt(st, "c1_acc%d" % i, [128, 512], F32) for i in range(2)]
                rACCs = [Res(), Res()]
                xv = xT[src].rearrange("(c p) t -> p c t", p=128)
                xo = xT[dst].rearrange("(c p) t -> p c t", p=128)
                srcs = (onaT, orwT, omT)
                rsrcs = (rscr("onaT"), rscr("orwT"), rscr("omT"))
                k = 0

                def loads(s):
                    t0 = s * 512
                    ld(Xs[s % 2][:], xv[:, :, t0:t0 + 512], [R_xT[src]], [rXs[s % 2]])
                    for b in range(3):
                        ld(OBs[s % 2][:, b * 4:(b + 1) * 4, :], srcs[b].rearrange("(c p) t -> p c t", p=128)[:, :, t0:t0 + 512],
                           [rsrcs[b]], [rOBs[s % 2]])

                loads(0)
                for s in range(NT):
                    t0 = s * 512
                    if s + 1 < NT:
                        loads(s + 1)
                    X, rX, OB, rOB = Xs[s % 2], rXs[s % 2], OBs[s % 2], rOBs[s % 2]
                    rmsnorm_fm(X, rX, XN, rXN, gain, rg, 512, SQ, rSQ, RSt, rRS)
                    for mc in range(8):
                        ACC, rACC = ACCs[mc % 2], rACCs[mc % 2]
                        for b in range(3):
                            bg = next_ps()
                            for kc in range(8):
                                mm(psum[bg][:, :], Wg[:, kc, b * 1024 + mc * 128:b * 1024 + (mc + 1) * 128], XN[:, kc, :],
                                   kc == 0, kc == 7, [rWg, rXN], [R_ps[bg]])
                            sb_ = k % 2
                            k += 1
                            act(SGt[sb_][:], psum[bg][:, :], AF.Sigmoid, [R_ps[bg]], [rSGt[sb_]])
                            bp = next_ps()
                            for kc in range(4):
                                mm(psum[bp][:, :], Wb[:, b * 4 + kc, mc * 128:(mc + 1) * 128], OB[:, b * 4 + kc, :],
                                   kc == 0, kc == 3, [rWb, rOB], [R_ps[bp]])
                            if b == 0:
                                tt("dve", ACC[:], SGt[sb_][:], psum[bp][:, :], ALU.mult, [rSGt[sb_], R_ps[bp]], [rACC])
                            else:
                                tt("dve", SGt[sb_][:], SGt[sb_][:], psum[bp][:, :], ALU.mult, [rSGt[sb_], R_ps[bp]],
                                   [rSGt[sb_]])
                                if b == 1:
                                    tt("pool", ACC[:], ACC[:], SGt[sb_][:], ALU.add, [rACC, rSGt[sb_]], [rACC])
                                else:
                                    tt("pool", MG[:, mc, :], ACC[:], SGt[sb_][:], ALU.add, [rACC, rSGt[sb_]], [rMG])
                    for oc in range(8):
                        bank = next_ps()
                        for kc in range(8):
                            mm(psum[bank][:, :], Wo[:, kc, oc * 128:(oc + 1) * 128], MG[:, kc, :], kc == 0, kc == 7,
                               [rWo, rMG], [R_ps[bank]])
                        tt(veng(), X[:, oc, :], X[:, oc, :], psum[bank][:, :], ALU.add, [rX, R_ps[bank]], [rX])
                    stor(xo[:, :, t0:t0 + 512], X[:], [rX], [R_xT[dst]], rX)
            S.barrier()

        def phase_a(l, src):
            with ExitStack() as st:
                Wi, rWi = load_w(st, "a_wi", W["w_in"][l], 8, 4000)
                gain, rg = load_vec_fm(st, "a_gain", W["attn_norm"][l], 8)
                KmT = sbt(st, "a_kmT", [128, NSLOT, 4, 256], BF16)
                Vm = sbt(st, "a_vm", [128, NSLOT, 2, 512], BF16)
                rKV = Res()
                with ExitStack() as st2:
                    Wkv, rWkv = load_w(st2, "a_wkv", W["w_mem_kv"][l], 8, 1024)
                    gm, rgm = load_vec_fm(st2, "a_gm", W["mem_norm"][l], 8)
                    MT = sbt(st2, "a_mt", [128, D], F32)
                    rMT = Res()
                    MS = sbt(st2, "a_ms", [128, D], F32)
                    rMS = Res()
                    MB = sbt(st2, "a_mb", [128, D], BF16)
                    rMB = Res()
                    ssq = sbt(st2, "a_ssq", [128, 1], F32)
                    rssq = Res()
                    memT = sbt(st2, "a_memT", [128, 8, 256], BF16)
                    rmemT = Res()
                    for sl in range(NSLOT):
                        for mc in range(2):
                            ld(MT[:], mem[sl, mc * 128:(mc + 1) * 128, :], (), [rMT])
                            act(MS[:], MT[:], AF.Square, [rMT], [rMS, rssq], accum_out=ssq[:])
                            ts("dve", ssq[:], ssq[:], 1.0 / D, 1e-6, ALU.mult, ALU.add, [rssq], [rssq])
                            ts("dve", ssq[:], ssq[:], -0.5, None, ALU.pow, None, [rssq], [rssq])
                            ts("dve", MB[:], MT[:], ssq[:, 0:1], None, ALU.mult, None, [rMT, rssq], [rMB])
                            for c in range(8):
                                tr(psT[:, c * 128:(c + 1) * 128], MB[:, c * 128:(c + 1) * 128], ident_b[:], [rMB, R_c],
                                   [R_psT])
                            for c in range(8):
                                ts(veng(), memT[:, c, mc * 128:(mc + 1) * 128], psT[:, c * 128:(c + 1) * 128],
                                   gm[:, c:c + 1], None, ALU.mult, None, [R_psT, rgm], [rmemT])
                        for h in range(4):
                            bank = next_ps()
                            for kc in range(8):
                                mm(psum[bank][:, 0:256], Wkv[:, kc, h * 128:(h + 1) * 128], memT[:, kc, :], kc == 0, kc == 7,
                                   [rWkv, rmemT], [R_ps[bank]])
                            cp("act", KmT[:, sl, h, :], psum[bank][:, 0:256], [R_ps[bank]], [rKV])
                        for mc in range(2):
                            bank = next_ps()
                            for kc in range(8):
                                mm(psum[bank][:, :], memT[:, kc, mc * 128:(mc + 1) * 128], Wkv[:, kc, 512:1024], kc == 0,
                                   kc == 7, [rWkv, rmemT], [R_ps[bank]])
                            cp("dve", Vm[:, sl, mc, :], psum[bank][:, :], [R_ps[bank]], [rKV])
                    S.barrier()
                cw = sbt(st, "a_cw", [128, 3, 16], F32)
                rcw = Res()
                memset("dve", cw[:], 0.0, [rcw])
                for k in range(3):
                    ld(cw[:, k, 0:15], W["rw_conv"][l, k, 0:1920].rearrange("(c p) -> p c", p=128), (), [rcw], nonc=True)
                    ld(cw[0:32, k, 15:16], W["rw_conv"][l, k, 1920:1952].rearrange("(c p) -> p c", p=32), (), [rcw],
                       nonc=True)
                dec0 = sbt(st, "a_dec0", [128, 2, 4], F32)
                a0 = sbt(st, "a_a0", [128, 2, 4], F32)
                rsm = Res()
                for d in range(2):
                    ld(dec0[:, d, :], W["rw_decay0"][l, d].rearrange("(c p) -> p c", p=128), (), [rsm], nonc=True)
                    ld(a0[:, d, :], W["rw_a0"][l, d].rearrange("(c p) -> p c", p=128), (), [rsm], nonc=True)
                kkv, r1 = load_vec_fm(st, "a_kk", W["rw_k_k"][l], 4)
                kav, r2 = load_vec_fm(st, "a_ka", W["rw_k_a"][l], 4)
                rkv, r3 = load_vec_fm(st, "a_rk", W["rw_r_k"][l], 4)
                D2 = sbt(st, "a_d2", [128, 512], BF16)
                A2 = sbt(st, "a_a2", [128, 512], BF16)
                G2 = sbt(st, "a_g2", [128, 2, 512], BF16)
                ld(D2[:], W["rw_decay2"][l].rearrange("d l f -> (d l) f"), (), [rsm], q="pool")
                ld(A2[:], W["rw_a2"][l].rearrange("d l f -> (d l) f"), (), [rsm], q="pool")
                ld(G2[:, 0, :], W["rw_g2"][l, 0:128, :], (), [rsm], q="pool")
                ld(G2[0:32, 1, :], W["rw_g2"][l, 128:160, :], (), [rsm], q="pool")
                scanm = sbt(st, "a_scanm", [128, 512], F32)
                ld(scanm[:], cst_in["scanmask"][:, :], (), [rsm])
                rsmall = [rsm, r1, r2, r3, rcw]

                XN = sbt(st, "a_xn", [128, 8, 512], BF16)
                rXN = Res()
                RSt = sbt(st, "a_rs", [128, 512], F32)
                rRS = Res()
                QK = sbt(st, "a_qk", [128, 8, 512], BF16)
                rQK = Res()
                SQ, rSQ = QK, rQK
                VT = sbt(st, "a_vt", [128, 4, 512], BF16)
                rVT = Res()
                QM = sbt(st, "a_qm", [128, 4, 512], BF16)
                rQM = Res()
                PT = [sbt(st, "a_pt%d" % i, [128, 512], BF16) for i in range(2)]
                rPT = [Res(), Res()]
                RD = sbt(st, "a_rd", [128, 512], F32)
                rRD = Res()
                OM = sbt(st, "a_om", [128, 4, 512], BF16)
                rOM = Res()
                ZC = sbt(st, "a_zc", [128, 16, 512], F32)
                rZC = Res()
                X, rX = ZC[:, 0:8, :], rZC
                HZ = sbt(st, "a_hz", [128, 16, max(NH5, 2)], F32)
                rHZ = Res()
                xv = xT[src].rearrange("(c p) t -> p c t", p=128)
                rwcols = [1536 + 128 * i for i in range(15)] + [1536 + 1920]
                rwm = [128] * 15 + [32]
                if NT > 1:
                    XH = sbt(st, "a_xh", [128, 8, NH5], F32)
                    rXH = Res()
                    XHN = sbt(st, "a_xhn", [128, 8, NH5], BF16)
                    rXHN = Res()
                    for c in range(8):
                        srcv = xT[src][c * 128:(c + 1) * 128, 511:T - 1].rearrange("p (j w) -> p j w", w=512)[:, :, 0:2]
                        ld(XH[:, c, :].rearrange("p (j w) -> p j w", w=2), srcv, [R_xT[src]], [rXH], nonc=True)
                    rmsnorm_fm(XH, rXH, XHN, rXHN, gain, rg, NH5, SQ, rSQ, RSt, rRS)
                    memset("dve", HZ[:], 0.0, [rHZ])
                    for zc in range(16):
                        bank = next_ps()
                        m = rwm[zc]
                        for kc in range(8):
                            mm(psum[bank][0:m, 0:NH5], Wi[:, kc, rwcols[zc]:rwcols[zc] + m], XHN[:, kc, :], kc == 0, kc == 7,
                               [rWi, rXHN], [R_ps[bank]])
                        tt("dve", HZ[0:m, zc, :].rearrange("p (j w) -> p j w", w=2),
                           psum[bank][0:m, 0:NH5].rearrange("p (j w) -> p j w", w=2),
                           bm[0:m, 0:NT - 1].unsqueeze(2).to_broadcast([m, NT - 1, 2]), ALU.mult, [R_ps[bank], R_c], [rHZ])

                def tmp(name, dt=F32):
                    return sbt(st, "a_t_" + name, [128, 512], dt), Res(name)

                TH, rTH = tmp("th", BF16)
                XAb, rXAb = tmp("xab", BF16)
                XGb = sbt(st, "a_t_xgb", [128, 2, 512], BF16)
                rXGb = Res()
                SGm, rSGm = tmp("sg")
                Pc, rPc = tmp("pc")
                Pe, rPe = tmp("pe")
                Pt, rPt = tmp("pt")
                E1, rE1 = tmp("e1")
                E2, rE2 = tmp("e2")
                Ee, rEe = tmp("ee")
                Eb, rEb = tmp("eb")
                AS, rAS = tmp("as")
                KD, rKD = tmp("kd")
                KK, rKK = tmp("kk")
                KQ, rKQ = tmp("kq", BF16)
                Bv, rBv = tmp("b")
                BON, rBON = tmp("bon")
                RKb, rRKb = tmp("rkb", BF16)
                TMP, rTMP = tmp("tmp")
                RI, rRI = TMP, rTMP
                OUTS = {}
                for nm in ("at", "bt", "bb", "rt", "kt", "kb"):
                    OUTS[nm] = (sbt(st, "a_o_" + nm, [128, 512], BF16), Res(nm))
                VRb, rVRb = tmp("vrb", BF16)
                Gb, rGb = tmp("gb", BF16)
                WTo = sbt(st, "a_wto", [128, 8], F32)
                rWTo = Res()

                for s in range(NT):
                    t0 = s * 512
                    slot = s // SLOT_ST
                    ld(X[:], xv[:, :, t0:t0 + 512], [R_xT[src]], [rX])
                    rmsnorm_fm(X, rX, XN, rXN, gain, rg, 512, SQ, rSQ, RSt, rRS)
                    def d_qk(c):
                        bank = next_ps()
                        for kc in range(8):
                            mm(psum[bank][:, :], Wi[:, kc, c * 128:(c + 1) * 128], XN[:, kc, :], kc == 0, kc == 7,
                               [rWi, rXN], [R_ps[bank]])
                        if c < 4:
                            act(QK[:, c, :], psum[bank][:, :], AF.Copy, [R_ps[bank]], [rQK], scale=0.125)
                        else:
                            cp("dve", QK[:, c, :], psum[bank][:, :], [R_ps[bank]], [rQK])
                        if c == 3:
                            stor(qT.rearrange("(c p) t -> p c t", p=128)[:, :, t0:t0 + 512], QK[:, 0:4, :], [rQK], [rscr("qT")], rQK)
                        if c == 7:
                            stor(kT.rearrange("(c p) t -> p c t", p=128)[:, :, t0:t0 + 512], QK[:, 4:8, :], [rQK], [rscr("kT")], rQK)

                    def d_v(tl):
                        bank = next_ps()
                        for kc in range(8):
                            mm(psum[bank][:, :], XN[:, kc, tl * 128:(tl + 1) * 128], Wi[:, kc, 1024:1536], kc == 0, kc == 7,
                               [rWi, rXN], [R_ps[bank]])
                        cp("act" if tl % 2 else "dve", VT[:, tl, :], psum[bank][:, :], [R_ps[bank]], [rVT])
                        if tl == 3:
                            stor(vtm[t0:t0 + 512, :].rearrange("(c p) f -> p c f", p=128), VT[:], [rVT], [rscr("vtm")], rVT)

                    def d_mq(h):
                        bank = next_ps()
                        for kc in range(8):
                            mm(psum[bank][:, :], Wi[:, kc, 3488 + h * 128:3488 + (h + 1) * 128], XN[:, kc, :], kc == 0, kc == 7,
                               [rWi, rXN], [R_ps[bank]])
                        act(QM[:, h, :], psum[bank][:, :], AF.Copy, [R_ps[bank]], [rQM], scale=128 ** -0.5)

                    def d_ma(h):
                        for mc in range(2):
                            bank = next_ps()
                            mm(psum[bank][:, :], KmT[:, slot, h, mc * 128:(mc + 1) * 128], QM[:, h, :], True, True,
                               [rKV, rQM], [R_ps[bank]])
                            act(PT[mc][:], psum[bank][:, :], AF.Exp, [R_ps[bank]], [rPT[mc]])
                        bo, bd_ = next_ps(), next_ps()
                        for mc in range(2):
                            mm(psum[bo][:, :], Vm[:, slot, mc, h * 128:(h + 1) * 128], PT[mc][:], mc == 0, mc == 1,
                               [rKV, rPT[mc]], [R_ps[bo]])
                        for mc in range(2):
                            mm(psum[bd_][:, :], ones_b[:], PT[mc][:], mc == 0, mc == 1, [R_c, rPT[mc]], [R_ps[bd_]])
                        act(RD[:], psum[bd_][:, :], AF.Ln, [R_ps[bd_]], [rRD])
                        act(RD[:], RD[:], AF.Exp, [rRD], [rRD], scale=-1.0)
                        tt("dve", OM[:, h, :], psum[bo][:, :], RD[:], ALU.mult, [R_ps[bo], rRD], [rOM])
                        if h == 3:
                            stor(omT.rearrange("(c p) t -> p c t", p=128)[:, :, t0:t0 + 512], OM[:], [rOM], [rscr("omT")], rOM)

                    def dense_slice(it):
                        d_qk(it)
                        if it % 2 == 0:
                            d_v(it // 2)
                            d_mq(it // 2)
                        else:
                            d_ma(it // 2)

                    rZCc = [Res() for _ in range(16)]
                    for r_ in rZCc:
                        for k_, v_ in list(rZC.r.items()) + list(rZC.w.items()):
                            if r_.r.get(k_, 0) < v_:
                                r_.r[k_] = v_
                    for z0 in range(0, 16, 2):
                        grp = []
                        for zc in (z0, z0 + 1):
                            bank = next_ps()
                            m = rwm[zc]
                            for kc in range(8):
                                mm(psum[bank][0:m, :], Wi[:, kc, rwcols[zc]:rwcols[zc] + m], XN[:, kc, :], kc == 0, kc == 7,
                                   [rWi, rXN], [R_ps[bank]])
                            grp.append((zc, m, bank))
                        for (zc, m, bank) in grp:
                            act(ZC[0:m, zc, :], psum[bank][0:m, :], AF.Copy, [R_ps[bank], rcw], [rZCc[zc]], scale=cw[0:m, 1, zc:zc + 1])
                        for (zc, m, bank) in grp:
                            stt("dve", ZC[0:m, zc, 1:512], psum[bank][0:m, 0:511], cw[0:m, 0, zc:zc + 1], ZC[0:m, zc, 1:512], ALU.mult,
                                ALU.add, [R_ps[bank], rcw, rZCc[zc]], [rZCc[zc]])
                        for (zc, m, bank) in grp:
                            stt("dve", ZC[0:m, zc, 0:511], psum[bank][0:m, 1:512], cw[0:m, 2, zc:zc + 1], ZC[0:m, zc, 0:511], ALU.mult,
                                ALU.add, [R_ps[bank], rcw, rZCc[zc]], [rZCc[zc]])
                        for (zc, m, bank) in grp:
                            if s > 0:
                                stt("dve", ZC[0:m, zc, 0:1], HZ[0:m, zc, 2 * (s - 1):2 * (s - 1) + 1], cw[0:m, 0, zc:zc + 1],
                                    ZC[0:m, zc, 0:1], ALU.mult, ALU.add, [rHZ, rcw, rZCc[zc]], [rZCc[zc]])
                        for (zc, m, bank) in grp:
                            if s < NT - 1:
                                stt("dve", ZC[0:m, zc, 511:512], HZ[0:m, zc, 2 * s + 1:2 * s + 2], cw[0:m, 2, zc:zc + 1],
                                    ZC[0:m, zc, 511:512], ALU.mult, ALU.add, [rHZ, rcw, rZCc[zc]], [rZCc[zc]])
                    rZC.r = {}
                    for r_ in rZCc:
                        for k_, v_ in r_.w.items():
                            if rZC.w.get(k_, 0) < v_:
                                rZC.w[k_] = v_
                    act(TH[:], ZC[:, 12, :], AF.Tanh, [rZC], [rTH])
                    cp("pool", XAb[:], ZC[:, 13, :], [rZC], [rXAb])
                    act(XGb[:, 0, :], ZC[:, 14, :], AF.Sigmoid, [rZC], [rXGb])
                    act(XGb[0:32, 1, :], ZC[0:32, 15, :], AF.Sigmoid, [rZC], [rXGb])
                    for hp in range(4):
                        fs = slice(hp * 128, (hp + 1) * 128)
                        Rr = ZC[:, hp, :]
                        Kr = ZC[:, 4 + hp, :]
                        Vr = ZC[:, 8 + hp, :]
                        mm(psum[4][:, :], G2[:, 0, fs], XGb[:, 0, :], True, False, rsmall + [rXGb], [R_ps[4]])
                        mm(psum[4][:, :], G2[0:32, 1, fs], XGb[0:32, 1, :], False, True, rsmall + [rXGb], [R_ps[4]])
                        cp("act", Gb[:], psum[4][:, :], [R_ps[4]], [rGb])
                        stor(rws["g"][fs, t0:t0 + 512], Gb[:], [rGb], [rscr("g")], rGb)
                        cp("pool", VRb[:], Vr, [rZC], [rVRb])
                        stor(rws["v"][fs, t0:t0 + 512], VRb[:], [rVRb], [rscr("v")], rVRb)
                        ts("dve", KK[:], Kr, kkv[:, hp:hp + 1], None, ALU.mult, None, [rZC] + rsmall, [rKK])
                        tt("pool", KQ[:], KK[:], KK[:], ALU.mult, [rKK], [rKQ])
                        mm(psum[4][:, :], bd64_b[:], KQ[:], True, True, [R_c, rKQ], [R_ps[4]])
                        ts("dve", RI[:], psum[4][:, :], 1e-24, None, ALU.max, None, [R_ps[4]], [rRI])
                        ts("dve", RI[:], RI[:], -0.5, None, ALU.pow, None, [rRI], [rRI])
                        tt("dve", KK[:], KK[:], RI[:], ALU.mult, [rKK, rRI], [rKK])
                        for d in range(2):
                            ds = slice(d * 64, (d + 1) * 64)
                            cexp = 0.6065306597126334
                            mm(psum[5][:, :], D2[ds, fs], TH[ds, :], True, True, rsmall + [rTH], [R_ps[5]])
                            mm(psum[6][:, :], A2[ds, fs], XAb[ds, :], True, True, rsmall + [rXAb], [R_ps[6]])
                            act(SGm[:], psum[5][:, :], AF.Sigmoid, [R_ps[5]] + rsmall, [rSGm], bias=dec0[:, d, hp:hp + 1])
                            act(AS[:], psum[6][:, :], AF.Sigmoid, [R_ps[6]] + rsmall, [rAS], bias=a0[:, d, hp:hp + 1])
                            S.op("dve", lambda e: e.tensor_tensor_scan(out=Pc[:], data0=scanm[:], data1=SGm[:], initial=0.0,
                                                                        op0=ALU.mult, op1=ALU.add), [rSGm] + rsmall, [rPc])
                            Pc3 = Pc[:].rearrange("p (c t) -> p c t", t=64)
                            if d == 1:
                                tt("pool", Pe[:].rearrange("p (c t) -> p c t", t=64), Pc3[:, :, 63:64].to_broadcast([128, 8, 64]),
                                   Pc3, ALU.subtract, [rPc], [rPe])
                                tt("pool", Pc[:], Pe[:], SGm[:], ALU.add, [rPe, rSGm], [rPc])
                                totcol = 0
                            else:
                                totcol = 63
                            tt("pool", Pe[:], Pc[:], SGm[:], ALU.subtract, [rPc, rSGm], [rPe])
                            tt("pool", Pt[:].rearrange("p (c t) -> p c t", t=64),
                               Pc3[:, :, totcol:totcol + 1].to_broadcast([128, 8, 64]), Pc3, ALU.subtract, [rPc], [rPt])
                            ts("dve", TMP[:], AS[:], -1.0, kav[:, hp:hp + 1], ALU.add, ALU.mult, [rAS] + rsmall, [rTMP])
                            stt("dve", KD[:], TMP[:], 1.0, Kr, ALU.add, ALU.mult, [rTMP, rZC], [rKD])
                            tt("dve", Bv[:], KK[:], AS[:], ALU.mult, [rKK, rAS], [rBv])
                            stt("dve", RKb[:], Rr, rkv[:, hp:hp + 1], KD[:], ALU.mult, ALU.mult, [rZC, rKD] + rsmall, [rRKb])
                            mm(psum[4][:, :], bd64_b[:], RKb[:], True, True, [R_c, rRKb], [R_ps[4]])
                            act(E1[:], Pc[:], AF.Exp, [rPc], [rE1], scale=-cexp)
                            act(E2[:], Pc[:], AF.Exp, [rPc], [rE2], scale=cexp)
                            act(Ee[:], Pe[:], AF.Exp, [rPe], [rEe], scale=-cexp)
                            act(Eb[:], Pt[:], AF.Exp, [rPt], [rEb], scale=-cexp)
                            if d == 0:
                                tt("dve", BON[:], psum[4][:, :], Vr, ALU.mult, [R_ps[4], rZC], [rBON])
                            else:
                                tt("dve", TMP[:], psum[4][:, :], Vr, ALU.mult, [R_ps[4], rZC], [rTMP])
                                tt("pool", BON[:], BON[:], TMP[:], ALU.add, [rBON, rTMP], [rBON])
                            cp("pool", WTo[:, :], E1[:].rearrange("p (c t) -> p c t", t=64)[:, :, totcol], [rE1], [rWTo])
                            stor(rws["wtot", d][fs, s * 8:(s + 1) * 8], WTo[:], [rWTo], [rscr(("wtot", d))], rWTo)
                            o, ro = OUTS["rt"]
                            tt("pool", o[:], Rr, E1[:], ALU.mult, [rZC, rE1], [ro])
                            o, ro = OUTS["bt"]
                            tt("dve", o[:], Bv[:], E2[:], ALU.mult, [rBv, rE2], [ro])
                            o, ro = OUTS["kt"]
                            tt("pool", o[:], KD[:], E2[:], ALU.mult, [rKD, rE2], [ro])
                            o, ro = OUTS["at"]
                            stt("dve", o[:], KK[:], -1.0, Ee[:], ALU.mult, ALU.mult, [rKK, rEe], [ro])
                            o, ro = OUTS["bb"]
                            tt("dve", o[:], Bv[:], Eb[:], ALU.mult, [rBv, rEb], [ro])
                            o, ro = OUTS["kb"]
                            tt("pool", o[:], KD[:], Eb[:], ALU.mult, [rKD, rEb], [ro])
                            for nm in ("rt", "bt", "kt", "at", "bb", "kb"):
                                o, ro = OUTS[nm]
                                stor(rws[nm, d][fs, t0:t0 + 512], o[:], [ro], [rscr((nm, d))], ro)
                            dense_slice(hp * 2 + d)
                        stor(rws["bonus"][fs, t0:t0 + 512], BON[:], [rBON], [rscr("bonus")], rBON)
            S.barrier()

        def phase_b1(l):
            with ExitStack() as st:
                Bt = sbt(st, "b1_bt", [64, 8 * 17 * 64], BF16)
                rBt = Res()
                ld(Bt[:], btab_in[l], (), [rBt], q="pool")
                Bt4 = Bt[:].rearrange("p (h o k) -> p h o k", h=8, o=17)
                RB = sbt(st, "b1_rb", [128, nslots_na], F32)
                rRB = Res()
                ld(RB[:], rowbias_in[:, :], (), [rRB])
                NR = min(24, ROWS)
                KW = [sbt(st, "b1_kw%d" % i, [64, 8, NR * 64], BF16) for i in range(2)]
                rKW = [Res(), Res()]
                QW = [sbt(st, "b1_qw%d" % i, [64, 8, 512], BF16) for i in range(2)]
                rQW = [Res(), Res()]
                VE = [sbt(st, "b1_ve%d" % i, [128, NR // 2, 512], BF16) for i in range(2)]
                rVE = [Res(), Res()]
                VO = [sbt(st, "b1_vo%d" % i, [128, NR // 2 - 1, 512], BF16) for i in range(2)]
                rVO = [Res(), Res()]
                PTt = [sbt(st, "b1_pt%d" % i, [128, 512], BF16) for i in range(2)]
                rPTt = [Res(), Res()]
                RDt = sbt(st, "b1_rd", [64, 512], F32)
                rRDt = Res()
                ON = [sbt(st, "b1_on%d" % i, [64, 8, 512], BF16) for i in range(2)]
                rON = [Res(), Res()]
                qv = qT.rearrange("(h d) t -> d h t", d=64)
                kv = kT.rearrange("(h d) t -> d h t", d=64)

                def loads(s):
                    b = s % 2
                    i0 = s * 8
                    rb = min(max(i0 - 8, 0), ROWS - NR)
                    ld(QW[b][:], qv[:, :, s * 512:(s + 1) * 512], [rscr("qT")], [rQW[b]])
                    ld(KW[b][:], kv[:, :, rb * 64:(rb + NR) * 64], [rscr("kT")], [rKW[b]])
                    ld(VE[b][:], vtm[rb * 64:(rb + NR) * 64, :].rearrange("(c p) f -> p c f", p=128), [rscr("vtm")], [rVE[b]])
                    ld(VO[b][:], vtm[rb * 64 + 64:(rb + NR) * 64 - 64, :].rearrange("(c p) f -> p c f", p=128), [rscr("vtm")],
                       [rVO[b]])
                    return rb

                slot = 0
                pk = 0
                sk = [0]
                rbs = {0: loads(0)}
                for s in range(NT):
                    b = s % 2
                    if s + 1 < NT:
                        rbs[s + 1] = loads(s + 1)
                    rb = rbs[s]
                    for iq in range(8):
                        i = s * 8 + iq
                        band = bands[i]
                        for ci, r in enumerate(band):
                            o = r - i + 8
                            kc0 = (r - rb) * 64
                            sk[0] = (sk[0] + 1) % 3
                            bank = sk[0]
                            for h in range(8):
                                mm(psum[bank][:, h * 64:(h + 1) * 64], KW[b][:, h, kc0:kc0 + 128], QW[b][:, h, iq * 64:(iq + 1) * 64],
                                   True, False, [rKW[b], rQW[b]], [R_ps[bank]])
                                mm(psum[bank][:, h * 64:(h + 1) * 64], Bt4[:, h, o:o + 2, :].rearrange("p o k -> p (o k)"),
                                   ident_b[0:64, 0:64], False, True, [rBt, R_c], [R_ps[bank]])
                            pb = pk % 2
                            pk += 1
                            act(PTt[pb][:], psum[bank][:, :], AF.Exp, [R_ps[bank], rRB], [rPTt[pb]], bias=RB[:, slot:slot + 1])
                            slot += 1
                            if (r - rb) % 2 == 0:
                                Vc, rVc = VE[b][:, (r - rb) // 2, :], rVE[b]
                            else:
                                Vc, rVc = VO[b][:, (r - rb - 1) // 2, :], rVO[b]
                            first, last = ci == 0, ci == len(band) - 1
                            bo, bd_ = (5, 6) if i % 2 == 0 else (3, 4)
                            for h in range(8):
                                mm(psum[bo][0:64, h * 64:(h + 1) * 64], Vc[:, h * 64:(h + 1) * 64], PTt[pb][:, h * 64:(h + 1) * 64],
                                   first and h == 0, last, [rVc, rPTt[pb]], [R_ps[bo]])
                            mm(psum[bd_][0:64, :], ones_b[:, 0:64], PTt[pb][:], first, last, [R_c, rPTt[pb]], [R_ps[bd_]])
                        act(RDt[:], psum[bd_][0:64, :], AF.Ln, [R_ps[bd_]], [rRDt])
                        act(RDt[:], RDt[:], AF.Exp, [rRDt], [rRDt], scale=-1.0)
                        tt("dve", ON[b][:, :, iq * 64:(iq + 1) * 64], psum[bo][0:64, :].rearrange("p (h q) -> p h q", h=8),
                           RDt[:].rearrange("p (h q) -> p h q", h=8), ALU.mult, [R_ps[bo], rRDt], [rON[b]])
                    stor(onaT.rearrange("(h d) t -> d h t", d=64)[:, :, s * 512:(s + 1) * 512], ON[b][:], [rON[b]],
                         [rscr("onaT")], rON[b])
            S.barrier()

        def phase_b2(l):
            with ExitStack() as st:
                mk = {}
                rM = Res()
                for nm in ("ls4", "li4", "us4", "ui4"):
                    mk[nm] = sbt(st, "b2_" + nm, [128, 512], F32)
                    ld(mk[nm][:], cst_in[nm][:, :], (), [rM])
                MS = [mk["ls4"], mk["us4"]]
                MI = [mk["li4"], mk["ui4"]]
                MP = [mk["us4"], mk["ls4"]]
                lnw = sbt(st, "b2_lnw", [128, 4], F32)
                lnb = sbt(st, "b2_lnb", [128, 4], F32)
                ld(lnw[:], W["rw_lnx_w"][l].rearrange("(c p) -> p c", p=128), (), [rM], nonc=True)
                ld(lnb[:], W["rw_lnx_b"][l].rearrange("(c p) -> p c", p=128), (), [rM], nonc=True)
                names = ("at", "bt", "bb", "rt", "kt", "kb", "v")
                IN = {nm: [sbt(st, "b2_i_%s%d" % (nm, i), [64, 8, 512], BF16) for i in range(2)] for nm in names}
                rIN = [Res(), Res()]
                WT = [sbt(st, "b2_wt%d" % i, [64, 8, 8], F32) for i in range(2)]
                AM = sbt(st, "b2_am", [128, 4, 8, 128], BF16)
                rAM = [Res(), Res()]
                PPp = [sbt(st, "b2_ppp%d" % i, [128, 8, 128], BF16) for i in range(2)]
                PPt = [sbt(st, "b2_ppt%d" % i, [128, 8, 128], BF16) for i in range(2)]
                rPP = [[Res(), Res()], [Res(), Res()]]
                rPT = [[Res(), Res()], [Res(), Res()]]
                Xc = [sbt(st, "b2_x%d" % i, [128, 8, 128], BF16) for i in range(2)]
                rXc = [[Res(), Res()], [Res(), Res()]]
                TMx = sbt(st, "b2_tm", [128, 9 * 192], BF16)
                rTM = [Res(), Res()]
                memset("dve", TMx[:], 0.0, rTM)
                TMf = TMx[:, :]
                TM = TMx[:, 0:8 * 192].rearrange("p (h q) -> p h q", h=8)
                RH = sbt(st, "b2_rh", [64, 8, 2, 128], BF16)
                rRH = Res()
                MTt = sbt(st, "b2_mt", [64, 8, 2, 128], BF16)
                rMTt = Res()
                MTf = sbt(st, "b2_mtf", [64, 8, 2, 64], F32)
                rMTf = Res()
                DW = sbt(st, "b2_dw", [64, 8, 2, 64], F32)
                rDW = Res()
                N0 = sbt(st, "b2_n0", [64, 8, 2, 64], F32)
                rN0 = Res()
                Sb = sbt(st, "b2_sb", [64, 8, 64], BF16)
                rSb = Res()
                YF = [sbt(st, "b2_yf%d" % i, [128, 512], F32) for i in range(2)]
                rYF = [Res(), Res()]
                YL = [sbt(st, "b2_yl%d" % i, [128, 512], F32) for i in range(2)]
                rYL = [Res(), Res()]
                YC = sbt(st, "b2_yc", [128, 512], F32)
                rYC = Res()
                YS = sbt(st, "b2_ys", [128, 512], F32)
                rYS = Res()
                YN = sbt(st, "b2_yn", [128, 512], BF16)
                rYN = Res()
                st8 = sbt(st, "b2_st8", [128, 8], F32)
                rst8 = Res()
                st8b = sbt(st, "b2_st8b", [128, 8], F32)
                rst8b = Res()
                BONt = [sbt(st, "b2_bon%d" % i, [128, 4, 128], F32) for i in range(2)]
                Gt = [sbt(st, "b2_g%d" % i, [128, 4, 128], BF16) for i in range(2)]
                rBG = [Res(), Res()]
                OT = sbt(st, "b2_ot", [128, 4, 128], F32)
                rOT = Res()
                OR = [sbt(st, "b2_or%d" % i, [128, 4, 128], BF16) for i in range(2)]
                rOR = [Res(), Res()]
                memset("dve", RH[:], 0.0, [rRH])
                memset("dve", MTt[:], 0.0, [rMTt])

                def hview(ap):
                    return ap.rearrange("(h j) t -> j h t", j=64)

                def v4(ap):
                    return ap.rearrange("p (h t) -> p h t", h=4)

                for d in range(2):
                    memset("dve", Sb[:], 0.0, [rSb])
                    order = list(range(NT)) if d == 0 else list(range(NT - 1, -1, -1))

                    def loads(idx, d=d, order=order):
                        s = order[idx]
                        b = idx % 2
                        for nm in names:
                            srcap = rws["v"] if nm == "v" else rws[nm, d]
                            rk = rscr("v") if nm == "v" else rscr((nm, d))
                            ld(IN[nm][b][:], hview(srcap)[:, :, s * 512:(s + 1) * 512], [rk], [rIN[b]])
                        ld(WT[b][:], hview(rws["wtot", d])[:, :, s * 8:(s + 1) * 8], [rscr(("wtot", d))], [rIN[b]], nonc=True)

                    loads(0)
                    tcount = 0
                    for idx, s in enumerate(order):
                        b = idx % 2
                        if idx + 1 < NT:
                            loads(idx + 1)
                        rI = rIN[b]
                        if d == 0 and s > 0:
                            ts("dve", Sb[:], Sb[:], bm[0:64, s - 1:s], None, ALU.mult, None, [rSb, R_c], [rSb])
                        if d == 1 and s < NT - 1:
                            ts("dve", Sb[:], Sb[:], bm[0:64, s:s + 1], None, ALU.mult, None, [rSb, R_c], [rSb])
                        tiles = list(range(4)) if d == 0 else [3, 2, 1, 0]
                        for tl in tiles:
                            tc_ = slice(tl * 128, (tl + 1) * 128)
                            gt0 = s * 512 + tl * 128
                            yb = tcount % 2
                            tcount += 1
                            if d == 1:
                                ld(YL[yb][:], yfw[gt0:gt0 + 128, :], [rscr("yfw")], [rYL[yb]])
                                ld(BONt[yb][:], rws["bonus"].rearrange("(c p) t -> p c t", p=128)[:, :, gt0:gt0 + 128],
                                   [rscr("bonus")], [rBG[yb]])
                                ld(Gt[yb][:], rws["g"].rearrange("(c p) t -> p c t", p=128)[:, :, gt0:gt0 + 128], [rscr("g")],
                                   [rBG[yb]])

                            def I(nm, h):
                                return IN[nm][b][:, h, tc_]

                            typs = (("bt", "at", MS), ("kt", "at", MS), ("bt", "rt", MI), ("kt", "rt", MI))
                            for ty, (ln, rn, msk) in enumerate(typs):
                                for hg in range(2):
                                    bank = next_ps()
                                    for j in range(4):
                                        h = hg * 4 + j
                                        mm(psum[bank][:, j * 128:(j + 1) * 128], I(ln, h), I(rn, h), True, True, [rI], [R_ps[bank]])
                                    tt("dve", AM[:, ty, hg * 4:(hg + 1) * 4, :], v4(psum[bank][:, :]), v4(msk[d][:]), ALU.mult,
                                       [R_ps[bank], rM], [rAM[hg]])
                            for hg in range(2):
                                bank = next_ps()
                                for j in range(4):
                                    h = hg * 4 + j
                                    mm(psum[bank][:, j * 128:(j + 1) * 128], I("at", h), I("bt", h), True, True, [rI], [R_ps[bank]])
                                tt("dve", PPp[0][:, hg * 4:(hg + 1) * 4, :], v4(psum[bank][:, :]), v4(MP[d][:]), ALU.mult,
                                   [R_ps[bank], rM], [rPP[0][hg]])
                            for hg in range(2):
                                for j in range(4):
                                    h = hg * 4 + j
                                    for q, nm in enumerate(("v", "bb", "kb", "at")):
                                        tr(psT[:, j * 256 + q * 64:j * 256 + (q + 1) * 64], I(nm, h), ident_b[0:64, 0:64],
                                           [rI, R_c], [R_psT])
                                pv4 = psT[:, :].rearrange("p (h q) -> p h q", h=4)
                                cp("act", TM[:, hg * 4:(hg + 1) * 4, :], pv4[:, :, 0:192], [R_psT], [rTM[hg]])
                                cp("act", Xc[0][:, hg * 4:(hg + 1) * 4, 0:64], pv4[:, :, 192:256], [R_psT], [rXc[0][hg]])
                            for hg in range(2):
                                bank = next_ps()
                                for j in range(4):
                                    h = hg * 4 + j
                                    mm(psum[bank][:, j * 64:(j + 1) * 64], AM[:, 1, h, :], TM[:, h, 0:64], True, True,
                                       [rAM[hg], rTM[hg]], [R_ps[bank]])
                                cp("dve", Xc[0][:, hg * 4:(hg + 1) * 4, 64:128],
                                   psum[bank][:, 0:256].rearrange("p (h i) -> p h i", h=4), [R_ps[bank]], [rXc[0][hg]])
                            if LVL < 2:
                                continue
                            for hg in range(2):
                                for j in range(4):
                                    h = hg * 4 + j
                                    mm(psum[hg][:, j * 128:(j + 1) * 128], ident_b[:], Xc[0][:, h, :], j == 0, False,
                                       [R_c, rXc[0][hg]], [R_ps[hg]])
                            for k in range(6):
                                cur, nxt = k % 2, (k + 1) % 2
                                for hg in range(2):
                                    rP = rAM[hg] if k == 0 else rPT[cur][hg]
                                    bx = hg
                                    for j in range(4):
                                        h = hg * 4 + j
                                        Pt_h = AM[:, 0, h, :] if k == 0 else PPt[cur][:, h, :]
                                        mm(psum[bx][:, j * 128:(j + 1) * 128], Pt_h, Xc[cur][:, h, :], False, k == 5,
                                           [rP, rXc[cur][hg]], [R_ps[bx]])
                                    cp("act", Xc[nxt][:, hg * 4:(hg + 1) * 4, :], v4(psum[bx][:, :]), [R_ps[bx]], [rXc[nxt][hg]])
                                    if k < 5:
                                        bp_, bt_ = 2 + hg, 5 + hg
                                        rPp = rPP[cur][hg]
                                        for j in range(4):
                                            h = hg * 4 + j
                                            Pt_h = AM[:, 0, h, :] if k == 0 else PPt[cur][:, h, :]
                                            Pp_h = PPp[cur][:, h, :]
                                            mm(psum[bp_][:, j * 128:(j + 1) * 128], Pt_h, Pp_h, True, True, [rP, rPp], [R_ps[bp_]])
                                            mm(psum[bt_][:, j * 128:(j + 1) * 128], Pp_h, Pt_h, True, True, [rP, rPp], [R_ps[bt_]])
                                        cp("dve", PPp[nxt][:, hg * 4:(hg + 1) * 4, :], v4(psum[bp_][:, :]), [R_ps[bp_]], [rPP[nxt][hg]])
                                        cp("act", PPt[nxt][:, hg * 4:(hg + 1) * 4, :], v4(psum[bt_][:, :]), [R_ps[bt_]], [rPT[nxt][hg]])
                            XF = Xc[0]
                            rXF = rXc[0]
                            if LVL < 3:
                                continue
                            for h in range(8):
                                hg = h // 4
                                ys = slice(h * 64, (h + 1) * 64)
                                mm(psum[4][:, ys], AM[:, 2, h, :], XF[:, h, 64:128], h == 0, False, [rAM[hg], rXF[hg]], [R_ps[4]])
                                mm(psum[4][:, ys], AM[:, 3, h, :], TM[:, h, 0:64], False, False, [rAM[hg], rTM[hg]], [R_ps[4]])
                            for hg in range(2):
                                bank = hg
                                for j in range(4):
                                    h = hg * 4 + j
                                    mm(psum[bank][0:64, j * 128:(j + 1) * 128], XF[:, h, 0:64], AM[:, 2, h, :], True, True,
                                       [rXF[hg], rAM[hg]], [R_ps[bank]])
                                for c in range(2):
                                    cs = slice(c * 64, (c + 1) * 64)
                                    tt("dve", RH[:, hg * 4:(hg + 1) * 4, c, cs], v4(psum[bank][0:64, :])[:, :, cs],
                                       IN["rt"][b][:, hg * 4:(hg + 1) * 4, tl * 128 + c * 64:tl * 128 + (c + 1) * 64], ALU.add,
                                       [R_ps[bank], rI], [rRH])
                            if LVL < 4:
                                continue
                            tt("pool", DW[:], ident_f[0:64, 0:64].unsqueeze(1).unsqueeze(1).to_broadcast([64, 8, 2, 64]),
                               WT[b][:, :, tl * 2:tl * 2 + 2].unsqueeze(3).to_broadcast([64, 8, 2, 64]), ALU.mult, [rI, R_c], [rDW])
                            for hg in range(2):
                                bm_, bn_ = 2 + hg, 5 + hg
                                for j in range(4):
                                    h = hg * 4 + j
                                    for c in range(2):
                                        ps_ = slice(0, 64) if c == 0 else slice(0, 128)
                                        col = (j * 2 + c) * 64
                                        mm(psum[bm_][:, col:col + 64], XF[ps_, h, 0:128], TM[ps_, h, 64:128], True, True,
                                           [rXF[hg], rTM[hg]], [R_ps[bm_]])
                                        mm(psum[bn_][:, col:col + 64], TM[ps_, h, 64:192], XF[ps_, h, 64:128], True, False,
                                           [rXF[hg], rTM[hg]], [R_ps[bn_]])
                                        mm(psum[bn_][:, col:col + 64], TMf[ps_, h * 192 + 128:h * 192 + 256], TM[ps_, h, 0:64], False, True,
                                           [rTM[hg]], [R_ps[bn_]])
                                hs = slice(hg * 4, (hg + 1) * 4)
                                cp("dve", MTf[:, hs, :, :], psum[bm_][0:64, :].rearrange("p (h c i) -> p h c i", h=4, c=2),
                                   [R_ps[bm_]], [rMTf])
                                tt("pool", MTf[:, hs, 1, :], MTf[:, hs, 1, :], MTf[:, hs, 0, :], ALU.subtract, [rMTf], [rMTf])
                                tt("pool", MTt[:, hs, :, 0:64], MTf[:, hs, :, :], DW[:, hs, :, :], ALU.add, [rMTf, rDW], [rMTt])
                                cp("act", N0[:, hs, :, :], psum[bn_][0:64, :].rearrange("p (h c i) -> p h c i", h=4, c=2),
                                   [R_ps[bn_]], [rN0])
                                tt("pool", N0[:, hs, 1, :], N0[:, hs, 1, :], N0[:, hs, 0, :], ALU.subtract, [rN0], [rN0])
                            if LVL < 5:
                                continue
                            for ci, c in enumerate((0, 1) if d == 0 else (1, 0)):
                                for h in range(8):
                                    ys = slice(h * 64, (h + 1) * 64)
                                    mm(psum[4][:, ys], RH[:, h, c, :], Sb[:, h, :], False, ci == 1, [rRH, rSb], [R_ps[4]])
                                bank = ci
                                for h in range(8):
                                    mm(psum[bank][:, h * 64:(h + 1) * 64], MTt[:, h, c, :], Sb[:, h, :], True, True,
                                       [rMTt, rSb], [R_ps[bank]])
                                tt("dve", Sb[:], psum[bank][0:64, :].rearrange("p (h i) -> p h i", h=8), N0[:, :, c, :], ALU.add,
                                   [R_ps[bank], rN0], [rSb])
                            if LVL < 6:
                                continue
                            if d == 0:
                                cp("act", YF[yb][:], psum[4][:, :], [R_ps[4]], [rYF[yb]])
                                stor(yfw[gt0:gt0 + 128, :], YF[yb][:], [rYF[yb]], [rscr("yfw")], rYF[yb])
                            else:
                                tt("dve", YF[yb][:], psum[4][:, :], YL[yb][:], ALU.add, [R_ps[4], rYL[yb]], [rYF[yb]])
                                Y3 = YF[yb][:].rearrange("p (h i) -> p h i", h=8)
                                S.op("dve", lambda e, Y3=Y3: e.tensor_reduce(out=st8[:], in_=Y3, axis=AX.X, op=ALU.add),
                                     [rYF[yb]], [rst8])
                                ts("dve", st8[:], st8[:], -1.0 / 64, None, ALU.mult, None, [rst8], [rst8])
                                tt("pool", YC[:].rearrange("p (h i) -> p h i", h=8), Y3,
                                   st8[:].unsqueeze(2).to_broadcast([128, 8, 64]), ALU.add, [rYF[yb], rst8], [rYC])
                                act(YS[:], YC[:], AF.Square, [rYC], [rYS])
                                S.op("dve", lambda e: e.tensor_reduce(out=st8b[:], in_=YS[:].rearrange("p (h i) -> p h i", h=8),
                                                                      axis=AX.X, op=ALU.add), [rYS], [rst8b])
                                ts("dve", st8b[:], st8b[:], 1.0 / 64, 64e-5, ALU.mult, ALU.add, [rst8b], [rst8b])
                                ts("dve", st8b[:], st8b[:], -0.5, None, ALU.pow, None, [rst8b], [rst8b])
                                tt("pool", YN[:].rearrange("p (h i) -> p h i", h=8), YC[:].rearrange("p (h i) -> p h i", h=8),
                                   st8b[:].unsqueeze(2).to_broadcast([128, 8, 64]), ALU.mult, [rYC, rst8b], [rYN])
                                for fc in range(4):
                                    tr(psT[:, fc * 128:(fc + 1) * 128], YN[:, fc * 128:(fc + 1) * 128], ident_b[:],
                                       [rYN, R_c], [R_psT])
                                pv = psT[:, 0:512].rearrange("p (c t) -> p c t", c=4)
                                tt("dve", OT[:], pv, lnw[:].unsqueeze(2).to_broadcast([128, 4, 128]), ALU.mult, [R_psT, rM], [rOT])
                                tt("pool", OT[:], OT[:], lnb[:].unsqueeze(2).to_broadcast([128, 4, 128]), ALU.add, [rOT, rM], [rOT])
                                tt("dve", OT[:], OT[:], BONt[yb][:], ALU.add, [rOT, rBG[yb]], [rOT])
                                tt("pool", OR[yb][:], OT[:], Gt[yb][:], ALU.mult, [rOT, rBG[yb]], [rOR[yb]])
                                stor(orwT.rearrange("(c p) t -> p c t", p=128)[:, :, gt0:gt0 + 128], OR[yb][:], [rOR[yb]],
                                     [rscr("orwT")], rOR[yb])
            S.barrier()

        if phases is None:
            phases = ["p0"] + sum([["a%d" % l, "b1%d" % l, "b2%d" % l, "c1%d" % l, "c2%d" % l] for l in range(depth)], []) + ["e"]
        for ph in phases:
            if ph == "p0":
                phase_p0()
            elif ph == "e":
                phase_e(0)
            elif ph[0] == "a":
                phase_a(int(ph[1:]), 0)
            elif ph[:2] == "b1":
                phase_b1(int(ph[2:]))
            elif ph[:2] == "b2":
                phase_b2(int(ph[2:]))
            elif ph[:2] == "c1":
                phase_c1(int(ph[2:]), 0, 1)
            elif ph[:2] == "c2":
                phase_c2(int(ph[2:]), 1, 0)
        S.emit()
        nc._n_inst = S.ninst
    return nc


def core_inputs(x_seqs, mem_seqs, T, RS, typ, weights, depth):
    NT = T // 512
    SLOT_ST = max(NT // 4, 1)
    NSLOT = NT // SLOT_ST
    ROWS = T // 64
    xin = np.zeros((T, D), np.float32)
    memv = np.zeros((NSLOT, 256, D), np.float32)
    if typ == "P":
        xin[:] = x_seqs[0]
        for sl in range(NSLOT):
            memv[sl] = mem_seqs[0]
        starts = {0}
    else:
        L = RS * 64
        for i, xs in enumerate(x_seqs):
            xin[i * L:(i + 1) * L] = xs
            sl0 = (i * L) // (SLOT_ST * 512)
            memv[sl0] = mem_seqs[i]
        starts = set(range(0, T, L))
    bmv = np.ones((128, max(NT - 1, 1)), np.float32)
    for j in range(NT - 1):
        if (j + 1) * 512 in starts:
            bmv[:, j] = 0.0
    bm2 = np.ones((128, max(T // 256 - 1, 1)), np.float32)
    for j in range(T // 256 - 1):
        if (j + 1) * 256 in starts:
            bm2[:, j] = 0.0
    m = {"xin": xin, "mem": memv, "bm": bmv, "bm2": bm2, "rowbias": na_rowbias(ROWS, RS, typ)}
    m["btab"] = np.stack([na_btab(weights["na_rpb"][l]).reshape(64, -1) for l in range(depth)])
    for k, v in make_consts().items():
        m["c_" + k] = v
    for k, v in weights.items():
        if k == "na_rpb":
            continue
        if k == "rw_r_k":
            v = v.reshape(v.shape[0], 512)
        if k == "final_norm":
            v = v.reshape(1, D)
        m[k] = np.ascontiguousarray(v, dtype=np.float32)
    return m


_CACHE = {}


def kernel(**inputs):
    T, RS, depth = 8192, 32, 2
    xp = np.asarray(inputs["x_prompt"], np.float32)
    xs = np.asarray(inputs["x_sample"], np.float32)
    mp = np.asarray(inputs["mem_prompt"], np.float32)
    ms = np.asarray(inputs["mem_sample"], np.float32)
    weights = {k: np.asarray(v, np.float32) for k, v in inputs.items()
               if k not in ("x_prompt", "x_sample", "mem_prompt", "mem_sample")}
    in_maps = []
    for c in range(4):
        in_maps.append(core_inputs([xp[c]], [mp[c]], T, RS, "P", weights, depth))
    for c in range(4):
        sq = [2 * c, 2 * c + 1, 2 * c, 2 * c + 1]
        in_maps.append(core_inputs([xs[i] for i in sq], [ms[i] for i in sq], T, RS, "S", weights, depth))
    if "nc" not in _CACHE:
        _CACHE["nc"] = build(T, RS, depth)
    res = run_bass_kernel_spmd(_CACHE["nc"], in_maps, core_ids=list(range(8)))
    yp = np.stack([res.results[c]["yout"] for c in range(4)]).astype(np.float32)
    ys = np.zeros_like(xs)
    for c in range(4):
        y = res.results[4 + c]["yout"]
        ys[2 * c] = y[0:2048]
        ys[2 * c + 1] = y[2048:4096]
    return (yp, ys)
```

```python
from contextlib import ExitStack
import os
LVL = int(os.environ.get('B2DBG', '9'))
B2X = int(os.environ.get('B2X', '0'))
import numpy as np
import concourse.bass as bass
import concourse.mybir as mybir
from concourse.bass_utils import run_bass_kernel_spmd

F32 = mybir.dt.float32
BF16 = mybir.dt.bfloat16
AF = mybir.ActivationFunctionType
ALU = mybir.AluOpType
AX = mybir.AxisListType

D = 1024
NEG = -30000.0
ENGS = ("pe", "act", "dve", "pool", "sp")


class Res:
    __slots__ = ("name", "w", "r", "dsem", "multi")

    def __init__(self, name="", multi=False):
        self.name = name
        self.w = {}
        self.r = {}
        self.dsem = None
        self.multi = multi


class Sched:
    def __init__(self, nc, stack, n_dma_sems=64):
        self.nc = nc
        self.q = {e: [] for e in ENGS}
        self.cnt = {e: 0 for e in ENGS}
        self.sem = {e: stack.enter_context(nc.semaphore("s_" + e)) for e in ENGS}
        self.dma_sems = [stack.enter_context(nc.semaphore("d%d" % i)) for i in range(n_dma_sems)]
        self.dma_cnt = [0] * n_dma_sems
        self.dma_next = 0
        self.seen = {e: {} for e in ENGS}
        self.ninst = 0

    def _semobj(self, key):
        return self.sem[key] if isinstance(key, str) else self.dma_sems[key]

    def _deps(self, eng, reads, writes):
        toks = {}
        for r in reads:
            for k, v in r.w.items():
                if toks.get(k, 0) < v:
                    toks[k] = v
        for w in writes:
            for k, v in w.w.items():
                if toks.get(k, 0) < v:
                    toks[k] = v
            for k, v in w.r.items():
                if toks.get(k, 0) < v:
                    toks[k] = v
        waits = []
        seen = self.seen[eng]
        for k, v in toks.items():
            if k == eng and eng == "pe":
                continue
            if seen.get(k, 0) >= v:
                continue
            seen[k] = v
            waits.append((k, v))
        return waits

    def _mark(self, tok, reads, writes):
        k, v = tok
        for r in reads:
            if r.r.get(k, 0) < v:
                r.r[k] = v
        for w in writes:
            w.w[k] = v
            if not w.multi:
                w.r = {}

    def op(self, eng, fn, reads=(), writes=()):
        waits = self._deps(eng, reads, writes)
        self.cnt[eng] += 1
        self._mark((eng, self.cnt[eng]), reads, writes)
        self.q[eng].append((waits, fn, (eng, 1)))
        self.ninst += 1

    def dma(self, eng, fn, reads=(), writes=(), sem_res=None):
        waits = self._deps(eng, reads, writes)
        if sem_res is None:
            sem_res = writes[0]
        if sem_res.dsem is None:
            sem_res.dsem = self.dma_next % len(self.dma_sems)
            self.dma_next += 1
        k = sem_res.dsem
        self.dma_cnt[k] += 16
        self._mark((k, self.dma_cnt[k]), reads, writes)
        self.q[eng].append((waits, fn, (k, 16)))
        self.ninst += 1

    def barrier(self):
        tot = {e: self.cnt[e] for e in ENGS if self.cnt[e]}
        for k, c in enumerate(self.dma_cnt):
            if c:
                tot[k] = c
        for e in ENGS:
            waits = []
            for k, v in tot.items():
                if k == e:
                    continue
                if self.seen[e].get(k, 0) < v:
                    self.seen[e][k] = v
                    waits.append((k, v))
            if waits:
                self.q[e].append((waits, None, None))

    def emit(self):
        nc = self.nc
        fin = {e: self.cnt[e] for e in ENGS if self.cnt[e]}
        for k, c in enumerate(self.dma_cnt):
            if c:
                fin[k] = c
        handles = {"pe": "tensor", "act": "scalar", "dve": "vector", "pool": "gpsimd", "sp": "sync"}
        with nc.Block() as block:
            for e in ENGS:
                def body(engine, ops=self.q[e], is_last=(e == "sp")):
                    for waits, fn, inc in ops:
                        for wi, (wk, wv) in enumerate(waits):
                            engine.wait_ge(self._semobj(wk), wv)
                            if wi < len(waits) - 1 or fn is None:
                                engine.nop(nofuse=True)
                        if fn is not None:
                            fn(engine).then_inc(self._semobj(inc[0]), inc[1])
                    if is_last:
                        for k, v in fin.items():
                            engine.wait_ge(self._semobj(k), v)
                getattr(block, handles[e])(body)


def na_bands(ROWS, RS):
    bands = []
    for i in range(ROWS):
        p0 = min(max(i - 4, 0), ROWS - 8)
        b, il = divmod(i, RS)
        s0 = b * RS + min(max(il - 4, 0), RS - 8)
        lo, hi = min(p0, s0), max(p0, s0) + 8
        n = (hi - lo + 1) // 2
        if lo + 2 * n > ROWS:
            lo = ROWS - 2 * n
        bands.append([lo + 2 * c for c in range(n)])
    return bands


def na_rowbias(ROWS, RS, typ):
    bands = na_bands(ROWS, RS)
    cols = []
    for i, band in enumerate(bands):
        if typ == "P":
            w0 = min(max(i - 4, 0), ROWS - 8)
        else:
            b, il = divmod(i, RS)
            w0 = b * RS + min(max(il - 4, 0), RS - 8)
        for r in band:
            col = np.full(128, NEG, np.float32)
            for j in range(2):
                if w0 <= r + j < w0 + 8:
                    col[j * 64:(j + 1) * 64] = 0.0
            cols.append(col)
    return np.stack(cols, axis=1)


def na_btab(rpb):
    H = rpb.shape[0]
    out = np.zeros((64, H, 17, 64), np.float32)
    qc = np.arange(64)
    c0 = np.clip(qc - 8, 0, 48)
    for off in range(-7, 8):
        blk = np.full((64, H, 64), NEG, np.float32)
        for q in range(64):
            ks = np.arange(c0[q], c0[q] + 16)
            blk[q][:, ks] = rpb[:, off + 7, ks - q + 15]
        out[:, :, off + 8, :] = blk
    return out


def make_consts():
    c = {}
    c["ident"] = np.eye(128, dtype=np.float32)
    bd = np.zeros((128, 128), np.float32)
    bd[:64, :64] = 1.0
    bd[64:, 64:] = 1.0
    c["bd64"] = bd
    sm = np.ones((128, 512), np.float32)
    sm[:, ::64] = 0.0
    c["scanmask"] = sm
    s = np.arange(128)[:, None]
    t = np.arange(128)[None, :]
    same = (s // 64) == (t // 64)
    LS = (same & (s < t)).astype(np.float32)
    LI = (same & (s <= t)).astype(np.float32)
    US = (same & (s > t)).astype(np.float32)
    UI = (same & (s >= t)).astype(np.float32)
    c["ls4"] = np.tile(LS, (1, 4))
    c["li4"] = np.tile(LI, (1, 4))
    c["us4"] = np.tile(US, (1, 4))
    c["ui4"] = np.tile(UI, (1, 4))
    return c


CONST_ORDER = ("ident", "bd64", "scanmask", "ls4", "li4", "us4", "ui4")


def build(T, RS, depth=2, debug_outs=(), phases=None):
    NT = T // 512
    NTL = T // 128
    ROWS = T // 64
    NCH = T // 64
    SLOT_ST = max(NT // 4, 1)
    NSLOT = NT // SLOT_ST
    bands = na_bands(ROWS, RS)
    nslots_na = sum(len(b) for b in bands)
    NH2 = 2 * (T // 256 - 1)
    NH5 = 2 * (NT - 1)

    nc = bass.Bass("TRN2", target_bir_lowering=False)

    def din(name, shape, dt=F32):
        return nc.dram_tensor(name, list(shape), dt, kind="ExternalInput").ap()

    def dscr(name, shape, dt):
        kind = "ExternalOutput" if name in debug_outs else "Internal"
        return nc.dram_tensor(name, list(shape), dt, kind=kind).ap()

    xin = din("xin", [T, D])
    mem = din("mem", [NSLOT, 256, D])
    bm_in = din("bm", [128, max(NT - 1, 1)])
    bm2_in = din("bm2", [128, max(T // 256 - 1, 1)])
    rowbias_in = din("rowbias", [128, nslots_na])
    btab_in = din("btab", [depth, 64, 8 * 17 * 64])
    cst_in = {k: din("c_" + k, v.shape) for k, v in make_consts().items()}
    W = {}
    for name, shape in (("attn_norm", [depth, D]), ("w_in", [depth, D, 7072]), ("rw_conv", [depth, 3, 1952]),
                        ("rw_decay0", [depth, 2, 512]), ("rw_decay2", [depth, 2, 64, 512]), ("rw_a0", [depth, 2, 512]),
                        ("rw_a2", [depth, 2, 64, 512]), ("rw_g2", [depth, 160, 512]), ("rw_k_k", [depth, 512]),
                        ("rw_k_a", [depth, 512]), ("rw_r_k", [depth, 512]), ("rw_lnx_w", [depth, 512]),
                        ("rw_lnx_b", [depth, 512]), ("mem_norm", [depth, D]), ("w_mem_kv", [depth, D, 1024]),
                        ("w_branch", [depth, 3, 512, D]), ("w_out", [depth, D, D]), ("ffn_norm", [depth, D]),
                        ("w_up", [depth, D, 5632]), ("ffn_conv", [depth, 3, 5632]), ("ffn_conv_b", [depth, 5632]),
                        ("w_down", [depth, 2816, D]), ("final_norm", [1, D])):
        W[name] = din(name, shape)
    yout = nc.dram_tensor("yout", [T, D], F32, kind="ExternalOutput").ap()

    xT = [dscr("xT0", [D, T], F32), dscr("xT1", [D, T], F32)]
    qT = dscr("qT", [512, T], BF16)
    kT = dscr("kT", [512, T], BF16)
    vtm = dscr("vtm", [T, 512], BF16)
    omT = dscr("omT", [512, T], BF16)
    onaT = dscr("onaT", [512, T], BF16)
    orwT = dscr("orwT", [512, T], BF16)
    rws = {}
    for d in range(2):
        for nm in ("at", "bt", "bb", "rt", "kt", "kb"):
            rws[nm, d] = dscr("rw_%s%d" % (nm, d), [512, T], BF16)
        rws["wtot", d] = dscr("rw_wtot%d" % d, [512, NCH], F32)
    rws["v"] = dscr("rw_v", [512, T], BF16)
    rws["bonus"] = dscr("rw_bonus", [512, T], F32)
    rws["g"] = dscr("rw_g", [512, T], BF16)
    yfw = dscr("yfw", [T, 512], F32)

    R_xT = [Res("xT0", True), Res("xT1", True)]
    R_scr = {}

    def rscr(key):
        if key not in R_scr:
            R_scr[key] = Res(str(key), True)
        return R_scr[key]

    with ExitStack() as top:
        S = Sched(nc, top)
        psum = [top.enter_context(nc.psum_tensor("ps%d" % i, [128, 512], F32)) for i in range(7)]
        psT = top.enter_context(nc.psum_tensor("psT", [128, 1024], BF16))
        R_ps = [Res("ps%d" % i) for i in range(7)]
        R_psT = Res("psT")

        def mm(out, lhsT, rhs, start, stop, reads, writes):
            S.op("pe", lambda e: e.matmul(out, lhsT=lhsT, rhs=rhs, start=start, stop=stop), reads, writes)

        def tr(out, in_, ident, reads, writes):
            S.op("pe", lambda e: e.transpose(out, in_, ident), reads, writes)

        def act(out, in_, func, reads, writes, bias=None, scale=None, accum_out=None):
            kw = {}
            if bias is not None:
                kw["bias"] = bias
            if scale is not None:
                kw["scale"] = scale
            if accum_out is not None:
                kw["accum_out"] = accum_out
            S.op("act", lambda e: e.activation(out=out, in_=in_, func=func, **kw), reads, writes)

        def is_ps(ap):
            return hasattr(ap, "space") and "PSUM" in str(ap.space)

        def tt(eng, out, in0, in1, op, reads, writes):
            if eng == "pool" and (is_ps(out) or is_ps(in0) or is_ps(in1)):
                eng = "dve"
            S.op(eng, lambda e: e.tensor_tensor(out=out, in0=in0, in1=in1, op=op), reads, writes)

        def ts(eng, out, in0, s1, s2, op0, op1, reads, writes):
            if eng == "pool" and (is_ps(out) or is_ps(in0)):
                eng = "dve"
            if op1 is None and op0 == ALU.pow:
                assert s1 == -0.5
                S.op("act", lambda e: e.activation(out=out, in_=in0, func=AF.Ln), reads, writes)
                S.op("act", lambda e: e.activation(out=out, in_=out, func=AF.Exp, scale=-0.5), list(reads) + list(writes), writes)
            elif op1 is None:
                S.op(eng, lambda e: e.tensor_scalar(out=out, in0=in0, scalar1=s1, scalar2=None, op0=op0), reads, writes)
            else:
                S.op(eng, lambda e: e.tensor_scalar(out=out, in0=in0, scalar1=s1, scalar2=s2, op0=op0, op1=op1), reads, writes)

        def stt(eng, out, in0, scalar, in1, op0, op1, reads, writes):
            eng = "dve"
            S.op(eng, lambda e: e.scalar_tensor_tensor(out=out, in0=in0, scalar=scalar, in1=in1, op0=op0, op1=op1), reads, writes)

        def cp(eng, out, in_, reads, writes):
            if eng == "act":
                S.op("act", lambda e: e.copy(out=out, in_=in_), reads, writes)
            else:
                S.op(eng, lambda e: e.tensor_copy(out=out, in_=in_), reads, writes)

        def memset(eng, ap, val, writes):
            S.op(eng, lambda e: e.memset(ap, val), (), writes)

        def ld(out, in_, reads, writes, q="sp", nonc=False):
            if nonc:
                def f(e):
                    with nc.allow_non_contiguous_dma(reason="small strided load"):
                        return e.dma_start(out=out, in_=in_)
                S.dma(q, f, reads, writes)
            else:
                S.dma(q, lambda e: e.dma_start(out=out, in_=in_), reads, writes)

        def stor(out, in_, reads, writes, sem_res, q="sp"):
            S.dma(q, lambda e: e.dma_start(out=out, in_=in_), reads, writes, sem_res=sem_res)

        rr = [0]

        def next_ps():
            rr[0] = (rr[0] + 1) % 4
            return rr[0]

        ve = [0]

        def veng():
            ve[0] ^= 1
            return "dve" if ve[0] else "pool"

        uniq = [0]

        def sbt(stack, name, shape, dt):
            uniq[0] += 1
            return stack.enter_context(nc.sbuf_tensor("sb%d_%s" % (uniq[0], name), list(shape), dt))

        ident_f = sbt(top, "ident_f", [128, 128], F32)
        ident_b = sbt(top, "ident_b", [128, 128], BF16)
        ones_b = sbt(top, "ones_b", [128, 128], BF16)
        bd64_b = sbt(top, "bd64_b", [128, 128], BF16)
        bd64_f = sbt(top, "bd64_f", [128, 128], F32)
        R_c = Res("consts")
        ld(ident_f[:], cst_in["ident"][:, :], (), [R_c])
        ld(ident_b[:], cst_in["ident"][:, :], (), [R_c], q="pool")
        ld(bd64_b[:], cst_in["bd64"][:, :], (), [R_c], q="pool")
        ld(bd64_f[:], cst_in["bd64"][:, :], (), [R_c])
        memset("dve", ones_b[:], 1.0, [R_c])
        bm = sbt(top, "bm", [128, max(NT - 1, 1)], F32)
        bm2 = sbt(top, "bm2", [128, max(T // 256 - 1, 1)], F32)
        ld(bm[:], bm_in[:, :], (), [R_c])
        ld(bm2[:], bm2_in[:, :], (), [R_c])

        def rmsnorm_fm(X, rX, XN, rXN, gain, rg, n, SQ, rSQ, RSt, rRS, bank=4):
            act(SQ[:, :, 0:n], X[:, :, 0:n], AF.Square, [rX], [rSQ])
            for c in range(8):
                mm(psum[bank][:, 0:n], ones_b[:], SQ[:, c, 0:n], c == 0, c == 7, [rSQ, R_c], [R_ps[bank]])
            ts("dve", RSt[:, 0:n], psum[bank][:, 0:n], 1.0 / D, 1e-6, ALU.mult, ALU.add, [R_ps[bank]], [rRS])
            ts("dve", RSt[:, 0:n], RSt[:, 0:n], -0.5, None, ALU.pow, None, [rRS], [rRS])
            for c in range(8):
                stt(veng(), XN[:, c, 0:n], X[:, c, 0:n], gain[:, c:c + 1], RSt[:, 0:n], ALU.mult, ALU.mult,
                    [rX, rRS, rg], [rXN])

        def load_vec_fm(stack, name, src, nchunk, q="sp"):
            t = sbt(stack, name, [128, nchunk], F32)
            r = Res(name)
            ld(t[:], src.rearrange("(c p) -> p c", p=128), (), [r], nonc=True)
            return t, r

        def load_w(stack, name, src, kc, ncols, col0=0, eng_q="pool"):
            t = sbt(stack, name, [128, kc, ncols], BF16)
            r = Res(name)
            for c in range(kc):
                ld(t[:, c, :], src[c * 128:(c + 1) * 128, col0:col0 + ncols], (), [r], q=eng_q)
            return t, r

        def phase_p0():
            with ExitStack() as st:
                XI = [sbt(st, "p0_xi%d" % i, [128, D], F32) for i in range(2)]
                rXI = [Res("xi0"), Res("xi1")]
                XO = [sbt(st, "p0_xo%d" % i, [128, 8, 128], F32) for i in range(2)]
                rXO = [Res("xo0"), Res("xo1")]
                for i in range(NTL):
                    b = i % 2
                    ld(XI[b][:], xin[i * 128:(i + 1) * 128, :], (), [rXI[b]])
                    for half in range(2):
                        bank = next_ps()
                        for c4 in range(4):
                            c = half * 4 + c4
                            tr(psum[bank][:, c4 * 128:(c4 + 1) * 128], XI[b][:, c * 128:(c + 1) * 128], ident_f[:],
                               [rXI[b], R_c], [R_ps[bank]])
                        cp("act" if half else "dve", XO[b][:, half * 4:(half + 1) * 4, :],
                           psum[bank][:, :].rearrange("p (c t) -> p c t", c=4), [R_ps[bank]], [rXO[b]])
                    stor(xT[0].rearrange("(c p) t -> p c t", p=128)[:, :, i * 128:(i + 1) * 128], XO[b][:],
                         [rXO[b]], [R_xT[0]], rXO[b])
            S.barrier()

        def phase_e(src):
            with ExitStack() as st:
                gain, rg = load_vec_fm(st, "e_gain", W["final_norm"][0], 8)
                X = [sbt(st, "e_x%d" % i, [128, 8, 512], F32) for i in range(2)]
                rX = [Res(), Res()]
                XN = sbt(st, "e_xn", [128, 8, 512], F32)
                rXN = Res()
                SQ = sbt(st, "e_sq", [128, 8, 512], BF16)
                rSQ = Res()
                RSt = sbt(st, "e_rs", [128, 512], F32)
                rRS = Res()
                YO = [sbt(st, "e_yo%d" % i, [128, D], F32) for i in range(2)]
                rYO = [Res(), Res()]
                xv = xT[src].rearrange("(c p) t -> p c t", p=128)
                ld(X[0][:], xv[:, :, 0:512], [R_xT[src]], [rX[0]])
                k = 0
                for s in range(NT):
                    b = s % 2
                    if s + 1 < NT:
                        ld(X[1 - b][:], xv[:, :, (s + 1) * 512:(s + 2) * 512], [R_xT[src]], [rX[1 - b]])
                    rmsnorm_fm(X[b], rX[b], XN, rXN, gain, rg, 512, SQ, rSQ, RSt, rRS)
                    for tl in range(4):
                        yb = k % 2
                        k += 1
                        for half in range(2):
                            bank = next_ps()
                            for c4 in range(4):
                                c = half * 4 + c4
                                tr(psum[bank][:, c4 * 128:(c4 + 1) * 128], XN[:, c, tl * 128:(tl + 1) * 128], ident_f[:],
                                   [rXN, R_c], [R_ps[bank]])
                            cp("act" if half else "dve", YO[yb][:, half * 512:(half + 1) * 512], psum[bank][:, :],
                               [R_ps[bank]], [rYO[yb]])
                        t0 = s * 512 + tl * 128
                        stor(yout[t0:t0 + 128, :], YO[yb][:], [rYO[yb]], [Res()], rYO[yb])
            S.barrier()

        def phase_c2(l, src, dst):
            TW = 256
            NTW = T // TW
            NW = TW + 2
            banks = (0, 1, 2, 3, 5, 6)
            bk = [0]

            def nb():
                bk[0] = (bk[0] + 1) % len(banks)
                return banks[bk[0]]

            with ExitStack() as st:
                Wu, rWu = load_w(st, "c2_wu", W["w_up"][l], 8, 5632)
                Wd, rWd = load_w(st, "c2_wd", W["w_down"][l], 22, D)
                gain, rg = load_vec_fm(st, "c2_gain", W["ffn_norm"][l], 8)
                cw = sbt(st, "c2_cw", [128, 3, 44], F32)
                rcw = Res()
                for k in range(3):
                    ld(cw[:, k, :], W["ffn_conv"][l, k].rearrange("(c p) -> p c", p=128), (), [rcw], nonc=True)
                cb, rcb = load_vec_fm(st, "c2_cb", W["ffn_conv_b"][l], 44)
                X = [sbt(st, "c2_x%d" % i, [128, 8, NW], F32) for i in range(2)]
                rX = [Res(), Res()]
                XN = sbt(st, "c2_xn", [128, 8, NW], BF16)
                rXN = Res()
                SQ = sbt(st, "c2_sq", [128, 8, NW], BF16)
                rSQ = Res()
                RSt = sbt(st, "c2_rs", [128, NW], F32)
                rRS = Res()
                G = sbt(st, "c2_g", [128, 22, TW], BF16)
                rG = Res()
                NB = 3
                CV = [sbt(st, "c2_cv%d" % i, [128, TW], F32) for i in range(NB)]
                rCV = [Res() for _ in range(NB)]
                CG = [sbt(st, "c2_cg%d" % i, [128, TW], F32) for i in range(NB)]
                rCG = [Res() for _ in range(NB)]
                SGt = [sbt(st, "c2_sg%d" % i, [128, TW], F32) for i in range(NB)]
                rSGt = [Res() for _ in range(NB)]
                xv = xT[src].rearrange("(c p) t -> p c t", p=128)
                xo = xT[dst].rearrange("(c p) t -> p c t", p=128)

                def load_x(s):
                    b = s % 2
                    t0 = s * TW
                    lo = max(t0 - 1, 0)
                    hi = min(t0 + TW + 1, T)
                    ld(X[b][:, :, lo - (t0 - 1):hi - (t0 - 1)], xv[:, :, lo:hi], [R_xT[src]], [rX[b]])
                    if s == 0:
                        memset("pool", X[b][:, :, 0:1], 0.0, [rX[b]])
                    if s == NTW - 1:
                        memset("pool", X[b][:, :, NW - 1:NW], 0.0, [rX[b]])

                load_x(0)
                for s in range(NTW):
                    b = s % 2
                    t0 = s * TW
                    if s + 1 < NTW:
                        load_x(s + 1)
                    rmsnorm_fm(X[b], rX[b], XN, rXN, gain, rg, NW, SQ, rSQ, RSt, rRS)
                    if s > 0:
                        ts("pool", XN[:, :, 0:1], XN[:, :, 0:1], bm2[:, s - 1:s], None, ALU.mult, None, [rXN, R_c], [rXN])
                    if s < NTW - 1:
                        ts("pool", XN[:, :, NW - 1:NW], XN[:, :, NW - 1:NW], bm2[:, s:s + 1], None, ALU.mult, None, [rXN, R_c], [rXN])
                    for j in range(22):
                        cbuf = j % NB
                        pair = ((j, CV[cbuf], rCV[cbuf], nb()), (22 + j, CG[cbuf], rCG[cbuf], nb()))
                        for (uc, Ct, rC, bank) in pair:
                            for kc in range(8):
                                mm(psum[bank][:, 0:NW], Wu[:, kc, uc * 128:(uc + 1) * 128], XN[:, kc, :], kc == 0, kc == 7,
                                   [rWu, rXN], [R_ps[bank]])
                        for (uc, Ct, rC, bank) in pair:
                            act(Ct[:, :], psum[bank][:, 1:TW + 1], AF.Identity, [R_ps[bank], rcw, rcb], [rC], bias=cb[:, uc:uc + 1],
                                scale=cw[:, 1, uc:uc + 1])
                        for (uc, Ct, rC, bank) in pair:
                            stt("dve", Ct[:, :], psum[bank][:, 0:TW], cw[:, 0, uc:uc + 1], Ct[:, :], ALU.mult, ALU.add,
                                [R_ps[bank], rcw, rC], [rC])
                        for (uc, Ct, rC, bank) in pair:
                            stt("dve", Ct[:, :], psum[bank][:, 2:TW + 2], cw[:, 2, uc:uc + 1], Ct[:, :], ALU.mult, ALU.add,
                                [R_ps[bank], rcw, rC], [rC])
                        act(SGt[cbuf][:], CG[cbuf][:], AF.Silu, [rCG[cbuf]], [rSGt[cbuf]])
                        tt("pool", G[:, j, :], SGt[cbuf][:], CV[cbuf][:], ALU.mult, [rSGt[cbuf], rCV[cbuf]], [rG])
                    for oc in range(8):
                        bank = nb()
                        for kc in range(22):
                            mm(psum[bank][:, 0:TW], Wd[:, kc, oc * 128:(oc + 1) * 128], G[:, kc, :], kc == 0, kc == 21,
                               [rWd, rG], [R_ps[bank]])
                        tt("dve", X[b][:, oc, 1:TW + 1], X[b][:, oc, 1:TW + 1], psum[bank][:, 0:TW], ALU.add, [rX[b], R_ps[bank]], [rX[b]])
                    stor(xo[:, :, t0:t0 + TW], X[b][:, :, 1:TW + 1], [rX[b]], [R_xT[dst]], rX[b])
            S.barrier()

        def phase_c1(l, src, dst):
            with ExitStack() as st:
                Wg, rWg = load_w(st, "c1_wg", W["w_in"][l], 8, 3072, col0=4000)
                Wb = sbt(st, "c1_wb", [128, 12, D], BF16)
                rWb = Res()
                for b in range(3):
                    for kc in range(4):
                        ld(Wb[:, b * 4 + kc, :], W["w_branch"][l, b, kc * 128:(kc + 1) * 128, :], (), [rWb], q="pool")
                Wo, rWo = load_w(st, "c1_wo", W["w_out"][l], 8, D)
                gain, rg = load_vec_fm(st, "c1_gain", W["attn_norm"][l], 8)
                Xs = [sbt(st, "c1_x%d" % i, [128, 8, 512], F32) for i in range(2)]
                rXs = [Res(), Res()]
                XN = sbt(st, "c1_xn", [128, 8, 512], BF16)
                rXN = Res()
                SQ = sbt(st, "c1_sq", [128, 8, 512], BF16)
                rSQ = Res()
                RSt = sbt(st, "c1_rs", [128, 512], F32)
                rRS = Res()
                OBs = [sbt(st, "c1_ob%d" % i, [128, 12, 512], BF16) for i in range(2)]
                rOBs = [Res(), Res()]
                MG = sbt(st, "c1_mg", [128, 8, 512], BF16)
                rMG = Res()
                SGt = [sbt(st, "c1_sg%d" % i, [128, 512], F32) for i in range(2)]
                rSGt = [Res(), Res()]
                ACCs = [sbt(st, "c1_acc%d" % i, [128, 512], F32) for i in range(2)]
                rACCs = [Res(), Res()]
                xv = xT[src].rearrange("(c p) t -> p c t", p=128)
                xo = xT[dst].rearrange("(c p) t -> p c t", p=128)
                srcs = (onaT, orwT, omT)
                rsrcs = (rscr("onaT"), rscr("orwT"), rscr("omT"))
                k = 0

                def loads(s):
                    t0 = s * 512
                    ld(Xs[s % 2][:], xv[:, :, t0:t0 + 512], [R_xT[src]], [rXs[s % 2]])
                    for b in range(3):
                        ld(OBs[s % 2][:, b * 4:(b + 1) * 4, :], srcs[b].rearrange("(c p) t -> p c t", p=128)[:, :, t0:t0 + 512],
                           [rsrcs[b]], [rOBs[s % 2]])

                loads(0)
                for s in range(NT):
                    t0 = s * 512
                    if s + 1 < NT:
                        loads(s + 1)
                    X, rX, OB, rOB = Xs[s % 2], rXs[s % 2], OBs[s % 2], rOBs[s % 2]
                    rmsnorm_fm(X, rX, XN, rXN, gain, rg, 512, SQ, rSQ, RSt, rRS)
                    for mc in range(8):
                        ACC, rACC = ACCs[mc % 2], rACCs[mc % 2]
                        for b in range(3):
                            bg = next_ps()
                            for kc in range(8):
                                mm(psum[bg][:, :], Wg[:, kc, b * 1024 + mc * 128:b * 1024 + (mc + 1) * 128], XN[:, kc, :],
                                   kc == 0, kc == 7, [rWg, rXN], [R_ps[bg]])
                            sb_ = k % 2
                            k += 1
                            act(SGt[sb_][:], psum[bg][:, :], AF.Sigmoid, [R_ps[bg]], [rSGt[sb_]])
                            bp = next_ps()
                            for kc in range(4):
                                mm(psum[bp][:, :], Wb[:, b * 4 + kc, mc * 128:(mc + 1) * 128], OB[:, b * 4 + kc, :],
                                   kc == 0, kc == 3, [rWb, rOB], [R_ps[bp]])
                            if b == 0:
                                tt("dve", ACC[:], SGt[sb_][:], psum[bp][:, :], ALU.mult, [rSGt[sb_], R_ps[bp]], [rACC])
                            else:
                                tt("dve", SGt[sb_][:], SGt[sb_][:], psum[bp][:, :], ALU.mult, [rSGt[sb_], R_ps[bp]],
                                   [rSGt[sb_]])
                                if b == 1:
                                    tt("pool", ACC[:], ACC[:], SGt[sb_][:], ALU.add, [rACC, rSGt[sb_]], [rACC])
                                else:
                                    tt("pool", MG[:, mc, :], ACC[:], SGt[sb_][:], ALU.add, [rACC, rSGt[sb_]], [rMG])
                    for oc in range(8):
                        bank = next_ps()
                        for kc in range(8):
                            mm(psum[bank][:, :], Wo[:, kc, oc * 128:(oc + 1) * 128], MG[:, kc, :], kc == 0, kc == 7,
                               [rWo, rMG], [R_ps[bank]])
                        tt(veng(), X[:, oc, :], X[:, oc, :], psum[bank][:, :], ALU.add, [rX, R_ps[bank]], [rX])
                    stor(xo[:, :, t0:t0 + 512], X[:], [rX], [R_xT[dst]], rX)
            S.barrier()

        def phase_a(l, src):
            with ExitStack() as st:
                Wi, rWi = load_w(st, "a_wi", W["w_in"][l], 8, 4000)
                gain, rg = load_vec_fm(st, "a_gain", W["attn_norm"][l], 8)
                KmT = sbt(st, "a_kmT", [128, NSLOT, 4, 256], BF16)
                Vm = sbt(st, "a_vm", [128, NSLOT, 2, 512], BF16)
                rKV = Res()
                with ExitStack() as st2:
                    Wkv, rWkv = load_w(st2, "a_wkv", W["w_mem_kv"][l], 8, 1024)
                    gm, rgm = load_vec_fm(st2, "a_gm", W["mem_norm"][l], 8)
                    MT = sbt(st2, "a_mt", [128, D], F32)
                    rMT = Res()
                    MS = sbt(st2, "a_ms", [128, D], F32)
                    rMS = Res()
                    MB = sbt(st2, "a_mb", [128, D], BF16)
                    rMB = Res()
                    ssq = sbt(st2, "a_ssq", [128, 1], F32)
                    rssq = Res()
                    memT = sbt(st2, "a_memT", [128, 8, 256], BF16)
                    rmemT = Res()
                    for sl in range(NSLOT):
                        for mc in range(2):
                            ld(MT[:], mem[sl, mc * 128:(mc + 1) * 128, :], (), [rMT])
                            act(MS[:], MT[:], AF.Square, [rMT], [rMS, rssq], accum_out=ssq[:])
                            ts("dve", ssq[:], ssq[:], 1.0 / D, 1e-6, ALU.mult, ALU.add, [rssq], [rssq])
                            ts("dve", ssq[:], ssq[:], -0.5, None, ALU.pow, None, [rssq], [rssq])
                            ts("dve", MB[:], MT[:], ssq[:, 0:1], None, ALU.mult, None, [rMT, rssq], [rMB])
                            for c in range(8):
                                tr(psT[:, c * 128:(c + 1) * 128], MB[:, c * 128:(c + 1) * 128], ident_b[:], [rMB, R_c],
                                   [R_psT])
                            for c in range(8):
                                ts(veng(), memT[:, c, mc * 128:(mc + 1) * 128], psT[:, c * 128:(c + 1) * 128],
                                   gm[:, c:c + 1], None, ALU.mult, None, [R_psT, rgm], [rmemT])
                        for h in range(4):
                            bank = next_ps()
                            for kc in range(8):
                                mm(psum[bank][:, 0:256], Wkv[:, kc, h * 128:(h + 1) * 128], memT[:, kc, :], kc == 0, kc == 7,
                                   [rWkv, rmemT], [R_ps[bank]])
                            cp("act", KmT[:, sl, h, :], psum[bank][:, 0:256], [R_ps[bank]], [rKV])
                        for mc in range(2):
                            bank = next_ps()
                            for kc in range(8):
                                mm(psum[bank][:, :], memT[:, kc, mc * 128:(mc + 1) * 128], Wkv[:, kc, 512:1024], kc == 0,
                                   kc == 7, [rWkv, rmemT], [R_ps[bank]])
                            cp("dve", Vm[:, sl, mc, :], psum[bank][:, :], [R_ps[bank]], [rKV])
                    S.barrier()
                cw = sbt(st, "a_cw", [128, 3, 16], F32)
                rcw = Res()
                memset("dve", cw[:], 0.0, [rcw])
                for k in range(3):
                    ld(cw[:, k, 0:15], W["rw_conv"][l, k, 0:1920].rearrange("(c p) -> p c", p=128), (), [rcw], nonc=True)
                    ld(cw[0:32, k, 15:16], W["rw_conv"][l, k, 1920:1952].rearrange("(c p) -> p c", p=32), (), [rcw],
                       nonc=True)
                dec0 = sbt(st, "a_dec0", [128, 2, 4], F32)
                a0 = sbt(st, "a_a0", [128, 2, 4], F32)
                rsm = Res()
                for d in range(2):
                    ld(dec0[:, d, :], W["rw_decay0"][l, d].rearrange("(c p) -> p c", p=128), (), [rsm], nonc=True)
                    ld(a0[:, d, :], W["rw_a0"][l, d].rearrange("(c p) -> p c", p=128), (), [rsm], nonc=True)
                kkv, r1 = load_vec_fm(st, "a_kk", W["rw_k_k"][l], 4)
                kav, r2 = load_vec_fm(st, "a_ka", W["rw_k_a"][l], 4)
                rkv, r3 = load_vec_fm(st, "a_rk", W["rw_r_k"][l], 4)
                D2 = sbt(st, "a_d2", [128, 512], BF16)
                A2 = sbt(st, "a_a2", [128, 512], BF16)
                G2 = sbt(st, "a_g2", [128, 2, 512], BF16)
                ld(D2[:], W["rw_decay2"][l].rearrange("d l f -> (d l) f"), (), [rsm], q="pool")
                ld(A2[:], W["rw_a2"][l].rearrange("d l f -> (d l) f"), (), [rsm], q="pool")
                ld(G2[:, 0, :], W["rw_g2"][l, 0:128, :], (), [rsm], q="pool")
                ld(G2[0:32, 1, :], W["rw_g2"][l, 128:160, :], (), [rsm], q="pool")
                scanm = sbt(st, "a_scanm", [128, 512], F32)
                ld(scanm[:], cst_in["scanmask"][:, :], (), [rsm])
                rsmall = [rsm, r1, r2, r3, rcw]

                XN = sbt(st, "a_xn", [128, 8, 512], BF16)
                rXN = Res()
                RSt = sbt(st, "a_rs", [128, 512], F32)
                rRS = Res()
                QK = sbt(st, "a_qk", [128, 8, 512], BF16)
                rQK = Res()
                SQ, rSQ = QK, rQK
                VT = sbt(st, "a_vt", [128, 4, 512], BF16)
                rVT = Res()
                QM = sbt(st, "a_qm", [128, 4, 512], BF16)
                rQM = Res()
                PT = [sbt(st, "a_pt%d" % i, [128, 512], BF16) for i in range(2)]
                rPT = [Res(), Res()]
                RD = sbt(st, "a_rd", [128, 512], F32)
                rRD = Res()
                OM = sbt(st, "a_om", [128, 4, 512], BF16)
                rOM = Res()
                ZC = sbt(st, "a_zc", [128, 16, 512], F32)
                rZC = Res()
                X, rX = ZC[:, 0:8, :], rZC
                HZ = sbt(st, "a_hz", [128, 16, max(NH5, 2)], F32)
                rHZ = Res()
                xv = xT[src].rearrange("(c p) t -> p c t", p=128)
                rwcols = [1536 + 128 * i for i in range(15)] + [1536 + 1920]
                rwm = [128] * 15 + [32]
                if NT > 1:
                    XH = sbt(st, "a_xh", [128, 8, NH5], F32)
                    rXH = Res()
                    XHN = sbt(st, "a_xhn", [128, 8, NH5], BF16)
                    rXHN = Res()
                    for c in range(8):
                        srcv = xT[src][c * 128:(c + 1) * 128, 511:T - 1].rearrange("p (j w) -> p j w", w=512)[:, :, 0:2]
                        ld(XH[:, c, :].rearrange("p (j w) -> p j w", w=2), srcv, [R_xT[src]], [rXH], nonc=True)
                    rmsnorm_fm(XH, rXH, XHN, rXHN, gain, rg, NH5, SQ, rSQ, RSt, rRS)
                    memset("dve", HZ[:], 0.0, [rHZ])
                    for zc in range(16):
                        bank = next_ps()
                        m = rwm[zc]
                        for kc in range(8):
                            mm(psum[bank][0:m, 0:NH5], Wi[:, kc, rwcols[zc]:rwcols[zc] + m], XHN[:, kc, :], kc == 0, kc == 7,
                               [rWi, rXHN], [R_ps[bank]])
                        tt("dve", HZ[0:m, zc, :].rearrange("p (j w) -> p j w", w=2),
                           psum[bank][0:m, 0:NH5].rearrange("p (j w) -> p j w", w=2),
                           bm[0:m, 0:NT - 1].unsqueeze(2).to_broadcast([m, NT - 1, 2]), ALU.mult, [R_ps[bank], R_c], [rHZ])

                def tmp(name, dt=F32):
                    return sbt(st, "a_t_" + name, [128, 512], dt), Res(name)

                TH, rTH = tmp("th", BF16)
                XAb, rXAb = tmp("xab", BF16)
                XGb = sbt(st, "a_t_xgb", [128, 2, 512], BF16)
                rXGb = Res()
                SGm, rSGm = tmp("sg")
                Pc, rPc = tmp("pc")
                Pe, rPe = tmp("pe")
                Pt, rPt = tmp("pt")
                E1, rE1 = tmp("e1")
                E2, rE2 = tmp("e2")
                Ee, rEe = tmp("ee")
                Eb, rEb = tmp("eb")
                AS, rAS = tmp("as")
                KD, rKD = tmp("kd")
                KK, rKK = tmp("kk")
                KQ, rKQ = tmp("kq", BF16)
                Bv, rBv = tmp("b")
                BON, rBON = tmp("bon")
                RKb, rRKb = tmp("rkb", BF16)
                TMP, rTMP = tmp("tmp")
                RI, rRI = TMP, rTMP
                OUTS = {}
                for nm in ("at", "bt", "bb", "rt", "kt", "kb"):
                    OUTS[nm] = (sbt(st, "a_o_" + nm, [128, 512], BF16), Res(nm))
                VRb, rVRb = tmp("vrb", BF16)
                Gb, rGb = tmp("gb", BF16)
                WTo = sbt(st, "a_wto", [128, 8], F32)
                rWTo = Res()

                for s in range(NT):
                    t0 = s * 512
                    slot = s // SLOT_ST
                    ld(X[:], xv[:, :, t0:t0 + 512], [R_xT[src]], [rX])
                    rmsnorm_fm(X, rX, XN, rXN, gain, rg, 512, SQ, rSQ, RSt, rRS)
                    def d_qk(c):
                        bank = next_ps()
                        for kc in range(8):
                            mm(psum[bank][:, :], Wi[:, kc, c * 128:(c + 1) * 128], XN[:, kc, :], kc == 0, kc == 7,
                               [rWi, rXN], [R_ps[bank]])
                        if c < 4:
                            act(QK[:, c, :], psum[bank][:, :], AF.Copy, [R_ps[bank]], [rQK], scale=0.125)
                        else:
                            cp("dve", QK[:, c, :], psum[bank][:, :], [R_ps[bank]], [rQK])
                        if c == 3:
                            stor(qT.rearrange("(c p) t -> p c t", p=128)[:, :, t0:t0 + 512], QK[:, 0:4, :], [rQK], [rscr("qT")], rQK)
                        if c == 7:
                            stor(kT.rearrange("(c p) t -> p c t", p=128)[:, :, t0:t0 + 512], QK[:, 4:8, :], [rQK], [rscr("kT")], rQK)

                    def d_v(tl):
                        bank = next_ps()
                        for kc in range(8):
                            mm(psum[bank][:, :], XN[:, kc, tl * 128:(tl + 1) * 128], Wi[:, kc, 1024:1536], kc == 0, kc == 7,
                               [rWi, rXN], [R_ps[bank]])
                        cp("act" if tl % 2 else "dve", VT[:, tl, :], psum[bank][:, :], [R_ps[bank]], [rVT])
                        if tl == 3:
                            stor(vtm[t0:t0 + 512, :].rearrange("(c p) f -> p c f", p=128), VT[:], [rVT], [rscr("vtm")], rVT)

                    def d_mq(h):
                        bank = next_ps()
                        for kc in range(8):
                            mm(psum[bank][:, :], Wi[:, kc, 3488 + h * 128:3488 + (h + 1) * 128], XN[:, kc, :], kc == 0, kc == 7,
                               [rWi, rXN], [R_ps[bank]])
                        act(QM[:, h, :], psum[bank][:, :], AF.Copy, [R_ps[bank]], [rQM], scale=128 ** -0.5)

                    def d_ma(h):
                        for mc in range(2):
                            bank = next_ps()
                            mm(psum[bank][:, :], KmT[:, slot, h, mc * 128:(mc + 1) * 128], QM[:, h, :], True, True,
                               [rKV, rQM], [R_ps[bank]])
                            act(PT[mc][:], psum[bank][:, :], AF.Exp, [R_ps[bank]], [rPT[mc]])
                        bo, bd_ = next_ps(), next_ps()
                        for mc in range(2):
                            mm(psum[bo][:, :], Vm[:, slot, mc, h * 128:(h + 1) * 128], PT[mc][:], mc == 0, mc == 1,
                               [rKV, rPT[mc]], [R_ps[bo]])
                        for mc in range(2):
                            mm(psum[bd_][:, :], ones_b[:], PT[mc][:], mc == 0, mc == 1, [R_c, rPT[mc]], [R_ps[bd_]])
                        act(RD[:], psum[bd_][:, :], AF.Ln, [R_ps[bd_]], [rRD])
                        act(RD[:], RD[:], AF.Exp, [rRD], [rRD], scale=-1.0)
                        tt("dve", OM[:, h, :], psum[bo][:, :], RD[:], ALU.mult, [R_ps[bo], rRD], [rOM])
                        if h == 3:
                            stor(omT.rearrange("(c p) t -> p c t", p=128)[:, :, t0:t0 + 512], OM[:], [rOM], [rscr("omT")], rOM)

                    def dense_slice(it):
                        d_qk(it)
                        if it % 2 == 0:
                            d_v(it // 2)
                            d_mq(it // 2)
                        else:
                            d_ma(it // 2)

                    rZCc = [Res() for _ in range(16)]
                    for r_ in rZCc:
                        for k_, v_ in list(rZC.r.items()) + list(rZC.w.items()):
                            if r_.r.get(k_, 0) < v_:
                                r_.r[k_] = v_
                    for z0 in range(0, 16, 2):
                        grp = []
                        for zc in (z0, z0 + 1):
                            bank = next_ps()
                            m = rwm[zc]
                            for kc in range(8):
                                mm(psum[bank][0:m, :], Wi[:, kc, rwcols[zc]:rwcols[zc] + m], XN[:, kc, :], kc == 0, kc == 7,
                                   [rWi, rXN], [R_ps[bank]])
                            grp.append((zc, m, bank))
                        for (zc, m, bank) in grp:
                            act(ZC[0:m, zc, :], psum[bank][0:m, :], AF.Copy, [R_ps[bank], rcw], [rZCc[zc]], scale=cw[0:m, 1, zc:zc + 1])
                        for (zc, m, bank) in grp:
                            stt("dve", ZC[0:m, zc, 1:512], psum[bank][0:m, 0:511], cw[0:m, 0, zc:zc + 1], ZC[0:m, zc, 1:512], ALU.mult,
                                ALU.add, [R_ps[bank], rcw, rZCc[zc]], [rZCc[zc]])
                        for (zc, m, bank) in grp:
                            stt("dve", ZC[0:m, zc, 0:511], psum[bank][0:m, 1:512], cw[0:m, 2, zc:zc + 1], ZC[0:m, zc, 0:511], ALU.mult,
                                ALU.add, [R_ps[bank], rcw, rZCc[zc]], [rZCc[zc]])
                        for (zc, m, bank) in grp:
                            if s > 0:
                                stt("dve", ZC[0:m, zc, 0:1], HZ[0:m, zc, 2 * (s - 1):2 * (s - 1) + 1], cw[0:m, 0, zc:zc + 1],
                                    ZC[0:m, zc, 0:1], ALU.mult, ALU.add, [rHZ, rcw, rZCc[zc]], [rZCc[zc]])
                        for (zc, m, bank) in grp:
                            if s < NT - 1:
                                stt("dve", ZC[0:m, zc, 511:512], HZ[0:m, zc, 2 * s + 1:2 * s + 2], cw[0:m, 2, zc:zc + 1],
                                    ZC[0:m, zc, 511:512], ALU.mult, ALU.add, [rHZ, rcw, rZCc[zc]], [rZCc[zc]])
                    rZC.r = {}
                    for r_ in rZCc:
                        for k_, v_ in r_.w.items():
                            if rZC.w.get(k_, 0) < v_:
                                rZC.w[k_] = v_
                    act(TH[:], ZC[:, 12, :], AF.Tanh, [rZC], [rTH])
                    cp("pool", XAb[:], ZC[:, 13, :], [rZC], [rXAb])
                    act(XGb[:, 0, :], ZC[:, 14, :], AF.Sigmoid, [rZC], [rXGb])
                    act(XGb[0:32, 1, :], ZC[0:32, 15, :], AF.Sigmoid, [rZC], [rXGb])
                    for hp in range(4):
                        fs = slice(hp * 128, (hp + 1) * 128)
                        Rr = ZC[:, hp, :]
                        Kr = ZC[:, 4 + hp, :]
                        Vr = ZC[:, 8 + hp, :]
                        mm(psum[4][:, :], G2[:, 0, fs], XGb[:, 0, :], True, False, rsmall + [rXGb], [R_ps[4]])
                        mm(psum[4][:, :], G2[0:32, 1, fs], XGb[0:32, 1, :], False, True, rsmall + [rXGb], [R_ps[4]])
                        cp("act", Gb[:], psum[4][:, :], [R_ps[4]], [rGb])
                        stor(rws["g"][fs, t0:t0 + 512], Gb[:], [rGb], [rscr("g")], rGb)
                        cp("pool", VRb[:], Vr, [rZC], [rVRb])
                        stor(rws["v"][fs, t0:t0 + 512], VRb[:], [rVRb], [rscr("v")], rVRb)
                        ts("dve", KK[:], Kr, kkv[:, hp:hp + 1], None, ALU.mult, None, [rZC] + rsmall, [rKK])
                        tt("pool", KQ[:], KK[:], KK[:], ALU.mult, [rKK], [rKQ])
                        mm(psum[4][:, :], bd64_b[:], KQ[:], True, True, [R_c, rKQ], [R_ps[4]])
                        ts("dve", RI[:], psum[4][:, :], 1e-24, None, ALU.max, None, [R_ps[4]], [rRI])
                        ts("dve", RI[:], RI[:], -0.5, None, ALU.pow, None, [rRI], [rRI])
                        tt("dve", KK[:], KK[:], RI[:], ALU.mult, [rKK, rRI], [rKK])
                        for d in range(2):
                            ds = slice(d * 64, (d + 1) * 64)
                            cexp = 0.6065306597126334
                            mm(psum[5][:, :], D2[ds, fs], TH[ds, :], True, True, rsmall + [rTH], [R_ps[5]])
                            mm(psum[6][:, :], A2[ds, fs], XAb[ds, :], True, True, rsmall + [rXAb], [R_ps[6]])
                            act(SGm[:], psum[5][:, :], AF.Sigmoid, [R_ps[5]] + rsmall, [rSGm], bias=dec0[:, d, hp:hp + 1])
                            act(AS[:], psum[6][:, :], AF.Sigmoid, [R_ps[6]] + rsmall, [rAS], bias=a0[:, d, hp:hp + 1])
                            S.op("dve", lambda e: e.tensor_tensor_scan(out=Pc[:], data0=scanm[:], data1=SGm[:], initial=0.0,
                                                                        op0=ALU.mult, op1=ALU.add), [rSGm] + rsmall, [rPc])
                            Pc3 = Pc[:].rearrange("p (c t) -> p c t", t=64)
                            if d == 1:
                                tt("pool", Pe[:].rearrange("p (c t) -> p c t", t=64), Pc3[:, :, 63:64].to_broadcast([128, 8, 64]),
                                   Pc3, ALU.subtract, [rPc], [rPe])
                                tt("pool", Pc[:], Pe[:], SGm[:], ALU.add, [rPe, rSGm], [rPc])
                                totcol = 0
                            else:
                                totcol = 63
                            tt("pool", Pe[:], Pc[:], SGm[:], ALU.subtract, [rPc, rSGm], [rPe])
                            tt("pool", Pt[:].rearrange("p (c t) -> p c t", t=64),
                               Pc3[:, :, totcol:totcol + 1].to_broadcast([128, 8, 64]), Pc3, ALU.subtract, [rPc], [rPt])
                            ts("dve", TMP[:], AS[:], -1.0, kav[:, hp:hp + 1], ALU.add, ALU.mult, [rAS] + rsmall, [rTMP])
                            stt("dve", KD[:], TMP[:], 1.0, Kr, ALU.add, ALU.mult, [rTMP, rZC], [rKD])
                            tt("dve", Bv[:], KK[:], AS[:], ALU.mult, [rKK, rAS], [rBv])
                            stt("dve", RKb[:], Rr, rkv[:, hp:hp + 1], KD[:], ALU.mult, ALU.mult, [rZC, rKD] + rsmall, [rRKb])
                            mm(psum[4][:, :], bd64_b[:], RKb[:], True, True, [R_c, rRKb], [R_ps[4]])
                            act(E1[:], Pc[:], AF.Exp, [rPc], [rE1], scale=-cexp)
                            act(E2[:], Pc[:], AF.Exp, [rPc], [rE2], scale=cexp)
                            act(Ee[:], Pe[:], AF.Exp, [rPe], [rEe], scale=-cexp)
                            act(Eb[:], Pt[:], AF.Exp, [rPt], [rEb], scale=-cexp)
                            if d == 0:
                                tt("dve", BON[:], psum[4][:, :], Vr, ALU.mult, [R_ps[4], rZC], [rBON])
                            else:
                                tt("dve", TMP[:], psum[4][:, :], Vr, ALU.mult, [R_ps[4], rZC], [rTMP])
                                tt("pool", BON[:], BON[:], TMP[:], ALU.add, [rBON, rTMP], [rBON])
                            cp("pool", WTo[:, :], E1[:].rearrange("p (c t) -> p c t", t=64)[:, :, totcol], [rE1], [rWTo])
                            stor(rws["wtot", d][fs, s * 8:(s + 1) * 8], WTo[:], [rWTo], [rscr(("wtot", d))], rWTo)
                            o, ro = OUTS["rt"]
                            tt("pool", o[:], Rr, E1[:], ALU.mult, [rZC, rE1], [ro])
                            o, ro = OUTS["bt"]
                            tt("dve", o[:], Bv[:], E2[:], ALU.mult, [rBv, rE2], [ro])
                            o, ro = OUTS["kt"]
                            tt("pool", o[:], KD[:], E2[:], ALU.mult, [rKD, rE2], [ro])
                            o, ro = OUTS["at"]
                            stt("dve", o[:], KK[:], -1.0, Ee[:], ALU.mult, ALU.mult, [rKK, rEe], [ro])
                            o, ro = OUTS["bb"]
                            tt("dve", o[:], Bv[:], Eb[:], ALU.mult, [rBv, rEb], [ro])
                            o, ro = OUTS["kb"]
                            tt("pool", o[:], KD[:], Eb[:], ALU.mult, [rKD, rEb], [ro])
                            for nm in ("rt", "bt", "kt", "at", "bb", "kb"):
                                o, ro = OUTS[nm]
                                stor(rws[nm, d][fs, t0:t0 + 512], o[:], [ro], [rscr((nm, d))], ro)
                            dense_slice(hp * 2 + d)
                        stor(rws["bonus"][fs, t0:t0 + 512], BON[:], [rBON], [rscr("bonus")], rBON)
            S.barrier()

        def phase_b1(l):
            with ExitStack() as st:
                Bt = sbt(st, "b1_bt", [64, 8 * 17 * 64], BF16)
                rBt = Res()
                ld(Bt[:], btab_in[l], (), [rBt], q="pool")
                Bt4 = Bt[:].rearrange("p (h o k) -> p h o k", h=8, o=17)
                RB = sbt(st, "b1_rb", [128, nslots_na], F32)
                rRB = Res()
                ld(RB[:], rowbias_in[:, :], (), [rRB])
                NR = min(24, ROWS)
                KW = [sbt(st, "b1_kw%d" % i, [64, 8, NR * 64], BF16) for i in range(2)]
                rKW = [Res(), Res()]
                QW = [sbt(st, "b1_qw%d" % i, [64, 8, 512], BF16) for i in range(2)]
                rQW = [Res(), Res()]
                VE = [sbt(st, "b1_ve%d" % i, [128, NR // 2, 512], BF16) for i in range(2)]
                rVE = [Res(), Res()]
                VO = [sbt(st, "b1_vo%d" % i, [128, NR // 2 - 1, 512], BF16) for i in range(2)]
                rVO = [Res(), Res()]
                PTt = [sbt(st, "b1_pt%d" % i, [128, 512], BF16) for i in range(2)]
                rPTt = [Res(), Res()]
                RDt = sbt(st, "b1_rd", [64, 512], F32)
                rRDt = Res()
                ON = [sbt(st, "b1_on%d" % i, [64, 8, 512], BF16) for i in range(2)]
                rON = [Res(), Res()]
                qv = qT.rearrange("(h d) t -> d h t", d=64)
                kv = kT.rearrange("(h d) t -> d h t", d=64)

                def loads(s):
                    b = s % 2
                    i0 = s * 8
                    rb = min(max(i0 - 8, 0), ROWS - NR)
                    ld(QW[b][:], qv[:, :, s * 512:(s + 1) * 512], [rscr("qT")], [rQW[b]])
                    ld(KW[b][:], kv[:, :, rb * 64:(rb + NR) * 64], [rscr("kT")], [rKW[b]])
                    ld(VE[b][:], vtm[rb * 64:(rb + NR) * 64, :].rearrange("(c p) f -> p c f", p=128), [rscr("vtm")], [rVE[b]])
                    ld(VO[b][:], vtm[rb * 64 + 64:(rb + NR) * 64 - 64, :].rearrange("(c p) f -> p c f", p=128), [rscr("vtm")],
                       [rVO[b]])
                    return rb

                slot = 0
                pk = 0
                sk = [0]
                rbs = {0: loads(0)}
                for s in range(NT):
                    b = s % 2
                    if s + 1 < NT:
                        rbs[s + 1] = loads(s + 1)
                    rb = rbs[s]
                    for iq in range(8):
                        i = s * 8 + iq
                        band = bands[i]
                        for ci, r in enumerate(band):
                            o = r - i + 8
                            kc0 = (r - rb) * 64
                            sk[0] = (sk[0] + 1) % 3
                            bank = sk[0]
                            for h in range(8):
                                mm(psum[bank][:, h * 64:(h + 1) * 64], KW[b][:, h, kc0:kc0 + 128], QW[b][:, h, iq * 64:(iq + 1) * 64],
                                   True, False, [rKW[b], rQW[b]], [R_ps[bank]])
                                mm(psum[bank][:, h * 64:(h + 1) * 64], Bt4[:, h, o:o + 2, :].rearrange("p o k -> p (o k)"),
                                   ident_b[0:64, 0:64], False, True, [rBt, R_c], [R_ps[bank]])
                            pb = pk % 2
                            pk += 1
                            act(PTt[pb][:], psum[bank][:, :], AF.Exp, [R_ps[bank], rRB], [rPTt[pb]], bias=RB[:, slot:slot + 1])
                            slot += 1
                            if (r - rb) % 2 == 0:
                                Vc, rVc = VE[b][:, (r - rb) // 2, :], rVE[b]
                            else:
                                Vc, rVc = VO[b][:, (r - rb - 1) // 2, :], rVO[b]
                            first, last = ci == 0, ci == len(band) - 1
                            bo, bd_ = (5, 6) if i % 2 == 0 else (3, 4)
                            for h in range(8):
                                mm(psum[bo][0:64, h * 64:(h + 1) * 64], Vc[:, h * 64:(h + 1) * 64], PTt[pb][:, h * 64:(h + 1) * 64],
                                   first and h == 0, last, [rVc, rPTt[pb]], [R_ps[bo]])
                            mm(psum[bd_][0:64, :], ones_b[:, 0:64], PTt[pb][:], first, last, [R_c, rPTt[pb]], [R_ps[bd_]])
                        act(RDt[:], psum[bd_][0:64, :], AF.Ln, [R_ps[bd_]], [rRDt])
                        act(RDt[:], RDt[:], AF.Exp, [rRDt], [rRDt], scale=-1.0)
                        tt("dve", ON[b][:, :, iq * 64:(iq + 1) * 64], psum[bo][0:64, :].rearrange("p (h q) -> p h q", h=8),
                           RDt[:].rearrange("p (h q) -> p h q", h=8), ALU.mult, [R_ps[bo], rRDt], [rON[b]])
                    stor(onaT.rearrange("(h d) t -> d h t", d=64)[:, :, s * 512:(s + 1) * 512], ON[b][:], [rON[b]],
                         [rscr("onaT")], rON[b])
            S.barrier()

        def phase_b2(l):
            with ExitStack() as st:
                mk = {}
                rM = Res()
                for nm in ("ls4", "li4", "us4", "ui4"):
                    mk[nm] = sbt(st, "b2_" + nm, [128, 512], F32)
                    ld(mk[nm][:], cst_in[nm][:, :], (), [rM])
                MS = [mk["ls4"], mk["us4"]]
                MI = [mk["li4"], mk["ui4"]]
                MP = [mk["us4"], mk["ls4"]]
                lnw = sbt(st, "b2_lnw", [128, 4], F32)
                lnb = sbt(st, "b2_lnb", [128, 4], F32)
                ld(lnw[:], W["rw_lnx_w"][l].rearrange("(c p) -> p c", p=128), (), [rM], nonc=True)
                ld(lnb[:], W["rw_lnx_b"][l].rearrange("(c p) -> p c", p=128), (), [rM], nonc=True)
                names = ("at", "bt", "bb", "rt", "kt", "kb", "v")
                IN = {nm: [sbt(st, "b2_i_%s%d" % (nm, i), [64, 8, 512], BF16) for i in range(2)] for nm in names}
                rIN = [Res(), Res()]
                WT = [sbt(st, "b2_wt%d" % i, [64, 8, 8], F32) for i in range(2)]
                AM = sbt(st, "b2_am", [128, 4, 8, 128], BF16)
                rAMt = [[Res(), Res()] for _ in range(4)]
                PPp = [sbt(st, "b2_ppp%d" % i, [128, 8, 128], BF16) for i in range(2)]
                PPt = [sbt(st, "b2_ppt%d" % i, [128, 8, 128], BF16) for i in range(2)]
                rPP = [[Res(), Res()], [Res(), Res()]]
                rPT = [[Res(), Res()], [Res(), Res()]]
                Xc = [sbt(st, "b2_x%d" % i, [128, 8, 128], BF16) for i in range(2)]
                rXc = [[Res(), Res()], [Res(), Res()]]
                TMx = sbt(st, "b2_tm", [128, 9 * 192], BF16)
                rTM = [Res(), Res()]
                memset("dve", TMx[:], 0.0, rTM)
                TMf = TMx[:, :]
                TM = TMx[:, 0:8 * 192].rearrange("p (h q) -> p h q", h=8)
                RH = sbt(st, "b2_rh", [64, 8, 2, 128], BF16)
                rRH = Res()
                MTt = sbt(st, "b2_mt", [64, 8, 2, 128], BF16)
                rMTt = Res()
                MTf = sbt(st, "b2_mtf", [64, 8, 2, 64], F32)
                rMTf = Res()
                DW = sbt(st, "b2_dw", [64, 8, 2, 64], F32)
                rDW = Res()
                N0 = sbt(st, "b2_n0", [64, 8, 2, 64], F32)
                rN0 = Res()
                Sb = sbt(st, "b2_sb", [64, 8, 64], BF16)
                rSb = Res()
                YF = [sbt(st, "b2_yf%d" % i, [128, 512], F32) for i in range(2)]
                rYF = [Res(), Res()]
                YL = [sbt(st, "b2_yl%d" % i, [128, 512], F32) for i in range(2)]
                rYL = [Res(), Res()]
                YC = sbt(st, "b2_yc", [128, 512], F32)
                rYC = Res()
                YS = sbt(st, "b2_ys", [128, 512], F32)
                rYS = Res()
                YN = sbt(st, "b2_yn", [128, 512], BF16)
                rYN = Res()
                st8 = sbt(st, "b2_st8", [128, 8], F32)
                rst8 = Res()
                st8b = sbt(st, "b2_st8b", [128, 8], F32)
                rst8b = Res()
                BONt = [sbt(st, "b2_bon%d" % i, [128, 4, 128], F32) for i in range(2)]
                Gt = [sbt(st, "b2_g%d" % i, [128, 4, 128], BF16) for i in range(2)]
                rBG = [Res(), Res()]
                OT = sbt(st, "b2_ot", [128, 4, 128], F32)
                rOT = Res()
                OR = [sbt(st, "b2_or%d" % i, [128, 4, 128], BF16) for i in range(2)]
                rOR = [Res(), Res()]
                memset("dve", RH[:], 0.0, [rRH])
                memset("dve", MTt[:], 0.0, [rMTt])

                def hview(ap):
                    return ap.rearrange("(h j) t -> j h t", j=64)

                def v4(ap):
                    return ap.rearrange("p (h t) -> p h t", h=4)

                for d in range(2):
                    memset("dve", Sb[:], 0.0, [rSb])
                    order = list(range(NT)) if d == 0 else list(range(NT - 1, -1, -1))

                    def loads(idx, d=d, order=order):
                        s = order[idx]
                        b = idx % 2
                        for nm in names:
                            srcap = rws["v"] if nm == "v" else rws[nm, d]
                            rk = rscr("v") if nm == "v" else rscr((nm, d))
                            ld(IN[nm][b][:], hview(srcap)[:, :, s * 512:(s + 1) * 512], [rk], [rIN[b]])
                        ld(WT[b][:], hview(rws["wtot", d])[:, :, s * 8:(s + 1) * 8], [rscr(("wtot", d))], [rIN[b]], nonc=True)

                    loads(0)
                    tcount = 0
                    for idx, s in enumerate(order):
                        b = idx % 2
                        if idx + 1 < NT:
                            loads(idx + 1)
                        rI = rIN[b]
                        if d == 0 and s > 0:
                            ts("dve", Sb[:], Sb[:], bm[0:64, s - 1:s], None, ALU.mult, None, [rSb, R_c], [rSb])
                        if d == 1 and s < NT - 1:
                            ts("dve", Sb[:], Sb[:], bm[0:64, s:s + 1], None, ALU.mult, None, [rSb, R_c], [rSb])
                        tiles = list(range(4)) if d == 0 else [3, 2, 1, 0]
                        for tl in tiles:
                            tc_ = slice(tl * 128, (tl + 1) * 128)
                            gt0 = s * 512 + tl * 128
                            yb = tcount % 2
                            tcount += 1
                            if d == 1:
                                ld(YL[yb][:], yfw[gt0:gt0 + 128, :], [rscr("yfw")], [rYL[yb]])
                                ld(BONt[yb][:], rws["bonus"].rearrange("(c p) t -> p c t", p=128)[:, :, gt0:gt0 + 128],
                                   [rscr("bonus")], [rBG[yb]])
                                ld(Gt[yb][:], rws["g"].rearrange("(c p) t -> p c t", p=128)[:, :, gt0:gt0 + 128], [rscr("g")],
                                   [rBG[yb]])

                            def I(nm, h):
                                return IN[nm][b][:, h, tc_]

                            typs = (("bt", "at", MS), ("kt", "at", MS), ("bt", "rt", MI), ("kt", "rt", MI))
                            for ty, (ln, rn, msk) in enumerate(typs):
                                for hg in range(2):
                                    bank = next_ps()
                                    for j in range(4):
                                        h = hg * 4 + j
                                        mm(psum[bank][:, j * 128:(j + 1) * 128], I(ln, h), I(rn, h), True, True, [rI], [R_ps[bank]])
                                    tt("dve", AM[:, ty, hg * 4:(hg + 1) * 4, :], v4(psum[bank][:, :]), v4(msk[d][:]), ALU.mult,
                                       [R_ps[bank], rM], [rAMt[ty][hg]])
                            for hg in range(2):
                                bank = next_ps()
                                for j in range(4):
                                    h = hg * 4 + j
                                    mm(psum[bank][:, j * 128:(j + 1) * 128], I("at", h), I("bt", h), True, True, [rI], [R_ps[bank]])
                                tt("dve", PPp[0][:, hg * 4:(hg + 1) * 4, :], v4(psum[bank][:, :]), v4(MP[d][:]), ALU.mult,
                                   [R_ps[bank], rM], [rPP[0][hg]])
                            for hg in range(2):
                                tb = psT[:, :] if hg == 0 else psum[5][:, :].bitcast(BF16)
                                rtb = R_psT if hg == 0 else R_ps[5]
                                for j in range(4):
                                    h = hg * 4 + j
                                    for q, nm in enumerate(("v", "bb", "kb", "at")):
                                        tr(tb[:, j * 256 + q * 64:j * 256 + (q + 1) * 64], I(nm, h), ident_b[0:64, 0:64],
                                           [rI, R_c], [rtb])
                                pv4 = tb.rearrange("p (h q) -> p h q", h=4)
                                cp("act", TM[:, hg * 4:(hg + 1) * 4, :], pv4[:, :, 0:192], [rtb], [rTM[hg]])
                                cp("act", Xc[0][:, hg * 4:(hg + 1) * 4, 0:64], pv4[:, :, 192:256], [rtb], [rXc[0][hg]])
                            for hg in range(2):
                                bank = next_ps()
                                for j in range(4):
                                    h = hg * 4 + j
                                    mm(psum[bank][:, j * 64:(j + 1) * 64], AM[:, 1, h, :], TM[:, h, 0:64], True, True,
                                       [rAMt[1][hg], rTM[hg]], [R_ps[bank]])
                                cp("dve", Xc[0][:, hg * 4:(hg + 1) * 4, 64:128],
                                   psum[bank][:, 0:256].rearrange("p (h i) -> p h i", h=4), [R_ps[bank]], [rXc[0][hg]])
                            if LVL < 2:
                                continue
                            for hg in range(2):
                                for j in range(4):
                                    h = hg * 4 + j
                                    mm(psum[hg][:, j * 128:(j + 1) * 128], ident_b[:], Xc[0][:, h, :], j == 0, False,
                                       [R_c, rXc[0][hg]], [R_ps[hg]])
                            for k in range(6):
                                cur, nxt = k % 2, (k + 1) % 2
                                for hg in range(2):
                                    rP = rAMt[0][hg] if k == 0 else rPT[cur][hg]
                                    bx = hg
                                    for j in range(4):
                                        h = hg * 4 + j
                                        Pt_h = AM[:, 0, h, :] if k == 0 else PPt[cur][:, h, :]
                                        mm(psum[bx][:, j * 128:(j + 1) * 128], Pt_h, Xc[cur][:, h, :], False, k == 5,
                                           [rP, rXc[cur][hg]], [R_ps[bx]])
                                    cp("act", Xc[nxt][:, hg * 4:(hg + 1) * 4, :], v4(psum[bx][:, :]), [R_ps[bx]], [rXc[nxt][hg]])
                                    if k < 5:
                                        bp_, bt_ = 2 + hg, 5 + hg
                                        rPp = rPP[cur][hg]
                                        for j in range(4):
                                            h = hg * 4 + j
                                            Pt_h = AM[:, 0, h, :] if k == 0 else PPt[cur][:, h, :]
                                            Pp_h = PPp[cur][:, h, :]
                                            mm(psum[bp_][:, j * 128:(j + 1) * 128], Pt_h, Pp_h, True, True, [rP, rPp], [R_ps[bp_]])
                                            mm(psum[bt_][:, j * 128:(j + 1) * 128], Pp_h, Pt_h, True, True, [rP, rPp], [R_ps[bt_]])
                                        cp("dve", PPp[nxt][:, hg * 4:(hg + 1) * 4, :], v4(psum[bp_][:, :]), [R_ps[bp_]], [rPP[nxt][hg]])
                                        cp("act", PPt[nxt][:, hg * 4:(hg + 1) * 4, :], v4(psum[bt_][:, :]), [R_ps[bt_]], [rPT[nxt][hg]])
                            XF = Xc[0]
                            rXF = rXc[0]
                            if LVL < 3:
                                continue
                            for h in range(8):
                                hg = h // 4
                                ys = slice(h * 64, (h + 1) * 64)
                                mm(psum[4][:, ys], AM[:, 2, h, :], XF[:, h, 64:128], h == 0, False, [rAMt[2][hg], rXF[hg]], [R_ps[4]])
                                mm(psum[4][:, ys], AM[:, 3, h, :], TM[:, h, 0:64], False, False, [rAMt[3][hg], rTM[hg]], [R_ps[4]])
                            for hg in range(2):
                                bank = hg
                                for j in range(4):
                                    h = hg * 4 + j
                                    mm(psum[bank][0:64, j * 128:(j + 1) * 128], XF[:, h, 0:64], AM[:, 2, h, :], True, True,
                                       [rXF[hg], rAMt[2][hg]], [R_ps[bank]])
                                for c in range(2):
                                    cs = slice(c * 64, (c + 1) * 64)
                                    tt("dve", RH[:, hg * 4:(hg + 1) * 4, c, cs], v4(psum[bank][0:64, :])[:, :, cs],
                                       IN["rt"][b][:, hg * 4:(hg + 1) * 4, tl * 128 + c * 64:tl * 128 + (c + 1) * 64], ALU.add,
                                       [R_ps[bank], rI], [rRH])
                            if LVL < 4:
                                continue
                            tt("pool", DW[:], ident_f[0:64, 0:64].unsqueeze(1).unsqueeze(1).to_broadcast([64, 8, 2, 64]),
                               WT[b][:, :, tl * 2:tl * 2 + 2].unsqueeze(3).to_broadcast([64, 8, 2, 64]), ALU.mult, [rI, R_c], [rDW])
                            for hg in range(2):
                                bm_, bn_ = 2 + hg, 5 + hg
                                for j in range(4):
                                    h = hg * 4 + j
                                    for c in range(2):
                                        ps_ = slice(0, 64) if c == 0 else slice(0, 128)
                                        col = (j * 2 + c) * 64
                                        mm(psum[bm_][:, col:col + 64], XF[ps_, h, 0:128], TM[ps_, h, 64:128], True, True,
                                           [rXF[hg], rTM[hg]], [R_ps[bm_]])
                                        mm(psum[bn_][:, col:col + 64], TM[ps_, h, 64:192], XF[ps_, h, 64:128], True, False,
                                           [rXF[hg], rTM[hg]], [R_ps[bn_]])
                                        mm(psum[bn_][:, col:col + 64], TMf[ps_, h * 192 + 128:h * 192 + 256], TM[ps_, h, 0:64], False, True,
                                           [rTM[hg]], [R_ps[bn_]])
                                hs = slice(hg * 4, (hg + 1) * 4)
                                cp("dve", MTf[:, hs, :, :], psum[bm_][0:64, :].rearrange("p (h c i) -> p h c i", h=4, c=2),
                                   [R_ps[bm_]], [rMTf])
                                tt("pool", MTf[:, hs, 1, :], MTf[:, hs, 1, :], MTf[:, hs, 0, :], ALU.subtract, [rMTf], [rMTf])
                                tt("pool", MTt[:, hs, :, 0:64], MTf[:, hs, :, :], DW[:, hs, :, :], ALU.add, [rMTf, rDW], [rMTt])
                                cp("act", N0[:, hs, :, :], psum[bn_][0:64, :].rearrange("p (h c i) -> p h c i", h=4, c=2),
                                   [R_ps[bn_]], [rN0])
                                tt("pool", N0[:, hs, 1, :], N0[:, hs, 1, :], N0[:, hs, 0, :], ALU.subtract, [rN0], [rN0])
                            if LVL < 5:
                                continue
                            for ci, c in enumerate((0, 1) if d == 0 else (1, 0)):
                                for h in range(8):
                                    ys = slice(h * 64, (h + 1) * 64)
                                    mm(psum[4][:, ys], RH[:, h, c, :], Sb[:, h, :], False, ci == 1, [rRH, rSb], [R_ps[4]])
                                bank = ci
                                for h in range(8):
                                    mm(psum[bank][:, h * 64:(h + 1) * 64], MTt[:, h, c, :], Sb[:, h, :], True, True,
                                       [rMTt, rSb], [R_ps[bank]])
                                tt("dve", Sb[:], psum[bank][0:64, :].rearrange("p (h i) -> p h i", h=8), N0[:, :, c, :], ALU.add,
                                   [R_ps[bank], rN0], [rSb])
                            if LVL < 6:
                                continue
                            if d == 0:
                                cp("act", YF[yb][:], psum[4][:, :], [R_ps[4]], [rYF[yb]])
                                stor(yfw[gt0:gt0 + 128, :], YF[yb][:], [rYF[yb]], [rscr("yfw")], rYF[yb])
                            else:
                                tt("dve", YF[yb][:], psum[4][:, :], YL[yb][:], ALU.add, [R_ps[4], rYL[yb]], [rYF[yb]])
                                Y3 = YF[yb][:].rearrange("p (h i) -> p h i", h=8)
                                S.op("dve", lambda e, Y3=Y3: e.tensor_reduce(out=st8[:], in_=Y3, axis=AX.X, op=ALU.add),
                                     [rYF[yb]], [rst8])
                                ts("dve", st8[:], st8[:], -1.0 / 64, None, ALU.mult, None, [rst8], [rst8])
                                tt("pool", YC[:].rearrange("p (h i) -> p h i", h=8), Y3,
                                   st8[:].unsqueeze(2).to_broadcast([128, 8, 64]), ALU.add, [rYF[yb], rst8], [rYC])
                                act(YS[:], YC[:], AF.Square, [rYC], [rYS])
                                S.op("dve", lambda e: e.tensor_reduce(out=st8b[:], in_=YS[:].rearrange("p (h i) -> p h i", h=8),
                                                                      axis=AX.X, op=ALU.add), [rYS], [rst8b])
                                ts("dve", st8b[:], st8b[:], 1.0 / 64, 64e-5, ALU.mult, ALU.add, [rst8b], [rst8b])
                                ts("dve", st8b[:], st8b[:], -0.5, None, ALU.pow, None, [rst8b], [rst8b])
                                tt("pool", YN[:].rearrange("p (h i) -> p h i", h=8), YC[:].rearrange("p (h i) -> p h i", h=8),
                                   st8b[:].unsqueeze(2).to_broadcast([128, 8, 64]), ALU.mult, [rYC, rst8b], [rYN])
                                for fc in range(4):
                                    tr(psT[:, fc * 128:(fc + 1) * 128], YN[:, fc * 128:(fc + 1) * 128], ident_b[:],
                                       [rYN, R_c], [R_psT])
                                pv = psT[:, 0:512].rearrange("p (c t) -> p c t", c=4)
                                tt("dve", OT[:], pv, lnw[:].unsqueeze(2).to_broadcast([128, 4, 128]), ALU.mult, [R_psT, rM], [rOT])
                                tt("pool", OT[:], OT[:], lnb[:].unsqueeze(2).to_broadcast([128, 4, 128]), ALU.add, [rOT, rM], [rOT])
                                tt("dve", OT[:], OT[:], BONt[yb][:], ALU.add, [rOT, rBG[yb]], [rOT])
                                tt("pool", OR[yb][:], OT[:], Gt[yb][:], ALU.mult, [rOT, rBG[yb]], [rOR[yb]])
                                stor(orwT.rearrange("(c p) t -> p c t", p=128)[:, :, gt0:gt0 + 128], OR[yb][:], [rOR[yb]],
                                     [rscr("orwT")], rOR[yb])
            S.barrier()

        if phases is None:
            phases = ["p0"] + sum([["a%d" % l, "b1%d" % l, "b2%d" % l, "c1%d" % l, "c2%d" % l] for l in range(depth)], []) + ["e"]
        for ph in phases:
            if ph == "p0":
                phase_p0()
            elif ph == "e":
                phase_e(0)
            elif ph[0] == "a":
                phase_a(int(ph[1:]), 0)
            elif ph[:2] == "b1":
                phase_b1(int(ph[2:]))
            elif ph[:2] == "b2":
                phase_b2(int(ph[2:]))
            elif ph[:2] == "c1":
                phase_c1(int(ph[2:]), 0, 1)
            elif ph[:2] == "c2":
                phase_c2(int(ph[2:]), 1, 0)
        S.emit()
        nc._n_inst = S.ninst
    return nc


def core_inputs(x_seqs, mem_seqs, T, RS, typ, weights, depth):
    NT = T // 512
    SLOT_ST = max(NT // 4, 1)
    NSLOT = NT // SLOT_ST
    ROWS = T // 64
    xin = np.zeros((T, D), np.float32)
    memv = np.zeros((NSLOT, 256, D), np.float32)
    if typ == "P":
        xin[:] = x_seqs[0]
        for sl in range(NSLOT):
            memv[sl] = mem_seqs[0]
        starts = {0}
    else:
        L = RS * 64
        for i, xs in enumerate(x_seqs):
            xin[i * L:(i + 1) * L] = xs
            sl0 = (i * L) // (SLOT_ST * 512)
            memv[sl0] = mem_seqs[i]
        starts = set(range(0, T, L))
    bmv = np.ones((128, max(NT - 1, 1)), np.float32)
    for j in range(NT - 1):
        if (j + 1) * 512 in starts:
            bmv[:, j] = 0.0
    bm2 = np.ones((128, max(T // 256 - 1, 1)), np.float32)
    for j in range(T // 256 - 1):
        if (j + 1) * 256 in starts:
            bm2[:, j] = 0.0
    m = {"xin": xin, "mem": memv, "bm": bmv, "bm2": bm2, "rowbias": na_rowbias(ROWS, RS, typ)}
    m["btab"] = np.stack([na_btab(weights["na_rpb"][l]).reshape(64, -1) for l in range(depth)])
    for k, v in make_consts().items():
        m["c_" + k] = v
    for k, v in weights.items():
        if k == "na_rpb":
            continue
        if k == "rw_r_k":
            v = v.reshape(v.shape[0], 512)
        if k == "final_norm":
            v = v.reshape(1, D)
        m[k] = np.ascontiguousarray(v, dtype=np.float32)
    return m


_CACHE = {}


def kernel(**inputs):
    T, RS, depth = 8192, 32, 2
    xp = np.asarray(inputs["x_prompt"], np.float32)
    xs = np.asarray(inputs["x_sample"], np.float32)
    mp = np.asarray(inputs["mem_prompt"], np.float32)
    ms = np.asarray(inputs["mem_sample"], np.float32)
    weights = {k: np.asarray(v, np.float32) for k, v in inputs.items()
               if k not in ("x_prompt", "x_sample", "mem_prompt", "mem_sample")}
    in_maps = []
    for c in range(4):
        in_maps.append(core_inputs([xp[c]], [mp[c]], T, RS, "P", weights, depth))
    for c in range(4):
        sq = [2 * c, 2 * c + 1, 2 * c, 2 * c + 1]
        in_maps.append(core_inputs([xs[i] for i in sq], [ms[i] for i in sq], T, RS, "S", weights, depth))
    if "nc" not in _CACHE:
        _CACHE["nc"] = build(T, RS, depth)
    res = run_bass_kernel_spmd(_CACHE["nc"], in_maps, core_ids=list(range(8)))
    yp = np.stack([res.results[c]["yout"] for c in range(4)]).astype(np.float32)
    ys = np.zeros_like(xs)
    for c in range(4):
        y = res.results[4 + c]["yout"]
        ys[2 * c] = y[0:2048]
        ys[2 * c + 1] = y[2048:4096]
    return (yp, ys)
```

```python
from contextlib import ExitStack
import os
LVL = int(os.environ.get('B2DBG', '9'))
B2X = int(os.environ.get('B2X', '0'))
import numpy as np
import concourse.bass as bass
import concourse.mybir as mybir
from concourse.bass_utils import run_bass_kernel_spmd

F32 = mybir.dt.float32
BF16 = mybir.dt.bfloat16
AF = mybir.ActivationFunctionType
ALU = mybir.AluOpType
AX = mybir.AxisListType

D = 1024
NEG = -30000.0
ENGS = ("pe", "act", "dve", "pool", "sp")


class Res:
    __slots__ = ("name", "w", "r", "dsem", "multi")

    def __init__(self, name="", multi=False):
        self.name = name
        self.w = {}
        self.r = {}
        self.dsem = None
        self.multi = multi


class Sched:
    def __init__(self, nc, stack, n_dma_sems=64):
        self.nc = nc
        self.q = {e: [] for e in ENGS}
        self.cnt = {e: 0 for e in ENGS}
        self.sem = {e: stack.enter_context(nc.semaphore("s_" + e)) for e in ENGS}
        self.dma_sems = [stack.enter_context(nc.semaphore("d%d" % i)) for i in range(n_dma_sems)]
        self.dma_cnt = [0] * n_dma_sems
        self.dma_next = 0
        self.seen = {e: {} for e in ENGS}
        self.ninst = 0

    def _semobj(self, key):
        return self.sem[key] if isinstance(key, str) else self.dma_sems[key]

    def _deps(self, eng, reads, writes):
        toks = {}
        for r in reads:
            for k, v in r.w.items():
                if toks.get(k, 0) < v:
                    toks[k] = v
        for w in writes:
            for k, v in w.w.items():
                if toks.get(k, 0) < v:
                    toks[k] = v
            for k, v in w.r.items():
                if toks.get(k, 0) < v:
                    toks[k] = v
        waits = []
        seen = self.seen[eng]
        for k, v in toks.items():
            if k == eng and eng == "pe":
                continue
            if seen.get(k, 0) >= v:
                continue
            seen[k] = v
            waits.append((k, v))
        return waits

    def _mark(self, tok, reads, writes):
        k, v = tok
        for r in reads:
            if r.r.get(k, 0) < v:
                r.r[k] = v
        for w in writes:
            w.w[k] = v
            if not w.multi:
                w.r = {}

    def op(self, eng, fn, reads=(), writes=()):
        waits = self._deps(eng, reads, writes)
        self.cnt[eng] += 1
        self._mark((eng, self.cnt[eng]), reads, writes)
        self.q[eng].append((waits, fn, (eng, 1)))
        self.ninst += 1

    def dma(self, eng, fn, reads=(), writes=(), sem_res=None):
        waits = self._deps(eng, reads, writes)
        if sem_res is None:
            sem_res = writes[0]
        if sem_res.dsem is None:
            sem_res.dsem = self.dma_next % len(self.dma_sems)
            self.dma_next += 1
        k = sem_res.dsem
        self.dma_cnt[k] += 16
        self._mark((k, self.dma_cnt[k]), reads, writes)
        self.q[eng].append((waits, fn, (k, 16)))
        self.ninst += 1

    def barrier(self):
        tot = {e: self.cnt[e] for e in ENGS if self.cnt[e]}
        for k, c in enumerate(self.dma_cnt):
            if c:
                tot[k] = c
        for e in ENGS:
            waits = []
            for k, v in tot.items():
                if k == e:
                    continue
                if self.seen[e].get(k, 0) < v:
                    self.seen[e][k] = v
                    waits.append((k, v))
            if waits:
                self.q[e].append((waits, None, None))

    def emit(self):
        nc = self.nc
        fin = {e: self.cnt[e] for e in ENGS if self.cnt[e]}
        for k, c in enumerate(self.dma_cnt):
            if c:
                fin[k] = c
        handles = {"pe": "tensor", "act": "scalar", "dve": "vector", "pool": "gpsimd", "sp": "sync"}
        with nc.Block() as block:
            for e in ENGS:
                def body(engine, ops=self.q[e], is_last=(e == "sp")):
                    for waits, fn, inc in ops:
                        for wi, (wk, wv) in enumerate(waits):
                            engine.wait_ge(self._semobj(wk), wv)
                            if wi < len(waits) - 1 or fn is None:
                                engine.nop(nofuse=True)
                        if fn is not None:
                            fn(engine).then_inc(self._semobj(inc[0]), inc[1])
                    if is_last:
                        for k, v in fin.items():
                            engine.wait_ge(self._semobj(k), v)
                getattr(block, handles[e])(body)


def na_bands(ROWS, RS):
    bands = []
    for i in range(ROWS):
        p0 = min(max(i - 4, 0), ROWS - 8)
        b, il = divmod(i, RS)
        s0 = b * RS + min(max(il - 4, 0), RS - 8)
        lo, hi = min(p0, s0), max(p0, s0) + 8
        n = (hi - lo + 1) // 2
        if lo + 2 * n > ROWS:
            lo = ROWS - 2 * n
        bands.append([lo + 2 * c for c in range(n)])
    return bands


def na_rowbias(ROWS, RS, typ):
    bands = na_bands(ROWS, RS)
    cols = []
    for i, band in enumerate(bands):
        if typ == "P":
            w0 = min(max(i - 4, 0), ROWS - 8)
        else:
            b, il = divmod(i, RS)
            w0 = b * RS + min(max(il - 4, 0), RS - 8)
        for r in band:
            col = np.full(128, NEG, np.float32)
            for j in range(2):
                if w0 <= r + j < w0 + 8:
                    col[j * 64:(j + 1) * 64] = 0.0
            cols.append(col)
    return np.stack(cols, axis=1)


def na_btab(rpb):
    H = rpb.shape[0]
    out = np.zeros((64, H, 17, 64), np.float32)
    qc = np.arange(64)
    c0 = np.clip(qc - 8, 0, 48)
    for off in range(-7, 8):
        blk = np.full((64, H, 64), NEG, np.float32)
        for q in range(64):
            ks = np.arange(c0[q], c0[q] + 16)
            blk[q][:, ks] = rpb[:, off + 7, ks - q + 15]
        out[:, :, off + 8, :] = blk
    return out


def make_consts():
    c = {}
    c["ident"] = np.eye(128, dtype=np.float32)
    bd = np.zeros((128, 128), np.float32)
    bd[:64, :64] = 1.0
    bd[64:, 64:] = 1.0
    c["bd64"] = bd
    sm = np.ones((128, 512), np.float32)
    sm[:, ::64] = 0.0
    c["scanmask"] = sm
    s = np.arange(128)[:, None]
    t = np.arange(128)[None, :]
    same = (s // 64) == (t // 64)
    LS = (same & (s < t)).astype(np.float32)
    LI = (same & (s <= t)).astype(np.float32)
    US = (same & (s > t)).astype(np.float32)
    UI = (same & (s >= t)).astype(np.float32)
    c["ls4"] = np.tile(LS, (1, 4))
    c["li4"] = np.tile(LI, (1, 4))
    c["us4"] = np.tile(US, (1, 4))
    c["ui4"] = np.tile(UI, (1, 4))
    return c


CONST_ORDER = ("ident", "bd64", "scanmask", "ls4", "li4", "us4", "ui4")


def build(T, RS, depth=2, debug_outs=(), phases=None):
    NT = T // 512
    NTL = T // 128
    ROWS = T // 64
    NCH = T // 64
    SLOT_ST = max(NT // 4, 1)
    NSLOT = NT // SLOT_ST
    bands = na_bands(ROWS, RS)
    nslots_na = sum(len(b) for b in bands)
    NH2 = 2 * (T // 256 - 1)
    NH5 = 2 * (NT - 1)

    nc = bass.Bass("TRN2", target_bir_lowering=False)

    def din(name, shape, dt=F32):
        return nc.dram_tensor(name, list(shape), dt, kind="ExternalInput").ap()

    def dscr(name, shape, dt):
        kind = "ExternalOutput" if name in debug_outs else "Internal"
        return nc.dram_tensor(name, list(shape), dt, kind=kind).ap()

    xin = din("xin", [T, D])
    mem = din("mem", [NSLOT, 256, D])
    bm_in = din("bm", [128, max(NT - 1, 1)])
    bm2_in = din("bm2", [128, max(T // 256 - 1, 1)])
    rowbias_in = din("rowbias", [128, nslots_na])
    btab_in = din("btab", [depth, 64, 8 * 17 * 64])
    cst_in = {k: din("c_" + k, v.shape) for k, v in make_consts().items()}
    W = {}
    for name, shape in (("attn_norm", [depth, D]), ("w_in", [depth, D, 7072]), ("rw_conv", [depth, 3, 1952]),
                        ("rw_decay0", [depth, 2, 512]), ("rw_decay2", [depth, 2, 64, 512]), ("rw_a0", [depth, 2, 512]),
                        ("rw_a2", [depth, 2, 64, 512]), ("rw_g2", [depth, 160, 512]), ("rw_k_k", [depth, 512]),
                        ("rw_k_a", [depth, 512]), ("rw_r_k", [depth, 512]), ("rw_lnx_w", [depth, 512]),
                        ("rw_lnx_b", [depth, 512]), ("mem_norm", [depth, D]), ("w_mem_kv", [depth, D, 1024]),
                        ("w_branch", [depth, 3, 512, D]), ("w_out", [depth, D, D]), ("ffn_norm", [depth, D]),
                        ("w_up", [depth, D, 5632]), ("ffn_conv", [depth, 3, 5632]), ("ffn_conv_b", [depth, 5632]),
                        ("w_down", [depth, 2816, D]), ("final_norm", [1, D])):
        W[name] = din(name, shape)
    yout = nc.dram_tensor("yout", [T, D], F32, kind="ExternalOutput").ap()

    xT = [dscr("xT0", [D, T], F32), dscr("xT1", [D, T], F32)]
    qT = dscr("qT", [512, T], BF16)
    kT = dscr("kT", [512, T], BF16)
    vtm = dscr("vtm", [T, 512], BF16)
    omT = dscr("omT", [512, T], BF16)
    onaT = dscr("onaT", [512, T], BF16)
    orwT = dscr("orwT", [512, T], BF16)
    rws = {}
    for d in range(2):
        for nm in ("at", "bt", "bb", "rt", "kt", "kb"):
            rws[nm, d] = dscr("rw_%s%d" % (nm, d), [512, T], BF16)
        rws["wtot", d] = dscr("rw_wtot%d" % d, [512, NCH], F32)
    rws["v"] = dscr("rw_v", [512, T], BF16)
    rws["bonus"] = dscr("rw_bonus", [512, T], F32)
    rws["g"] = dscr("rw_g", [512, T], BF16)
    yfw = dscr("yfw", [T, 512], F32)

    R_xT = [Res("xT0", True), Res("xT1", True)]
    R_scr = {}

    def rscr(key):
        if key not in R_scr:
            R_scr[key] = Res(str(key), True)
        return R_scr[key]

    with ExitStack() as top:
        S = Sched(nc, top)
        psum = [top.enter_context(nc.psum_tensor("ps%d" % i, [128, 512], F32)) for i in range(7)]
        psT = top.enter_context(nc.psum_tensor("psT", [128, 1024], BF16))
        R_ps = [Res("ps%d" % i) for i in range(7)]
        R_psT = Res("psT")

        def mm(out, lhsT, rhs, start, stop, reads, writes):
            S.op("pe", lambda e: e.matmul(out, lhsT=lhsT, rhs=rhs, start=start, stop=stop), reads, writes)

        def tr(out, in_, ident, reads, writes):
            S.op("pe", lambda e: e.transpose(out, in_, ident), reads, writes)

        def act(out, in_, func, reads, writes, bias=None, scale=None, accum_out=None):
            kw = {}
            if bias is not None:
                kw["bias"] = bias
            if scale is not None:
                kw["scale"] = scale
            if accum_out is not None:
                kw["accum_out"] = accum_out
            S.op("act", lambda e: e.activation(out=out, in_=in_, func=func, **kw), reads, writes)

        def is_ps(ap):
            return hasattr(ap, "space") and "PSUM" in str(ap.space)

        def tt(eng, out, in0, in1, op, reads, writes):
            if eng == "pool" and (is_ps(out) or is_ps(in0) or is_ps(in1)):
                eng = "dve"
            S.op(eng, lambda e: e.tensor_tensor(out=out, in0=in0, in1=in1, op=op), reads, writes)

        def ts(eng, out, in0, s1, s2, op0, op1, reads, writes):
            if eng == "pool" and (is_ps(out) or is_ps(in0)):
                eng = "dve"
            if op1 is None and op0 == ALU.pow:
                assert s1 == -0.5
                S.op("act", lambda e: e.activation(out=out, in_=in0, func=AF.Ln), reads, writes)
                S.op("act", lambda e: e.activation(out=out, in_=out, func=AF.Exp, scale=-0.5), list(reads) + list(writes), writes)
            elif op1 is None:
                S.op(eng, lambda e: e.tensor_scalar(out=out, in0=in0, scalar1=s1, scalar2=None, op0=op0), reads, writes)
            else:
                S.op(eng, lambda e: e.tensor_scalar(out=out, in0=in0, scalar1=s1, scalar2=s2, op0=op0, op1=op1), reads, writes)

        def stt(eng, out, in0, scalar, in1, op0, op1, reads, writes):
            eng = "dve"
            S.op(eng, lambda e: e.scalar_tensor_tensor(out=out, in0=in0, scalar=scalar, in1=in1, op0=op0, op1=op1), reads, writes)

        def cp(eng, out, in_, reads, writes):
            if eng == "act":
                S.op("act", lambda e: e.copy(out=out, in_=in_), reads, writes)
            else:
                S.op(eng, lambda e: e.tensor_copy(out=out, in_=in_), reads, writes)

        def memset(eng, ap, val, writes):
            S.op(eng, lambda e: e.memset(ap, val), (), writes)

        def ld(out, in_, reads, writes, q="sp", nonc=False):
            if nonc:
                def f(e):
                    with nc.allow_non_contiguous_dma(reason="small strided load"):
                        return e.dma_start(out=out, in_=in_)
                S.dma(q, f, reads, writes)
            else:
                S.dma(q, lambda e: e.dma_start(out=out, in_=in_), reads, writes)

        def stor(out, in_, reads, writes, sem_res, q="sp"):
            S.dma(q, lambda e: e.dma_start(out=out, in_=in_), reads, writes, sem_res=sem_res)

        rr = [0]

        def next_ps():
            rr[0] = (rr[0] + 1) % 4
            return rr[0]

        ve = [0]

        def veng():
            ve[0] ^= 1
            return "dve" if ve[0] else "pool"

        uniq = [0]

        def sbt(stack, name, shape, dt):
            uniq[0] += 1
            return stack.enter_context(nc.sbuf_tensor("sb%d_%s" % (uniq[0], name), list(shape), dt))

        ident_f = sbt(top, "ident_f", [128, 128], F32)
        ident_b = sbt(top, "ident_b", [128, 128], BF16)
        ones_b = sbt(top, "ones_b", [128, 128], BF16)
        bd64_b = sbt(top, "bd64_b", [128, 128], BF16)
        bd64_f = sbt(top, "bd64_f", [128, 128], F32)
        R_c = Res("consts")
        ld(ident_f[:], cst_in["ident"][:, :], (), [R_c])
        ld(ident_b[:], cst_in["ident"][:, :], (), [R_c], q="pool")
        ld(bd64_b[:], cst_in["bd64"][:, :], (), [R_c], q="pool")
        ld(bd64_f[:], cst_in["bd64"][:, :], (), [R_c])
        memset("dve", ones_b[:], 1.0, [R_c])
        bm = sbt(top, "bm", [128, max(NT - 1, 1)], F32)
        bm2 = sbt(top, "bm2", [128, max(T // 256 - 1, 1)], F32)
        ld(bm[:], bm_in[:, :], (), [R_c])
        ld(bm2[:], bm2_in[:, :], (), [R_c])

        def rmsnorm_fm(X, rX, XN, rXN, gain, rg, n, SQ, rSQ, RSt, rRS, bank=4):
            act(SQ[:, :, 0:n], X[:, :, 0:n], AF.Square, [rX], [rSQ])
            for c in range(8):
                mm(psum[bank][:, 0:n], ones_b[:], SQ[:, c, 0:n], c == 0, c == 7, [rSQ, R_c], [R_ps[bank]])
            ts("dve", RSt[:, 0:n], psum[bank][:, 0:n], 1.0 / D, 1e-6, ALU.mult, ALU.add, [R_ps[bank]], [rRS])
            ts("dve", RSt[:, 0:n], RSt[:, 0:n], -0.5, None, ALU.pow, None, [rRS], [rRS])
            for c in range(8):
                stt(veng(), XN[:, c, 0:n], X[:, c, 0:n], gain[:, c:c + 1], RSt[:, 0:n], ALU.mult, ALU.mult,
                    [rX, rRS, rg], [rXN])

        def load_vec_fm(stack, name, src, nchunk, q="sp"):
            t = sbt(stack, name, [128, nchunk], F32)
            r = Res(name)
            ld(t[:], src.rearrange("(c p) -> p c", p=128), (), [r], nonc=True)
            return t, r

        def load_w(stack, name, src, kc, ncols, col0=0, eng_q="pool"):
            t = sbt(stack, name, [128, kc, ncols], BF16)
            r = Res(name)
            for c in range(kc):
                ld(t[:, c, :], src[c * 128:(c + 1) * 128, col0:col0 + ncols], (), [r], q=eng_q)
            return t, r

        def phase_p0():
            with ExitStack() as st:
                XI = [sbt(st, "p0_xi%d" % i, [128, D], F32) for i in range(2)]
                rXI = [Res("xi0"), Res("xi1")]
                XO = [sbt(st, "p0_xo%d" % i, [128, 8, 128], F32) for i in range(2)]
                rXO = [Res("xo0"), Res("xo1")]
                for i in range(NTL):
                    b = i % 2
                    ld(XI[b][:], xin[i * 128:(i + 1) * 128, :], (), [rXI[b]])
                    for half in range(2):
                        bank = next_ps()
                        for c4 in range(4):
                            c = half * 4 + c4
                            tr(psum[bank][:, c4 * 128:(c4 + 1) * 128], XI[b][:, c * 128:(c + 1) * 128], ident_f[:],
                               [rXI[b], R_c], [R_ps[bank]])
                        cp("act" if half else "dve", XO[b][:, half * 4:(half + 1) * 4, :],
                           psum[bank][:, :].rearrange("p (c t) -> p c t", c=4), [R_ps[bank]], [rXO[b]])
                    stor(xT[0].rearrange("(c p) t -> p c t", p=128)[:, :, i * 128:(i + 1) * 128], XO[b][:],
                         [rXO[b]], [R_xT[0]], rXO[b])
            S.barrier()

        def phase_e(src):
            with ExitStack() as st:
                gain, rg = load_vec_fm(st, "e_gain", W["final_norm"][0], 8)
                X = [sbt(st, "e_x%d" % i, [128, 8, 512], F32) for i in range(2)]
                rX = [Res(), Res()]
                XN = sbt(st, "e_xn", [128, 8, 512], F32)
                rXN = Res()
                SQ = sbt(st, "e_sq", [128, 8, 512], BF16)
                rSQ = Res()
                RSt = sbt(st, "e_rs", [128, 512], F32)
                rRS = Res()
                YO = [sbt(st, "e_yo%d" % i, [128, D], F32) for i in range(2)]
                rYO = [Res(), Res()]
                xv = xT[src].rearrange("(c p) t -> p c t", p=128)
                ld(X[0][:], xv[:, :, 0:512], [R_xT[src]], [rX[0]])
                k = 0
                for s in range(NT):
                    b = s % 2
                    if s + 1 < NT:
                        ld(X[1 - b][:], xv[:, :, (s + 1) * 512:(s + 2) * 512], [R_xT[src]], [rX[1 - b]])
                    rmsnorm_fm(X[b], rX[b], XN, rXN, gain, rg, 512, SQ, rSQ, RSt, rRS)
                    for tl in range(4):
                        yb = k % 2
                        k += 1
                        for half in range(2):
                            bank = next_ps()
                            for c4 in range(4):
                                c = half * 4 + c4
                                tr(psum[bank][:, c4 * 128:(c4 + 1) * 128], XN[:, c, tl * 128:(tl + 1) * 128], ident_f[:],
                                   [rXN, R_c], [R_ps[bank]])
                            cp("act" if half else "dve", YO[yb][:, half * 512:(half + 1) * 512], psum[bank][:, :],
                               [R_ps[bank]], [rYO[yb]])
                        t0 = s * 512 + tl * 128
                        stor(yout[t0:t0 + 128, :], YO[yb][:], [rYO[yb]], [Res()], rYO[yb])
            S.barrier()

        def phase_c2(l, src, dst):
            TW = 256
            NTW = T // TW
            NW = TW + 2
            banks = (0, 1, 2, 3, 5, 6)
            bk = [0]

            def nb():
                bk[0] = (bk[0] + 1) % len(banks)
                return banks[bk[0]]

            with ExitStack() as st:
                Wu, rWu = load_w(st, "c2_wu", W["w_up"][l], 8, 5632)
                Wd, rWd = load_w(st, "c2_wd", W["w_down"][l], 22, D)
                gain, rg = load_vec_fm(st, "c2_gain", W["ffn_norm"][l], 8)
                cw = sbt(st, "c2_cw", [128, 3, 44], F32)
                rcw = Res()
                for k in range(3):
                    ld(cw[:, k, :], W["ffn_conv"][l, k].rearrange("(c p) -> p c", p=128), (), [rcw], nonc=True)
                cb, rcb = load_vec_fm(st, "c2_cb", W["ffn_conv_b"][l], 44)
                X = [sbt(st, "c2_x%d" % i, [128, 8, NW], F32) for i in range(2)]
                rX = [Res(), Res()]
                XN = sbt(st, "c2_xn", [128, 8, NW], BF16)
                rXN = Res()
                SQ = sbt(st, "c2_sq", [128, 8, NW], BF16)
                rSQ = Res()
                RSt = sbt(st, "c2_rs", [128, NW], F32)
                rRS = Res()
                G = sbt(st, "c2_g", [128, 22, TW], BF16)
                rG = Res()
                NB = 3
                CV = [sbt(st, "c2_cv%d" % i, [128, TW], F32) for i in range(NB)]
                rCV = [Res() for _ in range(NB)]
                CG = [sbt(st, "c2_cg%d" % i, [128, TW], F32) for i in range(NB)]
                rCG = [Res() for _ in range(NB)]
                SGt = [sbt(st, "c2_sg%d" % i, [128, TW], F32) for i in range(NB)]
                rSGt = [Res() for _ in range(NB)]
                xv = xT[src].rearrange("(c p) t -> p c t", p=128)
                xo = xT[dst].rearrange("(c p) t -> p c t", p=128)

                def load_x(s):
                    b = s % 2
                    t0 = s * TW
                    lo = max(t0 - 1, 0)
                    hi = min(t0 + TW + 1, T)
                    ld(X[b][:, :, lo - (t0 - 1):hi - (t0 - 1)], xv[:, :, lo:hi], [R_xT[src]], [rX[b]])
                    if s == 0:
                        memset("pool", X[b][:, :, 0:1], 0.0, [rX[b]])
                    if s == NTW - 1:
                        memset("pool", X[b][:, :, NW - 1:NW], 0.0, [rX[b]])

                load_x(0)
                for s in range(NTW):
                    b = s % 2
                    t0 = s * TW
                    if s + 1 < NTW:
                        load_x(s + 1)
                    rmsnorm_fm(X[b], rX[b], XN, rXN, gain, rg, NW, SQ, rSQ, RSt, rRS)
                    if s > 0:
                        ts("pool", XN[:, :, 0:1], XN[:, :, 0:1], bm2[:, s - 1:s], None, ALU.mult, None, [rXN, R_c], [rXN])
                    if s < NTW - 1:
                        ts("pool", XN[:, :, NW - 1:NW], XN[:, :, NW - 1:NW], bm2[:, s:s + 1], None, ALU.mult, None, [rXN, R_c], [rXN])
                    for j in range(22):
                        cbuf = j % NB
                        pair = ((j, CV[cbuf], rCV[cbuf], nb()), (22 + j, CG[cbuf], rCG[cbuf], nb()))
                        for (uc, Ct, rC, bank) in pair:
                            for kc in range(8):
                                mm(psum[bank][:, 0:NW], Wu[:, kc, uc * 128:(uc + 1) * 128], XN[:, kc, :], kc == 0, kc == 7,
                                   [rWu, rXN], [R_ps[bank]])
                        for (uc, Ct, rC, bank) in pair:
                            act(Ct[:, :], psum[bank][:, 1:TW + 1], AF.Identity, [R_ps[bank], rcw, rcb], [rC], bias=cb[:, uc:uc + 1],
                                scale=cw[:, 1, uc:uc + 1])
                        for (uc, Ct, rC, bank) in pair:
                            stt("dve", Ct[:, :], psum[bank][:, 0:TW], cw[:, 0, uc:uc + 1], Ct[:, :], ALU.mult, ALU.add,
                                [R_ps[bank], rcw, rC], [rC])
                        for (uc, Ct, rC, bank) in pair:
                            stt("dve", Ct[:, :], psum[bank][:, 2:TW + 2], cw[:, 2, uc:uc + 1], Ct[:, :], ALU.mult, ALU.add,
                                [R_ps[bank], rcw, rC], [rC])
                        act(SGt[cbuf][:], CG[cbuf][:], AF.Silu, [rCG[cbuf]], [rSGt[cbuf]])
                        tt("pool", G[:, j, :], SGt[cbuf][:], CV[cbuf][:], ALU.mult, [rSGt[cbuf], rCV[cbuf]], [rG])
                    for oc in range(8):
                        bank = nb()
                        for kc in range(22):
                            mm(psum[bank][:, 0:TW], Wd[:, kc, oc * 128:(oc + 1) * 128], G[:, kc, :], kc == 0, kc == 21,
                               [rWd, rG], [R_ps[bank]])
                        tt("dve", X[b][:, oc, 1:TW + 1], X[b][:, oc, 1:TW + 1], psum[bank][:, 0:TW], ALU.add, [rX[b], R_ps[bank]], [rX[b]])
                    stor(xo[:, :, t0:t0 + TW], X[b][:, :, 1:TW + 1], [rX[b]], [R_xT[dst]], rX[b])
            S.barrier()

        def phase_c1(l, src, dst):
            with ExitStack() as st:
                Wg, rWg = load_w(st, "c1_wg", W["w_in"][l], 8, 3072, col0=4000)
                Wb = sbt(st, "c1_wb", [128, 12, D], BF16)
                rWb = Res()
                for b in range(3):
                    for kc in range(4):
                        ld(Wb[:, b * 4 + kc, :], W["w_branch"][l, b, kc * 128:(kc + 1) * 128, :], (), [rWb], q="pool")
                Wo, rWo = load_w(st, "c1_wo", W["w_out"][l], 8, D)
                gain, rg = load_vec_fm(st, "c1_gain", W["attn_norm"][l], 8)
                Xs = [sbt(st, "c1_x%d" % i, [128, 8, 512], F32) for i in range(2)]
                rXs = [Res(), Res()]
                XN = sbt(st, "c1_xn", [128, 8, 512], BF16)
                rXN = Res()
                SQ = sbt(st, "c1_sq", [128, 8, 512], BF16)
                rSQ = Res()
                RSt = sbt(st, "c1_rs", [128, 512], F32)
                rRS = Res()
                OBs = [sbt(st, "c1_ob%d" % i, [128, 12, 512], BF16) for i in range(2)]
                rOBs = [Res(), Res()]
                MG = sbt(st, "c1_mg", [128, 8, 512], BF16)
                rMG = Res()
                SGt = [sbt(st, "c1_sg%d" % i, [128, 512], F32) for i in range(2)]
                rSGt = [Res(), Res()]
                ACCs = [sbt(st, "c1_acc%d" % i, [128, 512], F32) for i in range(2)]
                rACCs = [Res(), Res()]
                xv = xT[src].rearrange("(c p) t -> p c t", p=128)
                xo = xT[dst].rearrange("(c p) t -> p c t", p=128)
                srcs = (onaT, orwT, omT)
                rsrcs = (rscr("onaT"), rscr("orwT"), rscr("omT"))
                k = 0

                def loads(s):
                    t0 = s * 512
                    ld(Xs[s % 2][:], xv[:, :, t0:t0 + 512], [R_xT[src]], [rXs[s % 2]])
                    for b in range(3):
                        ld(OBs[s % 2][:, b * 4:(b + 1) * 4, :], srcs[b].rearrange("(c p) t -> p c t", p=128)[:, :, t0:t0 + 512],
                           [rsrcs[b]], [rOBs[s % 2]])

                loads(0)
                for s in range(NT):
                    t0 = s * 512
                    if s + 1 < NT:
                        loads(s + 1)
                    X, rX, OB, rOB = Xs[s % 2], rXs[s % 2], OBs[s % 2], rOBs[s % 2]
                    rmsnorm_fm(X, rX, XN, rXN, gain, rg, 512, SQ, rSQ, RSt, rRS)
                    for mc in range(8):
                        ACC, rACC = ACCs[mc % 2], rACCs[mc % 2]
                        for b in range(3):
                            bg = next_ps()
                            for kc in range(8):
                                mm(psum[bg][:, :], Wg[:, kc, b * 1024 + mc * 128:b * 1024 + (mc + 1) * 128], XN[:, kc, :],
                                   kc == 0, kc == 7, [rWg, rXN], [R_ps[bg]])
                            sb_ = k % 2
                            k += 1
                            act(SGt[sb_][:], psum[bg][:, :], AF.Sigmoid, [R_ps[bg]], [rSGt[sb_]])
                            bp = next_ps()
                            for kc in range(4):
                                mm(psum[bp][:, :], Wb[:, b * 4 + kc, mc * 128:(mc + 1) * 128], OB[:, b * 4 + kc, :],
                                   kc == 0, kc == 3, [rWb, rOB], [R_ps[bp]])
                            if b == 0:
                                tt("dve", ACC[:], SGt[sb_][:], psum[bp][:, :], ALU.mult, [rSGt[sb_], R_ps[bp]], [rACC])
                            else:
                                tt("dve", SGt[sb_][:], SGt[sb_][:], psum[bp][:, :], ALU.mult, [rSGt[sb_], R_ps[bp]],
                                   [rSGt[sb_]])
                                if b == 1:
                                    tt("pool", ACC[:], ACC[:], SGt[sb_][:], ALU.add, [rACC, rSGt[sb_]], [rACC])
                                else:
                                    tt("pool", MG[:, mc, :], ACC[:], SGt[sb_][:], ALU.add, [rACC, rSGt[sb_]], [rMG])
                    for oc in range(8):
                        bank = next_ps()
                        for kc in range(8):
                            mm(psum[bank][:, :], Wo[:, kc, oc * 128:(oc + 1) * 128], MG[:, kc, :], kc == 0, kc == 7,
                               [rWo, rMG], [R_ps[bank]])
                        tt(veng(), X[:, oc, :], X[:, oc, :], psum[bank][:, :], ALU.add, [rX, R_ps[bank]], [rX])
                    stor(xo[:, :, t0:t0 + 512], X[:], [rX], [R_xT[dst]], rX)
            S.barrier()

        def phase_a(l, src):
            with ExitStack() as st:
                Wi, rWi = load_w(st, "a_wi", W["w_in"][l], 8, 4000)
                gain, rg = load_vec_fm(st, "a_gain", W["attn_norm"][l], 8)
                KmT = sbt(st, "a_kmT", [128, NSLOT, 4, 256], BF16)
                Vm = sbt(st, "a_vm", [128, NSLOT, 2, 512], BF16)
                rKV = Res()
                with ExitStack() as st2:
                    Wkv, rWkv = load_w(st2, "a_wkv", W["w_mem_kv"][l], 8, 1024)
                    gm, rgm = load_vec_fm(st2, "a_gm", W["mem_norm"][l], 8)
                    MT = sbt(st2, "a_mt", [128, D], F32)
                    rMT = Res()
                    MS = sbt(st2, "a_ms", [128, D], F32)
                    rMS = Res()
                    MB = sbt(st2, "a_mb", [128, D], BF16)
                    rMB = Res()
                    ssq = sbt(st2, "a_ssq", [128, 1], F32)
                    rssq = Res()
                    memT = sbt(st2, "a_memT", [128, 8, 256], BF16)
                    rmemT = Res()
                    for sl in range(NSLOT):
                        for mc in range(2):
                            ld(MT[:], mem[sl, mc * 128:(mc + 1) * 128, :], (), [rMT])
                            act(MS[:], MT[:], AF.Square, [rMT], [rMS, rssq], accum_out=ssq[:])
                            ts("dve", ssq[:], ssq[:], 1.0 / D, 1e-6, ALU.mult, ALU.add, [rssq], [rssq])
                            ts("dve", ssq[:], ssq[:], -0.5, None, ALU.pow, None, [rssq], [rssq])
                            ts("dve", MB[:], MT[:], ssq[:, 0:1], None, ALU.mult, None, [rMT, rssq], [rMB])
                            for c in range(8):
                                tr(psT[:, c * 128:(c + 1) * 128], MB[:, c * 128:(c + 1) * 128], ident_b[:], [rMB, R_c],
                                   [R_psT])
                            for c in range(8):
                                ts(veng(), memT[:, c, mc * 128:(mc + 1) * 128], psT[:, c * 128:(c + 1) * 128],
                                   gm[:, c:c + 1], None, ALU.mult, None, [R_psT, rgm], [rmemT])
                        for h in range(4):
                            bank = next_ps()
                            for kc in range(8):
                                mm(psum[bank][:, 0:256], Wkv[:, kc, h * 128:(h + 1) * 128], memT[:, kc, :], kc == 0, kc == 7,
                                   [rWkv, rmemT], [R_ps[bank]])
                            cp("act", KmT[:, sl, h, :], psum[bank][:, 0:256], [R_ps[bank]], [rKV])
                        for mc in range(2):
                            bank = next_ps()
                            for kc in range(8):
                                mm(psum[bank][:, :], memT[:, kc, mc * 128:(mc + 1) * 128], Wkv[:, kc, 512:1024], kc == 0,
                                   kc == 7, [rWkv, rmemT], [R_ps[bank]])
                            cp("dve", Vm[:, sl, mc, :], psum[bank][:, :], [R_ps[bank]], [rKV])
                    S.barrier()
                cw = sbt(st, "a_cw", [128, 3, 16], F32)
                rcw = Res()
                memset("dve", cw[:], 0.0, [rcw])
                for k in range(3):
                    ld(cw[:, k, 0:15], W["rw_conv"][l, k, 0:1920].rearrange("(c p) -> p c", p=128), (), [rcw], nonc=True)
                    ld(cw[0:32, k, 15:16], W["rw_conv"][l, k, 1920:1952].rearrange("(c p) -> p c", p=32), (), [rcw],
                       nonc=True)
                dec0 = sbt(st, "a_dec0", [128, 2, 4], F32)
                a0 = sbt(st, "a_a0", [128, 2, 4], F32)
                rsm = Res()
                for d in range(2):
                    ld(dec0[:, d, :], W["rw_decay0"][l, d].rearrange("(c p) -> p c", p=128), (), [rsm], nonc=True)
                    ld(a0[:, d, :], W["rw_a0"][l, d].rearrange("(c p) -> p c", p=128), (), [rsm], nonc=True)
                kkv, r1 = load_vec_fm(st, "a_kk", W["rw_k_k"][l], 4)
                kav, r2 = load_vec_fm(st, "a_ka", W["rw_k_a"][l], 4)
                rkv, r3 = load_vec_fm(st, "a_rk", W["rw_r_k"][l], 4)
                D2 = sbt(st, "a_d2", [128, 512], BF16)
                A2 = sbt(st, "a_a2", [128, 512], BF16)
                G2 = sbt(st, "a_g2", [128, 2, 512], BF16)
                ld(D2[:], W["rw_decay2"][l].rearrange("d l f -> (d l) f"), (), [rsm], q="pool")
                ld(A2[:], W["rw_a2"][l].rearrange("d l f -> (d l) f"), (), [rsm], q="pool")
                ld(G2[:, 0, :], W["rw_g2"][l, 0:128, :], (), [rsm], q="pool")
                ld(G2[0:32, 1, :], W["rw_g2"][l, 128:160, :], (), [rsm], q="pool")
                scanm = sbt(st, "a_scanm", [128, 512], F32)
                ld(scanm[:], cst_in["scanmask"][:, :], (), [rsm])
                rsmall = [rsm, r1, r2, r3, rcw]

                XN = sbt(st, "a_xn", [128, 8, 512], BF16)
                rXN = Res()
                RSt = sbt(st, "a_rs", [128, 512], F32)
                rRS = Res()
                QK = sbt(st, "a_qk", [128, 8, 512], BF16)
                rQK = Res()
                SQ, rSQ = QK, rQK
                VT = sbt(st, "a_vt", [128, 4, 512], BF16)
                rVT = Res()
                QM = sbt(st, "a_qm", [128, 4, 512], BF16)
                rQM = Res()
                PT = [sbt(st, "a_pt%d" % i, [128, 512], BF16) for i in range(2)]
                rPT = [Res(), Res()]
                RD = sbt(st, "a_rd", [128, 512], F32)
                rRD = Res()
                OM = sbt(st, "a_om", [128, 4, 512], BF16)
                rOM = Res()
                ZC = sbt(st, "a_zc", [128, 16, 512], F32)
                rZC = Res()
                X, rX = ZC[:, 0:8, :], rZC
                HZ = sbt(st, "a_hz", [128, 16, max(NH5, 2)], F32)
                rHZ = Res()
                xv = xT[src].rearrange("(c p) t -> p c t", p=128)
                rwcols = [1536 + 128 * i for i in range(15)] + [1536 + 1920]
                rwm = [128] * 15 + [32]
                if NT > 1:
                    XH = sbt(st, "a_xh", [128, 8, NH5], F32)
                    rXH = Res()
                    XHN = sbt(st, "a_xhn", [128, 8, NH5], BF16)
                    rXHN = Res()
                    for c in range(8):
                        srcv = xT[src][c * 128:(c + 1) * 128, 511:T - 1].rearrange("p (j w) -> p j w", w=512)[:, :, 0:2]
                        ld(XH[:, c, :].rearrange("p (j w) -> p j w", w=2), srcv, [R_xT[src]], [rXH], nonc=True)
                    rmsnorm_fm(XH, rXH, XHN, rXHN, gain, rg, NH5, SQ, rSQ, RSt, rRS)
                    memset("dve", HZ[:], 0.0, [rHZ])
                    for zc in range(16):
                        bank = next_ps()
                        m = rwm[zc]
                        for kc in range(8):
                            mm(psum[bank][0:m, 0:NH5], Wi[:, kc, rwcols[zc]:rwcols[zc] + m], XHN[:, kc, :], kc == 0, kc == 7,
                               [rWi, rXHN], [R_ps[bank]])
                        tt("dve", HZ[0:m, zc, :].rearrange("p (j w) -> p j w", w=2),
                           psum[bank][0:m, 0:NH5].rearrange("p (j w) -> p j w", w=2),
                           bm[0:m, 0:NT - 1].unsqueeze(2).to_broadcast([m, NT - 1, 2]), ALU.mult, [R_ps[bank], R_c], [rHZ])

                def tmp(name, dt=F32):
                    return sbt(st, "a_t_" + name, [128, 512], dt), Res(name)

                TH, rTH = tmp("th", BF16)
                XAb, rXAb = tmp("xab", BF16)
                XGb = sbt(st, "a_t_xgb", [128, 2, 512], BF16)
                rXGb = Res()
                SGm, rSGm = tmp("sg")
                Pc, rPc = tmp("pc")
                Pe, rPe = tmp("pe")
                Pt, rPt = tmp("pt")
                E1, rE1 = tmp("e1")
                E2, rE2 = tmp("e2")
                Ee, rEe = tmp("ee")
                Eb, rEb = tmp("eb")
                AS, rAS = tmp("as")
                KD, rKD = tmp("kd")
                KK, rKK = tmp("kk")
                KQ, rKQ = tmp("kq", BF16)
                Bv, rBv = tmp("b")
                BON, rBON = tmp("bon")
                RKb, rRKb = tmp("rkb", BF16)
                TMP, rTMP = tmp("tmp")
                RI, rRI = TMP, rTMP
                OUTS = {}
                for nm in ("at", "bt", "bb", "rt", "kt", "kb"):
                    OUTS[nm] = (sbt(st, "a_o_" + nm, [128, 512], BF16), Res(nm))
                VRb, rVRb = tmp("vrb", BF16)
                Gb, rGb = tmp("gb", BF16)
                WTo = sbt(st, "a_wto", [128, 8], F32)
                rWTo = Res()

                for s in range(NT):
                    t0 = s * 512
                    slot = s // SLOT_ST
                    ld(X[:], xv[:, :, t0:t0 + 512], [R_xT[src]], [rX])
                    rmsnorm_fm(X, rX, XN, rXN, gain, rg, 512, SQ, rSQ, RSt, rRS)
                    def d_qk(c):
                        bank = next_ps()
                        for kc in range(8):
                            mm(psum[bank][:, :], Wi[:, kc, c * 128:(c + 1) * 128], XN[:, kc, :], kc == 0, kc == 7,
                               [rWi, rXN], [R_ps[bank]])
                        if c < 4:
                            act(QK[:, c, :], psum[bank][:, :], AF.Copy, [R_ps[bank]], [rQK], scale=0.125)
                        else:
                            cp("dve", QK[:, c, :], psum[bank][:, :], [R_ps[bank]], [rQK])
                        if c == 3:
                            stor(qT.rearrange("(c p) t -> p c t", p=128)[:, :, t0:t0 + 512], QK[:, 0:4, :], [rQK], [rscr("qT")], rQK)
                        if c == 7:
                            stor(kT.rearrange("(c p) t -> p c t", p=128)[:, :, t0:t0 + 512], QK[:, 4:8, :], [rQK], [rscr("kT")], rQK)

                    def d_v(tl):
                        bank = next_ps()
                        for kc in range(8):
                            mm(psum[bank][:, :], XN[:, kc, tl * 128:(tl + 1) * 128], Wi[:, kc, 1024:1536], kc == 0, kc == 7,
                               [rWi, rXN], [R_ps[bank]])
                        cp("act" if tl % 2 else "dve", VT[:, tl, :], psum[bank][:, :], [R_ps[bank]], [rVT])
                        if tl == 3:
                            stor(vtm[t0:t0 + 512, :].rearrange("(c p) f -> p c f", p=128), VT[:], [rVT], [rscr("vtm")], rVT)

                    def d_mq(h):
                        bank = next_ps()
                        for kc in range(8):
                            mm(psum[bank][:, :], Wi[:, kc, 3488 + h * 128:3488 + (h + 1) * 128], XN[:, kc, :], kc == 0, kc == 7,
                               [rWi, rXN], [R_ps[bank]])
                        act(QM[:, h, :], psum[bank][:, :], AF.Copy, [R_ps[bank]], [rQM], scale=128 ** -0.5)

                    def d_ma(h):
                        for mc in range(2):
                            bank = next_ps()
                            mm(psum[bank][:, :], KmT[:, slot, h, mc * 128:(mc + 1) * 128], QM[:, h, :], True, True,
                               [rKV, rQM], [R_ps[bank]])
                            act(PT[mc][:], psum[bank][:, :], AF.Exp, [R_ps[bank]], [rPT[mc]])
                        bo, bd_ = next_ps(), next_ps()
                        for mc in range(2):
                            mm(psum[bo][:, :], Vm[:, slot, mc, h * 128:(h + 1) * 128], PT[mc][:], mc == 0, mc == 1,
                               [rKV, rPT[mc]], [R_ps[bo]])
                        for mc in range(2):
                            mm(psum[bd_][:, :], ones_b[:], PT[mc][:], mc == 0, mc == 1, [R_c, rPT[mc]], [R_ps[bd_]])
                        act(RD[:], psum[bd_][:, :], AF.Ln, [R_ps[bd_]], [rRD])
                        act(RD[:], RD[:], AF.Exp, [rRD], [rRD], scale=-1.0)
                        tt("dve", OM[:, h, :], psum[bo][:, :], RD[:], ALU.mult, [R_ps[bo], rRD], [rOM])
                        if h == 3:
                            stor(omT.rearrange("(c p) t -> p c t", p=128)[:, :, t0:t0 + 512], OM[:], [rOM], [rscr("omT")], rOM)

                    def dense_slice(it):
                        d_qk(it)
                        if it % 2 == 0:
                            d_v(it // 2)
                            d_mq(it // 2)
                        else:
                            d_ma(it // 2)

                    rZCc = [Res() for _ in range(16)]
                    for r_ in rZCc:
                        for k_, v_ in list(rZC.r.items()) + list(rZC.w.items()):
                            if r_.r.get(k_, 0) < v_:
                                r_.r[k_] = v_
                    for z0 in range(0, 16, 2):
                        grp = []
                        for zc in (z0, z0 + 1):
                            bank = next_ps()
                            m = rwm[zc]
                            for kc in range(8):
                                mm(psum[bank][0:m, :], Wi[:, kc, rwcols[zc]:rwcols[zc] + m], XN[:, kc, :], kc == 0, kc == 7,
                                   [rWi, rXN], [R_ps[bank]])
                            grp.append((zc, m, bank))
                        for (zc, m, bank) in grp:
                            act(ZC[0:m, zc, :], psum[bank][0:m, :], AF.Copy, [R_ps[bank], rcw], [rZCc[zc]], scale=cw[0:m, 1, zc:zc + 1])
                        for (zc, m, bank) in grp:
                            stt("dve", ZC[0:m, zc, 1:512], psum[bank][0:m, 0:511], cw[0:m, 0, zc:zc + 1], ZC[0:m, zc, 1:512], ALU.mult,
                                ALU.add, [R_ps[bank], rcw, rZCc[zc]], [rZCc[zc]])
                        for (zc, m, bank) in grp:
                            stt("dve", ZC[0:m, zc, 0:511], psum[bank][0:m, 1:512], cw[0:m, 2, zc:zc + 1], ZC[0:m, zc, 0:511], ALU.mult,
                                ALU.add, [R_ps[bank], rcw, rZCc[zc]], [rZCc[zc]])
                        for (zc, m, bank) in grp:
                            if s > 0:
                                stt("dve", ZC[0:m, zc, 0:1], HZ[0:m, zc, 2 * (s - 1):2 * (s - 1) + 1], cw[0:m, 0, zc:zc + 1],
                                    ZC[0:m, zc, 0:1], ALU.mult, ALU.add, [rHZ, rcw, rZCc[zc]], [rZCc[zc]])
                        for (zc, m, bank) in grp:
                            if s < NT - 1:
                                stt("dve", ZC[0:m, zc, 511:512], HZ[0:m, zc, 2 * s + 1:2 * s + 2], cw[0:m, 2, zc:zc + 1],
                                    ZC[0:m, zc, 511:512], ALU.mult, ALU.add, [rHZ, rcw, rZCc[zc]], [rZCc[zc]])
                    rZC.r = {}
                    for r_ in rZCc:
                        for k_, v_ in r_.w.items():
                            if rZC.w.get(k_, 0) < v_:
                                rZC.w[k_] = v_
                    act(TH[:], ZC[:, 12, :], AF.Tanh, [rZC], [rTH])
                    cp("pool", XAb[:], ZC[:, 13, :], [rZC], [rXAb])
                    act(XGb[:, 0, :], ZC[:, 14, :], AF.Sigmoid, [rZC], [rXGb])
                    act(XGb[0:32, 1, :], ZC[0:32, 15, :], AF.Sigmoid, [rZC], [rXGb])
                    for hp in range(4):
                        fs = slice(hp * 128, (hp + 1) * 128)
                        Rr = ZC[:, hp, :]
                        Kr = ZC[:, 4 + hp, :]
                        Vr = ZC[:, 8 + hp, :]
                        mm(psum[4][:, :], G2[:, 0, fs], XGb[:, 0, :], True, False, rsmall + [rXGb], [R_ps[4]])
                        mm(psum[4][:, :], G2[0:32, 1, fs], XGb[0:32, 1, :], False, True, rsmall + [rXGb], [R_ps[4]])
                        cp("act", Gb[:], psum[4][:, :], [R_ps[4]], [rGb])
                        stor(rws["g"][fs, t0:t0 + 512], Gb[:], [rGb], [rscr("g")], rGb)
                        cp("pool", VRb[:], Vr, [rZC], [rVRb])
                        stor(rws["v"][fs, t0:t0 + 512], VRb[:], [rVRb], [rscr("v")], rVRb)
                        ts("dve", KK[:], Kr, kkv[:, hp:hp + 1], None, ALU.mult, None, [rZC] + rsmall, [rKK])
                        tt("pool", KQ[:], KK[:], KK[:], ALU.mult, [rKK], [rKQ])
                        mm(psum[4][:, :], bd64_b[:], KQ[:], True, True, [R_c, rKQ], [R_ps[4]])
                        ts("dve", RI[:], psum[4][:, :], 1e-24, None, ALU.max, None, [R_ps[4]], [rRI])
                        ts("dve", RI[:], RI[:], -0.5, None, ALU.pow, None, [rRI], [rRI])
                        tt("dve", KK[:], KK[:], RI[:], ALU.mult, [rKK, rRI], [rKK])
                        for d in range(2):
                            ds = slice(d * 64, (d + 1) * 64)
                            cexp = 0.6065306597126334
                            mm(psum[5][:, :], D2[ds, fs], TH[ds, :], True, True, rsmall + [rTH], [R_ps[5]])
                            mm(psum[6][:, :], A2[ds, fs], XAb[ds, :], True, True, rsmall + [rXAb], [R_ps[6]])
                            act(SGm[:], psum[5][:, :], AF.Sigmoid, [R_ps[5]] + rsmall, [rSGm], bias=dec0[:, d, hp:hp + 1])
                            act(AS[:], psum[6][:, :], AF.Sigmoid, [R_ps[6]] + rsmall, [rAS], bias=a0[:, d, hp:hp + 1])
                            S.op("dve", lambda e: e.tensor_tensor_scan(out=Pc[:], data0=scanm[:], data1=SGm[:], initial=0.0,
                                                                        op0=ALU.mult, op1=ALU.add), [rSGm] + rsmall, [rPc])
                            Pc3 = Pc[:].rearrange("p (c t) -> p c t", t=64)
                            if d == 1:
                                tt("pool", Pe[:].rearrange("p (c t) -> p c t", t=64), Pc3[:, :, 63:64].to_broadcast([128, 8, 64]),
                                   Pc3, ALU.subtract, [rPc], [rPe])
                                tt("pool", Pc[:], Pe[:], SGm[:], ALU.add, [rPe, rSGm], [rPc])
                                totcol = 0
                            else:
                                totcol = 63
                            tt("pool", Pe[:], Pc[:], SGm[:], ALU.subtract, [rPc, rSGm], [rPe])
                            tt("pool", Pt[:].rearrange("p (c t) -> p c t", t=64),
                               Pc3[:, :, totcol:totcol + 1].to_broadcast([128, 8, 64]), Pc3, ALU.subtract, [rPc], [rPt])
                            ts("dve", TMP[:], AS[:], -1.0, kav[:, hp:hp + 1], ALU.add, ALU.mult, [rAS] + rsmall, [rTMP])
                            stt("dve", KD[:], TMP[:], 1.0, Kr, ALU.add, ALU.mult, [rTMP, rZC], [rKD])
                            tt("dve", Bv[:], KK[:], AS[:], ALU.mult, [rKK, rAS], [rBv])
                            stt("dve", RKb[:], Rr, rkv[:, hp:hp + 1], KD[:], ALU.mult, ALU.mult, [rZC, rKD] + rsmall, [rRKb])
                            mm(psum[4][:, :], bd64_b[:], RKb[:], True, True, [R_c, rRKb], [R_ps[4]])
                            act(E1[:], Pc[:], AF.Exp, [rPc], [rE1], scale=-cexp)
                            act(E2[:], Pc[:], AF.Exp, [rPc], [rE2], scale=cexp)
                            act(Ee[:], Pe[:], AF.Exp, [rPe], [rEe], scale=-cexp)
                            act(Eb[:], Pt[:], AF.Exp, [rPt], [rEb], scale=-cexp)
                            if d == 0:
                                tt("dve", BON[:], psum[4][:, :], Vr, ALU.mult, [R_ps[4], rZC], [rBON])
                            else:
                                tt("dve", TMP[:], psum[4][:, :], Vr, ALU.mult, [R_ps[4], rZC], [rTMP])
                                tt("pool", BON[:], BON[:], TMP[:], ALU.add, [rBON, rTMP], [rBON])
                            cp("pool", WTo[:, :], E1[:].rearrange("p (c t) -> p c t", t=64)[:, :, totcol], [rE1], [rWTo])
                            stor(rws["wtot", d][fs, s * 8:(s + 1) * 8], WTo[:], [rWTo], [rscr(("wtot", d))], rWTo)
                            o, ro = OUTS["rt"]
                            tt("pool", o[:], Rr, E1[:], ALU.mult, [rZC, rE1], [ro])
                            o, ro = OUTS["bt"]
                            tt("dve", o[:], Bv[:], E2[:], ALU.mult, [rBv, rE2], [ro])
                            o, ro = OUTS["kt"]
                            tt("pool", o[:], KD[:], E2[:], ALU.mult, [rKD, rE2], [ro])
                            o, ro = OUTS["at"]
                            stt("dve", o[:], KK[:], -1.0, Ee[:], ALU.mult, ALU.mult, [rKK, rEe], [ro])
                            o, ro = OUTS["bb"]
                            tt("dve", o[:], Bv[:], Eb[:], ALU.mult, [rBv, rEb], [ro])
                            o, ro = OUTS["kb"]
                            tt("pool", o[:], KD[:], Eb[:], ALU.mult, [rKD, rEb], [ro])
                            for nm in ("rt", "bt", "kt", "at", "bb", "kb"):
                                o, ro = OUTS[nm]
                                stor(rws[nm, d][fs, t0:t0 + 512], o[:], [ro], [rscr((nm, d))], ro)
                            dense_slice(hp * 2 + d)
                        stor(rws["bonus"][fs, t0:t0 + 512], BON[:], [rBON], [rscr("bonus")], rBON)
            S.barrier()

        def phase_b1(l):
            with ExitStack() as st:
                Bt = sbt(st, "b1_bt", [64, 8 * 17 * 64], BF16)
                rBt = Res()
                ld(Bt[:], btab_in[l], (), [rBt], q="pool")
                Bt4 = Bt[:].rearrange("p (h o k) -> p h o k", h=8, o=17)
                RB = sbt(st, "b1_rb", [128, nslots_na], F32)
                rRB = Res()
                ld(RB[:], rowbias_in[:, :], (), [rRB])
                NR = min(24, ROWS)
                KW = [sbt(st, "b1_kw%d" % i, [64, 8, NR * 64], BF16) for i in range(2)]
                rKW = [Res(), Res()]
                QW = [sbt(st, "b1_qw%d" % i, [64, 8, 512], BF16) for i in range(2)]
                rQW = [Res(), Res()]
                VE = [sbt(st, "b1_ve%d" % i, [128, NR // 2, 512], BF16) for i in range(2)]
                rVE = [Res(), Res()]
                VO = [sbt(st, "b1_vo%d" % i, [128, NR // 2 - 1, 512], BF16) for i in range(2)]
                rVO = [Res(), Res()]
                PTt = [sbt(st, "b1_pt%d" % i, [128, 512], BF16) for i in range(2)]
                rPTt = [Res(), Res()]
                RDt = sbt(st, "b1_rd", [64, 512], F32)
                rRDt = Res()
                ON = [sbt(st, "b1_on%d" % i, [64, 8, 512], BF16) for i in range(2)]
                rON = [Res(), Res()]
                qv = qT.rearrange("(h d) t -> d h t", d=64)
                kv = kT.rearrange("(h d) t -> d h t", d=64)

                def loads(s):
                    b = s % 2
                    i0 = s * 8
                    rb = min(max(i0 - 8, 0), ROWS - NR)
                    ld(QW[b][:], qv[:, :, s * 512:(s + 1) * 512], [rscr("qT")], [rQW[b]])
                    ld(KW[b][:], kv[:, :, rb * 64:(rb + NR) * 64], [rscr("kT")], [rKW[b]])
                    ld(VE[b][:], vtm[rb * 64:(rb + NR) * 64, :].rearrange("(c p) f -> p c f", p=128), [rscr("vtm")], [rVE[b]])
                    ld(VO[b][:], vtm[rb * 64 + 64:(rb + NR) * 64 - 64, :].rearrange("(c p) f -> p c f", p=128), [rscr("vtm")],
                       [rVO[b]])
                    return rb

                slot = 0
                pk = 0
                sk = [0]
                rbs = {0: loads(0)}
                for s in range(NT):
                    b = s % 2
                    if s + 1 < NT:
                        rbs[s + 1] = loads(s + 1)
                    rb = rbs[s]
                    for iq in range(8):
                        i = s * 8 + iq
                        band = bands[i]
                        for ci, r in enumerate(band):
                            o = r - i + 8
                            kc0 = (r - rb) * 64
                            sk[0] = (sk[0] + 1) % 3
                            bank = sk[0]
                            for h in range(8):
                                mm(psum[bank][:, h * 64:(h + 1) * 64], KW[b][:, h, kc0:kc0 + 128], QW[b][:, h, iq * 64:(iq + 1) * 64],
                                   True, False, [rKW[b], rQW[b]], [R_ps[bank]])
                                mm(psum[bank][:, h * 64:(h + 1) * 64], Bt4[:, h, o:o + 2, :].rearrange("p o k -> p (o k)"),
                                   ident_b[0:64, 0:64], False, True, [rBt, R_c], [R_ps[bank]])
                            pb = pk % 2
                            pk += 1
                            act(PTt[pb][:], psum[bank][:, :], AF.Exp, [R_ps[bank], rRB], [rPTt[pb]], bias=RB[:, slot:slot + 1])
                            slot += 1
                            if (r - rb) % 2 == 0:
                                Vc, rVc = VE[b][:, (r - rb) // 2, :], rVE[b]
                            else:
                                Vc, rVc = VO[b][:, (r - rb - 1) // 2, :], rVO[b]
                            first, last = ci == 0, ci == len(band) - 1
                            bo, bd_ = (5, 6) if i % 2 == 0 else (3, 4)
                            for h in range(8):
                                mm(psum[bo][0:64, h * 64:(h + 1) * 64], Vc[:, h * 64:(h + 1) * 64], PTt[pb][:, h * 64:(h + 1) * 64],
                                   first and h == 0, last, [rVc, rPTt[pb]], [R_ps[bo]])
                            mm(psum[bd_][0:64, :], ones_b[:, 0:64], PTt[pb][:], first, last, [R_c, rPTt[pb]], [R_ps[bd_]])
                        act(RDt[:], psum[bd_][0:64, :], AF.Ln, [R_ps[bd_]], [rRDt])
                        act(RDt[:], RDt[:], AF.Exp, [rRDt], [rRDt], scale=-1.0)
                        tt("dve", ON[b][:, :, iq * 64:(iq + 1) * 64], psum[bo][0:64, :].rearrange("p (h q) -> p h q", h=8),
                           RDt[:].rearrange("p (h q) -> p h q", h=8), ALU.mult, [R_ps[bo], rRDt], [rON[b]])
                    stor(onaT.rearrange("(h d) t -> d h t", d=64)[:, :, s * 512:(s + 1) * 512], ON[b][:], [rON[b]],
                         [rscr("onaT")], rON[b])
            S.barrier()

        def phase_b2(l):
            with ExitStack() as st:
                mk = {}
                rM = Res()
                for nm in ("ls4", "li4", "us4", "ui4"):
                    mk[nm] = sbt(st, "b2_" + nm, [128, 512], F32)
                    ld(mk[nm][:], cst_in[nm][:, :], (), [rM])
                MS = [mk["ls4"], mk["us4"]]
                MI = [mk["li4"], mk["ui4"]]
                MP = [mk["us4"], mk["ls4"]]
                lnw = sbt(st, "b2_lnw", [128, 4], F32)
                lnb = sbt(st, "b2_lnb", [128, 4], F32)
                ld(lnw[:], W["rw_lnx_w"][l].rearrange("(c p) -> p c", p=128), (), [rM], nonc=True)
                ld(lnb[:], W["rw_lnx_b"][l].rearrange("(c p) -> p c", p=128), (), [rM], nonc=True)
                names = ("at", "bt", "bb", "rt", "kt", "kb", "v")
                IN = {nm: [sbt(st, "b2_i_%s%d" % (nm, i), [64, 8, 512], BF16) for i in range(2)] for nm in names}
                rIN = [Res(), Res()]
                WT = [sbt(st, "b2_wt%d" % i, [64, 8, 8], F32) for i in range(2)]
                AM = sbt(st, "b2_am", [128, 4, 8, 128], BF16)
                rAMt = [[Res(), Res()] for _ in range(4)]
                PPp = [sbt(st, "b2_ppp%d" % i, [128, 8, 128], BF16) for i in range(2)]
                PPt = [sbt(st, "b2_ppt%d" % i, [128, 8, 128], BF16) for i in range(2)]
                rPP = [[Res(), Res()], [Res(), Res()]]
                rPT = [[Res(), Res()], [Res(), Res()]]
                Xc = [sbt(st, "b2_x%d" % i, [128, 8, 128], BF16) for i in range(2)]
                rXc = [[Res(), Res()], [Res(), Res()]]
                TMx = sbt(st, "b2_tm", [128, 9 * 192], BF16)
                rTM = [Res(), Res()]
                memset("dve", TMx[:], 0.0, rTM)
                TMf = TMx[:, :]
                TM = TMx[:, 0:8 * 192].rearrange("p (h q) -> p h q", h=8)
                RH = sbt(st, "b2_rh", [64, 8, 2, 128], BF16)
                rRH = Res()
                MTt = sbt(st, "b2_mt", [64, 8, 2, 128], BF16)
                rMTt = Res()
                MTf = sbt(st, "b2_mtf", [64, 8, 2, 64], F32)
                rMTf = Res()
                DW = sbt(st, "b2_dw", [64, 8, 2, 64], F32)
                rDW = Res()
                N0 = sbt(st, "b2_n0", [64, 8, 2, 64], F32)
                rN0 = Res()
                Sb = sbt(st, "b2_sb", [64, 8, 64], BF16)
                rSb = Res()
                YF = [sbt(st, "b2_yf%d" % i, [128, 512], F32) for i in range(2)]
                rYF = [Res(), Res()]
                YL = [sbt(st, "b2_yl%d" % i, [128, 512], F32) for i in range(2)]
                rYL = [Res(), Res()]
                YC = sbt(st, "b2_yc", [128, 512], F32)
                rYC = Res()
                YS = sbt(st, "b2_ys", [128, 512], F32)
                rYS = Res()
                YN = sbt(st, "b2_yn", [128, 512], BF16)
                rYN = Res()
                st8 = sbt(st, "b2_st8", [128, 8], F32)
                rst8 = Res()
                st8b = sbt(st, "b2_st8b", [128, 8], F32)
                rst8b = Res()
                BONt = [sbt(st, "b2_bon%d" % i, [128, 4, 128], F32) for i in range(2)]
                Gt = [sbt(st, "b2_g%d" % i, [128, 4, 128], BF16) for i in range(2)]
                rBG = [Res(), Res()]
                OT = sbt(st, "b2_ot", [128, 4, 128], F32)
                rOT = Res()
                OR = [sbt(st, "b2_or%d" % i, [128, 4, 128], BF16) for i in range(2)]
                rOR = [Res(), Res()]
                memset("dve", RH[:], 0.0, [rRH])
                memset("dve", MTt[:], 0.0, [rMTt])

                def hview(ap):
                    return ap.rearrange("(h j) t -> j h t", j=64)

                def v4(ap):
                    return ap.rearrange("p (h t) -> p h t", h=4)

                def finalize(yb, gt0):
                    Y3 = YF[yb][:].rearrange("p (h i) -> p h i", h=8)
                    S.op("dve", lambda e, Y3=Y3: e.tensor_reduce(out=st8[:], in_=Y3, axis=AX.X, op=ALU.add),
                         [rYF[yb]], [rst8])
                    ts("dve", st8[:], st8[:], -1.0 / 64, None, ALU.mult, None, [rst8], [rst8])
                    tt("pool", YC[:].rearrange("p (h i) -> p h i", h=8), Y3,
                       st8[:].unsqueeze(2).to_broadcast([128, 8, 64]), ALU.add, [rYF[yb], rst8], [rYC])
                    act(YS[:], YC[:], AF.Square, [rYC], [rYS])
                    S.op("dve", lambda e: e.tensor_reduce(out=st8b[:], in_=YS[:].rearrange("p (h i) -> p h i", h=8),
                                                          axis=AX.X, op=ALU.add), [rYS], [rst8b])
                    ts("dve", st8b[:], st8b[:], 1.0 / 64, 64e-5, ALU.mult, ALU.add, [rst8b], [rst8b])
                    ts("dve", st8b[:], st8b[:], -0.5, None, ALU.pow, None, [rst8b], [rst8b])
                    tt("pool", YN[:].rearrange("p (h i) -> p h i", h=8), YC[:].rearrange("p (h i) -> p h i", h=8),
                       st8b[:].unsqueeze(2).to_broadcast([128, 8, 64]), ALU.mult, [rYC, rst8b], [rYN])
                    for fc in range(4):
                        tr(psT[:, fc * 128:(fc + 1) * 128], YN[:, fc * 128:(fc + 1) * 128], ident_b[:],
                           [rYN, R_c], [R_psT])
                    pv = psT[:, 0:512].rearrange("p (c t) -> p c t", c=4)
                    tt("dve", OT[:], pv, lnw[:].unsqueeze(2).to_broadcast([128, 4, 128]), ALU.mult, [R_psT, rM], [rOT])
                    tt("pool", OT[:], OT[:], lnb[:].unsqueeze(2).to_broadcast([128, 4, 128]), ALU.add, [rOT, rM], [rOT])
                    tt("dve", OT[:], OT[:], BONt[yb][:], ALU.add, [rOT, rBG[yb]], [rOT])
                    tt("pool", OR[yb][:], OT[:], Gt[yb][:], ALU.mult, [rOT, rBG[yb]], [rOR[yb]])
                    stor(orwT.rearrange("(c p) t -> p c t", p=128)[:, :, gt0:gt0 + 128], OR[yb][:], [rOR[yb]],
                         [rscr("orwT")], rOR[yb])

                pending = []
                for d in range(2):
                    memset("dve", Sb[:], 0.0, [rSb])
                    order = list(range(NT)) if d == 0 else list(range(NT - 1, -1, -1))

                    def loads(idx, d=d, order=order):
                        s = order[idx]
                        b = idx % 2
                        for nm in names:
                            srcap = rws["v"] if nm == "v" else rws[nm, d]
                            rk = rscr("v") if nm == "v" else rscr((nm, d))
                            ld(IN[nm][b][:], hview(srcap)[:, :, s * 512:(s + 1) * 512], [rk], [rIN[b]])
                        ld(WT[b][:], hview(rws["wtot", d])[:, :, s * 8:(s + 1) * 8], [rscr(("wtot", d))], [rIN[b]], nonc=True)

                    loads(0)
                    tcount = 0
                    for idx, s in enumerate(order):
                        b = idx % 2
                        if idx + 1 < NT:
                            loads(idx + 1)
                        rI = rIN[b]
                        if d == 0 and s > 0:
                            ts("dve", Sb[:], Sb[:], bm[0:64, s - 1:s], None, ALU.mult, None, [rSb, R_c], [rSb])
                        if d == 1 and s < NT - 1:
                            ts("dve", Sb[:], Sb[:], bm[0:64, s:s + 1], None, ALU.mult, None, [rSb, R_c], [rSb])
                        tiles = list(range(4)) if d == 0 else [3, 2, 1, 0]
                        for tl in tiles:
                            tc_ = slice(tl * 128, (tl + 1) * 128)
                            gt0 = s * 512 + tl * 128
                            yb = tcount % 2
                            tcount += 1
                            if d == 1:
                                ld(YL[yb][:], yfw[gt0:gt0 + 128, :], [rscr("yfw")], [rYL[yb]])
                                ld(BONt[yb][:], rws["bonus"].rearrange("(c p) t -> p c t", p=128)[:, :, gt0:gt0 + 128],
                                   [rscr("bonus")], [rBG[yb]])
                                ld(Gt[yb][:], rws["g"].rearrange("(c p) t -> p c t", p=128)[:, :, gt0:gt0 + 128], [rscr("g")],
                                   [rBG[yb]])

                            def I(nm, h):
                                return IN[nm][b][:, h, tc_]

                            typs = (("bt", "at", MS), ("kt", "at", MS), ("bt", "rt", MI), ("kt", "rt", MI))
                            for ty, (ln, rn, msk) in enumerate(typs):
                                for hg in range(2):
                                    bank = next_ps()
                                    for j in range(4):
                                        h = hg * 4 + j
                                        mm(psum[bank][:, j * 128:(j + 1) * 128], I(ln, h), I(rn, h), True, True, [rI], [R_ps[bank]])
                                    tt("dve", AM[:, ty, hg * 4:(hg + 1) * 4, :], v4(psum[bank][:, :]), v4(msk[d][:]), ALU.mult,
                                       [R_ps[bank], rM], [rAMt[ty][hg]])
                            for hg in range(2):
                                bank = next_ps()
                                for j in range(4):
                                    h = hg * 4 + j
                                    mm(psum[bank][:, j * 128:(j + 1) * 128], I("at", h), I("bt", h), True, True, [rI], [R_ps[bank]])
                                tt("dve", PPp[0][:, hg * 4:(hg + 1) * 4, :], v4(psum[bank][:, :]), v4(MP[d][:]), ALU.mult,
                                   [R_ps[bank], rM], [rPP[0][hg]])
                            for hg in range(2):
                                tb = psT[:, :] if hg == 0 else psum[5][:, :].bitcast(BF16)
                                rtb = R_psT if hg == 0 else R_ps[5]
                                for j in range(4):
                                    h = hg * 4 + j
                                    for q, nm in enumerate(("v", "bb", "kb", "at")):
                                        tr(tb[:, j * 256 + q * 64:j * 256 + (q + 1) * 64], I(nm, h), ident_b[0:64, 0:64],
                                           [rI, R_c], [rtb])
                                pv4 = tb.rearrange("p (h q) -> p h q", h=4)
                                cp("act", TM[:, hg * 4:(hg + 1) * 4, :], pv4[:, :, 0:192], [rtb], [rTM[hg]])
                                cp("act", Xc[0][:, hg * 4:(hg + 1) * 4, 0:64], pv4[:, :, 192:256], [rtb], [rXc[0][hg]])
                            for hg in range(2):
                                bank = next_ps()
                                for j in range(4):
                                    h = hg * 4 + j
                                    mm(psum[bank][:, j * 64:(j + 1) * 64], AM[:, 1, h, :], TM[:, h, 0:64], True, True,
                                       [rAMt[1][hg], rTM[hg]], [R_ps[bank]])
                                cp("dve", Xc[0][:, hg * 4:(hg + 1) * 4, 64:128],
                                   psum[bank][:, 0:256].rearrange("p (h i) -> p h i", h=4), [R_ps[bank]], [rXc[0][hg]])
                            if LVL < 2:
                                continue
                            for hg in range(2):
                                for j in range(4):
                                    h = hg * 4 + j
                                    mm(psum[hg][:, j * 128:(j + 1) * 128], ident_b[:], Xc[0][:, h, :], j == 0, False,
                                       [R_c, rXc[0][hg]], [R_ps[hg]])
                            for k in range(6):
                                cur, nxt = k % 2, (k + 1) % 2
                                for hg in range(2):
                                    rP = rAMt[0][hg] if k == 0 else rPT[cur][hg]
                                    bx = hg
                                    for j in range(4):
                                        h = hg * 4 + j
                                        Pt_h = AM[:, 0, h, :] if k == 0 else PPt[cur][:, h, :]
                                        mm(psum[bx][:, j * 128:(j + 1) * 128], Pt_h, Xc[cur][:, h, :], False, k == 5,
                                           [rP, rXc[cur][hg]], [R_ps[bx]])
                                    cp("act", Xc[nxt][:, hg * 4:(hg + 1) * 4, :], v4(psum[bx][:, :]), [R_ps[bx]], [rXc[nxt][hg]])
                                    if k < 5:
                                        bp_, bt_ = 2 + hg, 5 + hg
                                        rPp = rPP[cur][hg]
                                        for j in range(4):
                                            h = hg * 4 + j
                                            Pt_h = AM[:, 0, h, :] if k == 0 else PPt[cur][:, h, :]
                                            Pp_h = PPp[cur][:, h, :]
                                            mm(psum[bp_][:, j * 128:(j + 1) * 128], Pt_h, Pp_h, True, True, [rP, rPp], [R_ps[bp_]])
                                            mm(psum[bt_][:, j * 128:(j + 1) * 128], Pp_h, Pt_h, True, True, [rP, rPp], [R_ps[bt_]])
                                        cp("dve", PPp[nxt][:, hg * 4:(hg + 1) * 4, :], v4(psum[bp_][:, :]), [R_ps[bp_]], [rPP[nxt][hg]])
                                        cp("act", PPt[nxt][:, hg * 4:(hg + 1) * 4, :], v4(psum[bt_][:, :]), [R_ps[bt_]], [rPT[nxt][hg]])
                            XF = Xc[0]
                            rXF = rXc[0]
                            if LVL < 3:
                                continue
                            while pending:
                                finalize(*pending.pop(0))
                            for h in range(8):
                                hg = h // 4
                                ys = slice(h * 64, (h + 1) * 64)
                                mm(psum[4][:, ys], AM[:, 2, h, :], XF[:, h, 64:128], h == 0, False, [rAMt[2][hg], rXF[hg]], [R_ps[4]])
                                mm(psum[4][:, ys], AM[:, 3, h, :], TM[:, h, 0:64], False, False, [rAMt[3][hg], rTM[hg]], [R_ps[4]])
                            for hg in range(2):
                                bank = hg
                                for j in range(4):
                                    h = hg * 4 + j
                                    mm(psum[bank][0:64, j * 128:(j + 1) * 128], XF[:, h, 0:64], AM[:, 2, h, :], True, True,
                                       [rXF[hg], rAMt[2][hg]], [R_ps[bank]])
                                for c in range(2):
                                    cs = slice(c * 64, (c + 1) * 64)
                                    tt("dve", RH[:, hg * 4:(hg + 1) * 4, c, cs], v4(psum[bank][0:64, :])[:, :, cs],
                                       IN["rt"][b][:, hg * 4:(hg + 1) * 4, tl * 128 + c * 64:tl * 128 + (c + 1) * 64], ALU.add,
                                       [R_ps[bank], rI], [rRH])
                            if LVL < 4:
                                continue
                            tt("pool", DW[:], ident_f[0:64, 0:64].unsqueeze(1).unsqueeze(1).to_broadcast([64, 8, 2, 64]),
                               WT[b][:, :, tl * 2:tl * 2 + 2].unsqueeze(3).to_broadcast([64, 8, 2, 64]), ALU.mult, [rI, R_c], [rDW])
                            for hg in range(2):
                                bm_, bn_ = 2 + hg, 5 + hg
                                for j in range(4):
                                    h = hg * 4 + j
                                    for c in range(2):
                                        ps_ = slice(0, 64) if c == 0 else slice(0, 128)
                                        col = (j * 2 + c) * 64
                                        mm(psum[bm_][:, col:col + 64], XF[ps_, h, 0:128], TM[ps_, h, 64:128], True, True,
                                           [rXF[hg], rTM[hg]], [R_ps[bm_]])
                                        mm(psum[bn_][:, col:col + 64], TM[ps_, h, 64:192], XF[ps_, h, 64:128], True, False,
                                           [rXF[hg], rTM[hg]], [R_ps[bn_]])
                                        mm(psum[bn_][:, col:col + 64], TMf[ps_, h * 192 + 128:h * 192 + 256], TM[ps_, h, 0:64], False, True,
                                           [rTM[hg]], [R_ps[bn_]])
                                hs = slice(hg * 4, (hg + 1) * 4)
                                cp("dve", MTf[:, hs, :, :], psum[bm_][0:64, :].rearrange("p (h c i) -> p h c i", h=4, c=2),
                                   [R_ps[bm_]], [rMTf])
                                tt("pool", MTf[:, hs, 1, :], MTf[:, hs, 1, :], MTf[:, hs, 0, :], ALU.subtract, [rMTf], [rMTf])
                                tt("pool", MTt[:, hs, :, 0:64], MTf[:, hs, :, :], DW[:, hs, :, :], ALU.add, [rMTf, rDW], [rMTt])
                                cp("act", N0[:, hs, :, :], psum[bn_][0:64, :].rearrange("p (h c i) -> p h c i", h=4, c=2),
                                   [R_ps[bn_]], [rN0])
                                tt("pool", N0[:, hs, 1, :], N0[:, hs, 1, :], N0[:, hs, 0, :], ALU.subtract, [rN0], [rN0])
                            if LVL < 5:
                                continue
                            for ci, c in enumerate((0, 1) if d == 0 else (1, 0)):
                                for h in range(8):
                                    ys = slice(h * 64, (h + 1) * 64)
                                    mm(psum[4][:, ys], RH[:, h, c, :], Sb[:, h, :], False, ci == 1, [rRH, rSb], [R_ps[4]])
                                bank = ci
                                for h in range(8):
                                    mm(psum[bank][:, h * 64:(h + 1) * 64], MTt[:, h, c, :], Sb[:, h, :], True, True,
                                       [rMTt, rSb], [R_ps[bank]])
                                tt("dve", Sb[:], psum[bank][0:64, :].rearrange("p (h i) -> p h i", h=8), N0[:, :, c, :], ALU.add,
                                   [R_ps[bank], rN0], [rSb])
                            if LVL < 6:
                                continue
                            if d == 0:
                                cp("act", YF[yb][:], psum[4][:, :], [R_ps[4]], [rYF[yb]])
                                stor(yfw[gt0:gt0 + 128, :], YF[yb][:], [rYF[yb]], [rscr("yfw")], rYF[yb])
                            else:
                                tt("dve", YF[yb][:], psum[4][:, :], YL[yb][:], ALU.add, [R_ps[4], rYL[yb]], [rYF[yb]])
                                pending.append((yb, gt0))
                while pending:
                    finalize(*pending.pop(0))
            S.barrier()

        if phases is None:
            phases = ["p0"] + sum([["a%d" % l, "b1%d" % l, "b2%d" % l, "c1%d" % l, "c2%d" % l] for l in range(depth)], []) + ["e"]
        for ph in phases:
            if ph == "p0":
                phase_p0()
            elif ph == "e":
                phase_e(0)
            elif ph[0] == "a":
                phase_a(int(ph[1:]), 0)
            elif ph[:2] == "b1":
                phase_b1(int(ph[2:]))
            elif ph[:2] == "b2":
                phase_b2(int(ph[2:]))
            elif ph[:2] == "c1":
                phase_c1(int(ph[2:]), 0, 1)
            elif ph[:2] == "c2":
                phase_c2(int(ph[2:]), 1, 0)
        S.emit()
        nc._n_inst = S.ninst
    return nc


def core_inputs(x_seqs, mem_seqs, T, RS, typ, weights, depth):
    NT = T // 512
    SLOT_ST = max(NT // 4, 1)
    NSLOT = NT // SLOT_ST
    ROWS = T // 64
    xin = np.zeros((T, D), np.float32)
    memv = np.zeros((NSLOT, 256, D), np.float32)
    if typ == "P":
        xin[:] = x_seqs[0]
        for sl in range(NSLOT):
            memv[sl] = mem_seqs[0]
        starts = {0}
    else:
        L = RS * 64
        for i, xs in enumerate(x_seqs):
            xin[i * L:(i + 1) * L] = xs
            sl0 = (i * L) // (SLOT_ST * 512)
            memv[sl0] = mem_seqs[i]
        starts = set(range(0, T, L))
    bmv = np.ones((128, max(NT - 1, 1)), np.float32)
    for j in range(NT - 1):
        if (j + 1) * 512 in starts:
            bmv[:, j] = 0.0
    bm2 = np.ones((128, max(T // 256 - 1, 1)), np.float32)
    for j in range(T // 256 - 1):
        if (j + 1) * 256 in starts:
            bm2[:, j] = 0.0
    m = {"xin": xin, "mem": memv, "bm": bmv, "bm2": bm2, "rowbias": na_rowbias(ROWS, RS, typ)}
    m["btab"] = np.stack([na_btab(weights["na_rpb"][l]).reshape(64, -1) for l in range(depth)])
    for k, v in make_consts().items():
        m["c_" + k] = v
    for k, v in weights.items():
        if k == "na_rpb":
            continue
        if k == "rw_r_k":
            v = v.reshape(v.shape[0], 512)
        if k == "final_norm":
            v = v.reshape(1, D)
        m[k] = np.ascontiguousarray(v, dtype=np.float32)
    return m


_CACHE = {}


def kernel(**inputs):
    T, RS, depth = 8192, 32, 2
    xp = np.asarray(inputs["x_prompt"], np.float32)
    xs = np.asarray(inputs["x_sample"], np.float32)
    mp = np.asarray(inputs["mem_prompt"], np.float32)
    ms = np.asarray(inputs["mem_sample"], np.float32)
    weights = {k: np.asarray(v, np.float32) for k, v in inputs.items()
               if k not in ("x_prompt", "x_sample", "mem_prompt", "mem_sample")}
    in_maps = []
    for c in range(4):
        in_maps.append(core_inputs([xp[c]], [mp[c]], T, RS, "P", weights, depth))
    for c in range(4):
        sq = [2 * c, 2 * c + 1, 2 * c, 2 * c + 1]
        in_maps.append(core_inputs([xs[i] for i in sq], [ms[i] for i in sq], T, RS, "S", weights, depth))
    if "nc" not in _CACHE:
        _CACHE["nc"] = build(T, RS, depth)
    res = run_bass_kernel_spmd(_CACHE["nc"], in_maps, core_ids=list(range(8)))
    yp = np.stack([res.results[c]["yout"] for c in range(4)]).astype(np.float32)
    ys = np.zeros_like(xs)
    for c in range(4):
        y = res.results[4 + c]["yout"]
        ys[2 * c] = y[0:2048]
        ys[2 * c + 1] = y[2048:4096]
    return (yp, ys)
```

```python
from contextlib import ExitStack
import os
LVL = int(os.environ.get('B2DBG', '9'))
B2X = int(os.environ.get('B2X', '0'))
import numpy as np
import concourse.bass as bass
import concourse.mybir as mybir
from concourse.bass_utils import run_bass_kernel_spmd

F32 = mybir.dt.float32
BF16 = mybir.dt.bfloat16
AF = mybir.ActivationFunctionType
ALU = mybir.AluOpType
AX = mybir.AxisListType

D = 1024
NEG = -30000.0
ENGS = ("pe", "act", "dve", "pool", "sp")


class Res:
    __slots__ = ("name", "w", "r", "dsem", "multi")

    def __init__(self, name="", multi=False):
        self.name = name
        self.w = {}
        self.r = {}
        self.dsem = None
        self.multi = multi


class Sched:
    def __init__(self, nc, stack, n_dma_sems=96):
        self.nc = nc
        self.q = {e: [] for e in ENGS}
        self.cnt = {e: 0 for e in ENGS}
        self.sem = {e: stack.enter_context(nc.semaphore("s_" + e)) for e in ENGS}
        self.dma_sems = [stack.enter_context(nc.semaphore("d%d" % i)) for i in range(n_dma_sems)]
        self.dma_cnt = [0] * n_dma_sems
        self.dma_next = 0
        self.seen = {e: {} for e in ENGS}
        self.ninst = 0

    def _semobj(self, key):
        return self.sem[key] if isinstance(key, str) else self.dma_sems[key]

    def _deps(self, eng, reads, writes):
        toks = {}
        for r in reads:
            for k, v in r.w.items():
                if toks.get(k, 0) < v:
                    toks[k] = v
        for w in writes:
            for k, v in w.w.items():
                if toks.get(k, 0) < v:
                    toks[k] = v
            for k, v in w.r.items():
                if toks.get(k, 0) < v:
                    toks[k] = v
        waits = []
        seen = self.seen[eng]
        for k, v in toks.items():
            if k == eng and eng == "pe":
                continue
            if seen.get(k, 0) >= v:
                continue
            seen[k] = v
            waits.append((k, v))
        return waits

    def _mark(self, tok, reads, writes):
        k, v = tok
        for r in reads:
            if r.r.get(k, 0) < v:
                r.r[k] = v
        for w in writes:
            w.w[k] = v
            if not w.multi:
                w.r = {}

    def op(self, eng, fn, reads=(), writes=()):
        waits = self._deps(eng, reads, writes)
        self.cnt[eng] += 1
        self._mark((eng, self.cnt[eng]), reads, writes)
        self.q[eng].append((waits, fn, (eng, 1)))
        self.ninst += 1

    def dma(self, eng, fn, reads=(), writes=(), sem_res=None):
        waits = self._deps(eng, reads, writes)
        if sem_res is None:
            sem_res = writes[0]
        if sem_res.dsem is None:
            sem_res.dsem = self.dma_next % len(self.dma_sems)
            self.dma_next += 1
        k = sem_res.dsem
        self.dma_cnt[k] += 16
        self._mark((k, self.dma_cnt[k]), reads, writes)
        self.q[eng].append((waits, fn, (k, 16)))
        self.ninst += 1

    def barrier(self):
        tot = {e: self.cnt[e] for e in ENGS if self.cnt[e]}
        for k, c in enumerate(self.dma_cnt):
            if c:
                tot[k] = c
        for e in ENGS:
            waits = []
            for k, v in tot.items():
                if k == e:
                    continue
                if self.seen[e].get(k, 0) < v:
                    self.seen[e][k] = v
                    waits.append((k, v))
            if waits:
                self.q[e].append((waits, None, None))

    def emit(self):
        nc = self.nc
        fin = {e: self.cnt[e] for e in ENGS if self.cnt[e]}
        for k, c in enumerate(self.dma_cnt):
            if c:
                fin[k] = c
        handles = {"pe": "tensor", "act": "scalar", "dve": "vector", "pool": "gpsimd", "sp": "sync"}
        with nc.Block() as block:
            for e in ENGS:
                def body(engine, ops=self.q[e], is_last=(e == "sp")):
                    for waits, fn, inc in ops:
                        for wi, (wk, wv) in enumerate(waits):
                            engine.wait_ge(self._semobj(wk), wv)
                            if wi < len(waits) - 1 or fn is None:
                                engine.nop(nofuse=True)
                        if fn is not None:
                            fn(engine).then_inc(self._semobj(inc[0]), inc[1])
                    if is_last:
                        for k, v in fin.items():
                            engine.wait_ge(self._semobj(k), v)
                getattr(block, handles[e])(body)


def na_bands(ROWS, RS):
    bands = []
    for i in range(ROWS):
        p0 = min(max(i - 4, 0), ROWS - 8)
        b, il = divmod(i, RS)
        s0 = b * RS + min(max(il - 4, 0), RS - 8)
        lo, hi = min(p0, s0), max(p0, s0) + 8
        n = (hi - lo + 1) // 2
        if lo + 2 * n > ROWS:
            lo = ROWS - 2 * n
        bands.append([lo + 2 * c for c in range(n)])
    return bands


def na_rowbias(ROWS, RS, typ):
    bands = na_bands(ROWS, RS)
    cols = []
    for i, band in enumerate(bands):
        if typ == "P":
            w0 = min(max(i - 4, 0), ROWS - 8)
        else:
            b, il = divmod(i, RS)
            w0 = b * RS + min(max(il - 4, 0), RS - 8)
        for r in band:
            col = np.full(128, NEG, np.float32)
            for j in range(2):
                if w0 <= r + j < w0 + 8:
                    col[j * 64:(j + 1) * 64] = 0.0
            cols.append(col)
    return np.stack(cols, axis=1)


def na_btab(rpb):
    H = rpb.shape[0]
    out = np.zeros((64, H, 17, 64), np.float32)
    qc = np.arange(64)
    c0 = np.clip(qc - 8, 0, 48)
    for off in range(-7, 8):
        blk = np.full((64, H, 64), NEG, np.float32)
        for q in range(64):
            ks = np.arange(c0[q], c0[q] + 16)
            blk[q][:, ks] = rpb[:, off + 7, ks - q + 15]
        out[:, :, off + 8, :] = blk
    return out


def make_consts():
    c = {}
    c["ident"] = np.eye(128, dtype=np.float32)
    bd = np.zeros((128, 128), np.float32)
    bd[:64, :64] = 1.0
    bd[64:, 64:] = 1.0
    c["bd64"] = bd
    sm = np.ones((128, 512), np.float32)
    sm[:, ::64] = 0.0
    c["scanmask"] = sm
    s = np.arange(128)[:, None]
    t = np.arange(128)[None, :]
    same = (s // 64) == (t // 64)
    LS = (same & (s < t)).astype(np.float32)
    LI = (same & (s <= t)).astype(np.float32)
    US = (same & (s > t)).astype(np.float32)
    UI = (same & (s >= t)).astype(np.float32)
    c["ls4"] = np.tile(LS, (1, 4))
    c["li4"] = np.tile(LI, (1, 4))
    c["us4"] = np.tile(US, (1, 4))
    c["ui4"] = np.tile(UI, (1, 4))
    return c


CONST_ORDER = ("ident", "bd64", "scanmask", "ls4", "li4", "us4", "ui4")


def build(T, RS, depth=2, debug_outs=(), phases=None):
    NT = T // 512
    NTL = T // 128
    ROWS = T // 64
    NCH = T // 64
    SLOT_ST = max(NT // 4, 1)
    NSLOT = NT // SLOT_ST
    bands = na_bands(ROWS, RS)
    nslots_na = sum(len(b) for b in bands)
    NH2 = 2 * (T // 256 - 1)
    NH5 = 2 * (NT - 1)

    nc = bass.Bass("TRN2", target_bir_lowering=False)

    def din(name, shape, dt=F32):
        return nc.dram_tensor(name, list(shape), dt, kind="ExternalInput").ap()

    def dscr(name, shape, dt):
        kind = "ExternalOutput" if name in debug_outs else "Internal"
        return nc.dram_tensor(name, list(shape), dt, kind=kind).ap()

    xin = din("xin", [T, D])
    mem = din("mem", [NSLOT, 256, D])
    bm_in = din("bm", [128, max(NT - 1, 1)])
    bm2_in = din("bm2", [128, max(T // 256 - 1, 1)])
    rowbias_in = din("rowbias", [128, nslots_na])
    btab_in = din("btab", [depth, 64, 8 * 17 * 64])
    cst_in = {k: din("c_" + k, v.shape) for k, v in make_consts().items()}
    W = {}
    for name, shape in (("attn_norm", [depth, D]), ("w_in", [depth, D, 7072]), ("rw_conv", [depth, 3, 1952]),
                        ("rw_decay0", [depth, 2, 512]), ("rw_decay2", [depth, 2, 64, 512]), ("rw_a0", [depth, 2, 512]),
                        ("rw_a2", [depth, 2, 64, 512]), ("rw_g2", [depth, 160, 512]), ("rw_k_k", [depth, 512]),
                        ("rw_k_a", [depth, 512]), ("rw_r_k", [depth, 512]), ("rw_lnx_w", [depth, 512]),
                        ("rw_lnx_b", [depth, 512]), ("mem_norm", [depth, D]), ("w_mem_kv", [depth, D, 1024]),
                        ("w_branch", [depth, 3, 512, D]), ("w_out", [depth, D, D]), ("ffn_norm", [depth, D]),
                        ("w_up", [depth, D, 5632]), ("ffn_conv", [depth, 3, 5632]), ("ffn_conv_b", [depth, 5632]),
                        ("w_down", [depth, 2816, D]), ("final_norm", [1, D])):
        W[name] = din(name, shape)
    yout = nc.dram_tensor("yout", [T, D], F32, kind="ExternalOutput").ap()

    xT = [dscr("xT0", [D, T], F32), dscr("xT1", [D, T], F32)]
    qT = dscr("qT", [512, T], BF16)
    kT = dscr("kT", [512, T], BF16)
    vtm = dscr("vtm", [T, 512], BF16)
    omT = dscr("omT", [512, T], BF16)
    onaT = dscr("onaT", [512, T], BF16)
    orwT = dscr("orwT", [512, T], BF16)
    rws = {}
    for d in range(2):
        for nm in ("at", "bt", "bb", "rt", "kt", "kb"):
            rws[nm, d] = dscr("rw_%s%d" % (nm, d), [512, T], BF16)
        rws["wtot", d] = dscr("rw_wtot%d" % d, [512, NCH], F32)
    rws["v"] = dscr("rw_v", [512, T], BF16)
    rws["bonus"] = dscr("rw_bonus", [512, T], F32)
    rws["g"] = dscr("rw_g", [512, T], BF16)
    yfw = dscr("yfw", [T, 512], F32)

    R_xT = [Res("xT0", True), Res("xT1", True)]
    R_scr = {}

    def rscr(key):
        if key not in R_scr:
            R_scr[key] = Res(str(key), True)
        return R_scr[key]

    with ExitStack() as top:
        S = Sched(nc, top)
        psum = [top.enter_context(nc.psum_tensor("ps%d" % i, [128, 512], F32)) for i in range(7)]
        psT = top.enter_context(nc.psum_tensor("psT", [128, 1024], BF16))
        R_ps = [Res("ps%d" % i) for i in range(7)]
        R_psT = Res("psT")

        def mm(out, lhsT, rhs, start, stop, reads, writes):
            S.op("pe", lambda e: e.matmul(out, lhsT=lhsT, rhs=rhs, start=start, stop=stop), reads, writes)

        def tr(out, in_, ident, reads, writes):
            S.op("pe", lambda e: e.transpose(out, in_, ident), reads, writes)

        def act(out, in_, func, reads, writes, bias=None, scale=None, accum_out=None):
            kw = {}
            if bias is not None:
                kw["bias"] = bias
            if scale is not None:
                kw["scale"] = scale
            if accum_out is not None:
                kw["accum_out"] = accum_out
            S.op("act", lambda e: e.activation(out=out, in_=in_, func=func, **kw), reads, writes)

        def is_ps(ap):
            return hasattr(ap, "space") and "PSUM" in str(ap.space)

        def tt(eng, out, in0, in1, op, reads, writes):
            if eng == "pool" and (is_ps(out) or is_ps(in0) or is_ps(in1)):
                eng = "dve"
            S.op(eng, lambda e: e.tensor_tensor(out=out, in0=in0, in1=in1, op=op), reads, writes)

        def ts(eng, out, in0, s1, s2, op0, op1, reads, writes):
            if eng == "pool" and (is_ps(out) or is_ps(in0)):
                eng = "dve"
            if op1 is None and op0 == ALU.pow:
                assert s1 == -0.5
                S.op("act", lambda e: e.activation(out=out, in_=in0, func=AF.Ln), reads, writes)
                S.op("act", lambda e: e.activation(out=out, in_=out, func=AF.Exp, scale=-0.5), list(reads) + list(writes), writes)
            elif op1 is None:
                S.op(eng, lambda e: e.tensor_scalar(out=out, in0=in0, scalar1=s1, scalar2=None, op0=op0), reads, writes)
            else:
                S.op(eng, lambda e: e.tensor_scalar(out=out, in0=in0, scalar1=s1, scalar2=s2, op0=op0, op1=op1), reads, writes)

        def stt(eng, out, in0, scalar, in1, op0, op1, reads, writes):
            eng = "dve"
            S.op(eng, lambda e: e.scalar_tensor_tensor(out=out, in0=in0, scalar=scalar, in1=in1, op0=op0, op1=op1), reads, writes)

        def cp(eng, out, in_, reads, writes):
            if eng == "act":
                S.op("act", lambda e: e.copy(out=out, in_=in_), reads, writes)
            else:
                S.op(eng, lambda e: e.tensor_copy(out=out, in_=in_), reads, writes)

        def memset(eng, ap, val, writes):
            S.op(eng, lambda e: e.memset(ap, val), (), writes)

        def ld(out, in_, reads, writes, q="sp", nonc=False):
            if nonc:
                def f(e):
                    with nc.allow_non_contiguous_dma(reason="small strided load"):
                        return e.dma_start(out=out, in_=in_)
                S.dma(q, f, reads, writes)
            else:
                S.dma(q, lambda e: e.dma_start(out=out, in_=in_), reads, writes)

        def stor(out, in_, reads, writes, sem_res, q="sp"):
            S.dma(q, lambda e: e.dma_start(out=out, in_=in_), reads, writes, sem_res=sem_res)

        rr = [0]

        def next_ps():
            rr[0] = (rr[0] + 1) % 4
            return rr[0]

        ve = [0]

        def veng():
            ve[0] ^= 1
            return "dve" if ve[0] else "pool"

        uniq = [0]

        def sbt(stack, name, shape, dt):
            uniq[0] += 1
            return stack.enter_context(nc.sbuf_tensor("sb%d_%s" % (uniq[0], name), list(shape), dt))

        ident_f = sbt(top, "ident_f", [128, 128], F32)
        ident_b = sbt(top, "ident_b", [128, 128], BF16)
        ones_b = sbt(top, "ones_b", [128, 128], BF16)
        bd64_b = sbt(top, "bd64_b", [128, 128], BF16)
        bd64_f = sbt(top, "bd64_f", [128, 128], F32)
        R_c = Res("consts")
        ld(ident_f[:], cst_in["ident"][:, :], (), [R_c])
        ld(ident_b[:], cst_in["ident"][:, :], (), [R_c], q="pool")
        ld(bd64_b[:], cst_in["bd64"][:, :], (), [R_c], q="pool")
        ld(bd64_f[:], cst_in["bd64"][:, :], (), [R_c])
        memset("dve", ones_b[:], 1.0, [R_c])
        bm = sbt(top, "bm", [128, max(NT - 1, 1)], F32)
        bm2 = sbt(top, "bm2", [128, max(T // 256 - 1, 1)], F32)
        ld(bm[:], bm_in[:, :], (), [R_c])
        ld(bm2[:], bm2_in[:, :], (), [R_c])

        def split_res(parent, n):
            subs = [Res() for _ in range(n)]
            for r_ in subs:
                for k_, v_ in list(parent.r.items()) + list(parent.w.items()):
                    if r_.r.get(k_, 0) < v_:
                        r_.r[k_] = v_
            return subs

        def join_res(parent, subs):
            parent.r = {}
            for r_ in subs:
                for k_, v_ in r_.w.items():
                    if parent.w.get(k_, 0) < v_:
                        parent.w[k_] = v_

        def rmsnorm_fm(X, rX, XN, rXN, gain, rg, n, SQ, rSQ, RSt, rRS, bank=4):
            act(SQ[:, :, 0:n], X[:, :, 0:n], AF.Square, [rX], [rSQ])
            for c in range(8):
                mm(psum[bank][:, 0:n], ones_b[:], SQ[:, c, 0:n], c == 0, c == 7, [rSQ, R_c], [R_ps[bank]])
            ts("dve", RSt[:, 0:n], psum[bank][:, 0:n], 1.0 / D, 1e-6, ALU.mult, ALU.add, [R_ps[bank]], [rRS])
            ts("dve", RSt[:, 0:n], RSt[:, 0:n], -0.5, None, ALU.pow, None, [rRS], [rRS])
            rc_ = split_res(rXN, 8)
            for c in range(8):
                stt(veng(), XN[:, c, 0:n], X[:, c, 0:n], gain[:, c:c + 1], RSt[:, 0:n], ALU.mult, ALU.mult,
                    [rX, rRS, rg], [rc_[c]])
            join_res(rXN, rc_)

        def load_vec_fm(stack, name, src, nchunk, q="sp"):
            t = sbt(stack, name, [128, nchunk], F32)
            r = Res(name)
            ld(t[:], src.rearrange("(c p) -> p c", p=128), (), [r], nonc=True)
            return t, r

        def load_w(stack, name, src, kc, ncols, col0=0, eng_q="pool"):
            t = sbt(stack, name, [128, kc, ncols], BF16)
            r = Res(name)
            for c in range(kc):
                ld(t[:, c, :], src[c * 128:(c + 1) * 128, col0:col0 + ncols], (), [r], q=eng_q)
            return t, r

        def phase_p0():
            with ExitStack() as st:
                XI = [sbt(st, "p0_xi%d" % i, [128, D], F32) for i in range(2)]
                rXI = [Res("xi0"), Res("xi1")]
                XO = [sbt(st, "p0_xo%d" % i, [128, 8, 128], F32) for i in range(2)]
                rXO = [Res("xo0"), Res("xo1")]
                for i in range(NTL):
                    b = i % 2
                    ld(XI[b][:], xin[i * 128:(i + 1) * 128, :], (), [rXI[b]])
                    for half in range(2):
                        bank = next_ps()
                        for c4 in range(4):
                            c = half * 4 + c4
                            tr(psum[bank][:, c4 * 128:(c4 + 1) * 128], XI[b][:, c * 128:(c + 1) * 128], ident_f[:],
                               [rXI[b], R_c], [R_ps[bank]])
                        cp("act" if half else "dve", XO[b][:, half * 4:(half + 1) * 4, :],
                           psum[bank][:, :].rearrange("p (c t) -> p c t", c=4), [R_ps[bank]], [rXO[b]])
                    stor(xT[0].rearrange("(c p) t -> p c t", p=128)[:, :, i * 128:(i + 1) * 128], XO[b][:],
                         [rXO[b]], [R_xT[0]], rXO[b])
            S.barrier()

        def phase_e(src):
            with ExitStack() as st:
                gain, rg = load_vec_fm(st, "e_gain", W["final_norm"][0], 8)
                X = [sbt(st, "e_x%d" % i, [128, 8, 512], F32) for i in range(2)]
                rX = [Res(), Res()]
                XN = sbt(st, "e_xn", [128, 8, 512], F32)
                rXN = Res()
                SQ = sbt(st, "e_sq", [128, 8, 512], BF16)
                rSQ = Res()
                RSt = sbt(st, "e_rs", [128, 512], F32)
                rRS = Res()
                YO = [sbt(st, "e_yo%d" % i, [128, D], F32) for i in range(2)]
                rYO = [Res(), Res()]
                xv = xT[src].rearrange("(c p) t -> p c t", p=128)
                ld(X[0][:], xv[:, :, 0:512], [R_xT[src]], [rX[0]])
                k = 0
                for s in range(NT):
                    b = s % 2
                    if s + 1 < NT:
                        ld(X[1 - b][:], xv[:, :, (s + 1) * 512:(s + 2) * 512], [R_xT[src]], [rX[1 - b]])
                    rmsnorm_fm(X[b], rX[b], XN, rXN, gain, rg, 512, SQ, rSQ, RSt, rRS)
                    for tl in range(4):
                        yb = k % 2
                        k += 1
                        for half in range(2):
                            bank = next_ps()
                            for c4 in range(4):
                                c = half * 4 + c4
                                tr(psum[bank][:, c4 * 128:(c4 + 1) * 128], XN[:, c, tl * 128:(tl + 1) * 128], ident_f[:],
                                   [rXN, R_c], [R_ps[bank]])
                            cp("act" if half else "dve", YO[yb][:, half * 512:(half + 1) * 512], psum[bank][:, :],
                               [R_ps[bank]], [rYO[yb]])
                        t0 = s * 512 + tl * 128
                        stor(yout[t0:t0 + 128, :], YO[yb][:], [rYO[yb]], [Res()], rYO[yb])
            S.barrier()

        def phase_c2(l, src, dst):
            TW = 256
            NTW = T // TW
            NW = TW + 2
            banks = (0, 1, 2, 3, 5, 6)
            bk = [0]

            def nb():
                bk[0] = (bk[0] + 1) % len(banks)
                return banks[bk[0]]

            with ExitStack() as st:
                Wu, rWu = load_w(st, "c2_wu", W["w_up"][l], 8, 5632)
                Wd, rWd = load_w(st, "c2_wd", W["w_down"][l], 22, D)
                gain, rg = load_vec_fm(st, "c2_gain", W["ffn_norm"][l], 8)
                cw = sbt(st, "c2_cw", [128, 3, 44], F32)
                rcw = Res()
                for k in range(3):
                    ld(cw[:, k, :], W["ffn_conv"][l, k].rearrange("(c p) -> p c", p=128), (), [rcw], nonc=True)
                cb, rcb = load_vec_fm(st, "c2_cb", W["ffn_conv_b"][l], 44)
                X = [sbt(st, "c2_x%d" % i, [128, 8, NW], F32) for i in range(2)]
                rX = [Res(), Res()]
                XN = sbt(st, "c2_xn", [128, 8, NW], BF16)
                rXN = Res()
                SQ = sbt(st, "c2_sq", [128, 8, NW], BF16)
                rSQ = Res()
                RSt = sbt(st, "c2_rs", [128, NW], F32)
                rRS = Res()
                G = sbt(st, "c2_g", [128, 22, TW], BF16)
                rG = Res()
                NB = 3
                CV = [sbt(st, "c2_cv%d" % i, [128, TW], F32) for i in range(NB)]
                rCV = [Res() for _ in range(NB)]
                CG = [sbt(st, "c2_cg%d" % i, [128, TW], F32) for i in range(NB)]
                rCG = [Res() for _ in range(NB)]
                SGt = [sbt(st, "c2_sg%d" % i, [128, TW], F32) for i in range(NB)]
                rSGt = [Res() for _ in range(NB)]
                xv = xT[src].rearrange("(c p) t -> p c t", p=128)
                xo = xT[dst].rearrange("(c p) t -> p c t", p=128)

                def load_x(s):
                    b = s % 2
                    t0 = s * TW
                    lo = max(t0 - 1, 0)
                    hi = min(t0 + TW + 1, T)
                    ld(X[b][:, :, lo - (t0 - 1):hi - (t0 - 1)], xv[:, :, lo:hi], [R_xT[src]], [rX[b]])
                    if s == 0:
                        memset("pool", X[b][:, :, 0:1], 0.0, [rX[b]])
                    if s == NTW - 1:
                        memset("pool", X[b][:, :, NW - 1:NW], 0.0, [rX[b]])

                load_x(0)
                for s in range(NTW):
                    b = s % 2
                    t0 = s * TW
                    if s + 1 < NTW:
                        load_x(s + 1)
                    rmsnorm_fm(X[b], rX[b], XN, rXN, gain, rg, NW, SQ, rSQ, RSt, rRS)
                    if s > 0:
                        ts("pool", XN[:, :, 0:1], XN[:, :, 0:1], bm2[:, s - 1:s], None, ALU.mult, None, [rXN, R_c], [rXN])
                    if s < NTW - 1:
                        ts("pool", XN[:, :, NW - 1:NW], XN[:, :, NW - 1:NW], bm2[:, s:s + 1], None, ALU.mult, None, [rXN, R_c], [rXN])
                    rGj = split_res(rG, 22)
                    for j in range(22):
                        cbuf = j % NB
                        pair = ((j, CV[cbuf], rCV[cbuf], nb()), (22 + j, CG[cbuf], rCG[cbuf], nb()))
                        for (uc, Ct, rC, bank) in pair:
                            for kc in range(8):
                                mm(psum[bank][:, 0:NW], Wu[:, kc, uc * 128:(uc + 1) * 128], XN[:, kc, :], kc == 0, kc == 7,
                                   [rWu, rXN], [R_ps[bank]])
                        for (uc, Ct, rC, bank) in pair:
                            act(Ct[:, :], psum[bank][:, 1:TW + 1], AF.Identity, [R_ps[bank], rcw, rcb], [rC], bias=cb[:, uc:uc + 1],
                                scale=cw[:, 1, uc:uc + 1])
                        for (uc, Ct, rC, bank) in pair:
                            stt("dve", Ct[:, :], psum[bank][:, 0:TW], cw[:, 0, uc:uc + 1], Ct[:, :], ALU.mult, ALU.add,
                                [R_ps[bank], rcw, rC], [rC])
                        for (uc, Ct, rC, bank) in pair:
                            stt("dve", Ct[:, :], psum[bank][:, 2:TW + 2], cw[:, 2, uc:uc + 1], Ct[:, :], ALU.mult, ALU.add,
                                [R_ps[bank], rcw, rC], [rC])
                        act(SGt[cbuf][:], CG[cbuf][:], AF.Silu, [rCG[cbuf]], [rSGt[cbuf]])
                        tt("pool", G[:, j, :], SGt[cbuf][:], CV[cbuf][:], ALU.mult, [rSGt[cbuf], rCV[cbuf]], [rGj[j]])
                    join_res(rG, rGj)
                    for oc in range(8):
                        bank = nb()
                        for kc in range(22):
                            mm(psum[bank][:, 0:TW], Wd[:, kc, oc * 128:(oc + 1) * 128], G[:, kc, :], kc == 0, kc == 21,
                               [rWd, rG], [R_ps[bank]])
                        tt("dve", X[b][:, oc, 1:TW + 1], X[b][:, oc, 1:TW + 1], psum[bank][:, 0:TW], ALU.add, [rX[b], R_ps[bank]], [rX[b]])
                    stor(xo[:, :, t0:t0 + TW], X[b][:, :, 1:TW + 1], [rX[b]], [R_xT[dst]], rX[b])
            S.barrier()

        def phase_c1(l, src, dst):
            with ExitStack() as st:
                Wg, rWg = load_w(st, "c1_wg", W["w_in"][l], 8, 3072, col0=4000)
                Wb = sbt(st, "c1_wb", [128, 12, D], BF16)
                rWb = Res()
                for b in range(3):
                    for kc in range(4):
                        ld(Wb[:, b * 4 + kc, :], W["w_branch"][l, b, kc * 128:(kc + 1) * 128, :], (), [rWb], q="pool")
                Wo, rWo = load_w(st, "c1_wo", W["w_out"][l], 8, D)
                gain, rg = load_vec_fm(st, "c1_gain", W["attn_norm"][l], 8)
                Xs = [sbt(st, "c1_x%d" % i, [128, 8, 512], F32) for i in range(2)]
                rXs = [Res(), Res()]
                XN = sbt(st, "c1_xn", [128, 8, 512], BF16)
                rXN = Res()
                SQ = sbt(st, "c1_sq", [128, 8, 512], BF16)
                rSQ = Res()
                RSt = sbt(st, "c1_rs", [128, 512], F32)
                rRS = Res()
                OBs = [sbt(st, "c1_ob%d" % i, [128, 12, 512], BF16) for i in range(2)]
                rOBs = [Res(), Res()]
                MG = sbt(st, "c1_mg", [128, 8, 512], BF16)
                rMG = Res()
                SGt = [sbt(st, "c1_sg%d" % i, [128, 512], F32) for i in range(2)]
                rSGt = [Res(), Res()]
                ACCs = [sbt(st, "c1_acc%d" % i, [128, 512], F32) for i in range(2)]
                rACCs = [Res(), Res()]
                xv = xT[src].rearrange("(c p) t -> p c t", p=128)
                xo = xT[dst].rearrange("(c p) t -> p c t", p=128)
                srcs = (onaT, orwT, omT)
                rsrcs = (rscr("onaT"), rscr("orwT"), rscr("omT"))
                k = 0

                def loads(s):
                    t0 = s * 512
                    ld(Xs[s % 2][:], xv[:, :, t0:t0 + 512], [R_xT[src]], [rXs[s % 2]])
                    for b in range(3):
                        ld(OBs[s % 2][:, b * 4:(b + 1) * 4, :], srcs[b].rearrange("(c p) t -> p c t", p=128)[:, :, t0:t0 + 512],
                           [rsrcs[b]], [rOBs[s % 2]])

                loads(0)
                for s in range(NT):
                    t0 = s * 512
                    if s + 1 < NT:
                        loads(s + 1)
                    X, rX, OB, rOB = Xs[s % 2], rXs[s % 2], OBs[s % 2], rOBs[s % 2]
                    rmsnorm_fm(X, rX, XN, rXN, gain, rg, 512, SQ, rSQ, RSt, rRS)
                    rMGc = split_res(rMG, 8)
                    for mc in range(8):
                        ACC, rACC = ACCs[mc % 2], rACCs[mc % 2]
                        for b in range(3):
                            bg = next_ps()
                            for kc in range(8):
                                mm(psum[bg][:, :], Wg[:, kc, b * 1024 + mc * 128:b * 1024 + (mc + 1) * 128], XN[:, kc, :],
                                   kc == 0, kc == 7, [rWg, rXN], [R_ps[bg]])
                            sb_ = k % 2
                            k += 1
                            act(SGt[sb_][:], psum[bg][:, :], AF.Sigmoid, [R_ps[bg]], [rSGt[sb_]])
                            bp = next_ps()
                            for kc in range(4):
                                mm(psum[bp][:, :], Wb[:, b * 4 + kc, mc * 128:(mc + 1) * 128], OB[:, b * 4 + kc, :],
                                   kc == 0, kc == 3, [rWb, rOB], [R_ps[bp]])
                            if b == 0:
                                tt("dve", ACC[:], SGt[sb_][:], psum[bp][:, :], ALU.mult, [rSGt[sb_], R_ps[bp]], [rACC])
                            else:
                                tt("dve", SGt[sb_][:], SGt[sb_][:], psum[bp][:, :], ALU.mult, [rSGt[sb_], R_ps[bp]],
                                   [rSGt[sb_]])
                                if b == 1:
                                    tt("pool", ACC[:], ACC[:], SGt[sb_][:], ALU.add, [rACC, rSGt[sb_]], [rACC])
                                else:
                                    tt("pool", MG[:, mc, :], ACC[:], SGt[sb_][:], ALU.add, [rACC, rSGt[sb_]], [rMGc[mc]])
                    join_res(rMG, rMGc)
                    for oc in range(8):
                        bank = next_ps()
                        for kc in range(8):
                            mm(psum[bank][:, :], Wo[:, kc, oc * 128:(oc + 1) * 128], MG[:, kc, :], kc == 0, kc == 7,
                               [rWo, rMG], [R_ps[bank]])
                        tt(veng(), X[:, oc, :], X[:, oc, :], psum[bank][:, :], ALU.add, [rX, R_ps[bank]], [rX])
                    stor(xo[:, :, t0:t0 + 512], X[:], [rX], [R_xT[dst]], rX)
            S.barrier()

        def phase_a(l, src):
            with ExitStack() as st:
                Wi, rWi = load_w(st, "a_wi", W["w_in"][l], 8, 4000)
                gain, rg = load_vec_fm(st, "a_gain", W["attn_norm"][l], 8)
                KmT = sbt(st, "a_kmT", [128, NSLOT, 4, 256], BF16)
                Vm = sbt(st, "a_vm", [128, NSLOT, 2, 512], BF16)
                rKV = Res()
                with ExitStack() as st2:
                    Wkv, rWkv = load_w(st2, "a_wkv", W["w_mem_kv"][l], 8, 1024)
                    gm, rgm = load_vec_fm(st2, "a_gm", W["mem_norm"][l], 8)
                    MT = sbt(st2, "a_mt", [128, D], F32)
                    rMT = Res()
                    MS = sbt(st2, "a_ms", [128, D], F32)
                    rMS = Res()
                    MB = sbt(st2, "a_mb", [128, D], BF16)
                    rMB = Res()
                    ssq = sbt(st2, "a_ssq", [128, 1], F32)
                    rssq = Res()
                    memT = sbt(st2, "a_memT", [128, 8, 256], BF16)
                    rmemT = Res()
                    for sl in range(NSLOT):
                        for mc in range(2):
                            ld(MT[:], mem[sl, mc * 128:(mc + 1) * 128, :], (), [rMT])
                            act(MS[:], MT[:], AF.Square, [rMT], [rMS, rssq], accum_out=ssq[:])
                            ts("dve", ssq[:], ssq[:], 1.0 / D, 1e-6, ALU.mult, ALU.add, [rssq], [rssq])
                            ts("dve", ssq[:], ssq[:], -0.5, None, ALU.pow, None, [rssq], [rssq])
                            ts("dve", MB[:], MT[:], ssq[:, 0:1], None, ALU.mult, None, [rMT, rssq], [rMB])
                            for c in range(8):
                                tr(psT[:, c * 128:(c + 1) * 128], MB[:, c * 128:(c + 1) * 128], ident_b[:], [rMB, R_c],
                                   [R_psT])
                            for c in range(8):
                                ts(veng(), memT[:, c, mc * 128:(mc + 1) * 128], psT[:, c * 128:(c + 1) * 128],
                                   gm[:, c:c + 1], None, ALU.mult, None, [R_psT, rgm], [rmemT])
                        for h in range(4):
                            bank = next_ps()
                            for kc in range(8):
                                mm(psum[bank][:, 0:256], Wkv[:, kc, h * 128:(h + 1) * 128], memT[:, kc, :], kc == 0, kc == 7,
                                   [rWkv, rmemT], [R_ps[bank]])
                            cp("act", KmT[:, sl, h, :], psum[bank][:, 0:256], [R_ps[bank]], [rKV])
                        for mc in range(2):
                            bank = next_ps()
                            for kc in range(8):
                                mm(psum[bank][:, :], memT[:, kc, mc * 128:(mc + 1) * 128], Wkv[:, kc, 512:1024], kc == 0,
                                   kc == 7, [rWkv, rmemT], [R_ps[bank]])
                            cp("dve", Vm[:, sl, mc, :], psum[bank][:, :], [R_ps[bank]], [rKV])
                    S.barrier()
                cw = sbt(st, "a_cw", [128, 3, 16], F32)
                rcw = Res()
                memset("dve", cw[:], 0.0, [rcw])
                for k in range(3):
                    ld(cw[:, k, 0:15], W["rw_conv"][l, k, 0:1920].rearrange("(c p) -> p c", p=128), (), [rcw], nonc=True)
                    ld(cw[0:32, k, 15:16], W["rw_conv"][l, k, 1920:1952].rearrange("(c p) -> p c", p=32), (), [rcw],
                       nonc=True)
                dec0 = sbt(st, "a_dec0", [128, 2, 4], F32)
                a0 = sbt(st, "a_a0", [128, 2, 4], F32)
                rsm = Res()
                for d in range(2):
                    ld(dec0[:, d, :], W["rw_decay0"][l, d].rearrange("(c p) -> p c", p=128), (), [rsm], nonc=True)
                    ld(a0[:, d, :], W["rw_a0"][l, d].rearrange("(c p) -> p c", p=128), (), [rsm], nonc=True)
                kkv, r1 = load_vec_fm(st, "a_kk", W["rw_k_k"][l], 4)
                kav, r2 = load_vec_fm(st, "a_ka", W["rw_k_a"][l], 4)
                rkv, r3 = load_vec_fm(st, "a_rk", W["rw_r_k"][l], 4)
                D2 = sbt(st, "a_d2", [128, 512], BF16)
                A2 = sbt(st, "a_a2", [128, 512], BF16)
                G2 = sbt(st, "a_g2", [128, 2, 512], BF16)
                ld(D2[:], W["rw_decay2"][l].rearrange("d l f -> (d l) f"), (), [rsm], q="pool")
                ld(A2[:], W["rw_a2"][l].rearrange("d l f -> (d l) f"), (), [rsm], q="pool")
                ld(G2[:, 0, :], W["rw_g2"][l, 0:128, :], (), [rsm], q="pool")
                ld(G2[0:32, 1, :], W["rw_g2"][l, 128:160, :], (), [rsm], q="pool")
                scanm = sbt(st, "a_scanm", [128, 512], F32)
                ld(scanm[:], cst_in["scanmask"][:, :], (), [rsm])
                rsmall = [rsm, r1, r2, r3, rcw]

                XN = sbt(st, "a_xn", [128, 8, 512], BF16)
                rXN = Res()
                RSt = sbt(st, "a_rs", [128, 512], F32)
                rRS = Res()
                QK = sbt(st, "a_qk", [128, 8, 512], BF16)
                rQK = Res()
                SQ, rSQ = QK, rQK
                VT = sbt(st, "a_vt", [128, 4, 512], BF16)
                rVT = Res()
                QM = sbt(st, "a_qm", [128, 4, 512], BF16)
                rQM = Res()
                PT = [sbt(st, "a_pt%d" % i, [128, 512], BF16) for i in range(2)]
                rPT = [Res(), Res()]
                RD = sbt(st, "a_rd", [128, 512], F32)
                rRD = Res()
                OM = sbt(st, "a_om", [128, 4, 512], BF16)
                rOM = Res()
                ZC = sbt(st, "a_zc", [128, 16, 512], F32)
                rZC = Res()
                X, rX = ZC[:, 0:8, :], rZC
                HZ = sbt(st, "a_hz", [128, 16, max(NH5, 2)], F32)
                rHZ = Res()
                xv = xT[src].rearrange("(c p) t -> p c t", p=128)
                rwcols = [1536 + 128 * i for i in range(15)] + [1536 + 1920]
                rwm = [128] * 15 + [32]
                if NT > 1:
                    XH = sbt(st, "a_xh", [128, 8, NH5], F32)
                    rXH = Res()
                    XHN = sbt(st, "a_xhn", [128, 8, NH5], BF16)
                    rXHN = Res()
                    for c in range(8):
                        srcv = xT[src][c * 128:(c + 1) * 128, 511:T - 1].rearrange("p (j w) -> p j w", w=512)[:, :, 0:2]
                        ld(XH[:, c, :].rearrange("p (j w) -> p j w", w=2), srcv, [R_xT[src]], [rXH], nonc=True)
                    rmsnorm_fm(XH, rXH, XHN, rXHN, gain, rg, NH5, SQ, rSQ, RSt, rRS)
                    memset("dve", HZ[:], 0.0, [rHZ])
                    for zc in range(16):
                        bank = next_ps()
                        m = rwm[zc]
                        for kc in range(8):
                            mm(psum[bank][0:m, 0:NH5], Wi[:, kc, rwcols[zc]:rwcols[zc] + m], XHN[:, kc, :], kc == 0, kc == 7,
                               [rWi, rXHN], [R_ps[bank]])
                        tt("dve", HZ[0:m, zc, :].rearrange("p (j w) -> p j w", w=2),
                           psum[bank][0:m, 0:NH5].rearrange("p (j w) -> p j w", w=2),
                           bm[0:m, 0:NT - 1].unsqueeze(2).to_broadcast([m, NT - 1, 2]), ALU.mult, [R_ps[bank], R_c], [rHZ])

                def tmp(name, dt=F32):
                    return sbt(st, "a_t_" + name, [128, 512], dt), Res(name)

                TH, rTH = tmp("th", BF16)
                XAb, rXAb = tmp("xab", BF16)
                XGb = sbt(st, "a_t_xgb", [128, 2, 512], BF16)
                rXGb = Res()
                SGm, rSGm = tmp("sg")
                Pc, rPc = tmp("pc")
                Pe, rPe = tmp("pe")
                Pt, rPt = tmp("pt")
                E1, rE1 = tmp("e1")
                E2, rE2 = tmp("e2")
                Ee, rEe = tmp("ee")
                Eb, rEb = tmp("eb")
                AS, rAS = tmp("as")
                KD, rKD = tmp("kd")
                KK, rKK = tmp("kk")
                KQ, rKQ = tmp("kq", BF16)
                Bv, rBv = tmp("b")
                BON, rBON = tmp("bon")
                RKb, rRKb = tmp("rkb", BF16)
                TMP, rTMP = tmp("tmp")
                RI, rRI = TMP, rTMP
                OUTS = {}
                for nm in ("at", "bt", "bb", "rt", "kt", "kb"):
                    OUTS[nm] = (sbt(st, "a_o_" + nm, [128, 512], BF16), Res(nm))
                VRb, rVRb = tmp("vrb", BF16)
                Gb, rGb = tmp("gb", BF16)
                WTo = sbt(st, "a_wto", [128, 8], F32)
                rWTo = Res()

                for s in range(NT):
                    t0 = s * 512
                    slot = s // SLOT_ST
                    ld(X[:], xv[:, :, t0:t0 + 512], [R_xT[src]], [rX])
                    rmsnorm_fm(X, rX, XN, rXN, gain, rg, 512, SQ, rSQ, RSt, rRS)
                    def d_qk(c):
                        bank = next_ps()
                        for kc in range(8):
                            mm(psum[bank][:, :], Wi[:, kc, c * 128:(c + 1) * 128], XN[:, kc, :], kc == 0, kc == 7,
                               [rWi, rXN], [R_ps[bank]])
                        if c < 4:
                            act(QK[:, c, :], psum[bank][:, :], AF.Copy, [R_ps[bank]], [rQK], scale=0.125)
                        else:
                            cp("dve", QK[:, c, :], psum[bank][:, :], [R_ps[bank]], [rQK])
                        if c == 3:
                            stor(qT.rearrange("(c p) t -> p c t", p=128)[:, :, t0:t0 + 512], QK[:, 0:4, :], [rQK], [rscr("qT")], rQK)
                        if c == 7:
                            stor(kT.rearrange("(c p) t -> p c t", p=128)[:, :, t0:t0 + 512], QK[:, 4:8, :], [rQK], [rscr("kT")], rQK)

                    def d_v(tl):
                        bank = next_ps()
                        for kc in range(8):
                            mm(psum[bank][:, :], XN[:, kc, tl * 128:(tl + 1) * 128], Wi[:, kc, 1024:1536], kc == 0, kc == 7,
                               [rWi, rXN], [R_ps[bank]])
                        cp("act" if tl % 2 else "dve", VT[:, tl, :], psum[bank][:, :], [R_ps[bank]], [rVT])
                        if tl == 3:
                            stor(vtm[t0:t0 + 512, :].rearrange("(c p) f -> p c f", p=128), VT[:], [rVT], [rscr("vtm")], rVT)

                    def d_mq(h):
                        bank = next_ps()
                        for kc in range(8):
                            mm(psum[bank][:, :], Wi[:, kc, 3488 + h * 128:3488 + (h + 1) * 128], XN[:, kc, :], kc == 0, kc == 7,
                               [rWi, rXN], [R_ps[bank]])
                        act(QM[:, h, :], psum[bank][:, :], AF.Copy, [R_ps[bank]], [rQM], scale=128 ** -0.5)

                    def d_ma(h):
                        for mc in range(2):
                            bank = next_ps()
                            mm(psum[bank][:, :], KmT[:, slot, h, mc * 128:(mc + 1) * 128], QM[:, h, :], True, True,
                               [rKV, rQM], [R_ps[bank]])
                            act(PT[mc][:], psum[bank][:, :], AF.Exp, [R_ps[bank]], [rPT[mc]])
                        bo, bd_ = next_ps(), next_ps()
                        for mc in range(2):
                            mm(psum[bo][:, :], Vm[:, slot, mc, h * 128:(h + 1) * 128], PT[mc][:], mc == 0, mc == 1,
                               [rKV, rPT[mc]], [R_ps[bo]])
                        for mc in range(2):
                            mm(psum[bd_][:, :], ones_b[:], PT[mc][:], mc == 0, mc == 1, [R_c, rPT[mc]], [R_ps[bd_]])
                        act(RD[:], psum[bd_][:, :], AF.Ln, [R_ps[bd_]], [rRD])
                        act(RD[:], RD[:], AF.Exp, [rRD], [rRD], scale=-1.0)
                        tt("dve", OM[:, h, :], psum[bo][:, :], RD[:], ALU.mult, [R_ps[bo], rRD], [rOM])
                        if h == 3:
                            stor(omT.rearrange("(c p) t -> p c t", p=128)[:, :, t0:t0 + 512], OM[:], [rOM], [rscr("omT")], rOM)

                    def dense_slice(it):
                        d_qk(it)
                        if it % 2 == 0:
                            d_v(it // 2)
                            d_mq(it // 2)
                        else:
                            d_ma(it // 2)

                    rZCc = [Res() for _ in range(16)]
                    for r_ in rZCc:
                        for k_, v_ in list(rZC.r.items()) + list(rZC.w.items()):
                            if r_.r.get(k_, 0) < v_:
                                r_.r[k_] = v_
                    for z0 in range(0, 16, 2):
                        grp = []
                        for zc in (z0, z0 + 1):
                            bank = next_ps()
                            m = rwm[zc]
                            for kc in range(8):
                                mm(psum[bank][0:m, :], Wi[:, kc, rwcols[zc]:rwcols[zc] + m], XN[:, kc, :], kc == 0, kc == 7,
                                   [rWi, rXN], [R_ps[bank]])
                            grp.append((zc, m, bank))
                        for (zc, m, bank) in grp:
                            act(ZC[0:m, zc, :], psum[bank][0:m, :], AF.Copy, [R_ps[bank], rcw], [rZCc[zc]], scale=cw[0:m, 1, zc:zc + 1])
                        for (zc, m, bank) in grp:
                            stt("dve", ZC[0:m, zc, 1:512], psum[bank][0:m, 0:511], cw[0:m, 0, zc:zc + 1], ZC[0:m, zc, 1:512], ALU.mult,
                                ALU.add, [R_ps[bank], rcw, rZCc[zc]], [rZCc[zc]])
                        for (zc, m, bank) in grp:
                            stt("dve", ZC[0:m, zc, 0:511], psum[bank][0:m, 1:512], cw[0:m, 2, zc:zc + 1], ZC[0:m, zc, 0:511], ALU.mult,
                                ALU.add, [R_ps[bank], rcw, rZCc[zc]], [rZCc[zc]])
                        for (zc, m, bank) in grp:
                            if s > 0:
                                stt("dve", ZC[0:m, zc, 0:1], HZ[0:m, zc, 2 * (s - 1):2 * (s - 1) + 1], cw[0:m, 0, zc:zc + 1],
                                    ZC[0:m, zc, 0:1], ALU.mult, ALU.add, [rHZ, rcw, rZCc[zc]], [rZCc[zc]])
                        for (zc, m, bank) in grp:
                            if s < NT - 1:
                                stt("dve", ZC[0:m, zc, 511:512], HZ[0:m, zc, 2 * s + 1:2 * s + 2], cw[0:m, 2, zc:zc + 1],
                                    ZC[0:m, zc, 511:512], ALU.mult, ALU.add, [rHZ, rcw, rZCc[zc]], [rZCc[zc]])
                    rZC.r = {}
                    for r_ in rZCc:
                        for k_, v_ in r_.w.items():
                            if rZC.w.get(k_, 0) < v_:
                                rZC.w[k_] = v_
                    act(TH[:], ZC[:, 12, :], AF.Tanh, [rZC], [rTH])
                    cp("pool", XAb[:], ZC[:, 13, :], [rZC], [rXAb])
                    act(XGb[:, 0, :], ZC[:, 14, :], AF.Sigmoid, [rZC], [rXGb])
                    act(XGb[0:32, 1, :], ZC[0:32, 15, :], AF.Sigmoid, [rZC], [rXGb])
                    for hp in range(4):
                        fs = slice(hp * 128, (hp + 1) * 128)
                        Rr = ZC[:, hp, :]
                        Kr = ZC[:, 4 + hp, :]
                        Vr = ZC[:, 8 + hp, :]
                        mm(psum[4][:, :], G2[:, 0, fs], XGb[:, 0, :], True, False, rsmall + [rXGb], [R_ps[4]])
                        mm(psum[4][:, :], G2[0:32, 1, fs], XGb[0:32, 1, :], False, True, rsmall + [rXGb], [R_ps[4]])
                        cp("act", Gb[:], psum[4][:, :], [R_ps[4]], [rGb])
                        stor(rws["g"][fs, t0:t0 + 512], Gb[:], [rGb], [rscr("g")], rGb)
                        cp("pool", VRb[:], Vr, [rZC], [rVRb])
                        stor(rws["v"][fs, t0:t0 + 512], VRb[:], [rVRb], [rscr("v")], rVRb)
                        ts("dve", KK[:], Kr, kkv[:, hp:hp + 1], None, ALU.mult, None, [rZC] + rsmall, [rKK])
                        tt("pool", KQ[:], KK[:], KK[:], ALU.mult, [rKK], [rKQ])
                        mm(psum[4][:, :], bd64_b[:], KQ[:], True, True, [R_c, rKQ], [R_ps[4]])
                        ts("dve", RI[:], psum[4][:, :], 1e-24, None, ALU.max, None, [R_ps[4]], [rRI])
                        ts("dve", RI[:], RI[:], -0.5, None, ALU.pow, None, [rRI], [rRI])
                        tt("dve", KK[:], KK[:], RI[:], ALU.mult, [rKK, rRI], [rKK])
                        for d in range(2):
                            ds = slice(d * 64, (d + 1) * 64)
                            cexp = 0.6065306597126334
                            mm(psum[5][:, :], D2[ds, fs], TH[ds, :], True, True, rsmall + [rTH], [R_ps[5]])
                            mm(psum[6][:, :], A2[ds, fs], XAb[ds, :], True, True, rsmall + [rXAb], [R_ps[6]])
                            act(SGm[:], psum[5][:, :], AF.Sigmoid, [R_ps[5]] + rsmall, [rSGm], bias=dec0[:, d, hp:hp + 1])
                            act(AS[:], psum[6][:, :], AF.Sigmoid, [R_ps[6]] + rsmall, [rAS], bias=a0[:, d, hp:hp + 1])
                            S.op("dve", lambda e: e.tensor_tensor_scan(out=Pc[:], data0=scanm[:], data1=SGm[:], initial=0.0,
                                                                        op0=ALU.mult, op1=ALU.add), [rSGm] + rsmall, [rPc])
                            Pc3 = Pc[:].rearrange("p (c t) -> p c t", t=64)
                            if d == 1:
                                tt("pool", Pe[:].rearrange("p (c t) -> p c t", t=64), Pc3[:, :, 63:64].to_broadcast([128, 8, 64]),
                                   Pc3, ALU.subtract, [rPc], [rPe])
                                tt("pool", Pc[:], Pe[:], SGm[:], ALU.add, [rPe, rSGm], [rPc])
                                totcol = 0
                            else:
                                totcol = 63
                            tt("pool", Pe[:], Pc[:], SGm[:], ALU.subtract, [rPc, rSGm], [rPe])
                            tt("pool", Pt[:].rearrange("p (c t) -> p c t", t=64),
                               Pc3[:, :, totcol:totcol + 1].to_broadcast([128, 8, 64]), Pc3, ALU.subtract, [rPc], [rPt])
                            ts("dve", TMP[:], AS[:], -1.0, kav[:, hp:hp + 1], ALU.add, ALU.mult, [rAS] + rsmall, [rTMP])
                            stt("dve", KD[:], TMP[:], 1.0, Kr, ALU.add, ALU.mult, [rTMP, rZC], [rKD])
                            tt("dve", Bv[:], KK[:], AS[:], ALU.mult, [rKK, rAS], [rBv])
                            stt("dve", RKb[:], Rr, rkv[:, hp:hp + 1], KD[:], ALU.mult, ALU.mult, [rZC, rKD] + rsmall, [rRKb])
                            mm(psum[4][:, :], bd64_b[:], RKb[:], True, True, [R_c, rRKb], [R_ps[4]])
                            act(E1[:], Pc[:], AF.Exp, [rPc], [rE1], scale=-cexp)
                            act(E2[:], Pc[:], AF.Exp, [rPc], [rE2], scale=cexp)
                            act(Ee[:], Pe[:], AF.Exp, [rPe], [rEe], scale=-cexp)
                            act(Eb[:], Pt[:], AF.Exp, [rPt], [rEb], scale=-cexp)
                            if d == 0:
                                tt("dve", BON[:], psum[4][:, :], Vr, ALU.mult, [R_ps[4], rZC], [rBON])
                            else:
                                tt("dve", TMP[:], psum[4][:, :], Vr, ALU.mult, [R_ps[4], rZC], [rTMP])
                                tt("pool", BON[:], BON[:], TMP[:], ALU.add, [rBON, rTMP], [rBON])
                            cp("pool", WTo[:, :], E1[:].rearrange("p (c t) -> p c t", t=64)[:, :, totcol], [rE1], [rWTo])
                            stor(rws["wtot", d][fs, s * 8:(s + 1) * 8], WTo[:], [rWTo], [rscr(("wtot", d))], rWTo)
                            o, ro = OUTS["rt"]
                            tt("pool", o[:], Rr, E1[:], ALU.mult, [rZC, rE1], [ro])
                            o, ro = OUTS["bt"]
                            tt("dve", o[:], Bv[:], E2[:], ALU.mult, [rBv, rE2], [ro])
                            o, ro = OUTS["kt"]
                            tt("pool", o[:], KD[:], E2[:], ALU.mult, [rKD, rE2], [ro])
                            o, ro = OUTS["at"]
                            stt("dve", o[:], KK[:], -1.0, Ee[:], ALU.mult, ALU.mult, [rKK, rEe], [ro])
                            o, ro = OUTS["bb"]
                            tt("dve", o[:], Bv[:], Eb[:], ALU.mult, [rBv, rEb], [ro])
                            o, ro = OUTS["kb"]
                            tt("pool", o[:], KD[:], Eb[:], ALU.mult, [rKD, rEb], [ro])
                            for nm in ("rt", "bt", "kt", "at", "bb", "kb"):
                                o, ro = OUTS[nm]
                                stor(rws[nm, d][fs, t0:t0 + 512], o[:], [ro], [rscr((nm, d))], ro)
                            dense_slice(hp * 2 + d)
                        stor(rws["bonus"][fs, t0:t0 + 512], BON[:], [rBON], [rscr("bonus")], rBON)
            S.barrier()

        def phase_b1(l):
            with ExitStack() as st:
                Bt = sbt(st, "b1_bt", [64, 8 * 17 * 64], BF16)
                rBt = Res()
                ld(Bt[:], btab_in[l], (), [rBt], q="pool")
                Bt4 = Bt[:].rearrange("p (h o k) -> p h o k", h=8, o=17)
                RB = sbt(st, "b1_rb", [128, nslots_na], F32)
                rRB = Res()
                ld(RB[:], rowbias_in[:, :], (), [rRB])
                NR = min(24, ROWS)
                KW = [sbt(st, "b1_kw%d" % i, [64, 8, NR * 64], BF16) for i in range(2)]
                rKW = [Res(), Res()]
                QW = [sbt(st, "b1_qw%d" % i, [64, 8, 512], BF16) for i in range(2)]
                rQW = [Res(), Res()]
                VE = [sbt(st, "b1_ve%d" % i, [128, NR // 2, 512], BF16) for i in range(2)]
                rVE = [Res(), Res()]
                VO = [sbt(st, "b1_vo%d" % i, [128, NR // 2 - 1, 512], BF16) for i in range(2)]
                rVO = [Res(), Res()]
                PTt = [sbt(st, "b1_pt%d" % i, [128, 512], BF16) for i in range(2)]
                rPTt = [Res(), Res()]
                RDt = sbt(st, "b1_rd", [64, 512], F32)
                rRDt = Res()
                ON = [sbt(st, "b1_on%d" % i, [64, 8, 512], BF16) for i in range(2)]
                rON = [Res(), Res()]
                qv = qT.rearrange("(h d) t -> d h t", d=64)
                kv = kT.rearrange("(h d) t -> d h t", d=64)

                def loads(s):
                    b = s % 2
                    i0 = s * 8
                    rb = min(max(i0 - 8, 0), ROWS - NR)
                    ld(QW[b][:], qv[:, :, s * 512:(s + 1) * 512], [rscr("qT")], [rQW[b]])
                    ld(KW[b][:], kv[:, :, rb * 64:(rb + NR) * 64], [rscr("kT")], [rKW[b]])
                    ld(VE[b][:], vtm[rb * 64:(rb + NR) * 64, :].rearrange("(c p) f -> p c f", p=128), [rscr("vtm")], [rVE[b]])
                    ld(VO[b][:], vtm[rb * 64 + 64:(rb + NR) * 64 - 64, :].rearrange("(c p) f -> p c f", p=128), [rscr("vtm")],
                       [rVO[b]])
                    return rb

                slot = 0
                pk = 0
                sk = [0]
                rbs = {0: loads(0)}
                for s in range(NT):
                    b = s % 2
                    if s + 1 < NT:
                        rbs[s + 1] = loads(s + 1)
                    rb = rbs[s]
                    for iq in range(8):
                        i = s * 8 + iq
                        band = bands[i]
                        for ci, r in enumerate(band):
                            o = r - i + 8
                            kc0 = (r - rb) * 64
                            sk[0] = (sk[0] + 1) % 3
                            bank = sk[0]
                            for h in range(8):
                                mm(psum[bank][:, h * 64:(h + 1) * 64], KW[b][:, h, kc0:kc0 + 128], QW[b][:, h, iq * 64:(iq + 1) * 64],
                                   True, False, [rKW[b], rQW[b]], [R_ps[bank]])
                                mm(psum[bank][:, h * 64:(h + 1) * 64], Bt4[:, h, o:o + 2, :].rearrange("p o k -> p (o k)"),
                                   ident_b[0:64, 0:64], False, True, [rBt, R_c], [R_ps[bank]])
                            pb = pk % 2
                            pk += 1
                            act(PTt[pb][:], psum[bank][:, :], AF.Exp, [R_ps[bank], rRB], [rPTt[pb]], bias=RB[:, slot:slot + 1])
                            slot += 1
                            if (r - rb) % 2 == 0:
                                Vc, rVc = VE[b][:, (r - rb) // 2, :], rVE[b]
                            else:
                                Vc, rVc = VO[b][:, (r - rb - 1) // 2, :], rVO[b]
                            first, last = ci == 0, ci == len(band) - 1
                            bo, bd_ = (5, 6) if i % 2 == 0 else (3, 4)
                            for h in range(8):
                                mm(psum[bo][0:64, h * 64:(h + 1) * 64], Vc[:, h * 64:(h + 1) * 64], PTt[pb][:, h * 64:(h + 1) * 64],
                                   first and h == 0, last, [rVc, rPTt[pb]], [R_ps[bo]])
                            mm(psum[bd_][0:64, :], ones_b[:, 0:64], PTt[pb][:], first, last, [R_c, rPTt[pb]], [R_ps[bd_]])
                        act(RDt[:], psum[bd_][0:64, :], AF.Ln, [R_ps[bd_]], [rRDt])
                        act(RDt[:], RDt[:], AF.Exp, [rRDt], [rRDt], scale=-1.0)
                        tt("dve", ON[b][:, :, iq * 64:(iq + 1) * 64], psum[bo][0:64, :].rearrange("p (h q) -> p h q", h=8),
                           RDt[:].rearrange("p (h q) -> p h q", h=8), ALU.mult, [R_ps[bo], rRDt], [rON[b]])
                    stor(onaT.rearrange("(h d) t -> d h t", d=64)[:, :, s * 512:(s + 1) * 512], ON[b][:], [rON[b]],
                         [rscr("onaT")], rON[b])
            S.barrier()

        def phase_b2(l):
            with ExitStack() as st:
                mk = {}
                rM = Res()
                for nm in ("ls4", "li4", "us4", "ui4"):
                    mk[nm] = sbt(st, "b2_" + nm, [128, 512], F32)
                    ld(mk[nm][:], cst_in[nm][:, :], (), [rM])
                MS = [mk["ls4"], mk["us4"]]
                MI = [mk["li4"], mk["ui4"]]
                MP = [mk["us4"], mk["ls4"]]
                lnw = sbt(st, "b2_lnw", [128, 4], F32)
                lnb = sbt(st, "b2_lnb", [128, 4], F32)
                ld(lnw[:], W["rw_lnx_w"][l].rearrange("(c p) -> p c", p=128), (), [rM], nonc=True)
                ld(lnb[:], W["rw_lnx_b"][l].rearrange("(c p) -> p c", p=128), (), [rM], nonc=True)
                names = ("at", "bt", "bb", "rt", "kt", "kb", "v")
                IN = {nm: [sbt(st, "b2_i_%s%d" % (nm, i), [64, 8, 512], BF16) for i in range(2)] for nm in names}
                rIN = [Res(), Res()]
                WT = [sbt(st, "b2_wt%d" % i, [64, 8, 8], F32) for i in range(2)]
                AM = sbt(st, "b2_am", [128, 4, 8, 128], BF16)
                rAMt = [[Res(), Res()] for _ in range(4)]
                PPp = [sbt(st, "b2_ppp%d" % i, [128, 8, 128], BF16) for i in range(2)]
                PPt = [sbt(st, "b2_ppt%d" % i, [128, 8, 128], BF16) for i in range(2)]
                rPP = [[Res(), Res()], [Res(), Res()]]
                rPT = [[Res(), Res()], [Res(), Res()]]
                Xc = [sbt(st, "b2_x%d" % i, [128, 8, 128], BF16) for i in range(2)]
                rXc = [[Res(), Res()], [Res(), Res()]]
                TMx = sbt(st, "b2_tm", [128, 9 * 192], BF16)
                rTM = [Res(), Res()]
                memset("dve", TMx[:], 0.0, rTM)
                TMf = TMx[:, :]
                TM = TMx[:, 0:8 * 192].rearrange("p (h q) -> p h q", h=8)
                RH = sbt(st, "b2_rh", [64, 8, 2, 128], BF16)
                rRH = Res()
                MTt = sbt(st, "b2_mt", [64, 8, 2, 128], BF16)
                rMTt = Res()
                MTf = sbt(st, "b2_mtf", [64, 8, 2, 64], F32)
                rMTf = Res()
                DW = sbt(st, "b2_dw", [64, 8, 2, 64], F32)
                rDW = Res()
                N0 = sbt(st, "b2_n0", [64, 8, 2, 64], F32)
                rN0 = Res()
                Sb = sbt(st, "b2_sb", [64, 8, 64], BF16)
                rSb = Res()
                YF = [sbt(st, "b2_yf%d" % i, [128, 512], F32) for i in range(2)]
                rYF = [Res(), Res()]
                YL = [sbt(st, "b2_yl%d" % i, [128, 512], F32) for i in range(2)]
                rYL = [Res(), Res()]
                YC = sbt(st, "b2_yc", [128, 512], F32)
                rYC = Res()
                YS = sbt(st, "b2_ys", [128, 512], F32)
                rYS = Res()
                YN = sbt(st, "b2_yn", [128, 512], BF16)
                rYN = Res()
                st8 = sbt(st, "b2_st8", [128, 8], F32)
                rst8 = Res()
                st8b = sbt(st, "b2_st8b", [128, 8], F32)
                rst8b = Res()
                BONt = [sbt(st, "b2_bon%d" % i, [128, 4, 128], F32) for i in range(2)]
                Gt = [sbt(st, "b2_g%d" % i, [128, 4, 128], BF16) for i in range(2)]
                rBG = [Res(), Res()]
                OT = sbt(st, "b2_ot", [128, 4, 128], F32)
                rOT = Res()
                OR = [sbt(st, "b2_or%d" % i, [128, 4, 128], BF16) for i in range(2)]
                rOR = [Res(), Res()]
                memset("dve", RH[:], 0.0, [rRH])
                memset("dve", MTt[:], 0.0, [rMTt])

                def hview(ap):
                    return ap.rearrange("(h j) t -> j h t", j=64)

                def v4(ap):
                    return ap.rearrange("p (h t) -> p h t", h=4)

                def finalize(yb, gt0):
                    Y3 = YF[yb][:].rearrange("p (h i) -> p h i", h=8)
                    S.op("dve", lambda e, Y3=Y3: e.tensor_reduce(out=st8[:], in_=Y3, axis=AX.X, op=ALU.add),
                         [rYF[yb]], [rst8])
                    ts("dve", st8[:], st8[:], -1.0 / 64, None, ALU.mult, None, [rst8], [rst8])
                    tt("pool", YC[:].rearrange("p (h i) -> p h i", h=8), Y3,
                       st8[:].unsqueeze(2).to_broadcast([128, 8, 64]), ALU.add, [rYF[yb], rst8], [rYC])
                    act(YS[:], YC[:], AF.Square, [rYC], [rYS])
                    S.op("dve", lambda e: e.tensor_reduce(out=st8b[:], in_=YS[:].rearrange("p (h i) -> p h i", h=8),
                                                          axis=AX.X, op=ALU.add), [rYS], [rst8b])
                    ts("dve", st8b[:], st8b[:], 1.0 / 64, 64e-5, ALU.mult, ALU.add, [rst8b], [rst8b])
                    ts("dve", st8b[:], st8b[:], -0.5, None, ALU.pow, None, [rst8b], [rst8b])
                    tt("pool", YN[:].rearrange("p (h i) -> p h i", h=8), YC[:].rearrange("p (h i) -> p h i", h=8),
                       st8b[:].unsqueeze(2).to_broadcast([128, 8, 64]), ALU.mult, [rYC, rst8b], [rYN])
                    for fc in range(4):
                        tr(psT[:, fc * 128:(fc + 1) * 128], YN[:, fc * 128:(fc + 1) * 128], ident_b[:],
                           [rYN, R_c], [R_psT])
                    pv = psT[:, 0:512].rearrange("p (c t) -> p c t", c=4)
                    tt("dve", OT[:], pv, lnw[:].unsqueeze(2).to_broadcast([128, 4, 128]), ALU.mult, [R_psT, rM], [rOT])
                    tt("pool", OT[:], OT[:], lnb[:].unsqueeze(2).to_broadcast([128, 4, 128]), ALU.add, [rOT, rM], [rOT])
                    tt("dve", OT[:], OT[:], BONt[yb][:], ALU.add, [rOT, rBG[yb]], [rOT])
                    tt("pool", OR[yb][:], OT[:], Gt[yb][:], ALU.mult, [rOT, rBG[yb]], [rOR[yb]])
                    stor(orwT.rearrange("(c p) t -> p c t", p=128)[:, :, gt0:gt0 + 128], OR[yb][:], [rOR[yb]],
                         [rscr("orwT")], rOR[yb])

                pending = []
                for d in range(2):
                    memset("dve", Sb[:], 0.0, [rSb])
                    order = list(range(NT)) if d == 0 else list(range(NT - 1, -1, -1))

                    def loads(idx, d=d, order=order):
                        s = order[idx]
                        b = idx % 2
                        for nm in names:
                            srcap = rws["v"] if nm == "v" else rws[nm, d]
                            rk = rscr("v") if nm == "v" else rscr((nm, d))
                            ld(IN[nm][b][:], hview(srcap)[:, :, s * 512:(s + 1) * 512], [rk], [rIN[b]])
                        ld(WT[b][:], hview(rws["wtot", d])[:, :, s * 8:(s + 1) * 8], [rscr(("wtot", d))], [rIN[b]], nonc=True)

                    loads(0)
                    tcount = 0
                    for idx, s in enumerate(order):
                        b = idx % 2
                        if idx + 1 < NT:
                            loads(idx + 1)
                        rI = rIN[b]
                        if d == 0 and s > 0:
                            ts("dve", Sb[:], Sb[:], bm[0:64, s - 1:s], None, ALU.mult, None, [rSb, R_c], [rSb])
                        if d == 1 and s < NT - 1:
                            ts("dve", Sb[:], Sb[:], bm[0:64, s:s + 1], None, ALU.mult, None, [rSb, R_c], [rSb])
                        tiles = list(range(4)) if d == 0 else [3, 2, 1, 0]
                        for tl in tiles:
                            tc_ = slice(tl * 128, (tl + 1) * 128)
                            gt0 = s * 512 + tl * 128
                            yb = tcount % 2
                            tcount += 1
                            if d == 1:
                                ld(YL[yb][:], yfw[gt0:gt0 + 128, :], [rscr("yfw")], [rYL[yb]])
                                ld(BONt[yb][:], rws["bonus"].rearrange("(c p) t -> p c t", p=128)[:, :, gt0:gt0 + 128],
                                   [rscr("bonus")], [rBG[yb]])
                                ld(Gt[yb][:], rws["g"].rearrange("(c p) t -> p c t", p=128)[:, :, gt0:gt0 + 128], [rscr("g")],
                                   [rBG[yb]])

                            def I(nm, h):
                                return IN[nm][b][:, h, tc_]

                            typs = (("bt", "at", MS), ("kt", "at", MS), ("bt", "rt", MI), ("kt", "rt", MI))
                            for ty, (ln, rn, msk) in enumerate(typs):
                                for hg in range(2):
                                    bank = next_ps()
                                    for j in range(4):
                                        h = hg * 4 + j
                                        mm(psum[bank][:, j * 128:(j + 1) * 128], I(ln, h), I(rn, h), True, True, [rI], [R_ps[bank]])
                                    tt("dve", AM[:, ty, hg * 4:(hg + 1) * 4, :], v4(psum[bank][:, :]), v4(msk[d][:]), ALU.mult,
                                       [R_ps[bank], rM], [rAMt[ty][hg]])
                            for hg in range(2):
                                bank = next_ps()
                                for j in range(4):
                                    h = hg * 4 + j
                                    mm(psum[bank][:, j * 128:(j + 1) * 128], I("at", h), I("bt", h), True, True, [rI], [R_ps[bank]])
                                tt("dve", PPp[0][:, hg * 4:(hg + 1) * 4, :], v4(psum[bank][:, :]), v4(MP[d][:]), ALU.mult,
                                   [R_ps[bank], rM], [rPP[0][hg]])
                            for hg in range(2):
                                tb = psT[:, :] if hg == 0 else psum[5][:, :].bitcast(BF16)
                                rtb = R_psT if hg == 0 else R_ps[5]
                                for j in range(4):
                                    h = hg * 4 + j
                                    for q, nm in enumerate(("v", "bb", "kb", "at")):
                                        tr(tb[:, j * 256 + q * 64:j * 256 + (q + 1) * 64], I(nm, h), ident_b[0:64, 0:64],
                                           [rI, R_c], [rtb])
                                pv4 = tb.rearrange("p (h q) -> p h q", h=4)
                                cp("act", TM[:, hg * 4:(hg + 1) * 4, :], pv4[:, :, 0:192], [rtb], [rTM[hg]])
                                cp("act", Xc[0][:, hg * 4:(hg + 1) * 4, 0:64], pv4[:, :, 192:256], [rtb], [rXc[0][hg]])
                            for hg in range(2):
                                bank = next_ps()
                                for j in range(4):
                                    h = hg * 4 + j
                                    mm(psum[bank][:, j * 64:(j + 1) * 64], AM[:, 1, h, :], TM[:, h, 0:64], True, True,
                                       [rAMt[1][hg], rTM[hg]], [R_ps[bank]])
                                cp("dve", Xc[0][:, hg * 4:(hg + 1) * 4, 64:128],
                                   psum[bank][:, 0:256].rearrange("p (h i) -> p h i", h=4), [R_ps[bank]], [rXc[0][hg]])
                            if LVL < 2:
                                continue
                            for hg in range(2):
                                for j in range(4):
                                    h = hg * 4 + j
                                    mm(psum[hg][:, j * 128:(j + 1) * 128], ident_b[:], Xc[0][:, h, :], j == 0, False,
                                       [R_c, rXc[0][hg]], [R_ps[hg]])
                            for k in range(6):
                                cur, nxt = k % 2, (k + 1) % 2
                                for hg in range(2):
                                    rP = rAMt[0][hg] if k == 0 else rPT[cur][hg]
                                    bx = hg
                                    for j in range(4):
                                        h = hg * 4 + j
                                        Pt_h = AM[:, 0, h, :] if k == 0 else PPt[cur][:, h, :]
                                        mm(psum[bx][:, j * 128:(j + 1) * 128], Pt_h, Xc[cur][:, h, :], False, k == 5,
                                           [rP, rXc[cur][hg]], [R_ps[bx]])
                                    cp("act", Xc[nxt][:, hg * 4:(hg + 1) * 4, :], v4(psum[bx][:, :]), [R_ps[bx]], [rXc[nxt][hg]])
                                    if k < 5:
                                        bp_, bt_ = 2 + hg, 5 + hg
                                        rPp = rPP[cur][hg]
                                        for j in range(4):
                                            h = hg * 4 + j
                                            Pt_h = AM[:, 0, h, :] if k == 0 else PPt[cur][:, h, :]
                                            Pp_h = PPp[cur][:, h, :]
                                            mm(psum[bp_][:, j * 128:(j + 1) * 128], Pt_h, Pp_h, True, True, [rP, rPp], [R_ps[bp_]])
                                            mm(psum[bt_][:, j * 128:(j + 1) * 128], Pp_h, Pt_h, True, True, [rP, rPp], [R_ps[bt_]])
                                        cp("dve", PPp[nxt][:, hg * 4:(hg + 1) * 4, :], v4(psum[bp_][:, :]), [R_ps[bp_]], [rPP[nxt][hg]])
                                        cp("act", PPt[nxt][:, hg * 4:(hg + 1) * 4, :], v4(psum[bt_][:, :]), [R_ps[bt_]], [rPT[nxt][hg]])
                            XF = Xc[0]
                            rXF = rXc[0]
                            if LVL < 3:
                                continue
                            while pending:
                                finalize(*pending.pop(0))
                            for h in range(8):
                                hg = h // 4
                                ys = slice(h * 64, (h + 1) * 64)
                                mm(psum[4][:, ys], AM[:, 2, h, :], XF[:, h, 64:128], h == 0, False, [rAMt[2][hg], rXF[hg]], [R_ps[4]])
                                mm(psum[4][:, ys], AM[:, 3, h, :], TM[:, h, 0:64], False, False, [rAMt[3][hg], rTM[hg]], [R_ps[4]])
                            for hg in range(2):
                                bank = hg
                                for j in range(4):
                                    h = hg * 4 + j
                                    mm(psum[bank][0:64, j * 128:(j + 1) * 128], XF[:, h, 0:64], AM[:, 2, h, :], True, True,
                                       [rXF[hg], rAMt[2][hg]], [R_ps[bank]])
                                for c in range(2):
                                    cs = slice(c * 64, (c + 1) * 64)
                                    tt("dve", RH[:, hg * 4:(hg + 1) * 4, c, cs], v4(psum[bank][0:64, :])[:, :, cs],
                                       IN["rt"][b][:, hg * 4:(hg + 1) * 4, tl * 128 + c * 64:tl * 128 + (c + 1) * 64], ALU.add,
                                       [R_ps[bank], rI], [rRH])
                            if LVL < 4:
                                continue
                            tt("pool", DW[:], ident_f[0:64, 0:64].unsqueeze(1).unsqueeze(1).to_broadcast([64, 8, 2, 64]),
                               WT[b][:, :, tl * 2:tl * 2 + 2].unsqueeze(3).to_broadcast([64, 8, 2, 64]), ALU.mult, [rI, R_c], [rDW])
                            for hg in range(2):
                                bm_, bn_ = 2 + hg, 5 + hg
                                for j in range(4):
                                    h = hg * 4 + j
                                    for c in range(2):
                                        ps_ = slice(0, 64) if c == 0 else slice(0, 128)
                                        col = (j * 2 + c) * 64
                                        mm(psum[bm_][:, col:col + 64], XF[ps_, h, 0:128], TM[ps_, h, 64:128], True, True,
                                           [rXF[hg], rTM[hg]], [R_ps[bm_]])
                                        mm(psum[bn_][:, col:col + 64], TM[ps_, h, 64:192], XF[ps_, h, 64:128], True, False,
                                           [rXF[hg], rTM[hg]], [R_ps[bn_]])
                                        mm(psum[bn_][:, col:col + 64], TMf[ps_, h * 192 + 128:h * 192 + 256], TM[ps_, h, 0:64], False, True,
                                           [rTM[hg]], [R_ps[bn_]])
                                hs = slice(hg * 4, (hg + 1) * 4)
                                cp("dve", MTf[:, hs, :, :], psum[bm_][0:64, :].rearrange("p (h c i) -> p h c i", h=4, c=2),
                                   [R_ps[bm_]], [rMTf])
                                tt("pool", MTf[:, hs, 1, :], MTf[:, hs, 1, :], MTf[:, hs, 0, :], ALU.subtract, [rMTf], [rMTf])
                                tt("pool", MTt[:, hs, :, 0:64], MTf[:, hs, :, :], DW[:, hs, :, :], ALU.add, [rMTf, rDW], [rMTt])
                                cp("act", N0[:, hs, :, :], psum[bn_][0:64, :].rearrange("p (h c i) -> p h c i", h=4, c=2),
                                   [R_ps[bn_]], [rN0])
                                tt("pool", N0[:, hs, 1, :], N0[:, hs, 1, :], N0[:, hs, 0, :], ALU.subtract, [rN0], [rN0])
                            if LVL < 5:
                                continue
                            for ci, c in enumerate((0, 1) if d == 0 else (1, 0)):
                                for h in range(8):
                                    ys = slice(h * 64, (h + 1) * 64)
                                    mm(psum[4][:, ys], RH[:, h, c, :], Sb[:, h, :], False, ci == 1, [rRH, rSb], [R_ps[4]])
                                bank = ci
                                for h in range(8):
                                    mm(psum[bank][:, h * 64:(h + 1) * 64], MTt[:, h, c, :], Sb[:, h, :], True, True,
                                       [rMTt, rSb], [R_ps[bank]])
                                tt("dve", Sb[:], psum[bank][0:64, :].rearrange("p (h i) -> p h i", h=8), N0[:, :, c, :], ALU.add,
                                   [R_ps[bank], rN0], [rSb])
                            if LVL < 6:
                                continue
                            if d == 0:
                                cp("act", YF[yb][:], psum[4][:, :], [R_ps[4]], [rYF[yb]])
                                stor(yfw[gt0:gt0 + 128, :], YF[yb][:], [rYF[yb]], [rscr("yfw")], rYF[yb])
                            else:
                                tt("dve", YF[yb][:], psum[4][:, :], YL[yb][:], ALU.add, [R_ps[4], rYL[yb]], [rYF[yb]])
                                pending.append((yb, gt0))
                while pending:
                    finalize(*pending.pop(0))
            S.barrier()

        if phases is None:
            phases = ["p0"] + sum([["a%d" % l, "b1%d" % l, "b2%d" % l, "c1%d" % l, "c2%d" % l] for l in range(depth)], []) + ["e"]
        for ph in phases:
            if ph == "p0":
                phase_p0()
            elif ph == "e":
                phase_e(0)
            elif ph[0] == "a":
                phase_a(int(ph[1:]), 0)
            elif ph[:2] == "b1":
                phase_b1(int(ph[2:]))
            elif ph[:2] == "b2":
                phase_b2(int(ph[2:]))
            elif ph[:2] == "c1":
                phase_c1(int(ph[2:]), 0, 1)
            elif ph[:2] == "c2":
                phase_c2(int(ph[2:]), 1, 0)
        S.emit()
        nc._n_inst = S.ninst
    return nc


def core_inputs(x_seqs, mem_seqs, T, RS, typ, weights, depth):
    NT = T // 512
    SLOT_ST = max(NT // 4, 1)
    NSLOT = NT // SLOT_ST
    ROWS = T // 64
    xin = np.zeros((T, D), np.float32)
    memv = np.zeros((NSLOT, 256, D), np.float32)
    if typ == "P":
        xin[:] = x_seqs[0]
        for sl in range(NSLOT):
            memv[sl] = mem_seqs[0]
        starts = {0}
    else:
        L = RS * 64
        for i, xs in enumerate(x_seqs):
            xin[i * L:(i + 1) * L] = xs
            sl0 = (i * L) // (SLOT_ST * 512)
            memv[sl0] = mem_seqs[i]
        starts = set(range(0, T, L))
    bmv = np.ones((128, max(NT - 1, 1)), np.float32)
    for j in range(NT - 1):
        if (j + 1) * 512 in starts:
            bmv[:, j] = 0.0
    bm2 = np.ones((128, max(T // 256 - 1, 1)), np.float32)
    for j in range(T // 256 - 1):
        if (j + 1) * 256 in starts:
            bm2[:, j] = 0.0
    m = {"xin": xin, "mem": memv, "bm": bmv, "bm2": bm2, "rowbias": na_rowbias(ROWS, RS, typ)}
    m["btab"] = np.stack([na_btab(weights["na_rpb"][l]).reshape(64, -1) for l in range(depth)])
    for k, v in make_consts().items():
        m["c_" + k] = v
    for k, v in weights.items():
        if k == "na_rpb":
            continue
        if k == "rw_r_k":
            v = v.reshape(v.shape[0], 512)
        if k == "final_norm":
            v = v.reshape(1, D)
        m[k] = np.ascontiguousarray(v, dtype=np.float32)
    return m


_CACHE = {}


def kernel(**inputs):
    T, RS, depth = 8192, 32, 2
    xp = np.asarray(inputs["x_prompt"], np.float32)
    xs = np.asarray(inputs["x_sample"], np.float32)
    mp = np.asarray(inputs["mem_prompt"], np.float32)
    ms = np.asarray(inputs["mem_sample"], np.float32)
    weights = {k: np.asarray(v, np.float32) for k, v in inputs.items()
               if k not in ("x_prompt", "x_sample", "mem_prompt", "mem_sample")}
    in_maps = []
    for c in range(4):
        in_maps.append(core_inputs([xp[c]], [mp[c]], T, RS, "P", weights, depth))
    for c in range(4):
        sq = [2 * c, 2 * c + 1, 2 * c, 2 * c + 1]
        in_maps.append(core_inputs([xs[i] for i in sq], [ms[i] for i in sq], T, RS, "S", weights, depth))
    if "nc" not in _CACHE:
        _CACHE["nc"] = build(T, RS, depth)
    res = run_bass_kernel_spmd(_CACHE["nc"], in_maps, core_ids=list(range(8)))
    yp = np.stack([res.results[c]["yout"] for c in range(4)]).astype(np.float32)
    ys = np.zeros_like(xs)
    for c in range(4):
        y = res.results[4 + c]["yout"]
        ys[2 * c] = y[0:2048]
        ys[2 * c + 1] = y[2048:4096]
    return (yp, ys)
```
